# Optimizing a Trainium2 kernel written in Bass

```python
import math
import jax, jax.numpy as jnp
from jax import lax
import numpy as np

D_MODEL = 1024
BATCH = 2
SEQ = 8192
DEPTH = 1

CONV_CH = D_MODEL // 2
CONV_WIDTH = 31
MLA_HEADS = 8
QK_NOPE = D_MODEL // 16
QK_ROPE = D_MODEL // 32
V_DIM = D_MODEL // 16
Q_LORA = 3 * D_MODEL // 8
KV_LORA = D_MODEL // 4
N_BRANCH = 2
IN_COLS = 2 * CONV_CH + Q_LORA + KV_LORA + QK_ROPE + N_BRANCH * D_MODEL
MEM_LEN = 256
X_HEADS = 4
X_HEAD_DIM = D_MODEL // 8
D_FF = 4 * D_MODEL
Q_BLOCK = 128
ROPE_THETA = 10000.0
EPS = 1e-6

kernel_name = "hybrid_conformer_mla_gated_block"


def rms_norm(x, g):
    xf = x.astype(jnp.float32)
    y = xf * lax.rsqrt(jnp.mean(xf * xf, axis=-1, keepdims=True) + EPS)
    return (y * g.astype(jnp.float32)).astype(x.dtype)


def layer_norm(x, g, b):
    xf = x.astype(jnp.float32)
    mu = jnp.mean(xf, axis=-1, keepdims=True)
    var = jnp.mean(jnp.square(xf - mu), axis=-1, keepdims=True)
    y = (xf - mu) * lax.rsqrt(var + EPS)
    return (y * g.astype(jnp.float32) + b.astype(jnp.float32)).astype(x.dtype)


def rope_tables(positions):
    half = QK_ROPE // 2
    inv_freq = ROPE_THETA ** (-jnp.arange(half, dtype=jnp.float32) / half)
    ang = positions.astype(jnp.float32)[..., None] * inv_freq
    return jnp.cos(ang), jnp.sin(ang)


def apply_rope(t, cos, sin):
    half = t.shape[-1] // 2
    t1, t2 = t[..., :half], t[..., half:]
    c, s = cos.astype(t.dtype), sin.astype(t.dtype)
    return jnp.concatenate([t1 * c - t2 * s, t2 * c + t1 * s], axis=-1)


def conformer_conv(conv_in, conv_w, conv_b, ln_g, ln_b, w_conv_out):
    a, gt = jnp.split(conv_in, 2, axis=-1)
    z = a * jax.nn.sigmoid(gt)
    rhs = conv_w.astype(z.dtype).reshape(CONV_WIDTH, 1, CONV_CH)
    z = lax.conv_general_dilated(
        z, rhs, window_strides=(1,), padding=[(CONV_WIDTH - 1, 0)],
        dimension_numbers=("NWC", "WIO", "NWC"), feature_group_count=CONV_CH)
    z = z + conv_b
    z = layer_norm(z, ln_g, ln_b)
    z = jax.nn.silu(z)
    return z @ w_conv_out


def mla_attention(c_q, c_kv, k_rope_raw, cos, sin, q_norm_g, w_uq, kv_norm_g, w_ukv, w_mla_out):
    B, S, _ = c_q.shape
    q = rms_norm(c_q, q_norm_g) @ w_uq
    q = q.reshape(B, S, MLA_HEADS, QK_NOPE + QK_ROPE)
    q_nope, q_rope = q[..., :QK_NOPE], q[..., QK_NOPE:]
    q_rope = apply_rope(q_rope, cos[:, :, None, :], sin[:, :, None, :])
    kv = rms_norm(c_kv, kv_norm_g) @ w_ukv
    kv = kv.reshape(B, S, MLA_HEADS, QK_NOPE + V_DIM)
    k_nope, v = kv[..., :QK_NOPE], kv[..., QK_NOPE:]
    k_rope = apply_rope(k_rope_raw, cos, sin)

    scale = (QK_NOPE + QK_ROPE) ** -0.5
    n_blk = S // Q_BLOCK
    qn = (q_nope * scale).reshape(B, n_blk, Q_BLOCK, MLA_HEADS, QK_NOPE).transpose(1, 0, 2, 3, 4)
    qr = (q_rope * scale).reshape(B, n_blk, Q_BLOCK, MLA_HEADS, QK_ROPE).transpose(1, 0, 2, 3, 4)
    key_idx = jnp.arange(S)
    neg = jnp.finfo(jnp.float32).min

    def attend(args):
        qn_b, qr_b, blk = args
        s = jnp.einsum("bqhd,bkhd->bhqk", qn_b, k_nope, preferred_element_type=jnp.float32)
        s = s + jnp.einsum("bqhr,bkr->bhqk", qr_b, k_rope, preferred_element_type=jnp.float32)
        q_idx = blk * Q_BLOCK + jnp.arange(Q_BLOCK)
        mask = key_idx[None, :] <= q_idx[:, None]
        s = jnp.where(mask[None, None], s, neg)
        p = jax.nn.softmax(s, axis=-1).astype(v.dtype)
        return jnp.einsum("bhqk,bkhd->bqhd", p, v)

    o = lax.map(attend, (qn, qr, jnp.arange(n_blk)))
    o = o.transpose(1, 0, 2, 3, 4).reshape(B, S, MLA_HEADS * V_DIM)
    return o @ w_mla_out


def memory_cross_attention(u, mem_n, w_xq, w_xkv, w_xo):
    B, S, _ = u.shape
    q = (u @ w_xq).reshape(B, S, X_HEADS, X_HEAD_DIM) * (X_HEAD_DIM ** -0.5)
    kv = (mem_n @ w_xkv).reshape(B, MEM_LEN, 2, X_HEADS, X_HEAD_DIM)
    k, v = kv[:, :, 0], kv[:, :, 1]
    s = jnp.einsum("bqhd,bkhd->bhqk", q, k, preferred_element_type=jnp.float32)
    p = jax.nn.softmax(s, axis=-1).astype(v.dtype)
    o = jnp.einsum("bhqk,bkhd->bqhd", p, v).reshape(B, S, X_HEADS * X_HEAD_DIM)
    return o @ w_xo


def setup_inputs(seed: int = 0) -> dict:
    key = jax.random.key(seed)
    ks = jax.random.split(key, 32)
    f32 = jnp.float32

    def w(k, shape, fan_in):
        return jax.random.normal(k, shape, f32) * (fan_in ** -0.5)

    def gain(k, shape):
        return 1.0 + 0.02 * jax.random.normal(k, shape, f32)

    L = DEPTH
    x = jax.random.normal(ks[0], (BATCH, SEQ, D_MODEL), f32)
    mem = jax.random.normal(ks[1], (BATCH, MEM_LEN, D_MODEL), f32)
    offsets = jax.random.randint(ks[2], (BATCH, 1), 0, 4096, dtype=jnp.int32)
    positions = offsets + jnp.arange(SEQ, dtype=jnp.int32)[None, :]
    return {
        "x": x,
        "mem": mem,
        "positions": positions,
        "norm_mix_g": gain(ks[3], (L, D_MODEL)),
        "w_in": w(ks[4], (L, D_MODEL, IN_COLS), D_MODEL),
        "conv_w": w(ks[5], (L, CONV_WIDTH, CONV_CH), CONV_WIDTH),
        "conv_b": 0.02 * jax.random.normal(ks[6], (L, CONV_CH), f32),
        "conv_ln_g": gain(ks[7], (L, CONV_CH)),
        "conv_ln_b": 0.02 * jax.random.normal(ks[8], (L, CONV_CH), f32),
        "w_conv_out": w(ks[9], (L, CONV_CH, D_MODEL), CONV_CH),
        "q_norm_g": gain(ks[10], (L, Q_LORA)),
        "w_uq": w(ks[11], (L, Q_LORA, MLA_HEADS * (QK_NOPE + QK_ROPE)), Q_LORA),
        "kv_norm_g": gain(ks[12], (L, KV_LORA)),
        "w_ukv": w(ks[13], (L, KV_LORA, MLA_HEADS * (QK_NOPE + V_DIM)), KV_LORA),
        "w_mla_out": w(ks[14], (L, MLA_HEADS * V_DIM, D_MODEL), MLA_HEADS * V_DIM),
        "w_out": w(ks[15], (L, D_MODEL, D_MODEL), D_MODEL),
        "norm_xattn_g": gain(ks[16], (L, D_MODEL)),
        "norm_mem_g": gain(ks[17], (L, D_MODEL)),
        "w_xq": w(ks[18], (L, D_MODEL, X_HEADS * X_HEAD_DIM), D_MODEL),
        "w_xkv": w(ks[19], (L, D_MODEL, 2 * X_HEADS * X_HEAD_DIM), D_MODEL),
        "w_xo": w(ks[20], (L, X_HEADS * X_HEAD_DIM, D_MODEL), X_HEADS * X_HEAD_DIM),
        "norm_mlp_g": gain(ks[21], (L, D_MODEL)),
        "w_mlp1": w(ks[22], (L, D_MODEL, D_FF), D_MODEL),
        "w_mlp2": w(ks[23], (L, D_FF, D_MODEL), D_FF),
        "final_norm_g": gain(ks[24], (D_MODEL,)),
    }


def reference(x, mem, positions, norm_mix_g, w_in, conv_w, conv_b, conv_ln_g, conv_ln_b,
              w_conv_out, q_norm_g, w_uq, kv_norm_g, w_ukv, w_mla_out, w_out,
              norm_xattn_g, norm_mem_g, w_xq, w_xkv, w_xo, norm_mlp_g, w_mlp1, w_mlp2,
              final_norm_g):
    cos, sin = rope_tables(positions)
    B, S, _ = x.shape
    cut = np.cumsum([2 * CONV_CH, Q_LORA, KV_LORA, QK_ROPE]).tolist()
    h = x
    for l in range(DEPTH):
        u = rms_norm(h, norm_mix_g[l])
        proj = u @ w_in[l]
        conv_in, c_q, c_kv, k_rope_raw, gate_logits = jnp.split(proj, cut, axis=-1)
        conv_out = conformer_conv(conv_in, conv_w[l], conv_b[l], conv_ln_g[l], conv_ln_b[l],
                                  w_conv_out[l])
        mla_out = mla_attention(c_q, c_kv, k_rope_raw, cos, sin, q_norm_g[l], w_uq[l],
                                kv_norm_g[l], w_ukv[l], w_mla_out[l])
        gates = jax.nn.sigmoid(gate_logits).reshape(B, S, N_BRANCH, D_MODEL)
        merged = gates[:, :, 0] * conv_out + gates[:, :, 1] * mla_out
        h = h + merged @ w_out[l]
        u = rms_norm(h, norm_xattn_g[l])
        mem_n = rms_norm(mem, norm_mem_g[l])
        h = h + memory_cross_attention(u, mem_n, w_xq[l], w_xkv[l], w_xo[l])
        u = rms_norm(h, norm_mlp_g[l])
        h = h + jnp.square(jax.nn.relu(u @ w_mlp1[l])) @ w_mlp2[l]
    return rms_norm(h, final_norm_g)
```

```python
import math
import numpy as np
import concourse.bass as bass
import concourse.mybir as mybir
from concourse.alu_op_type import AluOpType as ALU
from concourse.bass_utils import run_bass_kernel_spmd

F32 = mybir.dt.float32
BF16 = mybir.dt.bfloat16
I32 = mybir.dt.int32
AF = mybir.ActivationFunctionType
AX = mybir.AxisListType

D = 1024
SEQ = 8192
T = 2048
NB = 4
NBLK = 20
NCAT = NBLK * 512
EPS = 1e-6
SCALE = 96.0 ** -0.5
NSLOT_UNITS = (3, 7, 11, 15)
NEG = -30000.0
TWO_PI = 2.0 * math.pi

V_GMIX, V_GX, V_GMLP, V_GFIN, V_GMEM = 0, 8, 16, 24, 32
V_CONVW = 40
V_CONVB = 164
V_LNG = 168
V_LNB = 172
V_GQ = 176
V_GKV = 179
V_INVF = 181
NV = 184


class Op:
    __slots__ = ("fn", "waits", "tl", "idx", "is_dma")

    def __init__(self, fn, waits, tl, idx, is_dma):
        self.fn, self.waits, self.tl, self.idx, self.is_dma = fn, waits, tl, idx, is_dma


class Prog:
    ENGS = ("pe", "act", "dve", "pool", "sp")
    COMP = ("pe", "act", "dve", "pool")

    def __init__(self, n_dma=24):
        self.ops = {e: [] for e in self.ENGS}
        self.reg = {}
        self.seen = {e: {} for e in self.ENGS}
        self.cnt = {}
        self.n_dma = n_dma
        self.rr = 0
        self.bar = {}
        self.rank = {}
        self.sigbase = {e: 0 for e in self.COMP}
        self.forced = set()

    def _add(self, eng, fn, reads, writes, tl, is_dma):
        idx = self.cnt.get(tl, 0) + 1
        self.cnt[tl] = idx
        need = dict(self.bar)

        def req(t, i):
            if need.get(t, 0) < i:
                need[t] = i

        for r in reads:
            e = self.reg.get(r)
            if e is not None and e[0] is not None:
                req(*e[0])
        for r in writes:
            e = self.reg.get(r)
            if e is not None:
                if e[0] is not None:
                    req(*e[0])
                for t, i in e[1].items():
                    req(t, i)
        if is_dma and idx > 1:
            req(tl, idx - 1)
        waits = []
        sn = self.seen[eng]
        for t, i in need.items():
            if t == eng and not is_dma:
                continue
            if sn.get(t, 0) >= i:
                continue
            sn[t] = i
            waits.append((t, i))
        self.ops[eng].append(Op(fn, waits, tl, idx, is_dma))
        for r in reads:
            e = self.reg.setdefault(r, [None, {}])
            if e[1].get(tl, 0) < idx:
                e[1][tl] = idx
        for r in writes:
            self.reg[r] = [(tl, idx), {}]
        return idx

    def op(self, eng, fn, reads=(), writes=()):
        self._add(eng, fn, reads, writes, eng, False)

    def dma(self, eng, fn, reads=(), writes=()):
        tl = "q%d" % self.rr
        self.rr = (self.rr + 1) % self.n_dma
        self._add(eng, fn, reads, writes, tl, True)

    def phase_end(self, sigfns=None):
        for eng in self.ENGS:
            for t in self.COMP:
                self.seen[eng][t] = self.cnt.get(t, 0)

    def finish_waits(self, eng):
        waits = []
        for t, i in self.cnt.items():
            if t.startswith("q") and self.seen[eng].get(t, 0) < i:
                self.seen[eng][t] = i
                waits.append((t, i))
        self.ops[eng].append(Op(None, waits, None, 0, False))

    def emit(self, block, sems):
        sig = {e: set() for e in self.COMP}
        for e in self.ENGS:
            for o in self.ops[e]:
                for t, i in o.waits:
                    if t in sig and (t, i) not in self.rank:
                        sig[t].add(i)
        for (t, i) in self.forced:
            sig[t].add(i)
        for t, s in sig.items():
            for r, i in enumerate(sorted(s)):
                self.rank[(t, i)] = self.sigbase[t] + r + 1
            self.sigbase[t] += len(s)
        rank = self.rank

        def val(t, i):
            return 16 * i if t.startswith("q") else rank[(t, i)]

        def run(e, handle):
            for o in self.ops[e]:
                for t, i in o.waits:
                    handle.wait_ge(sems[t], val(t, i))
                if o.fn is None:
                    continue
                ins = o.fn(handle)
                if o.is_dma:
                    ins.then_inc(sems[o.tl], 16)
                elif (o.tl, o.idx) in rank:
                    ins.then_inc(sems[o.tl], 1)

        if self.ops["pe"]:
            block.tensor(lambda h: run("pe", h))
        if self.ops["act"]:
            block.scalar(lambda h: run("act", h))
        if self.ops["dve"]:
            block.vector(lambda h: run("dve", h))
        if self.ops["pool"]:
            block.gpsimd(lambda h: run("pool", h))
        if self.ops["sp"]:
            block.sync(lambda h: run("sp", h))

        self.ops = {e: [] for e in self.ENGS}
        self.forced = set()


def build_program(debug=None):
    from contextlib import ExitStack
    nc = bass.Bass("TRN2", target_bir_lowering=False)
    P = Prog()
    dbg = debug or {}
    stop_after = dbg.get("stop", "Z")

    def din(name, shape, dt=F32):
        return nc.dram_tensor(name, list(shape), dt, kind="ExternalInput").ap()

    xcat = din("xcat", [D, NCAT + 128])
    poscat = din("poscat", [32, NCAT], I32)
    qa = din("qa", [16, T])
    ka = din("ka", [17, NCAT])
    cmat = din("cmat", [128, 256])
    vecs_d = din("vecs", [128, NV])
    memT = din("memT", [D, 256])
    w_in = din("w_in", [D, 3744])
    w_conv_out = din("w_conv_out", [512, D])
    w_uq = din("w_uq", [384, 768])
    w_ukv = din("w_ukv", [256, 1024])
    w_mla_out = din("w_mla_out", [512, D])
    w_out = din("w_out", [D, D])
    w_xq = din("w_xq", [D, 512])
    w_xkv = din("w_xkv", [D, 1024])
    w_xo = din("w_xo", [512, D])
    w_mlp1 = din("w_mlp1", [D, 4096])
    w_mlp2 = din("w_mlp2", [4096, D])
    out_d = nc.dram_tensor("out", [D, T], F32, kind="ExternalOutput").ap()
    dbg_d = nc.dram_tensor("dbg", [128, dbg["n"]], F32, kind="ExternalOutput").ap() if debug else None

    def kp(ap):
        return ap.rearrange("(k p) n -> p k n", p=128)

    xcat_v = kp(xcat)
    w_in_v = kp(w_in)

    def MM(out, lhsT, rhs, start, stop, reads, writes, **kw):
        P.op("pe", lambda e: e.matmul(out, lhsT=lhsT, rhs=rhs, start=start, stop=stop, **kw), reads, writes)

    def ACT(out, in_, func, reads, writes, **kw):
        P.op("act", lambda e: e.activation(out=out, in_=in_, func=func, **kw), reads, writes)

    def TT(eng, out, in0, in1, op, reads, writes):
        P.op(eng, lambda e: e.tensor_tensor(out=out, in0=in0, in1=in1, op=op), reads, writes)

    def TS(eng, out, in0, s1, s2, op0, op1, reads, writes):
        if op1 is None:
            P.op(eng, lambda e: e.tensor_scalar(out=out, in0=in0, scalar1=s1, scalar2=None, op0=op0), reads, writes)
        else:
            P.op(eng, lambda e: e.tensor_scalar(out=out, in0=in0, scalar1=s1, scalar2=s2, op0=op0, op1=op1), reads, writes)

    def STT(out, in0, scalar, in1, op0, op1, reads, writes):
        P.op("dve", lambda e: e.scalar_tensor_tensor(out=out, in0=in0, scalar=scalar, in1=in1, op0=op0, op1=op1), reads, writes)

    def CP(eng, out, in_, reads, writes):
        P.op(eng, lambda e: e.tensor_copy(out=out, in_=in_), reads, writes)

    def MS(eng, out, val, writes):
        P.op(eng, lambda e: e.memset(out, val), (), writes)

    def RECIP(out, in_, reads, writes):
        P.op("dve", lambda e: e.reciprocal(out=out, in_=in_), reads, writes)

    def DMA(eng, out, in_, reads, writes):
        P.dma(eng, lambda e: e.dma_start(out=out, in_=in_), reads, writes)

    def PSB(b):
        return "ps%d" % b

    with ExitStack() as es0:
        def T0(es, name, shape, dt):
            return es.enter_context(nc.sbuf_tensor("sb_" + name, list(shape), dt))

        ps = es0.enter_context(nc.psum_tensor("ps", [128, 8, 512], F32))
        sems = {}
        for t in list(Prog.COMP) + ["q%d" % i for i in range(P.n_dma)]:
            sems[t] = es0.enter_context(nc.semaphore("s_" + t))
        vecs = T0(es0, "vecs", [128, NV], F32)
        ident = T0(es0, "ident", [128, 128], BF16)
        tri = T0(es0, "tri", [128, 128], BF16)
        ones = T0(es0, "ones", [128, 128], BF16)
        onesf = T0(es0, "onesf", [128, 128], F32)
        epsb = T0(es0, "epsb", [128, 1], F32)
        scr = T0(es0, "scr", [128, 16], F32)
        rstd1 = T0(es0, "rstd1", [128, T], F32)
        Onorm = T0(es0, "Onorm", [128, 4, T], BF16)
        xst = T0(es0, "xst", [128, 4, 512], F32)

        def vcol(c, lo=0, hi=128):
            return vecs[lo:hi, c:c + 1]

        sigfns = {
            "pe": lambda e: e.matmul(ps[0:1, 7, 0:1], lhsT=ones[0:1, 0:1], rhs=ones[0:1, 0:1], start=True, stop=True),
            "act": lambda e: e.activation(out=scr[0:1, 0:1], in_=scr[0:1, 1:2], func=AF.Copy),
            "dve": lambda e: e.memset(scr[0:1, 2:3], 0.0),
            "pool": lambda e: e.memset(scr[0:1, 3:4], 0.0),
        }

        def end_phase(final=False):
            if final:
                P.finish_waits("sp")
            else:
                P.phase_end(sigfns)
            with nc.Block() as block:
                P.emit(block, sems)

        dumps = []

        def dump(ap, n, col):
            dumps.append((ap, n, col))

        def debug_finish():
            for (ap, n, col) in dumps:
                for o in range(0, n, 512):
                    w = min(512, n - o)
                    slot = (o // 512) % 4
                    CP("dve", xst[:, slot, 0:w], ap[:, o:o + w], [], ["xst%d" % slot])
                    DMA("sp", dbg_d[:, col + o:col + o + w], xst[:, slot, 0:w], ["xst%d" % slot], ["dbgout"])
            MS("dve", xst[:, 0, :], 0.0, ["xst0"])
            for c in range(8):
                for tb in range(4):
                    DMA("sp", out_d[c * 128:(c + 1) * 128, tb * 512:(tb + 1) * 512], xst[:, 0, :], ["xst0"], ["out"])
            end_phase(final=True)

        DMA("sp", vecs[:], vecs_d, [], ["vecs"])
        DMA("pool", ident[:], cmat[:, 0:128], [], ["ident"])
        DMA("pool", tri[:], cmat[:, 128:256], [], ["tri"])
        MS("dve", ones[:], 1.0, ["ones"])
        MS("dve", onesf[:], 1.0, ["onesf"])
        MS("dve", epsb[:], EPS, ["epsb"])
        MS("dve", scr[:], 0.0, ["scr"])

        def rstd_from_ps(bank, n, inv_count, out_ap, out_reg, tmp, tmp_reg):
            ACT(tmp[:, 0:n], ps[:, bank, 0:n], AF.Sqrt, [PSB(bank), "epsb"], [tmp_reg], bias=epsb[:], scale=inv_count)
            RECIP(out_ap, tmp[:, 0:n], [tmp_reg], [out_reg])

        RB = slice(64, 96)
        ring = [0]

        def load_xblock(col0, xb, gcol, sqt=None, width=512, tag="", eng="dve"):
            for k in range(8):
                slot = ring[0] % 4
                ring[0] += 1
                XS = "xst%d" % slot
                DMA("sp", xst[:, slot, 0:width], xcat_v[:, k, col0:col0 + width], [], [XS])
                TS(eng, xb[:, k, 0:width], xst[:, slot, 0:width], vcol(gcol + k), None, ALU.mult, None, [XS, "vecs"], ["xb%s%d" % (tag, k)])
                if sqt is not None:
                    ACT(sqt[:, k, 0:width], xst[:, slot, 0:width], AF.Square, [XS], ["sq%s%d" % (tag, k)])

        XBK = ["xb%d" % k for k in range(8)]
        SQK = ["sq%d" % k for k in range(8)]

        with ExitStack() as esAC:
            ckvn = T0(esAC, "ckvn", [128, 2, NCAT], BF16)
            Kb = T0(esAC, "Kb", [128, NCAT], BF16)
            cqn = T0(esAC, "cqn", [128, 3, T], BF16)
            qcos = T0(esAC, "qcos", [128, T], BF16)
            qsin = T0(esAC, "qsin", [128, T], BF16)
            kmxr = T0(esAC, "kmxr", [128, NBLK], F32)

            blks = dbg.get("blks", [b for b in range(NBLK) if b != 15])
            with ExitStack() as esA:
                xbs = [T0(esA, "xbA%d" % i, [128, 8, 512], BF16) for i in range(2)]
                sqs = [T0(esA, "sqA%d" % i, [128, 8, 512], BF16) for i in range(2)]
                wA = T0(esA, "wA", [128, 8, 832], BF16)
                ckv = T0(esA, "ckv", [128, 2, 512], F32)
                cq = T0(esA, "cq", [128, 3, 512], F32)
                sq2 = T0(esA, "sq2", [128, 3, 512], BF16)
                rtmp = T0(esA, "rtmp", [128, 512], F32)
                rstdA = T0(esA, "rstdA", [128, 512], F32)
                rkv = T0(esA, "rkv", [128, 512], F32)
                rq = T0(esA, "rq", [128, 512], F32)
                posi = T0(esA, "posi", [128, 512], I32)
                ti = T0(esA, "ti", [128, 512], I32)
                ang = T0(esA, "ang", [128, 512], F32)
                tf = T0(esA, "tf", [128, 512], F32)
                rr_ = T0(esA, "rr", [128, 512], F32)
                mm_ = T0(esA, "mm", [128, 512], F32)
                sinb = T0(esA, "sinb", [128, 512], F32)
                cosb = T0(esA, "cosb", [128, 512], F32)
                t1 = T0(esA, "t1A", [128, 512], F32)
                t2 = T0(esA, "t2A", [128, 512], F32)

                DMA("pool", wA[:, :, 0:640], w_in_v[:, :, 1024:1664], [], ["wA"])
                MS("dve", wA[:, :, 640:704], 0.0, ["wAz1"])
                MS("dve", wA[:, :, 736:800], 0.0, ["wAz2"])
                DMA("pool", wA[:, :, 704:736], w_in_v[:, :, 1664:1696], [], ["wAr"])
                DMA("pool", wA[:, :, 800:816], w_in_v[:, :, 1680:1696], [], ["wArot1"])
                DMA("pool", wA[:, :, 816:832], w_in_v[:, :, 1664:1680], [], ["wArot2"])
                TS("dve", wA[:, :, 800:816], wA[:, :, 800:816], -1.0, None, ALU.mult, None, ["wArot1"], ["wArot1"])
                WA_ALL = ["wA", "wAz1", "wAz2", "wAr", "wArot1", "wArot2"]
                DMA("pool", Kb[96:113, :], ka, [], ["Kconst"])
                MS("dve", kmxr[:], 0.0, ["kmxr"])

                for bn, bi in enumerate(blks):
                    own = bi >= 16
                    c0 = bi * 512
                    oc0 = (bi - 16) * 512
                    xb = xbs[bn % 2]
                    sq = sqs[bn % 2]
                    XBK = ["xb%d_%d" % (bn % 2, k) for k in range(8)]
                    SQK = ["sq%d_%d" % (bn % 2, k) for k in range(8)]
                    DMA("sp", posi[RB, :], poscat[:, c0:c0 + 512], [], ["posi"])
                    load_xblock(c0, xb, V_GMIX, sq, tag="%d_" % (bn % 2), eng="pool")
                    for k in range(8):
                        MM(ps[:, 0, :], ones[:], sq[:, k, :], k == 0, k == 7, ["ones", SQK[k]], [PSB(0)])
                    rs_ap = rstd1[:, oc0:oc0 + 512] if own else rstdA[:]
                    rs_reg = "rstd1" if own else "rstdA"
                    rstd_from_ps(0, 512, 1.0 / D, rs_ap, rs_reg, rtmp, "rtmp")
                    for c in range(2):
                        for k in range(8):
                            MM(ps[:, 1 + c, :], wA[:, k, 384 + c * 128:384 + (c + 1) * 128], xb[:, k, :], k == 0, k == 7,
                               WA_ALL + [XBK[k]], [PSB(1 + c)])
                    for j in range(2):
                        for k in range(8):
                            MM(ps[0:96, 3 + j, :], wA[:, k, 640 + j * 96:736 + j * 96], xb[:, k, :], k == 0, k == 7,
                               WA_ALL + [XBK[k]], [PSB(3 + j)])
                    if own:
                        for c in range(3):
                            for k in range(8):
                                MM(ps[:, 5 + c, :], wA[:, k, c * 128:(c + 1) * 128], xb[:, k, :], k == 0, k == 7,
                                   WA_ALL + [XBK[k]], [PSB(5 + c)])
                    for c in range(2):
                        TT("dve", ckv[:, c, :], ps[:, 1 + c, :], rs_ap, ALU.mult, [PSB(1 + c), rs_reg], ["ckv"])
                    ACT(sq2[:, 0:2, :], ckv[:], AF.Square, ["ckv"], ["sq2"])
                    for c in range(2):
                        MM(ps[:, 0, :], ones[:], sq2[:, c, :], c == 0, c == 1, ["ones", "sq2"], [PSB(0)])
                    rstd_from_ps(0, 512, 1.0 / 256, rkv[:], "rkv", rtmp, "rtmp")
                    for c in range(2):
                        STT(ckvn[:, c, c0:c0 + 512], ckv[:, c, :], vcol(V_GKV + c), rkv[:], ALU.mult, ALU.mult,
                            ["ckv", "rkv", "vecs"], ["ckvn%d" % bi])
                    CP("dve", ang[RB, :], posi[RB, :], ["posi"], ["ang"])
                    TS("dve", ang[RB, :], ang[RB, :], vcol(V_INVF, 64, 96), None, ALU.mult, None, ["ang", "vecs"], ["ang"])
                    TS("dve", ti[RB, :], ang[RB, :], 1.0 / TWO_PI, None, ALU.mult, None, ["ang"], ["ti"])
                    CP("dve", tf[RB, :], ti[RB, :], ["ti"], ["tf"])
                    STT(rr_[RB, :], tf[RB, :], -TWO_PI, ang[RB, :], ALU.mult, ALU.add, ["tf", "ang"], ["rr"])
                    TS("dve", rr_[RB, :], rr_[RB, :], math.pi, -math.pi, ALU.min, ALU.max, ["rr"], ["rr"])
                    TS("dve", mm_[RB, :], rr_[RB, :], math.pi / 2, -TWO_PI, ALU.is_gt, ALU.mult, ["rr"], ["mm"])
                    STT(mm_[RB, :], rr_[RB, :], math.pi / 2, mm_[RB, :], ALU.add, ALU.add, ["rr", "mm"], ["mm"])
                    TS("dve", mm_[RB, :], mm_[RB, :], math.pi, -math.pi, ALU.min, ALU.max, ["mm"], ["mm"])
                    ACT(sinb[RB, :], rr_[RB, :], AF.Sin, ["rr"], ["sinb"])
                    ACT(cosb[RB, :], mm_[RB, :], AF.Sin, ["mm"], ["cosb"])
                    if own:
                        TS("pool", qcos[RB, oc0:oc0 + 512], cosb[RB, :], SCALE, None, ALU.mult, None, ["cosb"], ["qcos"])
                        TS("pool", qsin[RB, oc0:oc0 + 512], sinb[RB, :], SCALE, None, ALU.mult, None, ["sinb"], ["qsin"])
                    TT("dve", t1[RB, :], ps[RB, 3, :], cosb[RB, :], ALU.mult, [PSB(3), "cosb"], ["t1"])
                    TT("dve", t2[RB, :], ps[RB, 4, :], sinb[RB, :], ALU.mult, [PSB(4), "sinb"], ["t2"])
                    TT("pool", t1[RB, :], t1[RB, :], t2[RB, :], ALU.add, ["t1", "t2"], ["t1"])
                    TT("dve", Kb[RB, c0:c0 + 512], t1[RB, :], rs_ap[RB, :], ALU.mult, ["t1", rs_reg], ["Kr%d" % bi])
                    P.op("dve", lambda e, c0=c0, bi=bi: e.tensor_reduce(out=kmxr[RB, bi:bi + 1], in_=Kb[RB, c0:c0 + 512], axis=AX.X,
                                                                       op=ALU.max, apply_absolute_value=True),
                         ["Kr%d" % bi], ["kmxr"])
                    if own:
                        for c in range(3):
                            TT("dve", cq[:, c, :], ps[:, 5 + c, :], rs_ap, ALU.mult, [PSB(5 + c), rs_reg], ["cq"])
                        ACT(sq2[:], cq[:], AF.Square, ["cq"], ["sq2"])
                        for c in range(3):
                            MM(ps[:, 0, :], ones[:], sq2[:, c, :], c == 0, c == 2, ["ones", "sq2"], [PSB(0)])
                        rstd_from_ps(0, 512, 1.0 / 384, rq[:], "rq", rtmp, "rtmp")
                        for c in range(3):
                            STT(cqn[:, c, oc0:oc0 + 512], cq[:, c, :], vcol(V_GQ + c), rq[:], ALU.mult, ALU.mult,
                                ["cq", "rq", "vecs"], ["cqn"])
                end_phase()

            if stop_after == "A":
                dump(ckvn[:, 0, :], NCAT, 0)
                dump(ckvn[:, 1, :], NCAT, NCAT)
                dump(Kb[:, :], NCAT, 2 * NCAT)
                for c in range(3):
                    dump(cqn[:, c, :], T, 3 * NCAT + c * T)
                dump(rstd1[:, :], T, 3 * NCAT + 3 * T)
                dump(qcos[:, :], T, 3 * NCAT + 4 * T)
                dump(qsin[:, :], T, 3 * NCAT + 5 * T)
                debug_finish()
                return nc
            with ExitStack() as esC:
                Vb = T0(esC, "Vb", [128, 80, 192], BF16)
                Qb = [T0(esC, "Qb%d" % i, [128, T], BF16) for i in range(2)]
                Pb = [T0(esC, "Pb%d" % i, [128, 2, 512], BF16) for i in range(3)]
                wukv = T0(esC, "wukv", [128, 2, 1024], BF16)
                wuq = T0(esC, "wuq", [128, 3, 768], BF16)
                wqrot = T0(esC, "wqrot", [128, 3, 8, 96], BF16)
                absq = T0(esC, "absq", [128, T], BF16)
                kmxmat = T0(esC, "kmxmat", [128, 97], BF16)
                kmxn = T0(esC, "kmxn", [128, NBLK], F32)
                kmxf = T0(esC, "kmxf", [128, 2], F32)
                rl = T0(esC, "rl", [128, 512], F32)
                bc = T0(esC, "bc", [128, 512], F32)
                t1 = T0(esC, "t1C", [128, 512], F32)
                t2 = T0(esC, "t2C", [128, 512], F32)

                DMA("pool", wukv[:], kp(w_ukv), [], ["wukv"])
                DMA("pool", wuq[:], kp(w_uq), [], ["wuq"])
                w_uq4 = w_uq.rearrange("(k p) (h c) -> p k h c", p=128, c=96)
                MS("dve", wqrot[:, :, :, 0:64], 0.0, ["wqrot0"])
                for c in range(3):
                    DMA("pool", wqrot[:, c, :, 64:80], w_uq4[:, c, :, 80:96], [], ["wqrot1_%d" % c])
                    DMA("pool", wqrot[:, c, :, 80:96], w_uq4[:, c, :, 64:80], [], ["wqrot2_%d" % c])
                TS("dve", wqrot[:, :, :, 64:80], wqrot[:, :, :, 64:80], -1.0, None, ALU.mult, None, ["wqrot1_0", "wqrot1_1", "wqrot1_2"], ["wqrot1"])
                WQR = ["wqrot0", "wqrot1", "wqrot2_0", "wqrot2_1", "wqrot2_2"]
                for i in range(2):
                    DMA("pool", Qb[i][97:113, :], qa, [], ["Qm%d" % i])
                MS("pool", Vb[:, :, 64:65], 1.0, ["Vc1"])
                MS("pool", Vb[:, :, 65:128], 0.0, ["Vc0"])
                MS("dve", kmxmat[:], 0.0, ["kmxmat"])
                MS("dve", kmxn[:], 0.0, ["kmxn"])
                P.op("dve", lambda e: e.tensor_reduce(out=kmxf[RB, 1:2], in_=kmxr[RB, :], axis=AX.X, op=ALU.max), ["kmxr"], ["kmxf1"])
                TS("dve", kmxmat[RB, 96:97], kmxf[RB, 1:2], 1.01, None, ALU.mult, None, ["kmxf1", "kmxmat"], ["kmxmat_r"])

                kv_blocks = [b for b in range(NBLK) if b != 15]

                def prepK(h, bi):
                    cols = bi * 512
                    for c in range(2):
                        MM(ps[0:64, 7, :], wukv[:, c, h * 128:h * 128 + 64], ckvn[:, c, cols:cols + 512], c == 0, c == 1,
                           ["wukv", "ckvn%d" % bi], [PSB(7)])
                    CP("dve", Kb[0:64, cols:cols + 512], ps[0:64, 7, :], [PSB(7)], ["Kn%d" % bi])
                    P.op("dve", lambda e: e.tensor_reduce(out=kmxn[0:64, bi:bi + 1], in_=ps[0:64, 7, :], axis=AX.X, op=ALU.max,
                                                          apply_absolute_value=True), [PSB(7)], ["kmxn"])

                def prepV(h, bi):
                    cols = bi * 512
                    vdat = 0 if h % 2 == 0 else 128
                    for t in range(4):
                        for c in range(2):
                            MM(ps[:, 6, t * 64:(t + 1) * 64], ckvn[:, c, cols + t * 128:cols + (t + 1) * 128],
                               wukv[:, c, h * 128 + 64:h * 128 + 128], c == 0, c == 1, ["wukv", "ckvn%d" % bi], [PSB(6)], skip_group_check=True)
                    CP("dve", Vb[:, bi * 4:(bi + 1) * 4, vdat:vdat + 64], ps[:, 6, 0:256].rearrange("p (t d) -> p t d", d=64),
                       [PSB(6)], ["V%d_%d" % (h % 2, bi)])

                def prepKV(h, bi):
                    prepK(h, bi)
                    prepV(h, bi)

                def prepQ(h, tbs=(0, 1, 2, 3)):
                    qb = h % 2
                    for tb in tbs:
                        cols = tb * 512
                        for c in range(3):
                            MM(ps[0:96, 6, :], wuq[:, c, h * 96:(h + 1) * 96], cqn[:, c, cols:cols + 512], c == 0, c == 2,
                               ["wuq", "cqn"], [PSB(6)])
                        for c in range(3):
                            MM(ps[0:96, 7, :], wqrot[:, c, h, :], cqn[:, c, cols:cols + 512], c == 0, c == 2,
                               WQR + ["cqn"], [PSB(7)])
                        TS("dve", Qb[qb][0:64, cols:cols + 512], ps[0:64, 6, :], SCALE, None, ALU.mult, None, [PSB(6)], ["Qn%d_%d" % (qb, tb)])
                        TT("dve", t1[RB, :], ps[RB, 6, :], qcos[RB, cols:cols + 512], ALU.mult, [PSB(6), "qcos"], ["t1"])
                        TT("dve", t2[RB, :], ps[RB, 7, :], qsin[RB, cols:cols + 512], ALU.mult, [PSB(7), "qsin"], ["t2"])
                        TT("pool", Qb[qb][RB, cols:cols + 512], t1[RB, :], t2[RB, :], ALU.add, ["t1", "t2"], ["Qr%d_%d" % (qb, tb)])
                        STT(absq[0:96, cols:cols + 512], Qb[qb][0:96, cols:cols + 512], -1.0, Qb[qb][0:96, cols:cols + 512], ALU.mult, ALU.max,
                            ["Qn%d_%d" % (qb, tb), "Qr%d_%d" % (qb, tb)], ["absq%d" % tb])

                def prepQstab(h):
                    qb = h % 2
                    P.op("dve", lambda e: e.tensor_reduce(out=kmxf[0:64, 0:1], in_=kmxn[0:64, :], axis=AX.X, op=ALU.max), ["kmxn"], ["kmxf0"])
                    TS("dve", kmxmat[0:64, 96:97], kmxf[0:64, 0:1], 1.01, None, ALU.mult, None, ["kmxf0", "kmxmat"], ["kmxmat_n"])
                    for tb in range(4):
                        cols = tb * 512
                        MM(ps[0:97, 7, :], kmxmat[0:96, 0:97], absq[0:96, cols:cols + 512], True, True,
                           ["kmxmat", "kmxmat_r", "kmxmat_n", "absq%d" % tb], [PSB(7)])
                        ACT(Qb[qb][96:97, cols:cols + 512], ps[96:97, 7, :], AF.Copy, [PSB(7)], ["Qs%d_%d" % (qb, tb)], scale=-1.0)

                grp = [0]

                def make_groups(h):
                    out = []
                    for si, s in enumerate((3, 2, 1, 0)):
                        tiles = [(kt, 0) for kt in range(4 * NSLOT_UNITS[s])] + [(64 + 4 * s + a, 128 * a) for a in range(4)]
                        ntile = len(tiles)
                        accb = 4 + ((h * 4 + si) % 2)
                        for g0 in range(0, ntile, 2):
                            gi = grp[0]
                            grp[0] += 1
                            out.append(dict(h=h, s=s, pair=tiles[g0:g0 + 2], g0=g0, ntile=ntile, accb=accb,
                                            gb=2 * (gi % 2), pi=gi % 3, last=(g0 + 2 >= ntile)))
                    return out

                def emit_S(G):
                    h, s, gb, pi = G["h"], G["s"], G["gb"], G["pi"]
                    qb = h % 2
                    qc = s * 512
                    pair = G["pair"]
                    PREG = "P%d" % pi
                    qreads = ["Qn%d_%d" % (qb, s), "Qr%d_%d" % (qb, s), "Qs%d_%d" % (qb, s), "Qm%d" % qb]
                    for i, (kt, off) in enumerate(pair):
                        bi = kt // 4
                        diag = bi >= 16
                        MM(ps[:, gb + i, off:512], Kb[0:113, kt * 128:(kt + 1) * 128], Qb[qb][0:113, qc + off:qc + 512], True, not diag,
                           ["Kn%d" % bi, "Kr%d" % bi, "Kconst"] + qreads, [PSB(gb + i)], skip_group_check=True)
                        if diag:
                            MM(ps[:, gb + i, off:off + 128], ident[:], tri[:], False, True, ["ident", "tri"], [PSB(gb + i)], skip_group_check=True)
                    if all(off == 0 for _, off in pair) and len(pair) == 2:
                        ACT(Pb[pi][:, 0:2, :], ps[:, gb:gb + 2, :], AF.Exp, [PSB(gb), PSB(gb + 1)], [PREG])
                    else:
                        for i, (kt, off) in enumerate(pair):
                            ACT(Pb[pi][:, i, off:512], ps[:, gb + i, off:512], AF.Exp, [PSB(gb + i)], [PREG])

                def emit_PV(G):
                    h, s, gb, pi, accb = G["h"], G["s"], G["gb"], G["pi"], G["accb"]
                    voff = 0 if h % 2 == 0 else 64
                    qc = s * 512
                    PREG = "P%d" % pi
                    for i, (kt, off) in enumerate(G["pair"]):
                        bi = kt // 4
                        first = (G["g0"] + i == 0)
                        last = (G["g0"] + i == G["ntile"] - 1)
                        MM(ps[:, accb, off:512], Vb[:, kt, voff:voff + 128], Pb[pi][:, i, off:512], first, last,
                           ["V%d_%d" % (h % 2, bi), "Vc1", "Vc0", PREG], [PSB(accb)], skip_group_check=True)
                    if G["last"]:
                        r0 = 64 if h % 2 == 0 else 0
                        rows = slice(0, 64) if h % 2 == 0 else slice(64, 128)
                        RECIP(rl[r0:r0 + 1, :], ps[r0:r0 + 1, accb, :], [PSB(accb)], ["rl"])
                        MM(ps[:, 6, :], onesf[r0:r0 + 1, :], rl[r0:r0 + 1, :], True, True, ["onesf", "rl"], [PSB(6)])
                        ACT(bc[:], ps[:, 6, :], AF.Copy, [PSB(6)], ["bc"])
                        TT("dve", Onorm[rows, h // 2, qc:qc + 512], ps[rows, accb, :], bc[rows, :], ALU.mult, [PSB(accb), "bc"], ["On%d_%d" % (h, s)])

                NH = dbg.get("nheads", 8)
                prepQ(0)
                for bi in kv_blocks:
                    prepKV(0, bi)
                prepQstab(0)
                freed = {3: [11, 12, 13, 14, 19], 2: [7, 8, 9, 10, 18], 1: [3, 4, 5, 6, 17], 0: [0, 1, 2, 16]}
                prev = None
                EVERY = dbg.get("every", 2)
                for h in range(NH):
                    tasks = []
                    if h + 1 < NH:
                        for tb in range(4):
                            tasks.append(lambda h=h, tb=tb: prepQ(h + 1, (tb,)))
                        for bi in kv_blocks:
                            tasks.append(lambda h=h, bi=bi: prepV(h + 1, bi))
                    groups = make_groups(h)
                    for gi_, G in enumerate(groups):
                        emit_S(G)
                        if prev is not None:
                            emit_PV(prev)
                        prev = G
                        if G["last"] and h + 1 < NH:
                            newt = [(lambda h=h, bi=bi: prepK(h + 1, bi)) for bi in freed[G["s"]]]
                            tasks = newt + tasks
                        if tasks and gi_ % EVERY == 0:
                            tasks.pop(0)()
                    while tasks:
                        tasks.pop(0)()
                    if h + 1 < NH:
                        prepQstab(h + 1)
                emit_PV(prev)
                end_phase()

            if stop_after == "C":
                for hp in range(4):
                    dump(Onorm[:, hp, :], T, hp * T)
                debug_finish()
                return nc
        def load_own_block(tb, xb, hT=None):
            col0 = SEQ + tb * 512
            for k in range(8):
                slot = ring[0] % 4
                ring[0] += 1
                XS = "xst%d" % slot
                DMA("sp", xst[:, slot, :], xcat_v[:, k, col0:col0 + 512], [], [XS])
                TS("dve", xb[:, k, :], xst[:, slot, :], vcol(V_GMIX + k), None, ALU.mult, None, [XS, "vecs"], ["xb%d" % k])
                if hT is not None:
                    CP("pool", hT[:, k, tb * 512:(tb + 1) * 512], xst[:, slot, :], [XS], ["h%d_%d" % (k, tb)])

        def blk_stats(hT, tb, hsq, rout, rout_reg, rtmp):
            ACT(hsq[:], hT[:, :, tb * 512:(tb + 1) * 512], AF.Square, ["h%d_%d" % (k, tb) for k in range(8)], ["hsq"])
            for k in range(8):
                MM(ps[:, 0, :], ones[:], hsq[:, k, :], k == 0, k == 7, ["ones", "hsq"], [PSB(0)])
            rstd_from_ps(0, 512, 1.0 / D, rout, rout_reg, rtmp, "rtmpS")

        with ExitStack() as esH:
            hT = T0(esH, "hT", [128, 8, T], F32)
            with ExitStack() as esCA:
                convact = T0(esCA, "convact", [128, 4, T], BF16)
                with ExitStack() as esZ:
                    zT = T0(esZ, "zT", [128, 4, 4, 544], BF16)
                    with ExitStack() as esD:
                        wconv = T0(esD, "wconv", [128, 8, 1024], BF16)
                        xb = T0(esD, "xbD", [128, 8, 512], BF16)
                        xh = T0(esD, "xh", [128, 8, 32], F32)
                        xbh = T0(esD, "xbh", [128, 8, 32], BF16)
                        sqh = T0(esD, "sqh", [128, 8, 32], BF16)
                        rh = T0(esD, "rh", [128, 32], F32)
                        rtmpD = T0(esD, "rtmpD", [128, 32], F32)
                        gs = [T0(esD, "gs%d" % i, [128, 512], F32) for i in range(2)]
                        sg = [T0(esD, "sg%d" % i, [128, 512], F32) for i in range(2)]
                        as_ = [T0(esD, "as%d" % i, [128, 512], F32) for i in range(2)]
                        gsh = T0(esD, "gsh", [128, 32], F32)
                        sgh = T0(esD, "sgh", [128, 32], F32)
                        ash = T0(esD, "ash", [128, 32], F32)
                        DMA("pool", wconv[:, :, 0:512], w_in_v[:, :, 0:512], [], ["wconv0"])
                        DMA("pool", wconv[:, :, 512:1024], w_in_v[:, :, 512:1024], [], ["wconv1"])
                        for tb in range(4):
                            load_own_block(tb, xb)
                            DMA("sp", xh[:], xcat_v[:, :, NCAT + tb * 32:NCAT + (tb + 1) * 32], [], ["xh"])
                            for k in range(8):
                                TS("dve", xbh[:, k, :], xh[:, k, :], vcol(V_GMIX + k), None, ALU.mult, None, ["xh", "vecs"], ["xbh"])
                            ACT(sqh[:], xh[:], AF.Square, ["xh"], ["sqh"])
                            for k in range(8):
                                MM(ps[:, 6, 0:32], ones[:], sqh[:, k, :], k == 0, k == 7, ["ones", "sqh"], [PSB(6)])
                            rstd_from_ps(6, 32, 1.0 / D, rh[:], "rh", rtmpD, "rtmpD")
                            rs = rstd1[:, tb * 512:(tb + 1) * 512]
                            for cc in range(4):
                                ba = 2 * (cc % 2)
                                bg = ba + 1
                                i2 = cc % 2
                                for k in range(8):
                                    MM(ps[:, ba, :], wconv[:, k, cc * 128:(cc + 1) * 128], xb[:, k, :], k == 0, k == 7, ["wconv0", XBK[k]], [PSB(ba)])
                                for k in range(8):
                                    MM(ps[:, bg, :], wconv[:, k, 512 + cc * 128:512 + (cc + 1) * 128], xb[:, k, :], k == 0, k == 7, ["wconv1", XBK[k]], [PSB(bg)])
                                for k in range(8):
                                    MM(ps[:, 4, cc * 32:(cc + 1) * 32], wconv[:, k, cc * 128:(cc + 1) * 128], xbh[:, k, :], k == 0, k == 7,
                                       ["wconv0", "xbh"], [PSB(4)], skip_group_check=True)
                                for k in range(8):
                                    MM(ps[:, 5, cc * 32:(cc + 1) * 32], wconv[:, k, 512 + cc * 128:512 + (cc + 1) * 128], xbh[:, k, :], k == 0, k == 7,
                                       ["wconv1", "xbh"], [PSB(5)], skip_group_check=True)
                                TT("dve", gs[i2][:], ps[:, bg, :], rs, ALU.mult, [PSB(bg), "rstd1"], ["gs%d" % i2])
                                ACT(sg[i2][:], gs[i2][:], AF.Sigmoid, ["gs%d" % i2], ["sg%d" % i2])
                                TT("dve", as_[i2][:], ps[:, ba, :], rs, ALU.mult, [PSB(ba), "rstd1"], ["as%d" % i2])
                                TT("pool", zT[:, cc, tb, 32:544], as_[i2][:], sg[i2][:], ALU.mult, ["as%d" % i2, "sg%d" % i2], ["z%d_%d" % (cc, tb)])
                                TT("dve", gsh[:], ps[:, 5, cc * 32:(cc + 1) * 32], rh[:], ALU.mult, [PSB(5), "rh"], ["gsh"])
                                ACT(sgh[:], gsh[:], AF.Sigmoid, ["gsh"], ["sgh"])
                                TT("dve", ash[:], ps[:, 4, cc * 32:(cc + 1) * 32], rh[:], ALU.mult, [PSB(4), "rh"], ["ash"])
                                TT("pool", zT[:, cc, tb, 0:32], ash[:], sgh[:], ALU.mult, ["ash", "sgh"], ["zh%d_%d" % (cc, tb)])
                        end_phase()
                    with ExitStack() as esD:
                        diag = T0(esD, "diag", [128, 4, 31, 128], BF16)
                        cv = T0(esD, "cv", [128, 4, 512], F32)
                        cvb = T0(esD, "cvb", [128, 4, 512], BF16)
                        cvsq = T0(esD, "cvsq", [128, 4, 512], BF16)
                        mean = T0(esD, "mean", [128, 512], F32)
                        msq = T0(esD, "msq", [128, 512], F32)
                        var = T0(esD, "var", [128, 512], F32)
                        sd = T0(esD, "sd", [128, 512], F32)
                        rsl = T0(esD, "rsl", [128, 512], F32)
                        y1 = [T0(esD, "y1_%d" % i, [128, 512], F32) for i in range(2)]
                        y2 = [T0(esD, "y2_%d" % i, [128, 512], F32) for i in range(2)]
                        for cc in range(4):
                            for tau in range(31):
                                TS("dve", diag[:, cc, tau, :], ident[:], vcol(V_CONVW + cc * 31 + tau), None, ALU.mult, None, ["ident", "vecs"], ["diag%d" % cc])
                        for tb in range(4):
                            for cc in range(4):
                                for tau in range(31):
                                    MM(ps[:, cc, :], diag[:, cc, tau, :], zT[:, cc, tb, tau + 2:tau + 2 + 512], tau == 0, tau == 30,
                                       ["diag%d" % cc, "z%d_%d" % (cc, tb), "zh%d_%d" % (cc, tb)], [PSB(cc)])
                                ACT(cv[:, cc, :], ps[:, cc, :], AF.Identity, [PSB(cc), "vecs"], ["cv%d" % cc], bias=vcol(V_CONVB + cc))
                                CP("pool", cvb[:, cc, :], cv[:, cc, :], ["cv%d" % cc], ["cvb%d" % cc])
                                ACT(cvsq[:, cc, :], cv[:, cc, :], AF.Square, ["cv%d" % cc], ["cvsq%d" % cc])
                            for cc in range(4):
                                MM(ps[:, 4, :], ones[:], cvb[:, cc, :], cc == 0, cc == 3, ["ones", "cvb%d" % cc], [PSB(4)])
                            for cc in range(4):
                                MM(ps[:, 5, :], ones[:], cvsq[:, cc, :], cc == 0, cc == 3, ["ones", "cvsq%d" % cc], [PSB(5)])
                            TS("dve", mean[:], ps[:, 4, :], 1.0 / 512, None, ALU.mult, None, [PSB(4)], ["mean"])
                            TT("pool", msq[:], mean[:], mean[:], ALU.mult, ["mean"], ["msq"])
                            STT(var[:], ps[:, 5, :], 1.0 / 512, msq[:], ALU.mult, ALU.subtract, [PSB(5), "msq"], ["var"])
                            TS("dve", var[:], var[:], 0.0, None, ALU.max, None, ["var"], ["var"])
                            ACT(sd[:], var[:], AF.Sqrt, ["var", "epsb"], ["sd"], bias=epsb[:], scale=1.0)
                            RECIP(rsl[:], sd[:], ["sd"], ["rsl"])
                            for cc in range(4):
                                i2 = cc % 2
                                TT("dve", y1[i2][:], cv[:, cc, :], mean[:], ALU.subtract, ["cv%d" % cc, "mean"], ["y1_%d" % i2])
                                TT("pool", y2[i2][:], y1[i2][:], rsl[:], ALU.mult, ["y1_%d" % i2, "rsl"], ["y2_%d" % i2])
                                ACT(convact[:, cc, tb * 512:(tb + 1) * 512], y2[i2][:], AF.Silu, ["y2_%d" % i2, "vecs"], ["ca%d_%d" % (cc, tb)],
                                    scale=vcol(V_LNG + cc), bias=vcol(V_LNB + cc))
                        end_phase()
                if stop_after == "D1":
                    for cc in range(4):
                        dump(convact[:, cc, :], T, cc * T)
                    debug_finish()
                    return nc
                with ExitStack() as esD:
                    xb = T0(esD, "xbD2", [128, 8, 512], BF16)
                    wco = T0(esD, "wco", [128, 4, 1024], BF16)
                    wmo = T0(esD, "wmo", [128, 4, 1024], BF16)
                    wout = T0(esD, "wout", [128, 8, 1024], BF16)
                    gw = [T0(esD, "gw%d" % i, [128, 8, 512], BF16) for i in range(2)]
                    sig = T0(esD, "sig", [128, 16, 512], BF16)
                    mg = T0(esD, "mg", [128, 8, 512], BF16)
                    gs = [T0(esD, "gsD%d" % i, [128, 512], F32) for i in range(2)]
                    m1 = [T0(esD, "m1_%d" % i, [128, 512], F32) for i in range(2)]
                    m2 = [T0(esD, "m2_%d" % i, [128, 512], F32) for i in range(2)]
                    DMA("pool", wco[:], kp(w_conv_out), [], ["wco"])
                    DMA("pool", wmo[:], kp(w_mla_out), [], ["wmo"])
                    w_out_v = kp(w_out)
                    DMA("pool", wout[:, :, 0:512], w_out_v[:, :, 0:512], [], ["wout0"])
                    DMA("pool", wout[:, :, 512:1024], w_out_v[:, :, 512:1024], [], ["wout1"])
                    gcount = 0
                    for tb in range(4):
                        tc_ = slice(tb * 512, (tb + 1) * 512)
                        load_own_block(tb, xb, hT)
                        for gi in range(4):
                            gb_ = gcount % 2
                            gcount += 1
                            DMA("pool", gw[gb_][:], w_in_v[:, :, 1696 + gi * 512:1696 + (gi + 1) * 512], [], ["gw%d" % gb_])
                            for j in range(4):
                                oc = gi * 4 + j
                                bank = oc % 4
                                i2 = oc % 2
                                for k in range(8):
                                    MM(ps[:, bank, :], gw[gb_][:, k, j * 128:(j + 1) * 128], xb[:, k, :], k == 0, k == 7, ["gw%d" % gb_, XBK[k]], [PSB(bank)])
                                TT("dve", gs[i2][:], ps[:, bank, :], rstd1[:, tc_], ALU.mult, [PSB(bank), "rstd1"], ["gsD%d" % i2])
                                ACT(sig[:, oc, :], gs[i2][:], AF.Sigmoid, ["gsD%d" % i2], ["sig%d" % oc])
                        for c in range(8):
                            b1 = 4 + (c % 2) * 2
                            b2 = b1 + 1
                            i2 = c % 2
                            for k4 in range(4):
                                MM(ps[:, b1, :], wco[:, k4, c * 128:(c + 1) * 128], convact[:, k4, tc_], k4 == 0, k4 == 3,
                                   ["wco", "ca%d_%d" % (k4, tb)], [PSB(b1)])
                            for hp in range(4):
                                MM(ps[:, b2, :], wmo[:, hp, c * 128:(c + 1) * 128], Onorm[:, hp, tc_], hp == 0, hp == 3,
                                   ["wmo", "On%d_%d" % (2 * hp, tb), "On%d_%d" % (2 * hp + 1, tb)], [PSB(b2)])
                            TT("dve", m1[i2][:], ps[:, b1, :], sig[:, c, :], ALU.mult, [PSB(b1), "sig%d" % c], ["m1_%d" % i2])
                            TT("dve", m2[i2][:], ps[:, b2, :], sig[:, 8 + c, :], ALU.mult, [PSB(b2), "sig%d" % (8 + c)], ["m2_%d" % i2])
                            TT("pool", mg[:, c, :], m1[i2][:], m2[i2][:], ALU.add, ["m1_%d" % i2, "m2_%d" % i2], ["mg%d" % c])
                        for c in range(8):
                            bank = c % 4
                            for k in range(8):
                                MM(ps[:, bank, :], wout[:, k, c * 128:(c + 1) * 128], mg[:, k, :], k == 0, k == 7,
                                   ["wout0", "wout1", "mg%d" % k], [PSB(bank)])
                            TT("dve", hT[:, c, tc_], ps[:, bank, :], hT[:, c, tc_], ALU.add, [PSB(bank), "h%d_%d" % (c, tb)], ["h%d_%d" % (c, tb)])
                    end_phase()
            if stop_after == "D2":
                for c in range(8):
                    dump(hT[:, c, :], T, c * T)
                debug_finish()
                return nc
            with ExitStack() as esE:
                memx = T0(esE, "memx", [128, 8, 256], F32)
                memb = T0(esE, "memb", [128, 8, 256], BF16)
                msqm = T0(esE, "msqm", [128, 8, 256], BF16)
                rmem = T0(esE, "rmem", [128, 256], F32)
                rmemT = T0(esE, "rmemT", [128, 2], F32)
                rtmpE = T0(esE, "rtmpE", [128, 512], F32)
                wxkv = T0(esE, "wxkv", [128, 8, 1024], BF16)
                wxq = T0(esE, "wxq", [128, 8, 512], BF16)
                wxo = T0(esE, "wxo", [128, 4, 1024], BF16)
                Kx = T0(esE, "Kx", [128, 4, 256], BF16)
                Vx = T0(esE, "Vx", [128, 2, 512], BF16)
                kxm = T0(esE, "kxm", [128, 4], F32)
                kmm = T0(esE, "kmm", [128, 4, 128], BF16)
                hb = T0(esE, "hb", [128, 8, 512], BF16)
                hsq = T0(esE, "hsqE", [128, 8, 512], BF16)
                rstd2 = T0(esE, "rstd2", [128, 512], F32)
                Qx = T0(esE, "Qx", [128, 4, 512], BF16)
                aq = T0(esE, "aq", [128, 4, 512], BF16)
                Px = [T0(esE, "Px%d" % i, [128, 512], BF16) for i in range(2)]
                lr = T0(esE, "lr", [128, 512], F32)
                Ox = T0(esE, "Ox", [128, 4, 512], BF16)
                DMA("sp", memx[:], kp(memT), [], ["memx"])
                w_xkv_v = kp(w_xkv)
                DMA("pool", wxkv[:, :, 0:512], w_xkv_v[:, :, 0:512], [], ["wxkv0"])
                DMA("pool", wxkv[:, :, 512:1024], w_xkv_v[:, :, 512:1024], [], ["wxkv1"])
                DMA("pool", wxq[:], kp(w_xq), [], ["wxq"])
                DMA("pool", wxo[:], kp(w_xo), [], ["wxo"])
                for k in range(8):
                    TS("dve", memb[:, k, :], memx[:, k, :], vcol(V_GMEM + k), None, ALU.mult, None, ["memx", "vecs"], ["memb"])
                ACT(msqm[:], memx[:], AF.Square, ["memx"], ["msqm"])
                for k in range(8):
                    MM(ps[:, 0, 0:256], ones[:], msqm[:, k, :], k == 0, k == 7, ["ones", "msqm"], [PSB(0)])
                rstd_from_ps(0, 256, 1.0 / D, rmem[:], "rmem", rtmpE, "rtmpE")
                for kt in range(2):
                    for k in range(8):
                        MM(ps[:, 1, kt:kt + 1], msqm[:, k, kt * 128:(kt + 1) * 128], ones[:, 0:1], k == 0, k == 7, ["ones", "msqm"], [PSB(1)],
                           skip_group_check=True)
                rstd_from_ps(1, 2, 1.0 / D, rmemT[:], "rmemT", rtmpE, "rtmpE")
                for h in range(4):
                    bank = 2 + h % 2
                    for k in range(8):
                        MM(ps[:, bank, 0:256], wxkv[:, k, h * 128:(h + 1) * 128], memb[:, k, :], k == 0, k == 7, ["wxkv0", "memb"], [PSB(bank)])
                    TT("dve", Kx[:, h, :], ps[:, bank, 0:256], rmem[:], ALU.mult, [PSB(bank), "rmem"], ["Kx%d" % h])
                    P.op("dve", lambda e, h=h: e.tensor_reduce(out=kxm[:, h:h + 1], in_=Kx[:, h, :], axis=AX.X, op=ALU.max, apply_absolute_value=True),
                         ["Kx%d" % h], ["kxm%d" % h])
                    TS("dve", kmm[:, h, :], ones[:], kxm[:, h:h + 1], -1.01, ALU.mult, ALU.mult, ["ones", "kxm%d" % h], ["kmm%d" % h])
                for kt in range(2):
                    for k in range(8):
                        MM(ps[:, 4 + kt, :], memb[:, k, kt * 128:(kt + 1) * 128], wxkv[:, k, 512:1024], k == 0, k == 7, ["wxkv1", "memb"], [PSB(4 + kt)])
                    TS("dve", Vx[:, kt, :], ps[:, 4 + kt, :], rmemT[:, kt:kt + 1], None, ALU.mult, None, [PSB(4 + kt), "rmemT"], ["Vx%d" % kt])
                XS_ = 128.0 ** -0.5
                for tb in range(4):
                    tc_ = slice(tb * 512, (tb + 1) * 512)
                    HR = ["h%d_%d" % (k, tb) for k in range(8)]
                    for k in range(8):
                        TS("dve", hb[:, k, :], hT[:, k, tc_], vcol(V_GX + k), None, ALU.mult, None, [HR[k], "vecs"], ["hb%d" % k])
                    blk_stats(hT, tb, hsq, rstd2[:], "rstd2", rtmpE)
                    for h in range(4):
                        for k in range(8):
                            MM(ps[:, 1, :], wxq[:, k, h * 128:(h + 1) * 128], hb[:, k, :], k == 0, k == 7, ["wxq", "hb%d" % k], [PSB(1)])
                        STT(Qx[:, h, :], ps[:, 1, :], XS_, rstd2[:], ALU.mult, ALU.mult, [PSB(1), "rstd2"], ["Qx%d" % h])
                        STT(aq[:, h, :], Qx[:, h, :], -1.0, Qx[:, h, :], ALU.mult, ALU.max, ["Qx%d" % h], ["aq%d" % h])
                        for kt in range(2):
                            bank = 2 + kt
                            MM(ps[:, bank, :], Kx[:, h, kt * 128:(kt + 1) * 128], Qx[:, h, :], True, False, ["Kx%d" % h, "Qx%d" % h], [PSB(bank)])
                            MM(ps[:, bank, :], kmm[:, h, :], aq[:, h, :], False, True, ["kmm%d" % h, "aq%d" % h], [PSB(bank)])
                            ACT(Px[kt][:], ps[:, bank, :], AF.Exp, [PSB(bank)], ["Px%d" % kt])
                        for kt in range(2):
                            MM(ps[:, 4, :], Vx[:, kt, h * 128:(h + 1) * 128], Px[kt][:], kt == 0, kt == 1, ["Vx%d" % kt, "Px%d" % kt], [PSB(4)])
                        for kt in range(2):
                            MM(ps[:, 5, :], ones[:], Px[kt][:], kt == 0, kt == 1, ["ones", "Px%d" % kt], [PSB(5)])
                        RECIP(lr[:], ps[:, 5, :], [PSB(5)], ["lr"])
                        TT("dve", Ox[:, h, :], ps[:, 4, :], lr[:], ALU.mult, [PSB(4), "lr"], ["Ox%d" % h])
                    for c in range(8):
                        bank = 6 + c % 2
                        for h in range(4):
                            MM(ps[:, bank, :], wxo[:, h, c * 128:(c + 1) * 128], Ox[:, h, :], h == 0, h == 3, ["wxo", "Ox%d" % h], [PSB(bank)])
                        TT("dve", hT[:, c, tc_], ps[:, bank, :], hT[:, c, tc_], ALU.add, [PSB(bank), "h%d_%d" % (c, tb)], ["h%d_%d" % (c, tb)])
                end_phase()
            if stop_after == "E":
                for c in range(8):
                    dump(hT[:, c, :], T, c * T)
                debug_finish()
                return nc
            with ExitStack() as esF:
                hb3 = T0(esF, "hb3", [128, 8, T], BF16)
                W1 = [T0(esF, "W1_%d" % i, [128, 8, 256], BF16) for i in range(2)]
                W2 = [T0(esF, "W2_%d" % i, [128, 2, 1024], BF16) for i in range(2)]
                hid = [T0(esF, "hid%d" % i, [128, 2, T], BF16) for i in range(2)]
                rstd3 = T0(esF, "rstd3", [128, T], F32)
                rstd4 = T0(esF, "rstd4", [128, 512], F32)
                hsq = T0(esF, "hsqF", [128, 8, 512], BF16)
                rtmpF = T0(esF, "rtmpF", [128, 512], F32)
                uu = [T0(esF, "uu%d" % i, [128, 512], F32) for i in range(2)]
                vv = [T0(esF, "vv%d" % i, [128, 512], F32) for i in range(2)]
                w1v = kp(w_mlp1)
                w2v = kp(w_mlp2)
                for tb in range(4):
                    tc_ = slice(tb * 512, (tb + 1) * 512)
                    for k in range(8):
                        TS("dve", hb3[:, k, tc_], hT[:, k, tc_], vcol(V_GMLP + k), None, ALU.mult, None, ["h%d_%d" % (k, tb), "vecs"], ["hb3_%d_%d" % (k, tb)])
                    blk_stats(hT, tb, hsq, rstd3[:, tc_], "rstd3_%d" % tb, rtmpF)
                cnt = 0
                for g in range(16):
                    gb_ = g % 2
                    DMA("pool", W1[gb_][:], w1v[:, :, g * 256:(g + 1) * 256], [], ["W1_%d" % gb_])
                    DMA("pool", W2[gb_][:], w2v[:, g * 2:(g + 1) * 2, :], [], ["W2_%d" % gb_])
                    for j in range(2):
                        for tb in range(4):
                            tc_ = slice(tb * 512, (tb + 1) * 512)
                            bank = cnt % 4
                            i2 = cnt % 2
                            cnt += 1
                            for k in range(8):
                                MM(ps[:, bank, :], W1[gb_][:, k, j * 128:(j + 1) * 128], hb3[:, k, tc_], k == 0, k == 7,
                                   ["W1_%d" % gb_, "hb3_%d_%d" % (k, tb)], [PSB(bank)])
                            ACT(uu[i2][:], ps[:, bank, :], AF.Relu, [PSB(bank)], ["uu%d" % i2])
                            TT("dve", vv[i2][:], uu[i2][:], rstd3[:, tc_], ALU.mult, ["uu%d" % i2, "rstd3_%d" % tb], ["vv%d" % i2])
                            TT("pool", hid[gb_][:, j, tc_], vv[i2][:], vv[i2][:], ALU.mult, ["vv%d" % i2], ["hid%d_%d_%d" % (gb_, j, tb)])
                    for c in range(8):
                        for tb in range(4):
                            tc_ = slice(tb * 512, (tb + 1) * 512)
                            bank = 4 + cnt % 4
                            cnt += 1
                            for j in range(2):
                                MM(ps[:, bank, :], W2[gb_][:, j, c * 128:(c + 1) * 128], hid[gb_][:, j, tc_], j == 0, j == 1,
                                   ["W2_%d" % gb_, "hid%d_%d_%d" % (gb_, j, tb)], [PSB(bank)])
                            TT("dve", hT[:, c, tc_], ps[:, bank, :], hT[:, c, tc_], ALU.add, [PSB(bank), "h%d_%d" % (c, tb)], ["h%d_%d" % (c, tb)])
                for tb in range(4):
                    tc_ = slice(tb * 512, (tb + 1) * 512)
                    blk_stats(hT, tb, hsq, rstd4[:], "rstd4", rtmpF)
                    for c in range(8):
                        slot = ring[0] % 4
                        ring[0] += 1
                        XS = "xst%d" % slot
                        STT(xst[:, slot, :], hT[:, c, tc_], vcol(V_GFIN + c), rstd4[:], ALU.mult, ALU.mult, ["h%d_%d" % (c, tb), "rstd4", "vecs"], [XS])
                        DMA("sp", out_d[c * 128:(c + 1) * 128, tc_], xst[:, slot, :], [XS], ["out"])
                end_phase(final=True)
    return nc


def own_chunks(j):
    return [j, 7 - j, 8 + j, 15 - j]


def make_vecs(inp):
    v = np.zeros((128, NV), np.float32)

    def colmajor(g, n):
        return np.ascontiguousarray(np.asarray(g, np.float32).reshape(n, 128).T)

    v[:, V_GMIX:V_GMIX + 8] = colmajor(inp["norm_mix_g"][0], 8)
    v[:, V_GX:V_GX + 8] = colmajor(inp["norm_xattn_g"][0], 8)
    v[:, V_GMLP:V_GMLP + 8] = colmajor(inp["norm_mlp_g"][0], 8)
    v[:, V_GFIN:V_GFIN + 8] = colmajor(inp["final_norm_g"], 8)
    v[:, V_GMEM:V_GMEM + 8] = colmajor(inp["norm_mem_g"][0], 8)
    cw = np.asarray(inp["conv_w"][0], np.float32)
    v[:, V_CONVW:V_CONVW + 124] = cw.T.reshape(4, 128, 31).transpose(1, 0, 2).reshape(128, 124)
    v[:, V_CONVB:V_CONVB + 4] = colmajor(inp["conv_b"][0], 4)
    v[:, V_LNG:V_LNG + 4] = colmajor(inp["conv_ln_g"][0], 4)
    v[:, V_LNB:V_LNB + 4] = colmajor(inp["conv_ln_b"][0], 4)
    v[:, V_GQ:V_GQ + 3] = colmajor(inp["q_norm_g"][0], 3)
    v[:, V_GKV:V_GKV + 2] = colmajor(inp["kv_norm_g"][0], 2)
    half = 16
    invf = (np.float32(10000.0) ** (-np.arange(half, dtype=np.float32) / np.float32(half))).astype(np.float32)
    v[64:80, V_INVF] = invf
    v[80:96, V_INVF] = invf
    return v


def make_core_inputs(inp, core, shared):
    b, j = core // 4, core % 4
    x = np.asarray(inp["x"], np.float32)
    pos = np.asarray(inp["positions"], np.int32)
    chunks = own_chunks(j)
    xT = shared["xT"][b]
    xcat = np.zeros((D, NCAT + 128), np.float32)
    xcat[:, :SEQ] = xT
    poscat = np.zeros((32, NCAT), np.int32)
    poscat[:, :SEQ] = pos[b][None, :]
    qa = np.zeros((16, T), np.float32)
    for s, c in enumerate(chunks):
        xcat[:, SEQ + s * 512:SEQ + (s + 1) * 512] = xT[:, c * 512:(c + 1) * 512]
        poscat[:, SEQ + s * 512:SEQ + (s + 1) * 512] = pos[b][None, c * 512:(c + 1) * 512]
        if c > 0:
            xcat[:, NCAT + s * 32:NCAT + (s + 1) * 32] = xT[:, c * 512 - 32:c * 512]
        for u in range(16):
            if u >= c:
                qa[u, s * 512:(s + 1) * 512] = NEG
    m = dict(shared["common"])
    m.update({"xcat": xcat, "poscat": poscat, "qa": qa, "memT": shared["memT"][b]})
    return m


def make_shared(inp):
    x = np.asarray(inp["x"], np.float32)
    shared = {"xT": [np.ascontiguousarray(x[b].T) for b in range(2)],
              "memT": [np.ascontiguousarray(np.asarray(inp["mem"], np.float32)[b].T) for b in range(2)]}
    ka = np.zeros((17, NCAT), np.float32)
    ka[0, :] = 1.0
    for u in range(16):
        ka[1 + u, u * 512:(u + 1) * 512] = 1.0
    cmat = np.zeros((128, 256), np.float32)
    cmat[:, :128] = np.eye(128, dtype=np.float32)
    kk, qq = np.meshgrid(np.arange(128), np.arange(128), indexing="ij")
    cmat[:, 128:] = np.where(kk > qq, NEG, 0.0).astype(np.float32)
    common = {"ka": ka, "cmat": cmat, "vecs": make_vecs(inp)}
    for name in ["w_in", "w_conv_out", "w_uq", "w_ukv", "w_mla_out", "w_out", "w_xq", "w_xkv", "w_xo", "w_mlp1", "w_mlp2"]:
        common[name] = np.ascontiguousarray(np.asarray(inp[name], np.float32)[0])
    shared["common"] = common
    return shared


_NC_CACHE = {}


def kernel(**inputs):
    shared = make_shared(inputs)
    in_maps = [make_core_inputs(inputs, c, shared) for c in range(8)]
    if "nc" not in _NC_CACHE:
        _NC_CACHE["nc"] = build_program()
    nc = _NC_CACHE["nc"]
    res = run_bass_kernel_spmd(nc, in_maps, core_ids=list(range(8)))
    out = np.zeros((2, SEQ, D), np.float32)
    for core in range(8):
        b, j = core // 4, core % 4
        o = res.results[core]["out"]
        for s, c in enumerate(own_chunks(j)):
            out[b, c * 512:(c + 1) * 512, :] = o[:, s * 512:(s + 1) * 512].T
    return out
```

```python
import math
import numpy as np
import concourse.bass as bass
import concourse.mybir as mybir
from concourse.alu_op_type import AluOpType as ALU
from concourse.bass_utils import run_bass_kernel_spmd

F32 = mybir.dt.float32
BF16 = mybir.dt.bfloat16
I32 = mybir.dt.int32
AF = mybir.ActivationFunctionType
AX = mybir.AxisListType

D = 1024
SEQ = 8192
T = 2048
NB = 4
NBLK = 20
NCAT = NBLK * 512
EPS = 1e-6
SCALE = 96.0 ** -0.5
NSLOT_UNITS = (3, 7, 11, 15)
NEG = -30000.0
TWO_PI = 2.0 * math.pi

V_GMIX, V_GX, V_GMLP, V_GFIN, V_GMEM = 0, 8, 16, 24, 32
V_CONVW = 40
V_CONVB = 164
V_LNG = 168
V_LNB = 172
V_GQ = 176
V_GKV = 179
V_INVF = 181
NV = 184


class Op:
    __slots__ = ("fn", "waits", "tl", "idx", "is_dma")

    def __init__(self, fn, waits, tl, idx, is_dma):
        self.fn, self.waits, self.tl, self.idx, self.is_dma = fn, waits, tl, idx, is_dma


class Prog:
    ENGS = ("pe", "act", "dve", "pool", "sp")
    COMP = ("pe", "act", "dve", "pool")

    def __init__(self, n_dma=24):
        self.ops = {e: [] for e in self.ENGS}
        self.reg = {}
        self.seen = {e: {} for e in self.ENGS}
        self.cnt = {}
        self.n_dma = n_dma
        self.rr = 0
        self.bar = {}
        self.rank = {}
        self.sigbase = {e: 0 for e in self.COMP}
        self.forced = set()

    def _add(self, eng, fn, reads, writes, tl, is_dma):
        idx = self.cnt.get(tl, 0) + 1
        self.cnt[tl] = idx
        need = dict(self.bar)

        def req(t, i):
            if need.get(t, 0) < i:
                need[t] = i

        for r in reads:
            e = self.reg.get(r)
            if e is not None and e[0] is not None:
                req(*e[0])
        for r in writes:
            e = self.reg.get(r)
            if e is not None:
                if e[0] is not None:
                    req(*e[0])
                for t, i in e[1].items():
                    req(t, i)
        if is_dma and idx > 1:
            req(tl, idx - 1)
        waits = []
        sn = self.seen[eng]
        for t, i in need.items():
            if t == eng and not is_dma:
                continue
            if sn.get(t, 0) >= i:
                continue
            sn[t] = i
            waits.append((t, i))
        self.ops[eng].append(Op(fn, waits, tl, idx, is_dma))
        for r in reads:
            e = self.reg.setdefault(r, [None, {}])
            if e[1].get(tl, 0) < idx:
                e[1][tl] = idx
        for r in writes:
            self.reg[r] = [(tl, idx), {}]
        return idx

    def op(self, eng, fn, reads=(), writes=()):
        self._add(eng, fn, reads, writes, eng, False)

    def dma(self, eng, fn, reads=(), writes=()):
        tl = "q%d" % self.rr
        self.rr = (self.rr + 1) % self.n_dma
        self._add(eng, fn, reads, writes, tl, True)

    def phase_end(self, sigfns=None):
        for eng in self.ENGS:
            for t in self.COMP:
                self.seen[eng][t] = self.cnt.get(t, 0)

    def finish_waits(self, eng):
        waits = []
        for t, i in self.cnt.items():
            if t.startswith("q") and self.seen[eng].get(t, 0) < i:
                self.seen[eng][t] = i
                waits.append((t, i))
        self.ops[eng].append(Op(None, waits, None, 0, False))

    def emit(self, block, sems):
        sig = {e: set() for e in self.COMP}
        for e in self.ENGS:
            for o in self.ops[e]:
                for t, i in o.waits:
                    if t in sig and (t, i) not in self.rank:
                        sig[t].add(i)
        for (t, i) in self.forced:
            sig[t].add(i)
        for t, s in sig.items():
            for r, i in enumerate(sorted(s)):
                self.rank[(t, i)] = self.sigbase[t] + r + 1
            self.sigbase[t] += len(s)
        rank = self.rank

        def val(t, i):
            return 16 * i if t.startswith("q") else rank[(t, i)]

        def run(e, handle):
            for o in self.ops[e]:
                for t, i in o.waits:
                    handle.wait_ge(sems[t], val(t, i))
                if o.fn is None:
                    continue
                ins = o.fn(handle)
                if o.is_dma:
                    ins.then_inc(sems[o.tl], 16)
                elif (o.tl, o.idx) in rank:
                    ins.then_inc(sems[o.tl], 1)

        if self.ops["pe"]:
            block.tensor(lambda h: run("pe", h))
        if self.ops["act"]:
            block.scalar(lambda h: run("act", h))
        if self.ops["dve"]:
            block.vector(lambda h: run("dve", h))
        if self.ops["pool"]:
            block.gpsimd(lambda h: run("pool", h))
        if self.ops["sp"]:
            block.sync(lambda h: run("sp", h))

        self.ops = {e: [] for e in self.ENGS}
        self.forced = set()


def build_program(debug=None):
    from contextlib import ExitStack
    nc = bass.Bass("TRN2", target_bir_lowering=False)
    P = Prog()
    dbg = debug or {}
    stop_after = dbg.get("stop", "Z")

    def din(name, shape, dt=F32):
        return nc.dram_tensor(name, list(shape), dt, kind="ExternalInput").ap()

    xcat = din("xcat", [D, NCAT + 128])
    poscat = din("poscat", [32, NCAT], I32)
    qa = din("qa", [16, T])
    ka = din("ka", [17, NCAT])
    cmat = din("cmat", [128, 352])
    vecs_d = din("vecs", [128, NV])
    memT = din("memT", [D, 256])
    w_in = din("w_in", [D, 3744])
    w_conv_out = din("w_conv_out", [512, D])
    w_uq = din("w_uq", [384, 768])
    w_ukv = din("w_ukv", [256, 1024])
    w_mla_out = din("w_mla_out", [512, D])
    w_out = din("w_out", [D, D])
    w_xq = din("w_xq", [D, 512])
    w_xkv = din("w_xkv", [D, 1024])
    w_xo = din("w_xo", [512, D])
    w_mlp1 = din("w_mlp1", [D, 4096])
    w_mlp2 = din("w_mlp2", [4096, D])
    out_d = nc.dram_tensor("out", [D, T], F32, kind="ExternalOutput").ap()
    dbg_d = nc.dram_tensor("dbg", [128, dbg["n"]], F32, kind="ExternalOutput").ap() if debug else None

    def kp(ap):
        return ap.rearrange("(k p) n -> p k n", p=128)

    xcat_v = kp(xcat)
    w_in_v = kp(w_in)

    def MM(out, lhsT, rhs, start, stop, reads, writes, **kw):
        P.op("pe", lambda e: e.matmul(out, lhsT=lhsT, rhs=rhs, start=start, stop=stop, **kw), reads, writes)

    def ACT(out, in_, func, reads, writes, **kw):
        P.op("act", lambda e: e.activation(out=out, in_=in_, func=func, **kw), reads, writes)

    def TT(eng, out, in0, in1, op, reads, writes):
        P.op(eng, lambda e: e.tensor_tensor(out=out, in0=in0, in1=in1, op=op), reads, writes)

    def TS(eng, out, in0, s1, s2, op0, op1, reads, writes):
        if op1 is None:
            P.op(eng, lambda e: e.tensor_scalar(out=out, in0=in0, scalar1=s1, scalar2=None, op0=op0), reads, writes)
        else:
            P.op(eng, lambda e: e.tensor_scalar(out=out, in0=in0, scalar1=s1, scalar2=s2, op0=op0, op1=op1), reads, writes)

    def STT(out, in0, scalar, in1, op0, op1, reads, writes):
        P.op("dve", lambda e: e.scalar_tensor_tensor(out=out, in0=in0, scalar=scalar, in1=in1, op0=op0, op1=op1), reads, writes)

    def CP(eng, out, in_, reads, writes):
        P.op(eng, lambda e: e.tensor_copy(out=out, in_=in_), reads, writes)

    def MS(eng, out, val, writes):
        P.op(eng, lambda e: e.memset(out, val), (), writes)

    def RECIP(out, in_, reads, writes):
        P.op("dve", lambda e: e.reciprocal(out=out, in_=in_), reads, writes)

    def DMA(eng, out, in_, reads, writes):
        P.dma(eng, lambda e: e.dma_start(out=out, in_=in_), reads, writes)

    def PSB(b):
        return "ps%d" % b

    with ExitStack() as es0:
        def T0(es, name, shape, dt):
            return es.enter_context(nc.sbuf_tensor("sb_" + name, list(shape), dt))

        ps = es0.enter_context(nc.psum_tensor("ps", [128, 8, 512], F32))
        sems = {}
        for t in list(Prog.COMP) + ["q%d" % i for i in range(P.n_dma)]:
            sems[t] = es0.enter_context(nc.semaphore("s_" + t))
        vecs = T0(es0, "vecs", [128, NV], F32)
        ident = T0(es0, "ident", [128, 128], BF16)
        tri = T0(es0, "tri", [128, 128], BF16)
        rotm = T0(es0, "rotm", [128, 96], BF16)
        ones = T0(es0, "ones", [128, 128], BF16)
        onesf = T0(es0, "onesf", [128, 128], F32)
        epsb = T0(es0, "epsb", [128, 1], F32)
        scr = T0(es0, "scr", [128, 16], F32)
        rstd1 = T0(es0, "rstd1", [128, T], F32)
        Onorm = T0(es0, "Onorm", [128, 4, T], BF16)
        xst = T0(es0, "xst", [128, 4, 512], F32)

        def vcol(c, lo=0, hi=128):
            return vecs[lo:hi, c:c + 1]

        sigfns = {
            "pe": lambda e: e.matmul(ps[0:1, 7, 0:1], lhsT=ones[0:1, 0:1], rhs=ones[0:1, 0:1], start=True, stop=True),
            "act": lambda e: e.activation(out=scr[0:1, 0:1], in_=scr[0:1, 1:2], func=AF.Copy),
            "dve": lambda e: e.memset(scr[0:1, 2:3], 0.0),
            "pool": lambda e: e.memset(scr[0:1, 3:4], 0.0),
        }

        def end_phase(final=False):
            if final:
                P.finish_waits("sp")
            else:
                P.phase_end(sigfns)
            with nc.Block() as block:
                P.emit(block, sems)

        dumps = []

        def dump(ap, n, col):
            dumps.append((ap, n, col))

        def debug_finish():
            for (ap, n, col) in dumps:
                for o in range(0, n, 512):
                    w = min(512, n - o)
                    slot = (o // 512) % 4
                    CP("dve", xst[:, slot, 0:w], ap[:, o:o + w], [], ["xst%d" % slot])
                    DMA("sp", dbg_d[:, col + o:col + o + w], xst[:, slot, 0:w], ["xst%d" % slot], ["dbgout"])
            MS("dve", xst[:, 0, :], 0.0, ["xst0"])
            for c in range(8):
                for tb in range(4):
                    DMA("sp", out_d[c * 128:(c + 1) * 128, tb * 512:(tb + 1) * 512], xst[:, 0, :], ["xst0"], ["out"])
            end_phase(final=True)

        DMA("sp", vecs[:], vecs_d, [], ["vecs"])
        DMA("pool", ident[:], cmat[:, 0:128], [], ["ident"])
        DMA("pool", tri[:], cmat[:, 128:256], [], ["tri"])
        DMA("pool", rotm[:], cmat[:, 256:352], [], ["rotm"])
        MS("dve", ones[:], 1.0, ["ones"])
        MS("dve", onesf[:], 1.0, ["onesf"])
        MS("dve", epsb[:], EPS, ["epsb"])
        MS("dve", scr[:], 0.0, ["scr"])

        def rstd_from_ps(bank, n, inv_count, out_ap, out_reg, tmp, tmp_reg):
            ACT(tmp[:, 0:n], ps[:, bank, 0:n], AF.Sqrt, [PSB(bank), "epsb"], [tmp_reg], bias=epsb[:], scale=inv_count)
            RECIP(out_ap, tmp[:, 0:n], [tmp_reg], [out_reg])

        RB = slice(64, 96)
        ring = [0]

        def load_xblock(col0, xb, gcol, sqt=None, width=512, tag="", eng="dve"):
            for k in range(8):
                slot = ring[0] % 4
                ring[0] += 1
                XS = "xst%d" % slot
                DMA("sp", xst[:, slot, 0:width], xcat_v[:, k, col0:col0 + width], [], [XS])
                TS(eng, xb[:, k, 0:width], xst[:, slot, 0:width], vcol(gcol + k), None, ALU.mult, None, [XS, "vecs"], ["xb%s%d" % (tag, k)])
                if sqt is not None:
                    ACT(sqt[:, k, 0:width], xst[:, slot, 0:width], AF.Square, [XS], ["sq%s%d" % (tag, k)])

        XBK = ["xb%d" % k for k in range(8)]
        SQK = ["sq%d" % k for k in range(8)]

        with ExitStack() as esAC:
            ckvn = T0(esAC, "ckvn", [128, 2, NCAT], BF16)
            Kb = T0(esAC, "Kb", [128, NCAT], BF16)
            cqn = T0(esAC, "cqn", [128, 3, T], BF16)
            qcos = T0(esAC, "qcos", [128, T], BF16)
            qsin = T0(esAC, "qsin", [128, T], BF16)
            kmxr = T0(esAC, "kmxr", [128, NBLK], F32)

            blks = dbg.get("blks", [b for b in range(NBLK) if b != 15])
            with ExitStack() as esA:
                xbs = [T0(esA, "xbA%d" % i, [128, 8, 512], BF16) for i in range(3)]
                sqs = [T0(esA, "sqA%d" % i, [128, 8, 512], BF16) for i in range(1)]
                wA = T0(esA, "wA", [128, 8, 832], BF16)
                ckv = T0(esA, "ckv", [128, 2, 512], F32)
                cq = T0(esA, "cq", [128, 3, 512], F32)
                sq2 = T0(esA, "sq2", [128, 3, 512], BF16)
                rtmp = T0(esA, "rtmp", [128, 512], F32)
                rstdA = T0(esA, "rstdA", [128, 512], F32)
                rkv = T0(esA, "rkv", [128, 512], F32)
                rq = T0(esA, "rq", [128, 512], F32)
                posi = T0(esA, "posi", [128, 512], I32)
                ti = T0(esA, "ti", [128, 512], I32)
                ang = T0(esA, "ang", [128, 512], F32)
                tf = T0(esA, "tf", [128, 512], F32)
                rr_ = T0(esA, "rr", [128, 512], F32)
                mm_ = T0(esA, "mm", [128, 512], F32)
                sinb = T0(esA, "sinb", [128, 512], F32)
                cosb = T0(esA, "cosb", [128, 512], F32)
                t1 = T0(esA, "t1A", [128, 512], F32)
                t2 = T0(esA, "t2A", [128, 512], F32)

                DMA("pool", wA[:, :, 0:640], w_in_v[:, :, 1024:1664], [], ["wA"])
                MS("dve", wA[:, :, 640:704], 0.0, ["wAz1"])
                MS("dve", wA[:, :, 736:800], 0.0, ["wAz2"])
                DMA("pool", wA[:, :, 704:736], w_in_v[:, :, 1664:1696], [], ["wAr"])
                DMA("pool", wA[:, :, 800:816], w_in_v[:, :, 1680:1696], [], ["wArot1"])
                DMA("pool", wA[:, :, 816:832], w_in_v[:, :, 1664:1680], [], ["wArot2"])
                TS("dve", wA[:, :, 800:816], wA[:, :, 800:816], -1.0, None, ALU.mult, None, ["wArot1"], ["wArot1"])
                WA_ALL = ["wA", "wAz1", "wAz2", "wAr", "wArot1", "wArot2"]
                for k in range(8):
                    TS("dve", wA[:, k, :], wA[:, k, :], vcol(V_GMIX + k), None, ALU.mult, None, WA_ALL + ["vecs"], WA_ALL)
                DMA("pool", Kb[96:113, :], ka, [], ["Kconst"])
                MS("dve", kmxr[:], 0.0, ["kmxr"])

                krs = T0(esA, "krs", [128, 512], BF16)

                def stage1(bn, bi):
                    c0 = bi * 512
                    xb = xbs[bn % 3]
                    sq = sqs[0]
                    XB = "xb%d" % (bn % 3)
                    SQ = "sq0"
                    B0 = 4 * (bn % 2)
                    DMA("pool", xb[:], xcat_v[:, :, c0:c0 + 512], [], [XB])
                    ACT(sq[:], xb[:], AF.Square, [XB], [SQ])
                    for k in range(8):
                        MM(ps[:, B0, :], ones[:], sq[:, k, :], k == 0, k == 7, ["ones", SQ], [PSB(B0)])
                    for c in range(2):
                        for k in range(8):
                            MM(ps[:, B0 + 1 + c, :], wA[:, k, 384 + c * 128:384 + (c + 1) * 128], xb[:, k, :], k == 0, k == 7,
                               WA_ALL + [XB], [PSB(B0 + 1 + c)])
                    for k in range(8):
                        MM(ps[0:96, B0 + 3, :], wA[:, k, 640:736], xb[:, k, :], k == 0, k == 7, WA_ALL + [XB], [PSB(B0 + 3)])

                def stage2(bn, bi):
                    own = bi >= 16
                    c0 = bi * 512
                    oc0 = (bi - 16) * 512
                    xb = xbs[bn % 3]
                    XB = "xb%d" % (bn % 3)
                    B0 = 4 * (bn % 2)
                    DMA("sp", posi[RB, :], poscat[:, c0:c0 + 512], [], ["posi"])
                    rs_ap = rstd1[:, oc0:oc0 + 512] if own else rstdA[:]
                    rs_reg = "rstd1" if own else "rstdA"
                    rstd_from_ps(B0, 512, 1.0 / D, rs_ap, rs_reg, rtmp, "rtmp")
                    for c in range(2):
                        TT("dve", ckv[:, c, :], ps[:, B0 + 1 + c, :], rs_ap, ALU.mult, [PSB(B0 + 1 + c), rs_reg], ["ckv"])
                    TT("dve", krs[RB, :], ps[RB, B0 + 3, :], rs_ap[RB, :], ALU.mult, [PSB(B0 + 3), rs_reg], ["krs"])
                    ACT(sq2[:, 0:2, :], ckv[:], AF.Square, ["ckv"], ["sq2"])
                    for c in range(2):
                        MM(ps[:, B0, :], ones[:], sq2[:, c, :], c == 0, c == 1, ["ones", "sq2"], [PSB(B0)])
                    MM(ps[0:96, B0 + 3, :], rotm[RB, :], krs[RB, :], True, True, ["rotm", "krs"], [PSB(B0 + 3)])
                    rstd_from_ps(B0, 512, 1.0 / 256, rkv[:], "rkv", rtmp, "rtmp")
                    for c in range(2):
                        STT(ckvn[:, c, c0:c0 + 512], ckv[:, c, :], vcol(V_GKV + c), rkv[:], ALU.mult, ALU.mult,
                            ["ckv", "rkv", "vecs"], ["ckvn%d" % bi])
                    if own:
                        for c in range(3):
                            for k in range(8):
                                MM(ps[:, B0 + c, :], wA[:, k, c * 128:(c + 1) * 128], xb[:, k, :], k == 0, k == 7,
                                   WA_ALL + [XB], [PSB(B0 + c)])
                    CP("dve", ang[RB, :], posi[RB, :], ["posi"], ["ang"])
                    TS("dve", ang[RB, :], ang[RB, :], vcol(V_INVF, 64, 96), None, ALU.mult, None, ["ang", "vecs"], ["ang"])
                    TS("dve", ti[RB, :], ang[RB, :], 1.0 / TWO_PI, None, ALU.mult, None, ["ang"], ["ti"])
                    CP("dve", tf[RB, :], ti[RB, :], ["ti"], ["tf"])
                    STT(rr_[RB, :], tf[RB, :], -TWO_PI, ang[RB, :], ALU.mult, ALU.add, ["tf", "ang"], ["rr"])
                    TS("dve", rr_[RB, :], rr_[RB, :], math.pi, -math.pi, ALU.min, ALU.max, ["rr"], ["rr"])
                    TS("dve", mm_[RB, :], rr_[RB, :], math.pi / 2, -TWO_PI, ALU.is_gt, ALU.mult, ["rr"], ["mm"])
                    STT(mm_[RB, :], rr_[RB, :], math.pi / 2, mm_[RB, :], ALU.add, ALU.add, ["rr", "mm"], ["mm"])
                    TS("dve", mm_[RB, :], mm_[RB, :], math.pi, -math.pi, ALU.min, ALU.max, ["mm"], ["mm"])
                    ACT(sinb[RB, :], rr_[RB, :], AF.Sin, ["rr"], ["sinb"])
                    ACT(cosb[RB, :], mm_[RB, :], AF.Sin, ["mm"], ["cosb"])
                    if own:
                        TS("dve", qcos[RB, oc0:oc0 + 512], cosb[RB, :], SCALE, None, ALU.mult, None, ["cosb"], ["qcos"])
                        TS("dve", qsin[RB, oc0:oc0 + 512], sinb[RB, :], SCALE, None, ALU.mult, None, ["sinb"], ["qsin"])
                    TT("dve", t1[RB, :], krs[RB, :], cosb[RB, :], ALU.mult, ["krs", "cosb"], ["t1"])
                    TT("dve", t2[RB, :], ps[RB, B0 + 3, :], sinb[RB, :], ALU.mult, [PSB(B0 + 3), "sinb"], ["t2"])
                    TT("dve", Kb[RB, c0:c0 + 512], t1[RB, :], t2[RB, :], ALU.add, ["t1", "t2"], ["Kr%d" % bi])
                    P.op("dve", lambda e: e.tensor_reduce(out=kmxr[RB, bi:bi + 1], in_=Kb[RB, c0:c0 + 512], axis=AX.X,
                                                          op=ALU.max, apply_absolute_value=True),
                         ["Kr%d" % bi], ["kmxr"])
                    if own:
                        for c in range(3):
                            TT("dve", cq[:, c, :], ps[:, B0 + c, :], rs_ap, ALU.mult, [PSB(B0 + c), rs_reg], ["cq"])
                        ACT(sq2[:], cq[:], AF.Square, ["cq"], ["sq2"])
                        for c in range(3):
                            MM(ps[:, B0 + 3, :], ones[:], sq2[:, c, :], c == 0, c == 2, ["ones", "sq2"], [PSB(B0 + 3)])
                        rstd_from_ps(B0 + 3, 512, 1.0 / 384, rq[:], "rq", rtmp, "rtmp")
                        for c in range(3):
                            STT(cqn[:, c, oc0:oc0 + 512], cq[:, c, :], vcol(V_GQ + c), rq[:], ALU.mult, ALU.mult,
                                ["cq", "rq", "vecs"], ["cqn"])

                for bn, bi in enumerate(blks):
                    stage1(bn, bi)
                    if bn > 0:
                        stage2(bn - 1, blks[bn - 1])
                stage2(len(blks) - 1, blks[-1])
                end_phase()

            if stop_after == "A":
                dump(ckvn[:, 0, :], NCAT, 0)
                dump(ckvn[:, 1, :], NCAT, NCAT)
                dump(Kb[:, :], NCAT, 2 * NCAT)
                for c in range(3):
                    dump(cqn[:, c, :], T, 3 * NCAT + c * T)
                dump(rstd1[:, :], T, 3 * NCAT + 3 * T)
                dump(qcos[:, :], T, 3 * NCAT + 4 * T)
                dump(qsin[:, :], T, 3 * NCAT + 5 * T)
                debug_finish()
                return nc
            with ExitStack() as esC:
                Vb = T0(esC, "Vb", [128, 80, 192], BF16)
                Qb = [T0(esC, "Qb%d" % i, [128, T], BF16) for i in range(2)]
                Pb = [T0(esC, "Pb%d" % i, [128, 2, 512], BF16) for i in range(3)]
                wukv = T0(esC, "wukv", [128, 2, 1024], BF16)
                wuq = T0(esC, "wuq", [128, 3, 768], BF16)
                wqrot = T0(esC, "wqrot", [128, 3, 8, 96], BF16)
                absq = T0(esC, "absq", [128, T], BF16)
                kmxmat = T0(esC, "kmxmat", [128, 97], BF16)
                kmxn = T0(esC, "kmxn", [128, NBLK], F32)
                kmxf = T0(esC, "kmxf", [128, 2], F32)
                rl = T0(esC, "rl", [128, 512], F32)
                bc = T0(esC, "bc", [128, 512], F32)
                t1 = T0(esC, "t1C", [128, 512], F32)
                t2 = T0(esC, "t2C", [128, 512], F32)

                DMA("pool", wukv[:], kp(w_ukv), [], ["wukv"])
                DMA("pool", wuq[:], kp(w_uq), [], ["wuq"])
                w_uq4 = w_uq.rearrange("(k p) (h c) -> p k h c", p=128, c=96)
                MS("dve", wqrot[:, :, :, 0:64], 0.0, ["wqrot0"])
                for c in range(3):
                    DMA("pool", wqrot[:, c, :, 64:80], w_uq4[:, c, :, 80:96], [], ["wqrot1_%d" % c])
                    DMA("pool", wqrot[:, c, :, 80:96], w_uq4[:, c, :, 64:80], [], ["wqrot2_%d" % c])
                TS("dve", wqrot[:, :, :, 64:80], wqrot[:, :, :, 64:80], -1.0, None, ALU.mult, None, ["wqrot1_0", "wqrot1_1", "wqrot1_2"], ["wqrot1"])
                WQR = ["wqrot0", "wqrot1", "wqrot2_0", "wqrot2_1", "wqrot2_2"]
                for i in range(2):
                    DMA("pool", Qb[i][97:113, :], qa, [], ["Qm%d" % i])
                MS("pool", Vb[:, :, 64:65], 1.0, ["Vc1"])
                MS("pool", Vb[:, :, 65:128], 0.0, ["Vc0"])
                MS("dve", kmxmat[:], 0.0, ["kmxmat"])
                MS("dve", kmxn[:], 0.0, ["kmxn"])
                P.op("dve", lambda e: e.tensor_reduce(out=kmxf[RB, 1:2], in_=kmxr[RB, :], axis=AX.X, op=ALU.max), ["kmxr"], ["kmxf1"])
                TS("dve", kmxmat[RB, 96:97], kmxf[RB, 1:2], 1.01, None, ALU.mult, None, ["kmxf1", "kmxmat"], ["kmxmat_r"])

                kv_blocks = [b for b in range(NBLK) if b != 15]

                def prepK(h, bi):
                    cols = bi * 512
                    for c in range(2):
                        MM(ps[0:64, 7, :], wukv[:, c, h * 128:h * 128 + 64], ckvn[:, c, cols:cols + 512], c == 0, c == 1,
                           ["wukv", "ckvn%d" % bi], [PSB(7)])
                    CP("dve", Kb[0:64, cols:cols + 512], ps[0:64, 7, :], [PSB(7)], ["Kn%d" % bi])
                    P.op("dve", lambda e: e.tensor_reduce(out=kmxn[0:64, bi:bi + 1], in_=ps[0:64, 7, :], axis=AX.X, op=ALU.max,
                                                          apply_absolute_value=True), [PSB(7)], ["kmxn"])

                def prepV(h, bi):
                    cols = bi * 512
                    vdat = 0 if h % 2 == 0 else 128
                    for t in range(4):
                        for c in range(2):
                            MM(ps[:, 6, t * 64:(t + 1) * 64], ckvn[:, c, cols + t * 128:cols + (t + 1) * 128],
                               wukv[:, c, h * 128 + 64:h * 128 + 128], c == 0, c == 1, ["wukv", "ckvn%d" % bi], [PSB(6)], skip_group_check=True)
                    CP("dve", Vb[:, bi * 4:(bi + 1) * 4, vdat:vdat + 64], ps[:, 6, 0:256].rearrange("p (t d) -> p t d", d=64),
                       [PSB(6)], ["V%d_%d" % (h % 2, bi)])

                def prepKV(h, bi):
                    prepK(h, bi)
                    prepV(h, bi)

                def prepQ(h, tbs=(0, 1, 2, 3)):
                    qb = h % 2
                    for tb in tbs:
                        cols = tb * 512
                        for c in range(3):
                            MM(ps[0:96, 6, :], wuq[:, c, h * 96:(h + 1) * 96], cqn[:, c, cols:cols + 512], c == 0, c == 2,
                               ["wuq", "cqn"], [PSB(6)])
                        for c in range(3):
                            MM(ps[0:96, 7, :], wqrot[:, c, h, :], cqn[:, c, cols:cols + 512], c == 0, c == 2,
                               WQR + ["cqn"], [PSB(7)])
                        TS("dve", Qb[qb][0:64, cols:cols + 512], ps[0:64, 6, :], SCALE, None, ALU.mult, None, [PSB(6)], ["Qn%d_%d" % (qb, tb)])
                        TT("dve", t1[RB, :], ps[RB, 6, :], qcos[RB, cols:cols + 512], ALU.mult, [PSB(6), "qcos"], ["t1"])
                        TT("dve", t2[RB, :], ps[RB, 7, :], qsin[RB, cols:cols + 512], ALU.mult, [PSB(7), "qsin"], ["t2"])
                        TT("pool", Qb[qb][RB, cols:cols + 512], t1[RB, :], t2[RB, :], ALU.add, ["t1", "t2"], ["Qr%d_%d" % (qb, tb)])
                        STT(absq[0:96, cols:cols + 512], Qb[qb][0:96, cols:cols + 512], -1.0, Qb[qb][0:96, cols:cols + 512], ALU.mult, ALU.max,
                            ["Qn%d_%d" % (qb, tb), "Qr%d_%d" % (qb, tb)], ["absq%d" % tb])

                def prepQstab(h):
                    qb = h % 2
                    P.op("dve", lambda e: e.tensor_reduce(out=kmxf[0:64, 0:1], in_=kmxn[0:64, :], axis=AX.X, op=ALU.max), ["kmxn"], ["kmxf0"])
                    TS("dve", kmxmat[0:64, 96:97], kmxf[0:64, 0:1], 1.01, None, ALU.mult, None, ["kmxf0", "kmxmat"], ["kmxmat_n"])
                    for tb in range(4):
                        cols = tb * 512
                        MM(ps[0:97, 7, :], kmxmat[0:96, 0:97], absq[0:96, cols:cols + 512], True, True,
                           ["kmxmat", "kmxmat_r", "kmxmat_n", "absq%d" % tb], [PSB(7)])
                        ACT(Qb[qb][96:97, cols:cols + 512], ps[96:97, 7, :], AF.Copy, [PSB(7)], ["Qs%d_%d" % (qb, tb)], scale=-1.0)

                grp = [0]

                def make_groups(h):
                    out = []
                    for si, s in enumerate((3, 2, 1, 0)):
                        tiles = [(kt, 0) for kt in range(4 * NSLOT_UNITS[s])] + [(64 + 4 * s + a, 128 * a) for a in range(4)]
                        ntile = len(tiles)
                        accb = 4 + ((h * 4 + si) % 2)
                        for g0 in range(0, ntile, 2):
                            gi = grp[0]
                            grp[0] += 1
                            out.append(dict(h=h, s=s, pair=tiles[g0:g0 + 2], g0=g0, ntile=ntile, accb=accb,
                                            gb=2 * (gi % 2), pi=gi % 3, last=(g0 + 2 >= ntile)))
                    return out

                def emit_S(G):
                    h, s, gb, pi = G["h"], G["s"], G["gb"], G["pi"]
                    qb = h % 2
                    qc = s * 512
                    pair = G["pair"]
                    PREG = "P%d" % pi
                    qreads = ["Qn%d_%d" % (qb, s), "Qr%d_%d" % (qb, s), "Qs%d_%d" % (qb, s), "Qm%d" % qb]
                    for i, (kt, off) in enumerate(pair):
                        bi = kt // 4
                        diag = bi >= 16
                        MM(ps[:, gb + i, off:512], Kb[0:113, kt * 128:(kt + 1) * 128], Qb[qb][0:113, qc + off:qc + 512], True, not diag,
                           ["Kn%d" % bi, "Kr%d" % bi, "Kconst"] + qreads, [PSB(gb + i)], skip_group_check=True)
                        if diag:
                            MM(ps[:, gb + i, off:off + 128], ident[:], tri[:], False, True, ["ident", "tri"], [PSB(gb + i)], skip_group_check=True)
                    if all(off == 0 for _, off in pair) and len(pair) == 2:
                        ACT(Pb[pi][:, 0:2, :], ps[:, gb:gb + 2, :], AF.Exp, [PSB(gb), PSB(gb + 1)], [PREG])
                    else:
                        for i, (kt, off) in enumerate(pair):
                            ACT(Pb[pi][:, i, off:512], ps[:, gb + i, off:512], AF.Exp, [PSB(gb + i)], [PREG])

                def emit_PV(G):
                    h, s, gb, pi, accb = G["h"], G["s"], G["gb"], G["pi"], G["accb"]
                    voff = 0 if h % 2 == 0 else 64
                    qc = s * 512
                    PREG = "P%d" % pi
                    for i, (kt, off) in enumerate(G["pair"]):
                        bi = kt // 4
                        first = (G["g0"] + i == 0)
                        last = (G["g0"] + i == G["ntile"] - 1)
                        MM(ps[:, accb, off:512], Vb[:, kt, voff:voff + 128], Pb[pi][:, i, off:512], first, last,
                           ["V%d_%d" % (h % 2, bi), "Vc1", "Vc0", PREG], [PSB(accb)], skip_group_check=True)
                    if G["last"]:
                        r0 = 64 if h % 2 == 0 else 0
                        rows = slice(0, 64) if h % 2 == 0 else slice(64, 128)
                        RECIP(rl[r0:r0 + 1, :], ps[r0:r0 + 1, accb, :], [PSB(accb)], ["rl"])
                        MM(ps[:, 6, :], onesf[r0:r0 + 1, :], rl[r0:r0 + 1, :], True, True, ["onesf", "rl"], [PSB(6)])
                        ACT(bc[:], ps[:, 6, :], AF.Copy, [PSB(6)], ["bc"])
                        TT("dve", Onorm[rows, h // 2, qc:qc + 512], ps[rows, accb, :], bc[rows, :], ALU.mult, [PSB(accb), "bc"], ["On%d_%d" % (h, s)])

                NH = dbg.get("nheads", 8)
                prepQ(0)
                for bi in kv_blocks:
                    prepKV(0, bi)
                prepQstab(0)
                freed = {3: [11, 12, 13, 14, 19], 2: [7, 8, 9, 10, 18], 1: [3, 4, 5, 6, 17], 0: [0, 1, 2, 16]}
                prev = None
                EVERY = dbg.get("every", 2)
                for h in range(NH):
                    tasks = []
                    if h + 1 < NH:
                        for tb in range(4):
                            tasks.append(lambda h=h, tb=tb: prepQ(h + 1, (tb,)))
                        for bi in kv_blocks:
                            tasks.append(lambda h=h, bi=bi: prepV(h + 1, bi))
                    groups = make_groups(h)
                    for gi_, G in enumerate(groups):
                        emit_S(G)
                        if prev is not None:
                            emit_PV(prev)
                        prev = G
                        if G["last"] and h + 1 < NH:
                            newt = [(lambda h=h, bi=bi: prepK(h + 1, bi)) for bi in freed[G["s"]]]
                            tasks = newt + tasks
                        if tasks and gi_ % EVERY == 0:
                            tasks.pop(0)()
                    while tasks:
                        tasks.pop(0)()
                    if h + 1 < NH:
                        prepQstab(h + 1)
                emit_PV(prev)
                end_phase()

            if stop_after == "C":
                for hp in range(4):
                    dump(Onorm[:, hp, :], T, hp * T)
                debug_finish()
                return nc
        def load_own_block(tb, xb, hT=None):
            col0 = SEQ + tb * 512
            for k in range(8):
                slot = ring[0] % 4
                ring[0] += 1
                XS = "xst%d" % slot
                DMA("sp", xst[:, slot, :], xcat_v[:, k, col0:col0 + 512], [], [XS])
                TS("dve", xb[:, k, :], xst[:, slot, :], vcol(V_GMIX + k), None, ALU.mult, None, [XS, "vecs"], ["xb%d" % k])
                if hT is not None:
                    CP("pool", hT[:, k, tb * 512:(tb + 1) * 512], xst[:, slot, :], [XS], ["h%d_%d" % (k, tb)])
            return

        def load_own_block_h(tb, xb, hT):
            col0 = SEQ + tb * 512
            for k in range(8):
                HR = "h%d_%d" % (k, tb)
                DMA("sp", hT[:, k, tb * 512:(tb + 1) * 512], xcat_v[:, k, col0:col0 + 512], [], [HR])
                TS("dve", xb[:, k, :], hT[:, k, tb * 512:(tb + 1) * 512], vcol(V_GMIX + k), None, ALU.mult, None, [HR, "vecs"], ["xb%d" % k])

        def blk_stats(hT, tb, hsq, rout, rout_reg, rtmp):
            ACT(hsq[:], hT[:, :, tb * 512:(tb + 1) * 512], AF.Square, ["h%d_%d" % (k, tb) for k in range(8)], ["hsq"])
            for k in range(8):
                MM(ps[:, 0, :], ones[:], hsq[:, k, :], k == 0, k == 7, ["ones", "hsq"], [PSB(0)])
            rstd_from_ps(0, 512, 1.0 / D, rout, rout_reg, rtmp, "rtmpS")

        with ExitStack() as esH:
            hT = T0(esH, "hT", [128, 8, T], F32)
            with ExitStack() as esCA:
                convact = T0(esCA, "convact", [128, 4, T], BF16)
                with ExitStack() as esZ:
                    zT = T0(esZ, "zT", [128, 4, 4, 544], BF16)
                    with ExitStack() as esD:
                        wconv = T0(esD, "wconv", [128, 8, 1024], BF16)
                        xb = T0(esD, "xbD", [128, 8, 512], BF16)
                        xh = T0(esD, "xh", [128, 8, 32], F32)
                        xbh = T0(esD, "xbh", [128, 8, 32], BF16)
                        sqh = T0(esD, "sqh", [128, 8, 32], BF16)
                        rh = T0(esD, "rh", [128, 32], F32)
                        rtmpD = T0(esD, "rtmpD", [128, 32], F32)
                        gs = [T0(esD, "gs%d" % i, [128, 512], F32) for i in range(2)]
                        sg = [T0(esD, "sg%d" % i, [128, 512], F32) for i in range(2)]
                        as_ = [T0(esD, "as%d" % i, [128, 512], F32) for i in range(2)]
                        gsh = T0(esD, "gsh", [128, 32], F32)
                        sgh = T0(esD, "sgh", [128, 32], F32)
                        ash = T0(esD, "ash", [128, 32], F32)
                        DMA("pool", wconv[:, :, 0:512], w_in_v[:, :, 0:512], [], ["wconv0"])
                        DMA("pool", wconv[:, :, 512:1024], w_in_v[:, :, 512:1024], [], ["wconv1"])
                        for tb in range(4):
                            load_own_block(tb, xb)
                            DMA("sp", xh[:], xcat_v[:, :, NCAT + tb * 32:NCAT + (tb + 1) * 32], [], ["xh"])
                            for k in range(8):
                                TS("dve", xbh[:, k, :], xh[:, k, :], vcol(V_GMIX + k), None, ALU.mult, None, ["xh", "vecs"], ["xbh"])
                            ACT(sqh[:], xh[:], AF.Square, ["xh"], ["sqh"])
                            for k in range(8):
                                MM(ps[:, 6, 0:32], ones[:], sqh[:, k, :], k == 0, k == 7, ["ones", "sqh"], [PSB(6)])
                            rstd_from_ps(6, 32, 1.0 / D, rh[:], "rh", rtmpD, "rtmpD")
                            rs = rstd1[:, tb * 512:(tb + 1) * 512]
                            for cc in range(4):
                                ba = 2 * (cc % 2)
                                bg = ba + 1
                                i2 = cc % 2
                                for k in range(8):
                                    MM(ps[:, ba, :], wconv[:, k, cc * 128:(cc + 1) * 128], xb[:, k, :], k == 0, k == 7, ["wconv0", XBK[k]], [PSB(ba)])
                                for k in range(8):
                                    MM(ps[:, bg, :], wconv[:, k, 512 + cc * 128:512 + (cc + 1) * 128], xb[:, k, :], k == 0, k == 7, ["wconv1", XBK[k]], [PSB(bg)])
                                for k in range(8):
                                    MM(ps[:, 4, cc * 32:(cc + 1) * 32], wconv[:, k, cc * 128:(cc + 1) * 128], xbh[:, k, :], k == 0, k == 7,
                                       ["wconv0", "xbh"], [PSB(4)], skip_group_check=True)
                                for k in range(8):
                                    MM(ps[:, 5, cc * 32:(cc + 1) * 32], wconv[:, k, 512 + cc * 128:512 + (cc + 1) * 128], xbh[:, k, :], k == 0, k == 7,
                                       ["wconv1", "xbh"], [PSB(5)], skip_group_check=True)
                                TT("dve", gs[i2][:], ps[:, bg, :], rs, ALU.mult, [PSB(bg), "rstd1"], ["gs%d" % i2])
                                ACT(sg[i2][:], gs[i2][:], AF.Sigmoid, ["gs%d" % i2], ["sg%d" % i2])
                                TT("dve", as_[i2][:], ps[:, ba, :], rs, ALU.mult, [PSB(ba), "rstd1"], ["as%d" % i2])
                                TT("pool", zT[:, cc, tb, 32:544], as_[i2][:], sg[i2][:], ALU.mult, ["as%d" % i2, "sg%d" % i2], ["z%d_%d" % (cc, tb)])
                                TT("dve", gsh[:], ps[:, 5, cc * 32:(cc + 1) * 32], rh[:], ALU.mult, [PSB(5), "rh"], ["gsh"])
                                ACT(sgh[:], gsh[:], AF.Sigmoid, ["gsh"], ["sgh"])
                                TT("dve", ash[:], ps[:, 4, cc * 32:(cc + 1) * 32], rh[:], ALU.mult, [PSB(4), "rh"], ["ash"])
                                TT("pool", zT[:, cc, tb, 0:32], ash[:], sgh[:], ALU.mult, ["ash", "sgh"], ["zh%d_%d" % (cc, tb)])
                        end_phase()
                    with ExitStack() as esD:
                        diag = T0(esD, "diag", [128, 4, 31, 128], BF16)
                        cv = T0(esD, "cv", [128, 4, 512], F32)
                        cvb = T0(esD, "cvb", [128, 4, 512], BF16)
                        cvsq = T0(esD, "cvsq", [128, 4, 512], BF16)
                        mean = T0(esD, "mean", [128, 512], F32)
                        msq = T0(esD, "msq", [128, 512], F32)
                        var = T0(esD, "var", [128, 512], F32)
                        sd = T0(esD, "sd", [128, 512], F32)
                        rsl = T0(esD, "rsl", [128, 512], F32)
                        y1 = [T0(esD, "y1_%d" % i, [128, 512], F32) for i in range(2)]
                        y2 = [T0(esD, "y2_%d" % i, [128, 512], F32) for i in range(2)]
                        for cc in range(4):
                            for tau in range(31):
                                TS("dve", diag[:, cc, tau, :], ident[:], vcol(V_CONVW + cc * 31 + tau), None, ALU.mult, None, ["ident", "vecs"], ["diag%d" % cc])
                        for tb in range(4):
                            for cc in range(4):
                                for tau in range(31):
                                    MM(ps[:, cc, :], diag[:, cc, tau, :], zT[:, cc, tb, tau + 2:tau + 2 + 512], tau == 0, tau == 30,
                                       ["diag%d" % cc, "z%d_%d" % (cc, tb), "zh%d_%d" % (cc, tb)], [PSB(cc)])
                                ACT(cv[:, cc, :], ps[:, cc, :], AF.Identity, [PSB(cc), "vecs"], ["cv%d" % cc], bias=vcol(V_CONVB + cc))
                                CP("pool", cvb[:, cc, :], cv[:, cc, :], ["cv%d" % cc], ["cvb%d" % cc])
                                ACT(cvsq[:, cc, :], cv[:, cc, :], AF.Square, ["cv%d" % cc], ["cvsq%d" % cc])
                            for cc in range(4):
                                MM(ps[:, 4, :], ones[:], cvb[:, cc, :], cc == 0, cc == 3, ["ones", "cvb%d" % cc], [PSB(4)])
                            for cc in range(4):
                                MM(ps[:, 5, :], ones[:], cvsq[:, cc, :], cc == 0, cc == 3, ["ones", "cvsq%d" % cc], [PSB(5)])
                            TS("dve", mean[:], ps[:, 4, :], 1.0 / 512, None, ALU.mult, None, [PSB(4)], ["mean"])
                            TT("pool", msq[:], mean[:], mean[:], ALU.mult, ["mean"], ["msq"])
                            STT(var[:], ps[:, 5, :], 1.0 / 512, msq[:], ALU.mult, ALU.subtract, [PSB(5), "msq"], ["var"])
                            TS("dve", var[:], var[:], 0.0, None, ALU.max, None, ["var"], ["var"])
                            ACT(sd[:], var[:], AF.Sqrt, ["var", "epsb"], ["sd"], bias=epsb[:], scale=1.0)
                            RECIP(rsl[:], sd[:], ["sd"], ["rsl"])
                            for cc in range(4):
                                i2 = cc % 2
                                TT("dve", y1[i2][:], cv[:, cc, :], mean[:], ALU.subtract, ["cv%d" % cc, "mean"], ["y1_%d" % i2])
                                TT("pool", y2[i2][:], y1[i2][:], rsl[:], ALU.mult, ["y1_%d" % i2, "rsl"], ["y2_%d" % i2])
                                ACT(convact[:, cc, tb * 512:(tb + 1) * 512], y2[i2][:], AF.Silu, ["y2_%d" % i2, "vecs"], ["ca%d_%d" % (cc, tb)],
                                    scale=vcol(V_LNG + cc), bias=vcol(V_LNB + cc))
                        end_phase()
                if stop_after == "D1":
                    for cc in range(4):
                        dump(convact[:, cc, :], T, cc * T)
                    debug_finish()
                    return nc
                with ExitStack() as esD:
                    xb = T0(esD, "xbD2", [128, 8, 512], BF16)
                    wco = T0(esD, "wco", [128, 4, 1024], BF16)
                    wmo = T0(esD, "wmo", [128, 4, 1024], BF16)
                    wout = T0(esD, "wout", [128, 8, 1024], BF16)
                    gw = [T0(esD, "gw%d" % i, [128, 8, 512], BF16) for i in range(2)]
                    sig = T0(esD, "sig", [128, 16, 512], BF16)
                    mg = T0(esD, "mg", [128, 8, 512], BF16)
                    gs = [T0(esD, "gsD%d" % i, [128, 512], F32) for i in range(2)]
                    m1 = [T0(esD, "m1_%d" % i, [128, 512], F32) for i in range(2)]
                    m2 = [T0(esD, "m2_%d" % i, [128, 512], F32) for i in range(2)]
                    DMA("pool", wco[:], kp(w_conv_out), [], ["wco"])
                    DMA("pool", wmo[:], kp(w_mla_out), [], ["wmo"])
                    w_out_v = kp(w_out)
                    DMA("pool", wout[:, :, 0:512], w_out_v[:, :, 0:512], [], ["wout0"])
                    DMA("pool", wout[:, :, 512:1024], w_out_v[:, :, 512:1024], [], ["wout1"])
                    gcount = 0
                    for tb in range(4):
                        tc_ = slice(tb * 512, (tb + 1) * 512)
                        load_own_block(tb, xb, hT)
                        for gi in range(4):
                            gb_ = gcount % 2
                            gcount += 1
                            DMA("pool", gw[gb_][:], w_in_v[:, :, 1696 + gi * 512:1696 + (gi + 1) * 512], [], ["gw%d" % gb_])
                            for j in range(4):
                                oc = gi * 4 + j
                                bank = oc % 4
                                i2 = oc % 2
                                for k in range(8):
                                    MM(ps[:, bank, :], gw[gb_][:, k, j * 128:(j + 1) * 128], xb[:, k, :], k == 0, k == 7, ["gw%d" % gb_, XBK[k]], [PSB(bank)])
                                TT("dve", gs[i2][:], ps[:, bank, :], rstd1[:, tc_], ALU.mult, [PSB(bank), "rstd1"], ["gsD%d" % i2])
                                ACT(sig[:, oc, :], gs[i2][:], AF.Sigmoid, ["gsD%d" % i2], ["sig%d" % oc])
                        for c in range(8):
                            b1 = 4 + (c % 2) * 2
                            b2 = b1 + 1
                            i2 = c % 2
                            for k4 in range(4):
                                MM(ps[:, b1, :], wco[:, k4, c * 128:(c + 1) * 128], convact[:, k4, tc_], k4 == 0, k4 == 3,
                                   ["wco", "ca%d_%d" % (k4, tb)], [PSB(b1)])
                            for hp in range(4):
                                MM(ps[:, b2, :], wmo[:, hp, c * 128:(c + 1) * 128], Onorm[:, hp, tc_], hp == 0, hp == 3,
                                   ["wmo", "On%d_%d" % (2 * hp, tb), "On%d_%d" % (2 * hp + 1, tb)], [PSB(b2)])
                            TT("dve", m1[i2][:], ps[:, b1, :], sig[:, c, :], ALU.mult, [PSB(b1), "sig%d" % c], ["m1_%d" % i2])
                            TT("dve", m2[i2][:], ps[:, b2, :], sig[:, 8 + c, :], ALU.mult, [PSB(b2), "sig%d" % (8 + c)], ["m2_%d" % i2])
                            TT("pool", mg[:, c, :], m1[i2][:], m2[i2][:], ALU.add, ["m1_%d" % i2, "m2_%d" % i2], ["mg%d" % c])
                        for c in range(8):
                            bank = c % 4
                            for k in range(8):
                                MM(ps[:, bank, :], wout[:, k, c * 128:(c + 1) * 128], mg[:, k, :], k == 0, k == 7,
                                   ["wout0", "wout1", "mg%d" % k], [PSB(bank)])
                            TT("dve", hT[:, c, tc_], ps[:, bank, :], hT[:, c, tc_], ALU.add, [PSB(bank), "h%d_%d" % (c, tb)], ["h%d_%d" % (c, tb)])
                    end_phase()
            if stop_after == "D2":
                for c in range(8):
                    dump(hT[:, c, :], T, c * T)
                debug_finish()
                return nc
            with ExitStack() as esE:
                memx = T0(esE, "memx", [128, 8, 256], F32)
                memb = T0(esE, "memb", [128, 8, 256], BF16)
                msqm = T0(esE, "msqm", [128, 8, 256], BF16)
                rmem = T0(esE, "rmem", [128, 256], F32)
                rmemT = T0(esE, "rmemT", [128, 2], F32)
                rtmpE = T0(esE, "rtmpE", [128, 512], F32)
                wxkv = T0(esE, "wxkv", [128, 8, 1024], BF16)
                wxq = T0(esE, "wxq", [128, 8, 512], BF16)
                wxo = T0(esE, "wxo", [128, 4, 1024], BF16)
                Kx = T0(esE, "Kx", [128, 4, 256], BF16)
                Vx = T0(esE, "Vx", [128, 2, 512], BF16)
                kxm = T0(esE, "kxm", [128, 4], F32)
                kmm = T0(esE, "kmm", [128, 4, 128], BF16)
                hb = T0(esE, "hb", [128, 8, 512], BF16)
                hsq = T0(esE, "hsqE", [128, 8, 512], BF16)
                rstd2 = T0(esE, "rstd2", [128, 512], F32)
                Qx = T0(esE, "Qx", [128, 4, 512], BF16)
                aq = T0(esE, "aq", [128, 4, 512], BF16)
                Px = [T0(esE, "Px%d" % i, [128, 512], BF16) for i in range(2)]
                lr = T0(esE, "lr", [128, 512], F32)
                Ox = T0(esE, "Ox", [128, 4, 512], BF16)
                DMA("sp", memx[:], kp(memT), [], ["memx"])
                w_xkv_v = kp(w_xkv)
                DMA("pool", wxkv[:, :, 0:512], w_xkv_v[:, :, 0:512], [], ["wxkv0"])
                DMA("pool", wxkv[:, :, 512:1024], w_xkv_v[:, :, 512:1024], [], ["wxkv1"])
                DMA("pool", wxq[:], kp(w_xq), [], ["wxq"])
                DMA("pool", wxo[:], kp(w_xo), [], ["wxo"])
                for k in range(8):
                    TS("dve", memb[:, k, :], memx[:, k, :], vcol(V_GMEM + k), None, ALU.mult, None, ["memx", "vecs"], ["memb"])
                ACT(msqm[:], memx[:], AF.Square, ["memx"], ["msqm"])
                for k in range(8):
                    MM(ps[:, 0, 0:256], ones[:], msqm[:, k, :], k == 0, k == 7, ["ones", "msqm"], [PSB(0)])
                rstd_from_ps(0, 256, 1.0 / D, rmem[:], "rmem", rtmpE, "rtmpE")
                for kt in range(2):
                    for k in range(8):
                        MM(ps[:, 1, kt:kt + 1], msqm[:, k, kt * 128:(kt + 1) * 128], ones[:, 0:1], k == 0, k == 7, ["ones", "msqm"], [PSB(1)],
                           skip_group_check=True)
                rstd_from_ps(1, 2, 1.0 / D, rmemT[:], "rmemT", rtmpE, "rtmpE")
                for h in range(4):
                    bank = 2 + h % 2
                    for k in range(8):
                        MM(ps[:, bank, 0:256], wxkv[:, k, h * 128:(h + 1) * 128], memb[:, k, :], k == 0, k == 7, ["wxkv0", "memb"], [PSB(bank)])
                    TT("dve", Kx[:, h, :], ps[:, bank, 0:256], rmem[:], ALU.mult, [PSB(bank), "rmem"], ["Kx%d" % h])
                    P.op("dve", lambda e, h=h: e.tensor_reduce(out=kxm[:, h:h + 1], in_=Kx[:, h, :], axis=AX.X, op=ALU.max, apply_absolute_value=True),
                         ["Kx%d" % h], ["kxm%d" % h])
                    TS("dve", kmm[:, h, :], ones[:], kxm[:, h:h + 1], -1.01, ALU.mult, ALU.mult, ["ones", "kxm%d" % h], ["kmm%d" % h])
                for kt in range(2):
                    for k in range(8):
                        MM(ps[:, 4 + kt, :], memb[:, k, kt * 128:(kt + 1) * 128], wxkv[:, k, 512:1024], k == 0, k == 7, ["wxkv1", "memb"], [PSB(4 + kt)])
                    TS("dve", Vx[:, kt, :], ps[:, 4 + kt, :], rmemT[:, kt:kt + 1], None, ALU.mult, None, [PSB(4 + kt), "rmemT"], ["Vx%d" % kt])
                XS_ = 128.0 ** -0.5
                for tb in range(4):
                    tc_ = slice(tb * 512, (tb + 1) * 512)
                    HR = ["h%d_%d" % (k, tb) for k in range(8)]
                    for k in range(8):
                        TS("dve", hb[:, k, :], hT[:, k, tc_], vcol(V_GX + k), None, ALU.mult, None, [HR[k], "vecs"], ["hb%d" % k])
                    blk_stats(hT, tb, hsq, rstd2[:], "rstd2", rtmpE)
                    for h in range(4):
                        for k in range(8):
                            MM(ps[:, 1, :], wxq[:, k, h * 128:(h + 1) * 128], hb[:, k, :], k == 0, k == 7, ["wxq", "hb%d" % k], [PSB(1)])
                        STT(Qx[:, h, :], ps[:, 1, :], XS_, rstd2[:], ALU.mult, ALU.mult, [PSB(1), "rstd2"], ["Qx%d" % h])
                        STT(aq[:, h, :], Qx[:, h, :], -1.0, Qx[:, h, :], ALU.mult, ALU.max, ["Qx%d" % h], ["aq%d" % h])
                        for kt in range(2):
                            bank = 2 + kt
                            MM(ps[:, bank, :], Kx[:, h, kt * 128:(kt + 1) * 128], Qx[:, h, :], True, False, ["Kx%d" % h, "Qx%d" % h], [PSB(bank)])
                            MM(ps[:, bank, :], kmm[:, h, :], aq[:, h, :], False, True, ["kmm%d" % h, "aq%d" % h], [PSB(bank)])
                            ACT(Px[kt][:], ps[:, bank, :], AF.Exp, [PSB(bank)], ["Px%d" % kt])
                        for kt in range(2):
                            MM(ps[:, 4, :], Vx[:, kt, h * 128:(h + 1) * 128], Px[kt][:], kt == 0, kt == 1, ["Vx%d" % kt, "Px%d" % kt], [PSB(4)])
                        for kt in range(2):
                            MM(ps[:, 5, :], ones[:], Px[kt][:], kt == 0, kt == 1, ["ones", "Px%d" % kt], [PSB(5)])
                        RECIP(lr[:], ps[:, 5, :], [PSB(5)], ["lr"])
                        TT("dve", Ox[:, h, :], ps[:, 4, :], lr[:], ALU.mult, [PSB(4), "lr"], ["Ox%d" % h])
                    for c in range(8):
                        bank = 6 + c % 2
                        for h in range(4):
                            MM(ps[:, bank, :], wxo[:, h, c * 128:(c + 1) * 128], Ox[:, h, :], h == 0, h == 3, ["wxo", "Ox%d" % h], [PSB(bank)])
                        TT("dve", hT[:, c, tc_], ps[:, bank, :], hT[:, c, tc_], ALU.add, [PSB(bank), "h%d_%d" % (c, tb)], ["h%d_%d" % (c, tb)])
                end_phase()
            if stop_after == "E":
                for c in range(8):
                    dump(hT[:, c, :], T, c * T)
                debug_finish()
                return nc
            with ExitStack() as esF:
                hb3 = T0(esF, "hb3", [128, 8, T], BF16)
                W1 = [T0(esF, "W1_%d" % i, [128, 8, 256], BF16) for i in range(2)]
                W2 = [T0(esF, "W2_%d" % i, [128, 2, 1024], BF16) for i in range(2)]
                hid = [T0(esF, "hid%d" % i, [128, 2, T], BF16) for i in range(2)]
                rstd3 = T0(esF, "rstd3", [128, T], F32)
                rstd4 = T0(esF, "rstd4", [128, 512], F32)
                hsq = T0(esF, "hsqF", [128, 8, 512], BF16)
                rtmpF = T0(esF, "rtmpF", [128, 512], F32)
                uu = [T0(esF, "uu%d" % i, [128, 512], F32) for i in range(2)]
                vv = [T0(esF, "vv%d" % i, [128, 512], F32) for i in range(2)]
                w1v = kp(w_mlp1)
                w2v = kp(w_mlp2)
                for tb in range(4):
                    tc_ = slice(tb * 512, (tb + 1) * 512)
                    for k in range(8):
                        TS("dve", hb3[:, k, tc_], hT[:, k, tc_], vcol(V_GMLP + k), None, ALU.mult, None, ["h%d_%d" % (k, tb), "vecs"], ["hb3_%d_%d" % (k, tb)])
                    blk_stats(hT, tb, hsq, rstd3[:, tc_], "rstd3_%d" % tb, rtmpF)
                cnt = 0
                for g in range(16):
                    gb_ = g % 2
                    DMA("pool", W1[gb_][:], w1v[:, :, g * 256:(g + 1) * 256], [], ["W1_%d" % gb_])
                    DMA("pool", W2[gb_][:], w2v[:, g * 2:(g + 1) * 2, :], [], ["W2_%d" % gb_])
                    for j in range(2):
                        for tb in range(4):
                            tc_ = slice(tb * 512, (tb + 1) * 512)
                            bank = cnt % 4
                            i2 = cnt % 2
                            cnt += 1
                            for k in range(8):
                                MM(ps[:, bank, :], W1[gb_][:, k, j * 128:(j + 1) * 128], hb3[:, k, tc_], k == 0, k == 7,
                                   ["W1_%d" % gb_, "hb3_%d_%d" % (k, tb)], [PSB(bank)])
                            ACT(uu[i2][:], ps[:, bank, :], AF.Relu, [PSB(bank)], ["uu%d" % i2])
                            TT("dve", vv[i2][:], uu[i2][:], rstd3[:, tc_], ALU.mult, ["uu%d" % i2, "rstd3_%d" % tb], ["vv%d" % i2])
                            TT("pool", hid[gb_][:, j, tc_], vv[i2][:], vv[i2][:], ALU.mult, ["vv%d" % i2], ["hid%d_%d_%d" % (gb_, j, tb)])
                    for c in range(8):
                        for tb in range(4):
                            tc_ = slice(tb * 512, (tb + 1) * 512)
                            bank = 4 + cnt % 4
                            cnt += 1
                            for j in range(2):
                                MM(ps[:, bank, :], W2[gb_][:, j, c * 128:(c + 1) * 128], hid[gb_][:, j, tc_], j == 0, j == 1,
                                   ["W2_%d" % gb_, "hid%d_%d_%d" % (gb_, j, tb)], [PSB(bank)])
                            TT("dve", hT[:, c, tc_], ps[:, bank, :], hT[:, c, tc_], ALU.add, [PSB(bank), "h%d_%d" % (c, tb)], ["h%d_%d" % (c, tb)])
                for tb in range(4):
                    tc_ = slice(tb * 512, (tb + 1) * 512)
                    blk_stats(hT, tb, hsq, rstd4[:], "rstd4", rtmpF)
                    for c in range(8):
                        slot = ring[0] % 4
                        ring[0] += 1
                        XS = "xst%d" % slot
                        STT(xst[:, slot, :], hT[:, c, tc_], vcol(V_GFIN + c), rstd4[:], ALU.mult, ALU.mult, ["h%d_%d" % (c, tb), "rstd4", "vecs"], [XS])
                        DMA("sp", out_d[c * 128:(c + 1) * 128, tc_], xst[:, slot, :], [XS], ["out"])
                end_phase(final=True)
    return nc


def own_chunks(j):
    return [j, 7 - j, 8 + j, 15 - j]


def make_vecs(inp):
    v = np.zeros((128, NV), np.float32)

    def colmajor(g, n):
        return np.ascontiguousarray(np.asarray(g, np.float32).reshape(n, 128).T)

    v[:, V_GMIX:V_GMIX + 8] = colmajor(inp["norm_mix_g"][0], 8)
    v[:, V_GX:V_GX + 8] = colmajor(inp["norm_xattn_g"][0], 8)
    v[:, V_GMLP:V_GMLP + 8] = colmajor(inp["norm_mlp_g"][0], 8)
    v[:, V_GFIN:V_GFIN + 8] = colmajor(inp["final_norm_g"], 8)
    v[:, V_GMEM:V_GMEM + 8] = colmajor(inp["norm_mem_g"][0], 8)
    cw = np.asarray(inp["conv_w"][0], np.float32)
    v[:, V_CONVW:V_CONVW + 124] = cw.T.reshape(4, 128, 31).transpose(1, 0, 2).reshape(128, 124)
    v[:, V_CONVB:V_CONVB + 4] = colmajor(inp["conv_b"][0], 4)
    v[:, V_LNG:V_LNG + 4] = colmajor(inp["conv_ln_g"][0], 4)
    v[:, V_LNB:V_LNB + 4] = colmajor(inp["conv_ln_b"][0], 4)
    v[:, V_GQ:V_GQ + 3] = colmajor(inp["q_norm_g"][0], 3)
    v[:, V_GKV:V_GKV + 2] = colmajor(inp["kv_norm_g"][0], 2)
    half = 16
    invf = (np.float32(10000.0) ** (-np.arange(half, dtype=np.float32) / np.float32(half))).astype(np.float32)
    v[64:80, V_INVF] = invf
    v[80:96, V_INVF] = invf
    return v


def make_core_inputs(inp, core, shared):
    b, j = core // 4, core % 4
    x = np.asarray(inp["x"], np.float32)
    pos = np.asarray(inp["positions"], np.int32)
    chunks = own_chunks(j)
    xT = shared["xT"][b]
    xcat = np.zeros((D, NCAT + 128), np.float32)
    xcat[:, :SEQ] = xT
    poscat = np.zeros((32, NCAT), np.int32)
    poscat[:, :SEQ] = pos[b][None, :]
    qa = np.zeros((16, T), np.float32)
    for s, c in enumerate(chunks):
        xcat[:, SEQ + s * 512:SEQ + (s + 1) * 512] = xT[:, c * 512:(c + 1) * 512]
        poscat[:, SEQ + s * 512:SEQ + (s + 1) * 512] = pos[b][None, c * 512:(c + 1) * 512]
        if c > 0:
            xcat[:, NCAT + s * 32:NCAT + (s + 1) * 32] = xT[:, c * 512 - 32:c * 512]
        for u in range(16):
            if u >= c:
                qa[u, s * 512:(s + 1) * 512] = NEG
    m = dict(shared["common"])
    m.update({"xcat": xcat, "poscat": poscat, "qa": qa, "memT": shared["memT"][b]})
    return m


def make_shared(inp):
    x = np.asarray(inp["x"], np.float32)
    shared = {"xT": [np.ascontiguousarray(x[b].T) for b in range(2)],
              "memT": [np.ascontiguousarray(np.asarray(inp["mem"], np.float32)[b].T) for b in range(2)]}
    ka = np.zeros((17, NCAT), np.float32)
    ka[0, :] = 1.0
    for u in range(16):
        ka[1 + u, u * 512:(u + 1) * 512] = 1.0
    cmat = np.zeros((128, 352), np.float32)
    for i in range(16):
        cmat[64 + 16 + i, 256 + 64 + i] = -1.0
        cmat[64 + i, 256 + 64 + 16 + i] = 1.0
    cmat[:, :128] = np.eye(128, dtype=np.float32)
    kk, qq = np.meshgrid(np.arange(128), np.arange(128), indexing="ij")
    cmat[:, 128:256] = np.where(kk > qq, NEG, 0.0).astype(np.float32)
    common = {"ka": ka, "cmat": cmat, "vecs": make_vecs(inp)}
    for name in ["w_in", "w_conv_out", "w_uq", "w_ukv", "w_mla_out", "w_out", "w_xq", "w_xkv", "w_xo", "w_mlp1", "w_mlp2"]:
        common[name] = np.ascontiguousarray(np.asarray(inp[name], np.float32)[0])
    shared["common"] = common
    return shared


_NC_CACHE = {}


def kernel(**inputs):
    shared = make_shared(inputs)
    in_maps = [make_core_inputs(inputs, c, shared) for c in range(8)]
    if "nc" not in _NC_CACHE:
        _NC_CACHE["nc"] = build_program()
    nc = _NC_CACHE["nc"]
    res = run_bass_kernel_spmd(nc, in_maps, core_ids=list(range(8)))
    out = np.zeros((2, SEQ, D), np.float32)
    for core in range(8):
        b, j = core // 4, core % 4
        o = res.results[core]["out"]
        for s, c in enumerate(own_chunks(j)):
            out[b, c * 512:(c + 1) * 512, :] = o[:, s * 512:(s + 1) * 512].T
    return out
```

```python
import math
import numpy as np
import concourse.bass as bass
import concourse.mybir as mybir
from concourse.alu_op_type import AluOpType as ALU
from concourse.bass_utils import run_bass_kernel_spmd

F32 = mybir.dt.float32
BF16 = mybir.dt.bfloat16
I32 = mybir.dt.int32
AF = mybir.ActivationFunctionType
AX = mybir.AxisListType

D = 1024
SEQ = 8192
T = 2048
NB = 4
NBLK = 20
NCAT = NBLK * 512
EPS = 1e-6
SCALE = 96.0 ** -0.5
NSLOT_UNITS = (3, 7, 11, 15)
NEG = -30000.0
TWO_PI = 2.0 * math.pi

V_GMIX, V_GX, V_GMLP, V_GFIN, V_GMEM = 0, 8, 16, 24, 32
V_CONVW = 40
V_CONVB = 164
V_LNG = 168
V_LNB = 172
V_GQ = 176
V_GKV = 179
V_INVF = 181
NV = 184


class Op:
    __slots__ = ("fn", "waits", "tl", "idx", "is_dma")

    def __init__(self, fn, waits, tl, idx, is_dma):
        self.fn, self.waits, self.tl, self.idx, self.is_dma = fn, waits, tl, idx, is_dma


class Prog:
    ENGS = ("pe", "act", "dve", "pool", "sp")
    COMP = ("pe", "act", "dve", "pool")

    def __init__(self, n_dma=24):
        self.ops = {e: [] for e in self.ENGS}
        self.reg = {}
        self.seen = {e: {} for e in self.ENGS}
        self.cnt = {}
        self.n_dma = n_dma
        self.rr = 0
        self.bar = {}
        self.rank = {}
        self.sigbase = {e: 0 for e in self.COMP}
        self.forced = set()

    def _add(self, eng, fn, reads, writes, tl, is_dma):
        idx = self.cnt.get(tl, 0) + 1
        self.cnt[tl] = idx
        need = dict(self.bar)

        def req(t, i):
            if need.get(t, 0) < i:
                need[t] = i

        for r in reads:
            e = self.reg.get(r)
            if e is not None and e[0] is not None:
                req(*e[0])
        for r in writes:
            e = self.reg.get(r)
            if e is not None:
                if e[0] is not None:
                    req(*e[0])
                for t, i in e[1].items():
                    req(t, i)
        if is_dma and idx > 1:
            req(tl, idx - 1)
        waits = []
        sn = self.seen[eng]
        for t, i in need.items():
            if t == eng and not is_dma:
                continue
            if sn.get(t, 0) >= i:
                continue
            sn[t] = i
            waits.append((t, i))
        self.ops[eng].append(Op(fn, waits, tl, idx, is_dma))
        for r in reads:
            e = self.reg.setdefault(r, [None, {}])
            if e[1].get(tl, 0) < idx:
                e[1][tl] = idx
        for r in writes:
            self.reg[r] = [(tl, idx), {}]
        return idx

    def op(self, eng, fn, reads=(), writes=()):
        self._add(eng, fn, reads, writes, eng, False)

    def dma(self, eng, fn, reads=(), writes=()):
        tl = "q%d" % self.rr
        self.rr = (self.rr + 1) % self.n_dma
        self._add(eng, fn, reads, writes, tl, True)

    def phase_end(self, sigfns=None):
        for eng in self.ENGS:
            for t in self.COMP:
                self.seen[eng][t] = self.cnt.get(t, 0)

    def finish_waits(self, eng):
        waits = []
        for t, i in self.cnt.items():
            if t.startswith("q") and self.seen[eng].get(t, 0) < i:
                self.seen[eng][t] = i
                waits.append((t, i))
        self.ops[eng].append(Op(None, waits, None, 0, False))

    def emit(self, block, sems):
        sig = {e: set() for e in self.COMP}
        for e in self.ENGS:
            for o in self.ops[e]:
                for t, i in o.waits:
                    if t in sig and (t, i) not in self.rank:
                        sig[t].add(i)
        for (t, i) in self.forced:
            sig[t].add(i)
        for t, s in sig.items():
            for r, i in enumerate(sorted(s)):
                self.rank[(t, i)] = self.sigbase[t] + r + 1
            self.sigbase[t] += len(s)
        rank = self.rank

        def val(t, i):
            return 16 * i if t.startswith("q") else rank[(t, i)]

        def run(e, handle):
            for o in self.ops[e]:
                for t, i in o.waits:
                    handle.wait_ge(sems[t], val(t, i))
                if o.fn is None:
                    continue
                ins = o.fn(handle)
                if o.is_dma:
                    ins.then_inc(sems[o.tl], 16)
                elif (o.tl, o.idx) in rank:
                    ins.then_inc(sems[o.tl], 1)

        if self.ops["pe"]:
            block.tensor(lambda h: run("pe", h))
        if self.ops["act"]:
            block.scalar(lambda h: run("act", h))
        if self.ops["dve"]:
            block.vector(lambda h: run("dve", h))
        if self.ops["pool"]:
            block.gpsimd(lambda h: run("pool", h))
        if self.ops["sp"]:
            block.sync(lambda h: run("sp", h))

        self.ops = {e: [] for e in self.ENGS}
        self.forced = set()


def build_program(debug=None):
    from contextlib import ExitStack
    nc = bass.Bass("TRN2", target_bir_lowering=False)
    P = Prog()
    dbg = debug or {}
    stop_after = dbg.get("stop", "Z")

    def din(name, shape, dt=F32):
        return nc.dram_tensor(name, list(shape), dt, kind="ExternalInput").ap()

    xcat = din("xcat", [D, NCAT + 128])
    poscat = din("poscat", [32, NCAT], I32)
    qa = din("qa", [16, T])
    ka = din("ka", [17, NCAT])
    cmat = din("cmat", [128, 352])
    vecs_d = din("vecs", [128, NV])
    memT = din("memT", [D, 256])
    w_in = din("w_in", [D, 3744])
    w_conv_out = din("w_conv_out", [512, D])
    w_uq = din("w_uq", [384, 768])
    w_ukv = din("w_ukv", [256, 1024])
    w_mla_out = din("w_mla_out", [512, D])
    w_out = din("w_out", [D, D])
    w_xq = din("w_xq", [D, 512])
    w_xkv = din("w_xkv", [D, 1024])
    w_xo = din("w_xo", [512, D])
    w_mlp1 = din("w_mlp1", [D, 4096])
    w_mlp2 = din("w_mlp2", [4096, D])
    out_d = nc.dram_tensor("out", [D, T], F32, kind="ExternalOutput").ap()
    dbg_d = nc.dram_tensor("dbg", [128, dbg["n"]], F32, kind="ExternalOutput").ap() if debug else None

    def kp(ap):
        return ap.rearrange("(k p) n -> p k n", p=128)

    xcat_v = kp(xcat)
    w_in_v = kp(w_in)

    def MM(out, lhsT, rhs, start, stop, reads, writes, **kw):
        P.op("pe", lambda e: e.matmul(out, lhsT=lhsT, rhs=rhs, start=start, stop=stop, **kw), reads, writes)

    def ACT(out, in_, func, reads, writes, **kw):
        P.op("act", lambda e: e.activation(out=out, in_=in_, func=func, **kw), reads, writes)

    def TT(eng, out, in0, in1, op, reads, writes):
        P.op(eng, lambda e: e.tensor_tensor(out=out, in0=in0, in1=in1, op=op), reads, writes)

    def TS(eng, out, in0, s1, s2, op0, op1, reads, writes):
        if op1 is None:
            P.op(eng, lambda e: e.tensor_scalar(out=out, in0=in0, scalar1=s1, scalar2=None, op0=op0), reads, writes)
        else:
            P.op(eng, lambda e: e.tensor_scalar(out=out, in0=in0, scalar1=s1, scalar2=s2, op0=op0, op1=op1), reads, writes)

    def STT(out, in0, scalar, in1, op0, op1, reads, writes):
        P.op("dve", lambda e: e.scalar_tensor_tensor(out=out, in0=in0, scalar=scalar, in1=in1, op0=op0, op1=op1), reads, writes)

    def CP(eng, out, in_, reads, writes):
        P.op(eng, lambda e: e.tensor_copy(out=out, in_=in_), reads, writes)

    def MS(eng, out, val, writes):
        P.op(eng, lambda e: e.memset(out, val), (), writes)

    def RECIP(out, in_, reads, writes):
        P.op("dve", lambda e: e.reciprocal(out=out, in_=in_), reads, writes)

    def DMA(eng, out, in_, reads, writes):
        P.dma(eng, lambda e: e.dma_start(out=out, in_=in_), reads, writes)

    def PSB(b):
        return "ps%d" % b

    with ExitStack() as es0:
        def T0(es, name, shape, dt):
            return es.enter_context(nc.sbuf_tensor("sb_" + name, list(shape), dt))

        ps = es0.enter_context(nc.psum_tensor("ps", [128, 8, 512], F32))
        sems = {}
        for t in list(Prog.COMP) + ["q%d" % i for i in range(P.n_dma)]:
            sems[t] = es0.enter_context(nc.semaphore("s_" + t))
        vecs = T0(es0, "vecs", [128, NV], F32)
        ident = T0(es0, "ident", [128, 128], BF16)
        tri = T0(es0, "tri", [128, 128], BF16)
        rotm = T0(es0, "rotm", [128, 96], BF16)
        ones = T0(es0, "ones", [128, 128], BF16)
        onesf = T0(es0, "onesf", [128, 128], F32)
        epsb = T0(es0, "epsb", [128, 1], F32)
        scr = T0(es0, "scr", [128, 16], F32)
        rstd1 = T0(es0, "rstd1", [128, T], F32)
        Onorm = T0(es0, "Onorm", [128, 4, T], BF16)
        xst = T0(es0, "xst", [128, 4, 512], F32)

        def vcol(c, lo=0, hi=128):
            return vecs[lo:hi, c:c + 1]

        sigfns = {
            "pe": lambda e: e.matmul(ps[0:1, 7, 0:1], lhsT=ones[0:1, 0:1], rhs=ones[0:1, 0:1], start=True, stop=True),
            "act": lambda e: e.activation(out=scr[0:1, 0:1], in_=scr[0:1, 1:2], func=AF.Copy),
            "dve": lambda e: e.memset(scr[0:1, 2:3], 0.0),
            "pool": lambda e: e.memset(scr[0:1, 3:4], 0.0),
        }

        def end_phase(final=False):
            if final:
                P.finish_waits("sp")
            else:
                P.phase_end(sigfns)
            with nc.Block() as block:
                P.emit(block, sems)

        dumps = []

        def dump(ap, n, col):
            dumps.append((ap, n, col))

        def debug_finish():
            for (ap, n, col) in dumps:
                for o in range(0, n, 512):
                    w = min(512, n - o)
                    slot = (o // 512) % 4
                    CP("dve", xst[:, slot, 0:w], ap[:, o:o + w], [], ["xst%d" % slot])
                    DMA("sp", dbg_d[:, col + o:col + o + w], xst[:, slot, 0:w], ["xst%d" % slot], ["dbgout"])
            MS("dve", xst[:, 0, :], 0.0, ["xst0"])
            for c in range(8):
                for tb in range(4):
                    DMA("sp", out_d[c * 128:(c + 1) * 128, tb * 512:(tb + 1) * 512], xst[:, 0, :], ["xst0"], ["out"])
            end_phase(final=True)

        DMA("sp", vecs[:], vecs_d, [], ["vecs"])
        DMA("pool", ident[:], cmat[:, 0:128], [], ["ident"])
        DMA("pool", tri[:], cmat[:, 128:256], [], ["tri"])
        DMA("pool", rotm[:], cmat[:, 256:352], [], ["rotm"])
        MS("dve", ones[:], 1.0, ["ones"])
        MS("dve", onesf[:], 1.0, ["onesf"])
        MS("dve", epsb[:], EPS, ["epsb"])
        MS("dve", scr[:], 0.0, ["scr"])

        def rstd_from_ps(bank, n, inv_count, out_ap, out_reg, tmp, tmp_reg):
            ACT(tmp[:, 0:n], ps[:, bank, 0:n], AF.Sqrt, [PSB(bank), "epsb"], [tmp_reg], bias=epsb[:], scale=inv_count)
            RECIP(out_ap, tmp[:, 0:n], [tmp_reg], [out_reg])

        RB = slice(64, 96)
        ring = [0]

        def load_xblock(col0, xb, gcol, sqt=None, width=512, tag="", eng="dve"):
            for k in range(8):
                slot = ring[0] % 4
                ring[0] += 1
                XS = "xst%d" % slot
                DMA("sp", xst[:, slot, 0:width], xcat_v[:, k, col0:col0 + width], [], [XS])
                TS(eng, xb[:, k, 0:width], xst[:, slot, 0:width], vcol(gcol + k), None, ALU.mult, None, [XS, "vecs"], ["xb%s%d" % (tag, k)])
                if sqt is not None:
                    ACT(sqt[:, k, 0:width], xst[:, slot, 0:width], AF.Square, [XS], ["sq%s%d" % (tag, k)])

        XBK = ["xb%d" % k for k in range(8)]
        SQK = ["sq%d" % k for k in range(8)]

        with ExitStack() as esAC:
            ckvn = T0(esAC, "ckvn", [128, 2, NCAT], BF16)
            Kb = T0(esAC, "Kb", [128, NCAT], BF16)
            cqn = T0(esAC, "cqn", [128, 3, T], BF16)
            qcos = T0(esAC, "qcos", [128, T], BF16)
            qsin = T0(esAC, "qsin", [128, T], BF16)
            kmxr = T0(esAC, "kmxr", [128, NBLK], F32)

            blks = dbg.get("blks", [b for b in range(NBLK) if b != 15])
            with ExitStack() as esA:
                xbs = [T0(esA, "xbA%d" % i, [128, 8, 512], BF16) for i in range(3)]
                sqs = [T0(esA, "sqA%d" % i, [128, 8, 512], BF16) for i in range(1)]
                wA = T0(esA, "wA", [128, 8, 832], BF16)
                ckv = T0(esA, "ckv", [128, 2, 512], F32)
                cq = T0(esA, "cq", [128, 3, 512], F32)
                sq2 = T0(esA, "sq2", [128, 3, 512], BF16)
                rtmp = T0(esA, "rtmp", [128, 512], F32)
                rstdA = T0(esA, "rstdA", [128, 512], F32)
                rkv = T0(esA, "rkv", [128, 512], F32)
                rq = T0(esA, "rq", [128, 512], F32)
                posi = T0(esA, "posi", [128, 512], I32)
                ti = T0(esA, "ti", [128, 512], I32)
                ang = T0(esA, "ang", [128, 512], F32)
                tf = T0(esA, "tf", [128, 512], F32)
                rr_ = T0(esA, "rr", [128, 512], F32)
                mm_ = T0(esA, "mm", [128, 512], F32)
                sinb = T0(esA, "sinb", [128, 512], F32)
                cosb = T0(esA, "cosb", [128, 512], F32)
                t1 = T0(esA, "t1A", [128, 512], F32)
                t2 = T0(esA, "t2A", [128, 512], F32)

                DMA("pool", wA[:, :, 0:640], w_in_v[:, :, 1024:1664], [], ["wA"])
                MS("dve", wA[:, :, 640:704], 0.0, ["wAz1"])
                MS("dve", wA[:, :, 736:800], 0.0, ["wAz2"])
                DMA("pool", wA[:, :, 704:736], w_in_v[:, :, 1664:1696], [], ["wAr"])
                DMA("pool", wA[:, :, 800:816], w_in_v[:, :, 1680:1696], [], ["wArot1"])
                DMA("pool", wA[:, :, 816:832], w_in_v[:, :, 1664:1680], [], ["wArot2"])
                TS("dve", wA[:, :, 800:816], wA[:, :, 800:816], -1.0, None, ALU.mult, None, ["wArot1"], ["wArot1"])
                WA_ALL = ["wA", "wAz1", "wAz2", "wAr", "wArot1", "wArot2"]
                for k in range(8):
                    TS("dve", wA[:, k, :], wA[:, k, :], vcol(V_GMIX + k), None, ALU.mult, None, WA_ALL + ["vecs"], WA_ALL)
                DMA("pool", Kb[96:113, :], ka, [], ["Kconst"])
                MS("dve", kmxr[:], 0.0, ["kmxr"])

                krs = T0(esA, "krs", [128, 512], BF16)

                def stage1(bn, bi):
                    c0 = bi * 512
                    xb = xbs[bn % 3]
                    sq = sqs[0]
                    XB = "xb%d" % (bn % 3)
                    SQ = "sq0"
                    B0 = 4 * (bn % 2)
                    DMA("pool", xb[:], xcat_v[:, :, c0:c0 + 512], [], [XB])
                    ACT(sq[:], xb[:], AF.Square, [XB], [SQ])
                    for k in range(8):
                        MM(ps[:, B0, :], ones[:], sq[:, k, :], k == 0, k == 7, ["ones", SQ], [PSB(B0)])
                    for c in range(2):
                        for k in range(8):
                            MM(ps[:, B0 + 1 + c, :], wA[:, k, 384 + c * 128:384 + (c + 1) * 128], xb[:, k, :], k == 0, k == 7,
                               WA_ALL + [XB], [PSB(B0 + 1 + c)])
                    for k in range(8):
                        MM(ps[0:96, B0 + 3, :], wA[:, k, 640:736], xb[:, k, :], k == 0, k == 7, WA_ALL + [XB], [PSB(B0 + 3)])

                def stage2(bn, bi):
                    own = bi >= 16
                    c0 = bi * 512
                    oc0 = (bi - 16) * 512
                    xb = xbs[bn % 3]
                    XB = "xb%d" % (bn % 3)
                    B0 = 4 * (bn % 2)
                    DMA("sp", posi[RB, :], poscat[:, c0:c0 + 512], [], ["posi"])
                    rs_ap = rstd1[:, oc0:oc0 + 512] if own else rstdA[:]
                    rs_reg = "rstd1" if own else "rstdA"
                    rstd_from_ps(B0, 512, 1.0 / D, rs_ap, rs_reg, rtmp, "rtmp")
                    for c in range(2):
                        TT("dve", ckv[:, c, :], ps[:, B0 + 1 + c, :], rs_ap, ALU.mult, [PSB(B0 + 1 + c), rs_reg], ["ckv"])
                    TT("dve", krs[RB, :], ps[RB, B0 + 3, :], rs_ap[RB, :], ALU.mult, [PSB(B0 + 3), rs_reg], ["krs"])
                    ACT(sq2[:, 0:2, :], ckv[:], AF.Square, ["ckv"], ["sq2"])
                    for c in range(2):
                        MM(ps[:, B0, :], ones[:], sq2[:, c, :], c == 0, c == 1, ["ones", "sq2"], [PSB(B0)])
                    MM(ps[0:96, B0 + 3, :], rotm[RB, :], krs[RB, :], True, True, ["rotm", "krs"], [PSB(B0 + 3)])
                    rstd_from_ps(B0, 512, 1.0 / 256, rkv[:], "rkv", rtmp, "rtmp")
                    for c in range(2):
                        STT(ckvn[:, c, c0:c0 + 512], ckv[:, c, :], vcol(V_GKV + c), rkv[:], ALU.mult, ALU.mult,
                            ["ckv", "rkv", "vecs"], ["ckvn%d" % bi])
                    if own:
                        for c in range(3):
                            for k in range(8):
                                MM(ps[:, B0 + c, :], wA[:, k, c * 128:(c + 1) * 128], xb[:, k, :], k == 0, k == 7,
                                   WA_ALL + [XB], [PSB(B0 + c)])
                    CP("dve", ang[RB, :], posi[RB, :], ["posi"], ["ang"])
                    TS("dve", ang[RB, :], ang[RB, :], vcol(V_INVF, 64, 96), None, ALU.mult, None, ["ang", "vecs"], ["ang"])
                    TS("dve", ti[RB, :], ang[RB, :], 1.0 / TWO_PI, None, ALU.mult, None, ["ang"], ["ti"])
                    CP("dve", tf[RB, :], ti[RB, :], ["ti"], ["tf"])
                    STT(rr_[RB, :], tf[RB, :], -TWO_PI, ang[RB, :], ALU.mult, ALU.add, ["tf", "ang"], ["rr"])
                    TS("dve", rr_[RB, :], rr_[RB, :], math.pi, -math.pi, ALU.min, ALU.max, ["rr"], ["rr"])
                    TS("dve", mm_[RB, :], rr_[RB, :], math.pi / 2, -TWO_PI, ALU.is_gt, ALU.mult, ["rr"], ["mm"])
                    STT(mm_[RB, :], rr_[RB, :], math.pi / 2, mm_[RB, :], ALU.add, ALU.add, ["rr", "mm"], ["mm"])
                    TS("dve", mm_[RB, :], mm_[RB, :], math.pi, -math.pi, ALU.min, ALU.max, ["mm"], ["mm"])
                    ACT(sinb[RB, :], rr_[RB, :], AF.Sin, ["rr"], ["sinb"])
                    ACT(cosb[RB, :], mm_[RB, :], AF.Sin, ["mm"], ["cosb"])
                    if own:
                        TS("dve", qcos[RB, oc0:oc0 + 512], cosb[RB, :], SCALE, None, ALU.mult, None, ["cosb"], ["qcos"])
                        TS("dve", qsin[RB, oc0:oc0 + 512], sinb[RB, :], SCALE, None, ALU.mult, None, ["sinb"], ["qsin"])
                    TT("dve", t1[RB, :], krs[RB, :], cosb[RB, :], ALU.mult, ["krs", "cosb"], ["t1"])
                    TT("dve", t2[RB, :], ps[RB, B0 + 3, :], sinb[RB, :], ALU.mult, [PSB(B0 + 3), "sinb"], ["t2"])
                    TT("dve", Kb[RB, c0:c0 + 512], t1[RB, :], t2[RB, :], ALU.add, ["t1", "t2"], ["Kr%d" % bi])
                    P.op("dve", lambda e: e.tensor_reduce(out=kmxr[RB, bi:bi + 1], in_=Kb[RB, c0:c0 + 512], axis=AX.X,
                                                          op=ALU.max, apply_absolute_value=True),
                         ["Kr%d" % bi], ["kmxr"])
                    if own:
                        for c in range(3):
                            TT("dve", cq[:, c, :], ps[:, B0 + c, :], rs_ap, ALU.mult, [PSB(B0 + c), rs_reg], ["cq"])
                        ACT(sq2[:], cq[:], AF.Square, ["cq"], ["sq2"])
                        for c in range(3):
                            MM(ps[:, B0 + 3, :], ones[:], sq2[:, c, :], c == 0, c == 2, ["ones", "sq2"], [PSB(B0 + 3)])
                        rstd_from_ps(B0 + 3, 512, 1.0 / 384, rq[:], "rq", rtmp, "rtmp")
                        for c in range(3):
                            STT(cqn[:, c, oc0:oc0 + 512], cq[:, c, :], vcol(V_GQ + c), rq[:], ALU.mult, ALU.mult,
                                ["cq", "rq", "vecs"], ["cqn"])

                for bn, bi in enumerate(blks):
                    stage1(bn, bi)
                    if bn > 0:
                        stage2(bn - 1, blks[bn - 1])
                stage2(len(blks) - 1, blks[-1])
                end_phase()

            if stop_after == "A":
                dump(ckvn[:, 0, :], NCAT, 0)
                dump(ckvn[:, 1, :], NCAT, NCAT)
                dump(Kb[:, :], NCAT, 2 * NCAT)
                for c in range(3):
                    dump(cqn[:, c, :], T, 3 * NCAT + c * T)
                dump(rstd1[:, :], T, 3 * NCAT + 3 * T)
                dump(qcos[:, :], T, 3 * NCAT + 4 * T)
                dump(qsin[:, :], T, 3 * NCAT + 5 * T)
                debug_finish()
                return nc
            with ExitStack() as esC:
                Vb = T0(esC, "Vb", [128, 80, 192], BF16)
                Qb = [T0(esC, "Qb%d" % i, [128, T], BF16) for i in range(2)]
                Pb = [T0(esC, "Pb%d" % i, [128, 2, 512], BF16) for i in range(3)]
                wukv = T0(esC, "wukv", [128, 2, 1024], BF16)
                wuq = T0(esC, "wuq", [128, 3, 768], BF16)
                wqrot = T0(esC, "wqrot", [128, 3, 8, 96], BF16)
                absq = T0(esC, "absq", [128, T], BF16)
                kmxmat = T0(esC, "kmxmat", [128, 97], BF16)
                kmxn = T0(esC, "kmxn", [128, NBLK], F32)
                kmxf = T0(esC, "kmxf", [128, 2], F32)
                rl = T0(esC, "rl", [128, 512], F32)
                bc = T0(esC, "bc", [128, 512], F32)
                t1 = T0(esC, "t1C", [128, 512], F32)
                t2 = T0(esC, "t2C", [128, 512], F32)

                DMA("pool", wukv[:], kp(w_ukv), [], ["wukv"])
                DMA("pool", wuq[:], kp(w_uq), [], ["wuq"])
                w_uq4 = w_uq.rearrange("(k p) (h c) -> p k h c", p=128, c=96)
                MS("dve", wqrot[:, :, :, 0:64], 0.0, ["wqrot0"])
                for c in range(3):
                    DMA("pool", wqrot[:, c, :, 64:80], w_uq4[:, c, :, 80:96], [], ["wqrot1_%d" % c])
                    DMA("pool", wqrot[:, c, :, 80:96], w_uq4[:, c, :, 64:80], [], ["wqrot2_%d" % c])
                TS("dve", wqrot[:, :, :, 64:80], wqrot[:, :, :, 64:80], -1.0, None, ALU.mult, None, ["wqrot1_0", "wqrot1_1", "wqrot1_2"], ["wqrot1"])
                WQR = ["wqrot0", "wqrot1", "wqrot2_0", "wqrot2_1", "wqrot2_2"]
                for i in range(2):
                    DMA("pool", Qb[i][97:113, :], qa, [], ["Qm%d" % i])
                MS("pool", Vb[:, :, 64:65], 1.0, ["Vc1"])
                MS("pool", Vb[:, :, 65:128], 0.0, ["Vc0"])
                MS("dve", kmxmat[:], 0.0, ["kmxmat"])
                MS("dve", kmxn[:], 0.0, ["kmxn"])
                P.op("dve", lambda e: e.tensor_reduce(out=kmxf[RB, 1:2], in_=kmxr[RB, :], axis=AX.X, op=ALU.max), ["kmxr"], ["kmxf1"])
                TS("dve", kmxmat[RB, 96:97], kmxf[RB, 1:2], 1.01, None, ALU.mult, None, ["kmxf1", "kmxmat"], ["kmxmat_r"])

                kv_blocks = [b for b in range(NBLK) if b != 15]

                def prepK(h, bi):
                    cols = bi * 512
                    for c in range(2):
                        MM(ps[0:64, 7, :], wukv[:, c, h * 128:h * 128 + 64], ckvn[:, c, cols:cols + 512], c == 0, c == 1,
                           ["wukv", "ckvn%d" % bi], [PSB(7)])
                    CP("dve", Kb[0:64, cols:cols + 512], ps[0:64, 7, :], [PSB(7)], ["Kn%d" % bi])
                    P.op("dve", lambda e: e.tensor_reduce(out=kmxn[0:64, bi:bi + 1], in_=ps[0:64, 7, :], axis=AX.X, op=ALU.max,
                                                          apply_absolute_value=True), [PSB(7)], ["kmxn"])

                def prepV(h, bi):
                    cols = bi * 512
                    vdat = 0 if h % 2 == 0 else 128
                    for t in range(4):
                        for c in range(2):
                            MM(ps[:, 6, t * 64:(t + 1) * 64], ckvn[:, c, cols + t * 128:cols + (t + 1) * 128],
                               wukv[:, c, h * 128 + 64:h * 128 + 128], c == 0, c == 1, ["wukv", "ckvn%d" % bi], [PSB(6)], skip_group_check=True)
                    CP("dve", Vb[:, bi * 4:(bi + 1) * 4, vdat:vdat + 64], ps[:, 6, 0:256].rearrange("p (t d) -> p t d", d=64),
                       [PSB(6)], ["V%d_%d" % (h % 2, bi)])

                def prepKV(h, bi):
                    prepK(h, bi)
                    prepV(h, bi)

                def prepQ(h, tbs=(0, 1, 2, 3)):
                    qb = h % 2
                    for tb in tbs:
                        cols = tb * 512
                        for c in range(3):
                            MM(ps[0:96, 6, :], wuq[:, c, h * 96:(h + 1) * 96], cqn[:, c, cols:cols + 512], c == 0, c == 2,
                               ["wuq", "cqn"], [PSB(6)])
                        for c in range(3):
                            MM(ps[0:96, 7, :], wqrot[:, c, h, :], cqn[:, c, cols:cols + 512], c == 0, c == 2,
                               WQR + ["cqn"], [PSB(7)])
                        TS("dve", Qb[qb][0:64, cols:cols + 512], ps[0:64, 6, :], SCALE, None, ALU.mult, None, [PSB(6)], ["Qn%d_%d" % (qb, tb)])
                        TT("dve", t1[RB, :], ps[RB, 6, :], qcos[RB, cols:cols + 512], ALU.mult, [PSB(6), "qcos"], ["t1"])
                        TT("dve", t2[RB, :], ps[RB, 7, :], qsin[RB, cols:cols + 512], ALU.mult, [PSB(7), "qsin"], ["t2"])
                        TT("pool", Qb[qb][RB, cols:cols + 512], t1[RB, :], t2[RB, :], ALU.add, ["t1", "t2"], ["Qr%d_%d" % (qb, tb)])
                        STT(absq[0:96, cols:cols + 512], Qb[qb][0:96, cols:cols + 512], -1.0, Qb[qb][0:96, cols:cols + 512], ALU.mult, ALU.max,
                            ["Qn%d_%d" % (qb, tb), "Qr%d_%d" % (qb, tb)], ["absq%d" % tb])

                def prepQstab(h):
                    qb = h % 2
                    P.op("dve", lambda e: e.tensor_reduce(out=kmxf[0:64, 0:1], in_=kmxn[0:64, :], axis=AX.X, op=ALU.max), ["kmxn"], ["kmxf0"])
                    TS("dve", kmxmat[0:64, 96:97], kmxf[0:64, 0:1], 1.01, None, ALU.mult, None, ["kmxf0", "kmxmat"], ["kmxmat_n"])
                    for tb in range(4):
                        cols = tb * 512
                        MM(ps[0:97, 7, :], kmxmat[0:96, 0:97], absq[0:96, cols:cols + 512], True, True,
                           ["kmxmat", "kmxmat_r", "kmxmat_n", "absq%d" % tb], [PSB(7)])
                        ACT(Qb[qb][96:97, cols:cols + 512], ps[96:97, 7, :], AF.Copy, [PSB(7)], ["Qs%d_%d" % (qb, tb)], scale=-1.0)

                grp = [0]

                def make_groups(h):
                    out = []
                    for si, s in enumerate((3, 2, 1, 0)):
                        tiles = [(kt, 0) for kt in range(4 * NSLOT_UNITS[s])] + [(64 + 4 * s + a, 128 * a) for a in range(4)]
                        ntile = len(tiles)
                        accb = 4 + ((h * 4 + si) % 2)
                        for g0 in range(0, ntile, 2):
                            gi = grp[0]
                            grp[0] += 1
                            out.append(dict(h=h, s=s, pair=tiles[g0:g0 + 2], g0=g0, ntile=ntile, accb=accb,
                                            gb=2 * (gi % 2), pi=gi % 3, last=(g0 + 2 >= ntile)))
                    return out

                def emit_S(G):
                    h, s, gb, pi = G["h"], G["s"], G["gb"], G["pi"]
                    qb = h % 2
                    qc = s * 512
                    pair = G["pair"]
                    PREG = "P%d" % pi
                    qreads = ["Qn%d_%d" % (qb, s), "Qr%d_%d" % (qb, s), "Qs%d_%d" % (qb, s), "Qm%d" % qb]
                    for i, (kt, off) in enumerate(pair):
                        bi = kt // 4
                        diag = bi >= 16
                        MM(ps[:, gb + i, off:512], Kb[0:113, kt * 128:(kt + 1) * 128], Qb[qb][0:113, qc + off:qc + 512], True, not diag,
                           ["Kn%d" % bi, "Kr%d" % bi, "Kconst"] + qreads, [PSB(gb + i)], skip_group_check=True)
                        if diag:
                            MM(ps[:, gb + i, off:off + 128], ident[:], tri[:], False, True, ["ident", "tri"], [PSB(gb + i)], skip_group_check=True)
                    if all(off == 0 for _, off in pair) and len(pair) == 2:
                        ACT(Pb[pi][:, 0:2, :], ps[:, gb:gb + 2, :], AF.Exp, [PSB(gb), PSB(gb + 1)], [PREG])
                    else:
                        for i, (kt, off) in enumerate(pair):
                            ACT(Pb[pi][:, i, off:512], ps[:, gb + i, off:512], AF.Exp, [PSB(gb + i)], [PREG])

                def emit_PV(G):
                    h, s, gb, pi, accb = G["h"], G["s"], G["gb"], G["pi"], G["accb"]
                    voff = 0 if h % 2 == 0 else 64
                    qc = s * 512
                    PREG = "P%d" % pi
                    for i, (kt, off) in enumerate(G["pair"]):
                        bi = kt // 4
                        first = (G["g0"] + i == 0)
                        last = (G["g0"] + i == G["ntile"] - 1)
                        MM(ps[:, accb, off:512], Vb[:, kt, voff:voff + 128], Pb[pi][:, i, off:512], first, last,
                           ["V%d_%d" % (h % 2, bi), "Vc1", "Vc0", PREG], [PSB(accb)], skip_group_check=True)
                    if G["last"]:
                        r0 = 64 if h % 2 == 0 else 0
                        rows = slice(0, 64) if h % 2 == 0 else slice(64, 128)
                        RECIP(rl[r0:r0 + 1, :], ps[r0:r0 + 1, accb, :], [PSB(accb)], ["rl"])
                        MM(ps[:, 6, :], onesf[r0:r0 + 1, :], rl[r0:r0 + 1, :], True, True, ["onesf", "rl"], [PSB(6)])
                        ACT(bc[:], ps[:, 6, :], AF.Copy, [PSB(6)], ["bc"])
                        TT("dve", Onorm[rows, h // 2, qc:qc + 512], ps[rows, accb, :], bc[rows, :], ALU.mult, [PSB(accb), "bc"], ["On%d_%d" % (h, s)])

                NH = dbg.get("nheads", 8)
                prepQ(0)
                for bi in kv_blocks:
                    prepKV(0, bi)
                prepQstab(0)
                freed = {3: [11, 12, 13, 14, 19], 2: [7, 8, 9, 10, 18], 1: [3, 4, 5, 6, 17], 0: [0, 1, 2, 16]}
                prev = None
                EVERY = dbg.get("every", 2)
                for h in range(NH):
                    tasks = []
                    if h + 1 < NH:
                        for tb in range(4):
                            tasks.append(lambda h=h, tb=tb: prepQ(h + 1, (tb,)))
                        for bi in kv_blocks:
                            tasks.append(lambda h=h, bi=bi: prepV(h + 1, bi))
                    groups = make_groups(h)
                    for gi_, G in enumerate(groups):
                        emit_S(G)
                        if prev is not None:
                            emit_PV(prev)
                        prev = G
                        if G["last"] and h + 1 < NH:
                            newt = [(lambda h=h, bi=bi: prepK(h + 1, bi)) for bi in freed[G["s"]]]
                            tasks = newt + tasks
                        if tasks and gi_ % EVERY == 0:
                            tasks.pop(0)()
                    while tasks:
                        tasks.pop(0)()
                    if h + 1 < NH:
                        prepQstab(h + 1)
                emit_PV(prev)
                end_phase()

            if stop_after == "C":
                for hp in range(4):
                    dump(Onorm[:, hp, :], T, hp * T)
                debug_finish()
                return nc
        def load_own_block(tb, xb, hT=None):
            col0 = SEQ + tb * 512
            for k in range(8):
                slot = ring[0] % 4
                ring[0] += 1
                XS = "xst%d" % slot
                DMA("sp", xst[:, slot, :], xcat_v[:, k, col0:col0 + 512], [], [XS])
                TS("dve", xb[:, k, :], xst[:, slot, :], vcol(V_GMIX + k), None, ALU.mult, None, [XS, "vecs"], ["xb%d" % k])
                if hT is not None:
                    ACT(hT[:, k, tb * 512:(tb + 1) * 512], xst[:, slot, :], AF.Copy, [XS], ["h%d_%d" % (k, tb)])
            return

        def load_own_block_h(tb, xb, hT):
            col0 = SEQ + tb * 512
            for k in range(8):
                HR = "h%d_%d" % (k, tb)
                DMA("sp", hT[:, k, tb * 512:(tb + 1) * 512], xcat_v[:, k, col0:col0 + 512], [], [HR])
                TS("dve", xb[:, k, :], hT[:, k, tb * 512:(tb + 1) * 512], vcol(V_GMIX + k), None, ALU.mult, None, [HR, "vecs"], ["xb%d" % k])

        def blk_stats(hT, tb, hsq, rout, rout_reg, rtmp):
            ACT(hsq[:], hT[:, :, tb * 512:(tb + 1) * 512], AF.Square, ["h%d_%d" % (k, tb) for k in range(8)], ["hsq"])
            for k in range(8):
                MM(ps[:, 0, :], ones[:], hsq[:, k, :], k == 0, k == 7, ["ones", "hsq"], [PSB(0)])
            rstd_from_ps(0, 512, 1.0 / D, rout, rout_reg, rtmp, "rtmpS")

        with ExitStack() as esH:
            hT = T0(esH, "hT", [128, 8, T], F32)
            with ExitStack() as esCA:
                convact = T0(esCA, "convact", [128, 4, T], BF16)
                with ExitStack() as esZ:
                    zT = T0(esZ, "zT", [128, 4, 4, 544], BF16)
                    with ExitStack() as esD:
                        wconv = T0(esD, "wconv", [128, 8, 1024], BF16)
                        xb = T0(esD, "xbD", [128, 8, 512], BF16)
                        xh = T0(esD, "xh", [128, 8, 32], F32)
                        xbh = T0(esD, "xbh", [128, 8, 32], BF16)
                        sqh = T0(esD, "sqh", [128, 8, 32], BF16)
                        rh = T0(esD, "rh", [128, 32], F32)
                        rtmpD = T0(esD, "rtmpD", [128, 32], F32)
                        gs = [T0(esD, "gs%d" % i, [128, 512], F32) for i in range(2)]
                        sg = [T0(esD, "sg%d" % i, [128, 512], F32) for i in range(2)]
                        as_ = [T0(esD, "as%d" % i, [128, 512], F32) for i in range(2)]
                        gsh = T0(esD, "gsh", [128, 32], F32)
                        sgh = T0(esD, "sgh", [128, 32], F32)
                        ash = T0(esD, "ash", [128, 32], F32)
                        DMA("pool", wconv[:, :, 0:512], w_in_v[:, :, 0:512], [], ["wconv0"])
                        DMA("pool", wconv[:, :, 512:1024], w_in_v[:, :, 512:1024], [], ["wconv1"])
                        for tb in range(4):
                            load_own_block(tb, xb)
                            DMA("sp", xh[:], xcat_v[:, :, NCAT + tb * 32:NCAT + (tb + 1) * 32], [], ["xh"])
                            for k in range(8):
                                TS("dve", xbh[:, k, :], xh[:, k, :], vcol(V_GMIX + k), None, ALU.mult, None, ["xh", "vecs"], ["xbh"])
                            ACT(sqh[:], xh[:], AF.Square, ["xh"], ["sqh"])
                            for k in range(8):
                                MM(ps[:, 6, 0:32], ones[:], sqh[:, k, :], k == 0, k == 7, ["ones", "sqh"], [PSB(6)])
                            rstd_from_ps(6, 32, 1.0 / D, rh[:], "rh", rtmpD, "rtmpD")
                            rs = rstd1[:, tb * 512:(tb + 1) * 512]
                            for cc in range(4):
                                ba = 2 * (cc % 2)
                                bg = ba + 1
                                i2 = cc % 2
                                for k in range(8):
                                    MM(ps[:, ba, :], wconv[:, k, cc * 128:(cc + 1) * 128], xb[:, k, :], k == 0, k == 7, ["wconv0", XBK[k]], [PSB(ba)])
                                for k in range(8):
                                    MM(ps[:, bg, :], wconv[:, k, 512 + cc * 128:512 + (cc + 1) * 128], xb[:, k, :], k == 0, k == 7, ["wconv1", XBK[k]], [PSB(bg)])
                                for k in range(8):
                                    MM(ps[:, 4, cc * 32:(cc + 1) * 32], wconv[:, k, cc * 128:(cc + 1) * 128], xbh[:, k, :], k == 0, k == 7,
                                       ["wconv0", "xbh"], [PSB(4)], skip_group_check=True)
                                for k in range(8):
                                    MM(ps[:, 5, cc * 32:(cc + 1) * 32], wconv[:, k, 512 + cc * 128:512 + (cc + 1) * 128], xbh[:, k, :], k == 0, k == 7,
                                       ["wconv1", "xbh"], [PSB(5)], skip_group_check=True)
                                TT("dve", gs[i2][:], ps[:, bg, :], rs, ALU.mult, [PSB(bg), "rstd1"], ["gs%d" % i2])
                                ACT(sg[i2][:], gs[i2][:], AF.Sigmoid, ["gs%d" % i2], ["sg%d" % i2])
                                TT("dve", as_[i2][:], ps[:, ba, :], rs, ALU.mult, [PSB(ba), "rstd1"], ["as%d" % i2])
                                TT("pool", zT[:, cc, tb, 32:544], as_[i2][:], sg[i2][:], ALU.mult, ["as%d" % i2, "sg%d" % i2], ["z%d_%d" % (cc, tb)])
                                TT("dve", gsh[:], ps[:, 5, cc * 32:(cc + 1) * 32], rh[:], ALU.mult, [PSB(5), "rh"], ["gsh"])
                                ACT(sgh[:], gsh[:], AF.Sigmoid, ["gsh"], ["sgh"])
                                TT("dve", ash[:], ps[:, 4, cc * 32:(cc + 1) * 32], rh[:], ALU.mult, [PSB(4), "rh"], ["ash"])
                                TT("pool", zT[:, cc, tb, 0:32], ash[:], sgh[:], ALU.mult, ["ash", "sgh"], ["zh%d_%d" % (cc, tb)])
                        end_phase()
                    with ExitStack() as esD:
                        diag = T0(esD, "diag", [128, 4, 31, 128], BF16)
                        cv = T0(esD, "cv", [128, 4, 512], F32)
                        cvb = T0(esD, "cvb", [128, 4, 512], BF16)
                        cvsq = T0(esD, "cvsq", [128, 4, 512], BF16)
                        mean = T0(esD, "mean", [128, 512], F32)
                        msq = T0(esD, "msq", [128, 512], F32)
                        var = T0(esD, "var", [128, 512], F32)
                        sd = T0(esD, "sd", [128, 512], F32)
                        rsl = T0(esD, "rsl", [128, 512], F32)
                        y1 = [T0(esD, "y1_%d" % i, [128, 512], F32) for i in range(2)]
                        y2 = [T0(esD, "y2_%d" % i, [128, 512], F32) for i in range(2)]
                        for cc in range(4):
                            for tau in range(31):
                                TS("dve", diag[:, cc, tau, :], ident[:], vcol(V_CONVW + cc * 31 + tau), None, ALU.mult, None, ["ident", "vecs"], ["diag%d" % cc])
                        for tb in range(4):
                            for cc in range(4):
                                for tau in range(31):
                                    MM(ps[:, cc, :], diag[:, cc, tau, :], zT[:, cc, tb, tau + 2:tau + 2 + 512], tau == 0, tau == 30,
                                       ["diag%d" % cc, "z%d_%d" % (cc, tb), "zh%d_%d" % (cc, tb)], [PSB(cc)])
                                ACT(cv[:, cc, :], ps[:, cc, :], AF.Identity, [PSB(cc), "vecs"], ["cv%d" % cc], bias=vcol(V_CONVB + cc))
                                CP("pool", cvb[:, cc, :], cv[:, cc, :], ["cv%d" % cc], ["cvb%d" % cc])
                                ACT(cvsq[:, cc, :], cv[:, cc, :], AF.Square, ["cv%d" % cc], ["cvsq%d" % cc])
                            for cc in range(4):
                                MM(ps[:, 4, :], ones[:], cvb[:, cc, :], cc == 0, cc == 3, ["ones", "cvb%d" % cc], [PSB(4)])
                            for cc in range(4):
                                MM(ps[:, 5, :], ones[:], cvsq[:, cc, :], cc == 0, cc == 3, ["ones", "cvsq%d" % cc], [PSB(5)])
                            TS("dve", mean[:], ps[:, 4, :], 1.0 / 512, None, ALU.mult, None, [PSB(4)], ["mean"])
                            TT("pool", msq[:], mean[:], mean[:], ALU.mult, ["mean"], ["msq"])
                            STT(var[:], ps[:, 5, :], 1.0 / 512, msq[:], ALU.mult, ALU.subtract, [PSB(5), "msq"], ["var"])
                            TS("dve", var[:], var[:], 0.0, None, ALU.max, None, ["var"], ["var"])
                            ACT(sd[:], var[:], AF.Sqrt, ["var", "epsb"], ["sd"], bias=epsb[:], scale=1.0)
                            RECIP(rsl[:], sd[:], ["sd"], ["rsl"])
                            for cc in range(4):
                                i2 = cc % 2
                                TT("dve", y1[i2][:], cv[:, cc, :], mean[:], ALU.subtract, ["cv%d" % cc, "mean"], ["y1_%d" % i2])
                                TT("pool", y2[i2][:], y1[i2][:], rsl[:], ALU.mult, ["y1_%d" % i2, "rsl"], ["y2_%d" % i2])
                                ACT(convact[:, cc, tb * 512:(tb + 1) * 512], y2[i2][:], AF.Silu, ["y2_%d" % i2, "vecs"], ["ca%d_%d" % (cc, tb)],
                                    scale=vcol(V_LNG + cc), bias=vcol(V_LNB + cc))
                        end_phase()
                if stop_after == "D1":
                    for cc in range(4):
                        dump(convact[:, cc, :], T, cc * T)
                    debug_finish()
                    return nc
                with ExitStack() as esD:
                    xb = T0(esD, "xbD2", [128, 8, 512], BF16)
                    wco = T0(esD, "wco", [128, 4, 1024], BF16)
                    wmo = T0(esD, "wmo", [128, 4, 1024], BF16)
                    wout = T0(esD, "wout", [128, 8, 1024], BF16)
                    gw = [T0(esD, "gw%d" % i, [128, 8, 512], BF16) for i in range(2)]
                    sig = T0(esD, "sig", [128, 16, 512], BF16)
                    mg = T0(esD, "mg", [128, 8, 512], BF16)
                    gs = [T0(esD, "gsD%d" % i, [128, 512], F32) for i in range(2)]
                    m1 = [T0(esD, "m1_%d" % i, [128, 512], F32) for i in range(2)]
                    m2 = [T0(esD, "m2_%d" % i, [128, 512], F32) for i in range(2)]
                    DMA("pool", wco[:], kp(w_conv_out), [], ["wco"])
                    DMA("pool", wmo[:], kp(w_mla_out), [], ["wmo"])
                    w_out_v = kp(w_out)
                    DMA("pool", wout[:, :, 0:512], w_out_v[:, :, 0:512], [], ["wout0"])
                    DMA("pool", wout[:, :, 512:1024], w_out_v[:, :, 512:1024], [], ["wout1"])
                    gcount = 0
                    for tb in range(4):
                        tc_ = slice(tb * 512, (tb + 1) * 512)
                        load_own_block(tb, xb, hT)
                        for gi in range(4):
                            gb_ = gcount % 2
                            gcount += 1
                            DMA("pool", gw[gb_][:], w_in_v[:, :, 1696 + gi * 512:1696 + (gi + 1) * 512], [], ["gw%d" % gb_])
                            for j in range(4):
                                oc = gi * 4 + j
                                bank = oc % 4
                                i2 = oc % 2
                                for k in range(8):
                                    MM(ps[:, bank, :], gw[gb_][:, k, j * 128:(j + 1) * 128], xb[:, k, :], k == 0, k == 7, ["gw%d" % gb_, XBK[k]], [PSB(bank)])
                                TT("dve", gs[i2][:], ps[:, bank, :], rstd1[:, tc_], ALU.mult, [PSB(bank), "rstd1"], ["gsD%d" % i2])
                                ACT(sig[:, oc, :], gs[i2][:], AF.Sigmoid, ["gsD%d" % i2], ["sig%d" % oc])
                        for c in range(8):
                            b1 = 4 + (c % 2) * 2
                            b2 = b1 + 1
                            i2 = c % 2
                            for k4 in range(4):
                                MM(ps[:, b1, :], wco[:, k4, c * 128:(c + 1) * 128], convact[:, k4, tc_], k4 == 0, k4 == 3,
                                   ["wco", "ca%d_%d" % (k4, tb)], [PSB(b1)])
                            for hp in range(4):
                                MM(ps[:, b2, :], wmo[:, hp, c * 128:(c + 1) * 128], Onorm[:, hp, tc_], hp == 0, hp == 3,
                                   ["wmo", "On%d_%d" % (2 * hp, tb), "On%d_%d" % (2 * hp + 1, tb)], [PSB(b2)])
                            TT("dve", m1[i2][:], ps[:, b1, :], sig[:, c, :], ALU.mult, [PSB(b1), "sig%d" % c], ["m1_%d" % i2])
                            TT("dve", m2[i2][:], ps[:, b2, :], sig[:, 8 + c, :], ALU.mult, [PSB(b2), "sig%d" % (8 + c)], ["m2_%d" % i2])
                            TT("dve", mg[:, c, :], m1[i2][:], m2[i2][:], ALU.add, ["m1_%d" % i2, "m2_%d" % i2], ["mg%d" % c])
                        for c in range(8):
                            bank = c % 4
                            for k in range(8):
                                MM(ps[:, bank, :], wout[:, k, c * 128:(c + 1) * 128], mg[:, k, :], k == 0, k == 7,
                                   ["wout0", "wout1", "mg%d" % k], [PSB(bank)])
                            TT("dve", hT[:, c, tc_], ps[:, bank, :], hT[:, c, tc_], ALU.add, [PSB(bank), "h%d_%d" % (c, tb)], ["h%d_%d" % (c, tb)])
                    end_phase()
            if stop_after == "D2":
                for c in range(8):
                    dump(hT[:, c, :], T, c * T)
                debug_finish()
                return nc
            with ExitStack() as esE:
                memx = T0(esE, "memx", [128, 8, 256], F32)
                memb = T0(esE, "memb", [128, 8, 256], BF16)
                msqm = T0(esE, "msqm", [128, 8, 256], BF16)
                rmem = T0(esE, "rmem", [128, 256], F32)
                rmemT = T0(esE, "rmemT", [128, 2], F32)
                rtmpE = T0(esE, "rtmpE", [128, 512], F32)
                wxkv = T0(esE, "wxkv", [128, 8, 1024], BF16)
                wxq = T0(esE, "wxq", [128, 8, 512], BF16)
                wxo = T0(esE, "wxo", [128, 4, 1024], BF16)
                Kx = T0(esE, "Kx", [128, 4, 256], BF16)
                Vx = T0(esE, "Vx", [128, 2, 512], BF16)
                kxm = T0(esE, "kxm", [128, 4], F32)
                kmm = T0(esE, "kmm", [128, 4, 128], BF16)
                hb = T0(esE, "hb", [128, 8, 512], BF16)
                hsq = T0(esE, "hsqE", [128, 8, 512], BF16)
                rstd2 = T0(esE, "rstd2", [128, 512], F32)
                Qx = T0(esE, "Qx", [128, 4, 512], BF16)
                aq = T0(esE, "aq", [128, 4, 512], BF16)
                Px = [T0(esE, "Px%d" % i, [128, 512], BF16) for i in range(2)]
                lr = T0(esE, "lr", [128, 512], F32)
                Ox = T0(esE, "Ox", [128, 4, 512], BF16)
                DMA("sp", memx[:], kp(memT), [], ["memx"])
                w_xkv_v = kp(w_xkv)
                DMA("pool", wxkv[:, :, 0:512], w_xkv_v[:, :, 0:512], [], ["wxkv0"])
                DMA("pool", wxkv[:, :, 512:1024], w_xkv_v[:, :, 512:1024], [], ["wxkv1"])
                DMA("pool", wxq[:], kp(w_xq), [], ["wxq"])
                DMA("pool", wxo[:], kp(w_xo), [], ["wxo"])
                for k in range(8):
                    TS("dve", memb[:, k, :], memx[:, k, :], vcol(V_GMEM + k), None, ALU.mult, None, ["memx", "vecs"], ["memb"])
                ACT(msqm[:], memx[:], AF.Square, ["memx"], ["msqm"])
                for k in range(8):
                    MM(ps[:, 0, 0:256], ones[:], msqm[:, k, :], k == 0, k == 7, ["ones", "msqm"], [PSB(0)])
                rstd_from_ps(0, 256, 1.0 / D, rmem[:], "rmem", rtmpE, "rtmpE")
                for kt in range(2):
                    for k in range(8):
                        MM(ps[:, 1, kt:kt + 1], msqm[:, k, kt * 128:(kt + 1) * 128], ones[:, 0:1], k == 0, k == 7, ["ones", "msqm"], [PSB(1)],
                           skip_group_check=True)
                rstd_from_ps(1, 2, 1.0 / D, rmemT[:], "rmemT", rtmpE, "rtmpE")
                for h in range(4):
                    bank = 2 + h % 2
                    for k in range(8):
                        MM(ps[:, bank, 0:256], wxkv[:, k, h * 128:(h + 1) * 128], memb[:, k, :], k == 0, k == 7, ["wxkv0", "memb"], [PSB(bank)])
                    TT("dve", Kx[:, h, :], ps[:, bank, 0:256], rmem[:], ALU.mult, [PSB(bank), "rmem"], ["Kx%d" % h])
                    P.op("dve", lambda e, h=h: e.tensor_reduce(out=kxm[:, h:h + 1], in_=Kx[:, h, :], axis=AX.X, op=ALU.max, apply_absolute_value=True),
                         ["Kx%d" % h], ["kxm%d" % h])
                    TS("dve", kmm[:, h, :], ones[:], kxm[:, h:h + 1], -1.01, ALU.mult, ALU.mult, ["ones", "kxm%d" % h], ["kmm%d" % h])
                for kt in range(2):
                    for k in range(8):
                        MM(ps[:, 4 + kt, :], memb[:, k, kt * 128:(kt + 1) * 128], wxkv[:, k, 512:1024], k == 0, k == 7, ["wxkv1", "memb"], [PSB(4 + kt)])
                    TS("dve", Vx[:, kt, :], ps[:, 4 + kt, :], rmemT[:, kt:kt + 1], None, ALU.mult, None, [PSB(4 + kt), "rmemT"], ["Vx%d" % kt])
                XS_ = 128.0 ** -0.5
                for tb in range(4):
                    tc_ = slice(tb * 512, (tb + 1) * 512)
                    HR = ["h%d_%d" % (k, tb) for k in range(8)]
                    for k in range(8):
                        TS("dve", hb[:, k, :], hT[:, k, tc_], vcol(V_GX + k), None, ALU.mult, None, [HR[k], "vecs"], ["hb%d" % k])
                    blk_stats(hT, tb, hsq, rstd2[:], "rstd2", rtmpE)
                    for h in range(4):
                        for k in range(8):
                            MM(ps[:, 1, :], wxq[:, k, h * 128:(h + 1) * 128], hb[:, k, :], k == 0, k == 7, ["wxq", "hb%d" % k], [PSB(1)])
                        STT(Qx[:, h, :], ps[:, 1, :], XS_, rstd2[:], ALU.mult, ALU.mult, [PSB(1), "rstd2"], ["Qx%d" % h])
                        STT(aq[:, h, :], Qx[:, h, :], -1.0, Qx[:, h, :], ALU.mult, ALU.max, ["Qx%d" % h], ["aq%d" % h])
                        for kt in range(2):
                            bank = 2 + kt
                            MM(ps[:, bank, :], Kx[:, h, kt * 128:(kt + 1) * 128], Qx[:, h, :], True, False, ["Kx%d" % h, "Qx%d" % h], [PSB(bank)])
                            MM(ps[:, bank, :], kmm[:, h, :], aq[:, h, :], False, True, ["kmm%d" % h, "aq%d" % h], [PSB(bank)])
                            ACT(Px[kt][:], ps[:, bank, :], AF.Exp, [PSB(bank)], ["Px%d" % kt])
                        for kt in range(2):
                            MM(ps[:, 4, :], Vx[:, kt, h * 128:(h + 1) * 128], Px[kt][:], kt == 0, kt == 1, ["Vx%d" % kt, "Px%d" % kt], [PSB(4)])
                        for kt in range(2):
                            MM(ps[:, 5, :], ones[:], Px[kt][:], kt == 0, kt == 1, ["ones", "Px%d" % kt], [PSB(5)])
                        RECIP(lr[:], ps[:, 5, :], [PSB(5)], ["lr"])
                        TT("dve", Ox[:, h, :], ps[:, 4, :], lr[:], ALU.mult, [PSB(4), "lr"], ["Ox%d" % h])
                    for c in range(8):
                        bank = 6 + c % 2
                        for h in range(4):
                            MM(ps[:, bank, :], wxo[:, h, c * 128:(c + 1) * 128], Ox[:, h, :], h == 0, h == 3, ["wxo", "Ox%d" % h], [PSB(bank)])
                        TT("dve", hT[:, c, tc_], ps[:, bank, :], hT[:, c, tc_], ALU.add, [PSB(bank), "h%d_%d" % (c, tb)], ["h%d_%d" % (c, tb)])
                end_phase()
            if stop_after == "E":
                for c in range(8):
                    dump(hT[:, c, :], T, c * T)
                debug_finish()
                return nc
            with ExitStack() as esF:
                hb3 = T0(esF, "hb3", [128, 8, T], BF16)
                W1 = [T0(esF, "W1_%d" % i, [128, 8, 256], BF16) for i in range(2)]
                W2 = [T0(esF, "W2_%d" % i, [128, 2, 1024], BF16) for i in range(2)]
                hid = [T0(esF, "hid%d" % i, [128, 2, T], BF16) for i in range(2)]
                rstd3 = T0(esF, "rstd3", [128, T], F32)
                rstd4 = T0(esF, "rstd4", [128, 512], F32)
                hsq = T0(esF, "hsqF", [128, 8, 512], BF16)
                rtmpF = T0(esF, "rtmpF", [128, 512], F32)
                uu = [T0(esF, "uu%d" % i, [128, 512], F32) for i in range(2)]
                vv = [T0(esF, "vv%d" % i, [128, 512], F32) for i in range(2)]
                w1v = kp(w_mlp1)
                w2v = kp(w_mlp2)
                for tb in range(4):
                    tc_ = slice(tb * 512, (tb + 1) * 512)
                    for k in range(8):
                        TS("dve", hb3[:, k, tc_], hT[:, k, tc_], vcol(V_GMLP + k), None, ALU.mult, None, ["h%d_%d" % (k, tb), "vecs"], ["hb3_%d_%d" % (k, tb)])
                    blk_stats(hT, tb, hsq, rstd3[:, tc_], "rstd3_%d" % tb, rtmpF)
                cnt = 0
                for g in range(16):
                    gb_ = g % 2
                    DMA("pool", W1[gb_][:], w1v[:, :, g * 256:(g + 1) * 256], [], ["W1_%d" % gb_])
                    DMA("pool", W2[gb_][:], w2v[:, g * 2:(g + 1) * 2, :], [], ["W2_%d" % gb_])
                    for j in range(2):
                        for tb in range(4):
                            tc_ = slice(tb * 512, (tb + 1) * 512)
                            bank = cnt % 4
                            i2 = cnt % 2
                            cnt += 1
                            for k in range(8):
                                MM(ps[:, bank, :], W1[gb_][:, k, j * 128:(j + 1) * 128], hb3[:, k, tc_], k == 0, k == 7,
                                   ["W1_%d" % gb_, "hb3_%d_%d" % (k, tb)], [PSB(bank)])
                            ACT(uu[i2][:], ps[:, bank, :], AF.Relu, [PSB(bank)], ["uu%d" % i2])
                            TT("dve", vv[i2][:], uu[i2][:], rstd3[:, tc_], ALU.mult, ["uu%d" % i2, "rstd3_%d" % tb], ["vv%d" % i2])
                            TT("pool", hid[gb_][:, j, tc_], vv[i2][:], vv[i2][:], ALU.mult, ["vv%d" % i2], ["hid%d_%d_%d" % (gb_, j, tb)])
                    for c in range(8):
                        for tb in range(4):
                            tc_ = slice(tb * 512, (tb + 1) * 512)
                            bank = 4 + cnt % 4
                            cnt += 1
                            for j in range(2):
                                MM(ps[:, bank, :], W2[gb_][:, j, c * 128:(c + 1) * 128], hid[gb_][:, j, tc_], j == 0, j == 1,
                                   ["W2_%d" % gb_, "hid%d_%d_%d" % (gb_, j, tb)], [PSB(bank)])
                            TT("dve", hT[:, c, tc_], ps[:, bank, :], hT[:, c, tc_], ALU.add, [PSB(bank), "h%d_%d" % (c, tb)], ["h%d_%d" % (c, tb)])
                for tb in range(4):
                    tc_ = slice(tb * 512, (tb + 1) * 512)
                    blk_stats(hT, tb, hsq, rstd4[:], "rstd4", rtmpF)
                    for c in range(8):
                        slot = ring[0] % 4
                        ring[0] += 1
                        XS = "xst%d" % slot
                        STT(xst[:, slot, :], hT[:, c, tc_], vcol(V_GFIN + c), rstd4[:], ALU.mult, ALU.mult, ["h%d_%d" % (c, tb), "rstd4", "vecs"], [XS])
                        DMA("sp", out_d[c * 128:(c + 1) * 128, tc_], xst[:, slot, :], [XS], ["out"])
                end_phase(final=True)
    return nc


def own_chunks(j):
    return [j, 7 - j, 8 + j, 15 - j]


def make_vecs(inp):
    v = np.zeros((128, NV), np.float32)

    def colmajor(g, n):
        return np.ascontiguousarray(np.asarray(g, np.float32).reshape(n, 128).T)

    v[:, V_GMIX:V_GMIX + 8] = colmajor(inp["norm_mix_g"][0], 8)
    v[:, V_GX:V_GX + 8] = colmajor(inp["norm_xattn_g"][0], 8)
    v[:, V_GMLP:V_GMLP + 8] = colmajor(inp["norm_mlp_g"][0], 8)
    v[:, V_GFIN:V_GFIN + 8] = colmajor(inp["final_norm_g"], 8)
    v[:, V_GMEM:V_GMEM + 8] = colmajor(inp["norm_mem_g"][0], 8)
    cw = np.asarray(inp["conv_w"][0], np.float32)
    v[:, V_CONVW:V_CONVW + 124] = cw.T.reshape(4, 128, 31).transpose(1, 0, 2).reshape(128, 124)
    v[:, V_CONVB:V_CONVB + 4] = colmajor(inp["conv_b"][0], 4)
    v[:, V_LNG:V_LNG + 4] = colmajor(inp["conv_ln_g"][0], 4)
    v[:, V_LNB:V_LNB + 4] = colmajor(inp["conv_ln_b"][0], 4)
    v[:, V_GQ:V_GQ + 3] = colmajor(inp["q_norm_g"][0], 3)
    v[:, V_GKV:V_GKV + 2] = colmajor(inp["kv_norm_g"][0], 2)
    half = 16
    invf = (np.float32(10000.0) ** (-np.arange(half, dtype=np.float32) / np.float32(half))).astype(np.float32)
    v[64:80, V_INVF] = invf
    v[80:96, V_INVF] = invf
    return v


def make_core_inputs(inp, core, shared):
    b, j = core // 4, core % 4
    x = np.asarray(inp["x"], np.float32)
    pos = np.asarray(inp["positions"], np.int32)
    chunks = own_chunks(j)
    xT = shared["xT"][b]
    xcat = np.zeros((D, NCAT + 128), np.float32)
    xcat[:, :SEQ] = xT
    poscat = np.zeros((32, NCAT), np.int32)
    poscat[:, :SEQ] = pos[b][None, :]
    qa = np.zeros((16, T), np.float32)
    for s, c in enumerate(chunks):
        xcat[:, SEQ + s * 512:SEQ + (s + 1) * 512] = xT[:, c * 512:(c + 1) * 512]
        poscat[:, SEQ + s * 512:SEQ + (s + 1) * 512] = pos[b][None, c * 512:(c + 1) * 512]
        if c > 0:
            xcat[:, NCAT + s * 32:NCAT + (s + 1) * 32] = xT[:, c * 512 - 32:c * 512]
        for u in range(16):
            if u >= c:
                qa[u, s * 512:(s + 1) * 512] = NEG
    m = dict(shared["common"])
    m.update({"xcat": xcat, "poscat": poscat, "qa": qa, "memT": shared["memT"][b]})
    return m


def make_shared(inp):
    x = np.asarray(inp["x"], np.float32)
    shared = {"xT": [np.ascontiguousarray(x[b].T) for b in range(2)],
              "memT": [np.ascontiguousarray(np.asarray(inp["mem"], np.float32)[b].T) for b in range(2)]}
    ka = np.zeros((17, NCAT), np.float32)
    ka[0, :] = 1.0
    for u in range(16):
        ka[1 + u, u * 512:(u + 1) * 512] = 1.0
    cmat = np.zeros((128, 352), np.float32)
    for i in range(16):
        cmat[64 + 16 + i, 256 + 64 + i] = -1.0
        cmat[64 + i, 256 + 64 + 16 + i] = 1.0
    cmat[:, :128] = np.eye(128, dtype=np.float32)
    kk, qq = np.meshgrid(np.arange(128), np.arange(128), indexing="ij")
    cmat[:, 128:256] = np.where(kk > qq, NEG, 0.0).astype(np.float32)
    common = {"ka": ka, "cmat": cmat, "vecs": make_vecs(inp)}
    for name in ["w_in", "w_conv_out", "w_uq", "w_ukv", "w_mla_out", "w_out", "w_xq", "w_xkv", "w_xo", "w_mlp1", "w_mlp2"]:
        common[name] = np.ascontiguousarray(np.asarray(inp[name], np.float32)[0])
    shared["common"] = common
    return shared


_NC_CACHE = {}


def kernel(**inputs):
    shared = make_shared(inputs)
    in_maps = [make_core_inputs(inputs, c, shared) for c in range(8)]
    if "nc" not in _NC_CACHE:
        _NC_CACHE["nc"] = build_program()
    nc = _NC_CACHE["nc"]
    res = run_bass_kernel_spmd(nc, in_maps, core_ids=list(range(8)))
    out = np.zeros((2, SEQ, D), np.float32)
    for core in range(8):
        b, j = core // 4, core % 4
        o = res.results[core]["out"]
        for s, c in enumerate(own_chunks(j)):
            out[b, c * 512:(c + 1) * 512, :] = o[:, s * 512:(s + 1) * 512].T
    return out
```

```python
import math
import numpy as np
import concourse.bass as bass
import concourse.mybir as mybir
from concourse.alu_op_type import AluOpType as ALU
from concourse.bass_utils import run_bass_kernel_spmd

F32 = mybir.dt.float32
BF16 = mybir.dt.bfloat16
I32 = mybir.dt.int32
AF = mybir.ActivationFunctionType
AX = mybir.AxisListType

D = 1024
SEQ = 8192
T = 2048
NB = 4
NBLK = 20
NCAT = NBLK * 512
EPS = 1e-6
SCALE = 96.0 ** -0.5
NSLOT_UNITS = (3, 7, 11, 15)
NEG = -30000.0
TWO_PI = 2.0 * math.pi

V_GMIX, V_GX, V_GMLP, V_GFIN, V_GMEM = 0, 8, 16, 24, 32
V_CONVW = 40
V_CONVB = 164
V_LNG = 168
V_LNB = 172
V_GQ = 176
V_GKV = 179
V_INVF = 181
NV = 184


class Op:
    __slots__ = ("fn", "waits", "tl", "idx", "is_dma")

    def __init__(self, fn, waits, tl, idx, is_dma):
        self.fn, self.waits, self.tl, self.idx, self.is_dma = fn, waits, tl, idx, is_dma


class Prog:
    ENGS = ("pe", "act", "dve", "pool", "sp")
    COMP = ("pe", "act", "dve", "pool")

    def __init__(self, n_dma=24):
        self.ops = {e: [] for e in self.ENGS}
        self.reg = {}
        self.seen = {e: {} for e in self.ENGS}
        self.cnt = {}
        self.n_dma = n_dma
        self.rr = 0
        self.bar = {}
        self.rank = {}
        self.sigbase = {e: 0 for e in self.COMP}
        self.forced = set()

    def _add(self, eng, fn, reads, writes, tl, is_dma):
        idx = self.cnt.get(tl, 0) + 1
        self.cnt[tl] = idx
        need = dict(self.bar)

        def req(t, i):
            if need.get(t, 0) < i:
                need[t] = i

        for r in reads:
            e = self.reg.get(r)
            if e is not None and e[0] is not None:
                req(*e[0])
        for r in writes:
            e = self.reg.get(r)
            if e is not None:
                if e[0] is not None:
                    req(*e[0])
                for t, i in e[1].items():
                    req(t, i)
        if is_dma and idx > 1:
            req(tl, idx - 1)
        waits = []
        sn = self.seen[eng]
        for t, i in need.items():
            if t == eng and not is_dma:
                continue
            if sn.get(t, 0) >= i:
                continue
            sn[t] = i
            waits.append((t, i))
        self.ops[eng].append(Op(fn, waits, tl, idx, is_dma))
        for r in reads:
            e = self.reg.setdefault(r, [None, {}])
            if e[1].get(tl, 0) < idx:
                e[1][tl] = idx
        for r in writes:
            self.reg[r] = [(tl, idx), {}]
        return idx

    def op(self, eng, fn, reads=(), writes=()):
        self._add(eng, fn, reads, writes, eng, False)

    def dma(self, eng, fn, reads=(), writes=()):
        tl = "q%d" % self.rr
        self.rr = (self.rr + 1) % self.n_dma
        self._add(eng, fn, reads, writes, tl, True)

    def phase_end(self, sigfns=None):
        for eng in self.ENGS:
            for t in self.COMP:
                self.seen[eng][t] = self.cnt.get(t, 0)

    def finish_waits(self, eng):
        waits = []
        for t, i in self.cnt.items():
            if t.startswith("q") and self.seen[eng].get(t, 0) < i:
                self.seen[eng][t] = i
                waits.append((t, i))
        self.ops[eng].append(Op(None, waits, None, 0, False))

    def emit(self, block, sems):
        sig = {e: set() for e in self.COMP}
        for e in self.ENGS:
            for o in self.ops[e]:
                for t, i in o.waits:
                    if t in sig and (t, i) not in self.rank:
                        sig[t].add(i)
        for (t, i) in self.forced:
            sig[t].add(i)
        for t, s in sig.items():
            for r, i in enumerate(sorted(s)):
                self.rank[(t, i)] = self.sigbase[t] + r + 1
            self.sigbase[t] += len(s)
        rank = self.rank

        def val(t, i):
            return 16 * i if t.startswith("q") else rank[(t, i)]

        def run(e, handle):
            for o in self.ops[e]:
                for t, i in o.waits:
                    handle.wait_ge(sems[t], val(t, i))
                if o.fn is None:
                    continue
                ins = o.fn(handle)
                if o.is_dma:
                    ins.then_inc(sems[o.tl], 16)
                elif (o.tl, o.idx) in rank:
                    ins.then_inc(sems[o.tl], 1)

        if self.ops["pe"]:
            block.tensor(lambda h: run("pe", h))
        if self.ops["act"]:
            block.scalar(lambda h: run("act", h))
        if self.ops["dve"]:
            block.vector(lambda h: run("dve", h))
        if self.ops["pool"]:
            block.gpsimd(lambda h: run("pool", h))
        if self.ops["sp"]:
            block.sync(lambda h: run("sp", h))

        self.ops = {e: [] for e in self.ENGS}
        self.forced = set()


def build_program(debug=None):
    from contextlib import ExitStack
    nc = bass.Bass("TRN2", target_bir_lowering=False)
    P = Prog()
    dbg = debug or {}
    stop_after = dbg.get("stop", "Z")

    def din(name, shape, dt=F32):
        return nc.dram_tensor(name, list(shape), dt, kind="ExternalInput").ap()

    xcat = din("xcat", [D, NCAT + 128])
    poscat = din("poscat", [32, NCAT], I32)
    qa = din("qa", [16, T])
    ka = din("ka", [17, NCAT])
    cmat = din("cmat", [128, 352])
    vecs_d = din("vecs", [128, NV])
    memT = din("memT", [D, 256])
    w_in = din("w_in", [D, 3744])
    w_conv_out = din("w_conv_out", [512, D])
    w_uq = din("w_uq", [384, 768])
    w_ukv = din("w_ukv", [256, 1024])
    w_mla_out = din("w_mla_out", [512, D])
    w_out = din("w_out", [D, D])
    w_xq = din("w_xq", [D, 512])
    w_xkv = din("w_xkv", [D, 1024])
    w_xo = din("w_xo", [512, D])
    w_mlp1 = din("w_mlp1", [D, 4096])
    w_mlp2 = din("w_mlp2", [4096, D])
    out_d = nc.dram_tensor("out", [D, T], F32, kind="ExternalOutput").ap()
    dbg_d = nc.dram_tensor("dbg", [128, dbg["n"]], F32, kind="ExternalOutput").ap() if debug else None

    def kp(ap):
        return ap.rearrange("(k p) n -> p k n", p=128)

    xcat_v = kp(xcat)
    w_in_v = kp(w_in)

    def MM(out, lhsT, rhs, start, stop, reads, writes, **kw):
        P.op("pe", lambda e: e.matmul(out, lhsT=lhsT, rhs=rhs, start=start, stop=stop, **kw), reads, writes)

    def ACT(out, in_, func, reads, writes, **kw):
        P.op("act", lambda e: e.activation(out=out, in_=in_, func=func, **kw), reads, writes)

    def TT(eng, out, in0, in1, op, reads, writes):
        P.op(eng, lambda e: e.tensor_tensor(out=out, in0=in0, in1=in1, op=op), reads, writes)

    def TS(eng, out, in0, s1, s2, op0, op1, reads, writes):
        if op1 is None:
            P.op(eng, lambda e: e.tensor_scalar(out=out, in0=in0, scalar1=s1, scalar2=None, op0=op0), reads, writes)
        else:
            P.op(eng, lambda e: e.tensor_scalar(out=out, in0=in0, scalar1=s1, scalar2=s2, op0=op0, op1=op1), reads, writes)

    def STT(out, in0, scalar, in1, op0, op1, reads, writes):
        P.op("dve", lambda e: e.scalar_tensor_tensor(out=out, in0=in0, scalar=scalar, in1=in1, op0=op0, op1=op1), reads, writes)

    def CP(eng, out, in_, reads, writes):
        P.op(eng, lambda e: e.tensor_copy(out=out, in_=in_), reads, writes)

    def MS(eng, out, val, writes):
        P.op(eng, lambda e: e.memset(out, val), (), writes)

    def RECIP(out, in_, reads, writes):
        P.op("dve", lambda e: e.reciprocal(out=out, in_=in_), reads, writes)

    def DMA(eng, out, in_, reads, writes):
        P.dma(eng, lambda e: e.dma_start(out=out, in_=in_), reads, writes)

    def PSB(b):
        return "ps%d" % b

    with ExitStack() as es0:
        def T0(es, name, shape, dt):
            return es.enter_context(nc.sbuf_tensor("sb_" + name, list(shape), dt))

        ps = es0.enter_context(nc.psum_tensor("ps", [128, 8, 512], F32))
        sems = {}
        for t in list(Prog.COMP) + ["q%d" % i for i in range(P.n_dma)]:
            sems[t] = es0.enter_context(nc.semaphore("s_" + t))
        vecs = T0(es0, "vecs", [128, NV], F32)
        ident = T0(es0, "ident", [128, 128], BF16)
        tri = T0(es0, "tri", [128, 128], BF16)
        rotm = T0(es0, "rotm", [128, 96], BF16)
        ones = T0(es0, "ones", [128, 128], BF16)
        onesf = T0(es0, "onesf", [128, 128], F32)
        epsb = T0(es0, "epsb", [128, 1], F32)
        scr = T0(es0, "scr", [128, 16], F32)
        rstd1 = T0(es0, "rstd1", [128, T], F32)
        Onorm = T0(es0, "Onorm", [128, 4, T], BF16)
        xst = T0(es0, "xst", [128, 4, 512], F32)

        def vcol(c, lo=0, hi=128):
            return vecs[lo:hi, c:c + 1]

        sigfns = {
            "pe": lambda e: e.matmul(ps[0:1, 7, 0:1], lhsT=ones[0:1, 0:1], rhs=ones[0:1, 0:1], start=True, stop=True),
            "act": lambda e: e.activation(out=scr[0:1, 0:1], in_=scr[0:1, 1:2], func=AF.Copy),
            "dve": lambda e: e.memset(scr[0:1, 2:3], 0.0),
            "pool": lambda e: e.memset(scr[0:1, 3:4], 0.0),
        }

        def end_phase(final=False):
            if final:
                P.finish_waits("sp")
            else:
                P.phase_end(sigfns)
            with nc.Block() as block:
                P.emit(block, sems)

        dumps = []

        def dump(ap, n, col):
            dumps.append((ap, n, col))

        def debug_finish():
            for (ap, n, col) in dumps:
                for o in range(0, n, 512):
                    w = min(512, n - o)
                    slot = (o // 512) % 4
                    CP("dve", xst[:, slot, 0:w], ap[:, o:o + w], [], ["xst%d" % slot])
                    DMA("sp", dbg_d[:, col + o:col + o + w], xst[:, slot, 0:w], ["xst%d" % slot], ["dbgout"])
            MS("dve", xst[:, 0, :], 0.0, ["xst0"])
            for c in range(8):
                for tb in range(4):
                    DMA("sp", out_d[c * 128:(c + 1) * 128, tb * 512:(tb + 1) * 512], xst[:, 0, :], ["xst0"], ["out"])
            end_phase(final=True)

        DMA("sp", vecs[:], vecs_d, [], ["vecs"])
        DMA("pool", ident[:], cmat[:, 0:128], [], ["ident"])
        DMA("pool", tri[:], cmat[:, 128:256], [], ["tri"])
        DMA("pool", rotm[:], cmat[:, 256:352], [], ["rotm"])
        MS("dve", ones[:], 1.0, ["ones"])
        MS("dve", onesf[:], 1.0, ["onesf"])
        MS("dve", epsb[:], EPS, ["epsb"])
        MS("dve", scr[:], 0.0, ["scr"])

        def rstd_from_ps(bank, n, inv_count, out_ap, out_reg, tmp, tmp_reg):
            ACT(tmp[:, 0:n], ps[:, bank, 0:n], AF.Sqrt, [PSB(bank), "epsb"], [tmp_reg], bias=epsb[:], scale=inv_count)
            RECIP(out_ap, tmp[:, 0:n], [tmp_reg], [out_reg])

        RB = slice(64, 96)
        ring = [0]

        def load_xblock(col0, xb, gcol, sqt=None, width=512, tag="", eng="dve"):
            for k in range(8):
                slot = ring[0] % 4
                ring[0] += 1
                XS = "xst%d" % slot
                DMA("sp", xst[:, slot, 0:width], xcat_v[:, k, col0:col0 + width], [], [XS])
                TS(eng, xb[:, k, 0:width], xst[:, slot, 0:width], vcol(gcol + k), None, ALU.mult, None, [XS, "vecs"], ["xb%s%d" % (tag, k)])
                if sqt is not None:
                    ACT(sqt[:, k, 0:width], xst[:, slot, 0:width], AF.Square, [XS], ["sq%s%d" % (tag, k)])

        XBK = ["xb%d" % k for k in range(8)]
        SQK = ["sq%d" % k for k in range(8)]

        with ExitStack() as esAC:
            ckvn = T0(esAC, "ckvn", [128, 2, NCAT], BF16)
            Kb = T0(esAC, "Kb", [128, NCAT], BF16)
            cqn = T0(esAC, "cqn", [128, 3, T], BF16)
            qcos = T0(esAC, "qcos", [128, T], BF16)
            qsin = T0(esAC, "qsin", [128, T], BF16)
            kmxr = T0(esAC, "kmxr", [128, NBLK], F32)

            blks = dbg.get("blks", [b for b in range(NBLK) if b != 15])
            with ExitStack() as esA:
                xbs = [T0(esA, "xbA%d" % i, [128, 8, 512], BF16) for i in range(3)]
                sqs = [T0(esA, "sqA%d" % i, [128, 8, 512], BF16) for i in range(1)]
                wA = T0(esA, "wA", [128, 8, 832], BF16)
                ckv = T0(esA, "ckv", [128, 2, 512], F32)
                cq = T0(esA, "cq", [128, 3, 512], F32)
                sq2 = T0(esA, "sq2", [128, 3, 512], BF16)
                rtmp = T0(esA, "rtmp", [128, 512], F32)
                rstdA = T0(esA, "rstdA", [128, 512], F32)
                rkv = T0(esA, "rkv", [128, 512], F32)
                rq = T0(esA, "rq", [128, 512], F32)
                posi = T0(esA, "posi", [128, 512], I32)
                ti = T0(esA, "ti", [128, 512], I32)
                ang = T0(esA, "ang", [128, 512], F32)
                tf = T0(esA, "tf", [128, 512], F32)
                rr_ = T0(esA, "rr", [128, 512], F32)
                mm_ = T0(esA, "mm", [128, 512], F32)
                sinb = T0(esA, "sinb", [128, 512], F32)
                cosb = T0(esA, "cosb", [128, 512], F32)
                t1 = T0(esA, "t1A", [128, 512], F32)
                t2 = T0(esA, "t2A", [128, 512], F32)

                DMA("pool", wA[:, :, 0:640], w_in_v[:, :, 1024:1664], [], ["wA"])
                MS("dve", wA[:, :, 640:704], 0.0, ["wAz1"])
                MS("dve", wA[:, :, 736:800], 0.0, ["wAz2"])
                DMA("pool", wA[:, :, 704:736], w_in_v[:, :, 1664:1696], [], ["wAr"])
                DMA("pool", wA[:, :, 800:816], w_in_v[:, :, 1680:1696], [], ["wArot1"])
                DMA("pool", wA[:, :, 816:832], w_in_v[:, :, 1664:1680], [], ["wArot2"])
                TS("dve", wA[:, :, 800:816], wA[:, :, 800:816], -1.0, None, ALU.mult, None, ["wArot1"], ["wArot1"])
                WA_ALL = ["wA", "wAz1", "wAz2", "wAr", "wArot1", "wArot2"]
                for k in range(8):
                    TS("dve", wA[:, k, :], wA[:, k, :], vcol(V_GMIX + k), None, ALU.mult, None, WA_ALL + ["vecs"], WA_ALL)
                DMA("pool", Kb[96:113, :], ka, [], ["Kconst"])
                MS("dve", kmxr[:], 0.0, ["kmxr"])

                krs = T0(esA, "krs", [128, 512], BF16)

                def stage1(bn, bi):
                    c0 = bi * 512
                    xb = xbs[bn % 3]
                    sq = sqs[0]
                    XB = "xb%d" % (bn % 3)
                    SQ = "sq0"
                    B0 = 4 * (bn % 2)
                    DMA("pool", xb[:], xcat_v[:, :, c0:c0 + 512], [], [XB])
                    ACT(sq[:], xb[:], AF.Square, [XB], [SQ])
                    for k in range(8):
                        MM(ps[:, B0, :], ones[:], sq[:, k, :], k == 0, k == 7, ["ones", SQ], [PSB(B0)])
                    for c in range(2):
                        for k in range(8):
                            MM(ps[:, B0 + 1 + c, :], wA[:, k, 384 + c * 128:384 + (c + 1) * 128], xb[:, k, :], k == 0, k == 7,
                               WA_ALL + [XB], [PSB(B0 + 1 + c)])
                    for k in range(8):
                        MM(ps[0:96, B0 + 3, :], wA[:, k, 640:736], xb[:, k, :], k == 0, k == 7, WA_ALL + [XB], [PSB(B0 + 3)])

                def stage2(bn, bi):
                    own = bi >= 16
                    c0 = bi * 512
                    oc0 = (bi - 16) * 512
                    xb = xbs[bn % 3]
                    XB = "xb%d" % (bn % 3)
                    B0 = 4 * (bn % 2)
                    DMA("sp", posi[RB, :], poscat[:, c0:c0 + 512], [], ["posi"])
                    rs_ap = rstd1[:, oc0:oc0 + 512] if own else rstdA[:]
                    rs_reg = "rstd1" if own else "rstdA"
                    rstd_from_ps(B0, 512, 1.0 / D, rs_ap, rs_reg, rtmp, "rtmp")
                    for c in range(2):
                        TT("dve", ckv[:, c, :], ps[:, B0 + 1 + c, :], rs_ap, ALU.mult, [PSB(B0 + 1 + c), rs_reg], ["ckv"])
                    TT("dve", krs[RB, :], ps[RB, B0 + 3, :], rs_ap[RB, :], ALU.mult, [PSB(B0 + 3), rs_reg], ["krs"])
                    ACT(sq2[:, 0:2, :], ckv[:], AF.Square, ["ckv"], ["sq2"])
                    for c in range(2):
                        MM(ps[:, B0, :], ones[:], sq2[:, c, :], c == 0, c == 1, ["ones", "sq2"], [PSB(B0)])
                    MM(ps[0:96, B0 + 3, :], rotm[RB, :], krs[RB, :], True, True, ["rotm", "krs"], [PSB(B0 + 3)])
                    rstd_from_ps(B0, 512, 1.0 / 256, rkv[:], "rkv", rtmp, "rtmp")
                    for c in range(2):
                        STT(ckvn[:, c, c0:c0 + 512], ckv[:, c, :], vcol(V_GKV + c), rkv[:], ALU.mult, ALU.mult,
                            ["ckv", "rkv", "vecs"], ["ckvn%d" % bi])
                    if own:
                        for c in range(3):
                            for k in range(8):
                                MM(ps[:, B0 + c, :], wA[:, k, c * 128:(c + 1) * 128], xb[:, k, :], k == 0, k == 7,
                                   WA_ALL + [XB], [PSB(B0 + c)])
                    CP("dve", ang[RB, :], posi[RB, :], ["posi"], ["ang"])
                    TS("dve", ang[RB, :], ang[RB, :], vcol(V_INVF, 64, 96), None, ALU.mult, None, ["ang", "vecs"], ["ang"])
                    TS("dve", ti[RB, :], ang[RB, :], 1.0 / TWO_PI, None, ALU.mult, None, ["ang"], ["ti"])
                    CP("dve", tf[RB, :], ti[RB, :], ["ti"], ["tf"])
                    STT(rr_[RB, :], tf[RB, :], -TWO_PI, ang[RB, :], ALU.mult, ALU.add, ["tf", "ang"], ["rr"])
                    TS("dve", rr_[RB, :], rr_[RB, :], math.pi, -math.pi, ALU.min, ALU.max, ["rr"], ["rr"])
                    TS("dve", mm_[RB, :], rr_[RB, :], math.pi / 2, -TWO_PI, ALU.is_gt, ALU.mult, ["rr"], ["mm"])
                    STT(mm_[RB, :], rr_[RB, :], math.pi / 2, mm_[RB, :], ALU.add, ALU.add, ["rr", "mm"], ["mm"])
                    TS("dve", mm_[RB, :], mm_[RB, :], math.pi, -math.pi, ALU.min, ALU.max, ["mm"], ["mm"])
                    ACT(sinb[RB, :], rr_[RB, :], AF.Sin, ["rr"], ["sinb"])
                    ACT(cosb[RB, :], mm_[RB, :], AF.Sin, ["mm"], ["cosb"])
                    if own:
                        TS("dve", qcos[RB, oc0:oc0 + 512], cosb[RB, :], SCALE, None, ALU.mult, None, ["cosb"], ["qcos"])
                        TS("dve", qsin[RB, oc0:oc0 + 512], sinb[RB, :], SCALE, None, ALU.mult, None, ["sinb"], ["qsin"])
                    TT("dve", t1[RB, :], krs[RB, :], cosb[RB, :], ALU.mult, ["krs", "cosb"], ["t1"])
                    TT("dve", t2[RB, :], ps[RB, B0 + 3, :], sinb[RB, :], ALU.mult, [PSB(B0 + 3), "sinb"], ["t2"])
                    TT("dve", Kb[RB, c0:c0 + 512], t1[RB, :], t2[RB, :], ALU.add, ["t1", "t2"], ["Kr%d" % bi])
                    P.op("dve", lambda e: e.tensor_reduce(out=kmxr[RB, bi:bi + 1], in_=Kb[RB, c0:c0 + 512], axis=AX.X,
                                                          op=ALU.max, apply_absolute_value=True),
                         ["Kr%d" % bi], ["kmxr"])
                    if own:
                        for c in range(3):
                            TT("dve", cq[:, c, :], ps[:, B0 + c, :], rs_ap, ALU.mult, [PSB(B0 + c), rs_reg], ["cq"])
                        ACT(sq2[:], cq[:], AF.Square, ["cq"], ["sq2"])
                        for c in range(3):
                            MM(ps[:, B0 + 3, :], ones[:], sq2[:, c, :], c == 0, c == 2, ["ones", "sq2"], [PSB(B0 + 3)])
                        rstd_from_ps(B0 + 3, 512, 1.0 / 384, rq[:], "rq", rtmp, "rtmp")
                        for c in range(3):
                            STT(cqn[:, c, oc0:oc0 + 512], cq[:, c, :], vcol(V_GQ + c), rq[:], ALU.mult, ALU.mult,
                                ["cq", "rq", "vecs"], ["cqn"])

                for bn, bi in enumerate(blks):
                    stage1(bn, bi)
                    if bn > 0:
                        stage2(bn - 1, blks[bn - 1])
                stage2(len(blks) - 1, blks[-1])
                end_phase()

            if stop_after == "A":
                dump(ckvn[:, 0, :], NCAT, 0)
                dump(ckvn[:, 1, :], NCAT, NCAT)
                dump(Kb[:, :], NCAT, 2 * NCAT)
                for c in range(3):
                    dump(cqn[:, c, :], T, 3 * NCAT + c * T)
                dump(rstd1[:, :], T, 3 * NCAT + 3 * T)
                dump(qcos[:, :], T, 3 * NCAT + 4 * T)
                dump(qsin[:, :], T, 3 * NCAT + 5 * T)
                debug_finish()
                return nc
            with ExitStack() as esC:
                Vb = T0(esC, "Vb", [128, 80, 192], BF16)
                Qb = [T0(esC, "Qb%d" % i, [128, T], BF16) for i in range(2)]
                Pb = [T0(esC, "Pb%d" % i, [128, 2, 512], BF16) for i in range(4)]
                accs = [T0(esC, "accs%d" % i, [128, 512], F32) for i in range(2)]
                wukv = T0(esC, "wukv", [128, 2, 1024], BF16)
                wuq = T0(esC, "wuq", [128, 3, 768], BF16)
                wqrot = T0(esC, "wqrot", [128, 3, 8, 96], BF16)
                absq = T0(esC, "absq", [128, T], BF16)
                kmxmat = T0(esC, "kmxmat", [128, 97], BF16)
                kmxn = T0(esC, "kmxn", [128, NBLK], F32)
                kmxf = T0(esC, "kmxf", [128, 2], F32)
                rl = T0(esC, "rl", [128, 512], F32)
                bc = T0(esC, "bc", [128, 512], F32)
                t1 = T0(esC, "t1C", [128, 512], F32)
                t2 = T0(esC, "t2C", [128, 512], F32)

                DMA("pool", wukv[:], kp(w_ukv), [], ["wukv"])
                DMA("pool", wuq[:], kp(w_uq), [], ["wuq"])
                w_uq4 = w_uq.rearrange("(k p) (h c) -> p k h c", p=128, c=96)
                MS("dve", wqrot[:, :, :, 0:64], 0.0, ["wqrot0"])
                for c in range(3):
                    DMA("pool", wqrot[:, c, :, 64:80], w_uq4[:, c, :, 80:96], [], ["wqrot1_%d" % c])
                    DMA("pool", wqrot[:, c, :, 80:96], w_uq4[:, c, :, 64:80], [], ["wqrot2_%d" % c])
                TS("dve", wqrot[:, :, :, 64:80], wqrot[:, :, :, 64:80], -1.0, None, ALU.mult, None, ["wqrot1_0", "wqrot1_1", "wqrot1_2"], ["wqrot1"])
                WQR = ["wqrot0", "wqrot1", "wqrot2_0", "wqrot2_1", "wqrot2_2"]
                for i in range(2):
                    DMA("pool", Qb[i][97:113, :], qa, [], ["Qm%d" % i])
                MS("pool", Vb[:, :, 64:65], 1.0, ["Vc1"])
                MS("pool", Vb[:, :, 65:128], 0.0, ["Vc0"])
                MS("dve", kmxmat[:], 0.0, ["kmxmat"])
                MS("dve", kmxn[:], 0.0, ["kmxn"])
                P.op("dve", lambda e: e.tensor_reduce(out=kmxf[RB, 1:2], in_=kmxr[RB, :], axis=AX.X, op=ALU.max), ["kmxr"], ["kmxf1"])
                TS("dve", kmxmat[RB, 96:97], kmxf[RB, 1:2], 1.01, None, ALU.mult, None, ["kmxf1", "kmxmat"], ["kmxmat_r"])

                kv_blocks = [b for b in range(NBLK) if b != 15]

                def prepK(h, bi):
                    cols = bi * 512
                    for c in range(2):
                        MM(ps[0:64, 7, :], wukv[:, c, h * 128:h * 128 + 64], ckvn[:, c, cols:cols + 512], c == 0, c == 1,
                           ["wukv", "ckvn%d" % bi], [PSB(7)])
                    CP("dve", Kb[0:64, cols:cols + 512], ps[0:64, 7, :], [PSB(7)], ["Kn%d" % bi])
                    P.op("dve", lambda e: e.tensor_reduce(out=kmxn[0:64, bi:bi + 1], in_=ps[0:64, 7, :], axis=AX.X, op=ALU.max,
                                                          apply_absolute_value=True), [PSB(7)], ["kmxn"])

                def prepV(h, bi):
                    cols = bi * 512
                    vdat = 0 if h % 2 == 0 else 128
                    for t in range(4):
                        for c in range(2):
                            MM(ps[:, 7, t * 64:(t + 1) * 64], ckvn[:, c, cols + t * 128:cols + (t + 1) * 128],
                               wukv[:, c, h * 128 + 64:h * 128 + 128], c == 0, c == 1, ["wukv", "ckvn%d" % bi], [PSB(7)], skip_group_check=True)
                    CP("dve", Vb[:, bi * 4:(bi + 1) * 4, vdat:vdat + 64], ps[:, 7, 0:256].rearrange("p (t d) -> p t d", d=64),
                       [PSB(7)], ["V%d_%d" % (h % 2, bi)])

                def prepKV(h, bi):
                    prepK(h, bi)
                    prepV(h, bi)

                def prepQ(h, tbs=(0, 1, 2, 3)):
                    qb = h % 2
                    for tb in tbs:
                        for hf in range(2):
                            cols = tb * 512 + hf * 256
                            cs = slice(cols, cols + 256)
                            for c in range(3):
                                MM(ps[0:96, 7, 0:256], wuq[:, c, h * 96:(h + 1) * 96], cqn[:, c, cs], c == 0, c == 2,
                                   ["wuq", "cqn"], [PSB(7)], skip_group_check=True)
                            for c in range(3):
                                MM(ps[0:96, 7, 256:512], wqrot[:, c, h, :], cqn[:, c, cs], c == 0, c == 2,
                                   WQR + ["cqn"], [PSB(7)], skip_group_check=True)
                            QN = "Qn%d_%d_%d" % (qb, tb, hf)
                            QR = "Qr%d_%d_%d" % (qb, tb, hf)
                            TS("dve", Qb[qb][0:64, cs], ps[0:64, 7, 0:256], SCALE, None, ALU.mult, None, [PSB(7)], [QN])
                            TT("dve", t1[RB, 0:256], ps[RB, 7, 0:256], qcos[RB, cs], ALU.mult, [PSB(7), "qcos"], ["t1"])
                            TT("dve", t2[RB, 0:256], ps[RB, 7, 256:512], qsin[RB, cs], ALU.mult, [PSB(7), "qsin"], ["t2"])
                            TT("dve", Qb[qb][RB, cs], t1[RB, 0:256], t2[RB, 0:256], ALU.add, ["t1", "t2"], [QR])
                            STT(absq[0:96, cs], Qb[qb][0:96, cs], -1.0, Qb[qb][0:96, cs], ALU.mult, ALU.max, [QN, QR], ["absq%d_%d" % (tb, hf)])

                def prepQstab(h):
                    qb = h % 2
                    P.op("dve", lambda e: e.tensor_reduce(out=kmxf[0:64, 0:1], in_=kmxn[0:64, :], axis=AX.X, op=ALU.max), ["kmxn"], ["kmxf0"])
                    TS("dve", kmxmat[0:64, 96:97], kmxf[0:64, 0:1], 1.01, None, ALU.mult, None, ["kmxf0", "kmxmat"], ["kmxmat_n"])
                    for tb in range(4):
                        cols = tb * 512
                        MM(ps[0:97, 7, :], kmxmat[0:96, 0:97], absq[0:96, cols:cols + 512], True, True,
                           ["kmxmat", "kmxmat_r", "kmxmat_n", "absq%d_0" % tb, "absq%d_1" % tb], [PSB(7)])
                        ACT(Qb[qb][96:97, cols:cols + 512], ps[96:97, 7, :], AF.Copy, [PSB(7)], ["Qs%d_%d" % (qb, tb)], scale=-1.0)

                grp = [0]

                def make_groups(h):
                    out = []
                    for si, s in enumerate((3, 2, 1, 0)):
                        tiles = [(kt, 0) for kt in range(4 * NSLOT_UNITS[s])] + [(64 + 4 * s + a, 128 * a) for a in range(4)]
                        ntile = len(tiles)
                        accb = 6
                        for g0 in range(0, ntile, 2):
                            gi = grp[0]
                            grp[0] += 1
                            out.append(dict(h=h, s=s, pair=tiles[g0:g0 + 2], g0=g0, ntile=ntile, accb=accb,
                                            gb=2 * (gi % 3), pi=gi % 4, last=(g0 + 2 >= ntile), si=si))
                    return out

                def emit_S(G):
                    h, s, gb, pi = G["h"], G["s"], G["gb"], G["pi"]
                    qb = h % 2
                    qc = s * 512
                    pair = G["pair"]
                    PREG = "P%d" % pi
                    qreads = ["Qn%d_%d_0" % (qb, s), "Qr%d_%d_0" % (qb, s), "Qn%d_%d_1" % (qb, s), "Qr%d_%d_1" % (qb, s), "Qs%d_%d" % (qb, s), "Qm%d" % qb]
                    for i, (kt, off) in enumerate(pair):
                        bi = kt // 4
                        diag = bi >= 16
                        MM(ps[:, gb + i, off:512], Kb[0:113, kt * 128:(kt + 1) * 128], Qb[qb][0:113, qc + off:qc + 512], True, not diag,
                           ["Kn%d" % bi, "Kr%d" % bi, "Kconst"] + qreads, [PSB(gb + i)], skip_group_check=True)
                        if diag:
                            MM(ps[:, gb + i, off:off + 128], ident[:], tri[:], False, True, ["ident", "tri"], [PSB(gb + i)], skip_group_check=True)
                    if all(off == 0 for _, off in pair) and len(pair) == 2:
                        ACT(Pb[pi][:, 0:2, :], ps[:, gb:gb + 2, :], AF.Exp, [PSB(gb), PSB(gb + 1)], [PREG])
                    else:
                        for i, (kt, off) in enumerate(pair):
                            ACT(Pb[pi][:, i, off:512], ps[:, gb + i, off:512], AF.Exp, [PSB(gb + i)], [PREG])

                def emit_PV(G):
                    h, s, gb, pi, accb = G["h"], G["s"], G["gb"], G["pi"], G["accb"]
                    voff = 0 if h % 2 == 0 else 64
                    qc = s * 512
                    PREG = "P%d" % pi
                    for i, (kt, off) in enumerate(G["pair"]):
                        bi = kt // 4
                        first = (G["g0"] + i == 0)
                        last = (G["g0"] + i == G["ntile"] - 1)
                        MM(ps[:, accb, off:512], Vb[:, kt, voff:voff + 128], Pb[pi][:, i, off:512], first, last,
                           ["V%d_%d" % (h % 2, bi), "Vc1", "Vc0", PREG], [PSB(accb)], skip_group_check=True)
                    if G["last"]:
                        r0 = 64 if h % 2 == 0 else 0
                        rows = slice(0, 64) if h % 2 == 0 else slice(64, 128)
                        ai = (h * 4 + G["si"]) % 2
                        AC = "accs%d" % ai
                        ACT(accs[ai][:], ps[:, accb, :], AF.Copy, [PSB(accb)], [AC])
                        RECIP(rl[r0:r0 + 1, :], accs[ai][r0:r0 + 1, :], [AC], ["rl"])
                        MM(ps[:, 7, :], onesf[r0:r0 + 1, :], rl[r0:r0 + 1, :], True, True, ["onesf", "rl"], [PSB(7)])
                        TT("dve", Onorm[rows, h // 2, qc:qc + 512], accs[ai][rows, :], ps[rows, 7, :], ALU.mult, [AC, PSB(7)], ["On%d_%d" % (h, s)])

                NH = dbg.get("nheads", 8)
                prepQ(0)
                for bi in kv_blocks:
                    prepKV(0, bi)
                prepQstab(0)
                freed = {3: [11, 12, 13, 14, 19], 2: [7, 8, 9, 10, 18], 1: [3, 4, 5, 6, 17], 0: [0, 1, 2, 16]}
                pend = []
                EVERY = dbg.get("every", 2)
                for h in range(NH):
                    tasks = []
                    if h + 1 < NH:
                        for tb in range(4):
                            tasks.append(lambda h=h, tb=tb: prepQ(h + 1, (tb,)))
                        for bi in kv_blocks:
                            tasks.append(lambda h=h, bi=bi: prepV(h + 1, bi))
                    groups = make_groups(h)
                    for gi_, G in enumerate(groups):
                        emit_S(G)
                        pend.append(G)
                        if len(pend) > 2:
                            emit_PV(pend.pop(0))
                        if G["last"] and h + 1 < NH:
                            newt = [(lambda h=h, bi=bi: prepK(h + 1, bi)) for bi in freed[G["s"]]]
                            tasks = newt + tasks
                        if tasks and gi_ % EVERY == 0:
                            tasks.pop(0)()
                    while tasks:
                        tasks.pop(0)()
                    if h + 1 < NH:
                        prepQstab(h + 1)
                while pend:
                    emit_PV(pend.pop(0))
                end_phase()

            if stop_after == "C":
                for hp in range(4):
                    dump(Onorm[:, hp, :], T, hp * T)
                debug_finish()
                return nc
        def load_own_block(tb, xb, hT=None):
            col0 = SEQ + tb * 512
            for k in range(8):
                slot = ring[0] % 4
                ring[0] += 1
                XS = "xst%d" % slot
                DMA("sp", xst[:, slot, :], xcat_v[:, k, col0:col0 + 512], [], [XS])
                TS("dve", xb[:, k, :], xst[:, slot, :], vcol(V_GMIX + k), None, ALU.mult, None, [XS, "vecs"], ["xb%d" % k])
                if hT is not None:
                    ACT(hT[:, k, tb * 512:(tb + 1) * 512], xst[:, slot, :], AF.Copy, [XS], ["h%d_%d" % (k, tb)])
            return

        def load_own_block_h(tb, xb, hT):
            col0 = SEQ + tb * 512
            for k in range(8):
                HR = "h%d_%d" % (k, tb)
                DMA("sp", hT[:, k, tb * 512:(tb + 1) * 512], xcat_v[:, k, col0:col0 + 512], [], [HR])
                TS("dve", xb[:, k, :], hT[:, k, tb * 512:(tb + 1) * 512], vcol(V_GMIX + k), None, ALU.mult, None, [HR, "vecs"], ["xb%d" % k])

        def blk_stats(hT, tb, hsq, rout, rout_reg, rtmp):
            ACT(hsq[:], hT[:, :, tb * 512:(tb + 1) * 512], AF.Square, ["h%d_%d" % (k, tb) for k in range(8)], ["hsq"])
            for k in range(8):
                MM(ps[:, 0, :], ones[:], hsq[:, k, :], k == 0, k == 7, ["ones", "hsq"], [PSB(0)])
            rstd_from_ps(0, 512, 1.0 / D, rout, rout_reg, rtmp, "rtmpS")

        with ExitStack() as esH:
            hT = T0(esH, "hT", [128, 8, T], F32)
            with ExitStack() as esCA:
                convact = T0(esCA, "convact", [128, 4, T], BF16)
                with ExitStack() as esZ:
                    zT = T0(esZ, "zT", [128, 4, 4, 544], BF16)
                    with ExitStack() as esD:
                        wconv = T0(esD, "wconv", [128, 8, 1024], BF16)
                        xb = T0(esD, "xbD", [128, 8, 512], BF16)
                        xh = T0(esD, "xh", [128, 8, 32], F32)
                        xbh = T0(esD, "xbh", [128, 8, 32], BF16)
                        sqh = T0(esD, "sqh", [128, 8, 32], BF16)
                        rh = T0(esD, "rh", [128, 32], F32)
                        rtmpD = T0(esD, "rtmpD", [128, 32], F32)
                        gs = [T0(esD, "gs%d" % i, [128, 512], F32) for i in range(2)]
                        sg = [T0(esD, "sg%d" % i, [128, 512], F32) for i in range(2)]
                        as_ = [T0(esD, "as%d" % i, [128, 512], F32) for i in range(2)]
                        gsh = T0(esD, "gsh", [128, 32], F32)
                        sgh = T0(esD, "sgh", [128, 32], F32)
                        ash = T0(esD, "ash", [128, 32], F32)
                        DMA("pool", wconv[:, :, 0:512], w_in_v[:, :, 0:512], [], ["wconv0"])
                        DMA("pool", wconv[:, :, 512:1024], w_in_v[:, :, 512:1024], [], ["wconv1"])
                        for tb in range(4):
                            load_own_block(tb, xb)
                            DMA("sp", xh[:], xcat_v[:, :, NCAT + tb * 32:NCAT + (tb + 1) * 32], [], ["xh"])
                            for k in range(8):
                                TS("dve", xbh[:, k, :], xh[:, k, :], vcol(V_GMIX + k), None, ALU.mult, None, ["xh", "vecs"], ["xbh"])
                            ACT(sqh[:], xh[:], AF.Square, ["xh"], ["sqh"])
                            for k in range(8):
                                MM(ps[:, 6, 0:32], ones[:], sqh[:, k, :], k == 0, k == 7, ["ones", "sqh"], [PSB(6)])
                            rstd_from_ps(6, 32, 1.0 / D, rh[:], "rh", rtmpD, "rtmpD")
                            rs = rstd1[:, tb * 512:(tb + 1) * 512]
                            for cc in range(4):
                                ba = 2 * (cc % 2)
                                bg = ba + 1
                                i2 = cc % 2
                                for k in range(8):
                                    MM(ps[:, ba, :], wconv[:, k, cc * 128:(cc + 1) * 128], xb[:, k, :], k == 0, k == 7, ["wconv0", XBK[k]], [PSB(ba)])
                                for k in range(8):
                                    MM(ps[:, bg, :], wconv[:, k, 512 + cc * 128:512 + (cc + 1) * 128], xb[:, k, :], k == 0, k == 7, ["wconv1", XBK[k]], [PSB(bg)])
                                for k in range(8):
                                    MM(ps[:, 4, cc * 32:(cc + 1) * 32], wconv[:, k, cc * 128:(cc + 1) * 128], xbh[:, k, :], k == 0, k == 7,
                                       ["wconv0", "xbh"], [PSB(4)], skip_group_check=True)
                                for k in range(8):
                                    MM(ps[:, 5, cc * 32:(cc + 1) * 32], wconv[:, k, 512 + cc * 128:512 + (cc + 1) * 128], xbh[:, k, :], k == 0, k == 7,
                                       ["wconv1", "xbh"], [PSB(5)], skip_group_check=True)
                                TT("dve", gs[i2][:], ps[:, bg, :], rs, ALU.mult, [PSB(bg), "rstd1"], ["gs%d" % i2])
                                ACT(sg[i2][:], gs[i2][:], AF.Sigmoid, ["gs%d" % i2], ["sg%d" % i2])
                                TT("dve", as_[i2][:], ps[:, ba, :], rs, ALU.mult, [PSB(ba), "rstd1"], ["as%d" % i2])
                                TT("dve", zT[:, cc, tb, 32:544], as_[i2][:], sg[i2][:], ALU.mult, ["as%d" % i2, "sg%d" % i2], ["z%d_%d" % (cc, tb)])
                                TT("dve", gsh[:], ps[:, 5, cc * 32:(cc + 1) * 32], rh[:], ALU.mult, [PSB(5), "rh"], ["gsh"])
                                ACT(sgh[:], gsh[:], AF.Sigmoid, ["gsh"], ["sgh"])
                                TT("dve", ash[:], ps[:, 4, cc * 32:(cc + 1) * 32], rh[:], ALU.mult, [PSB(4), "rh"], ["ash"])
                                TT("dve", zT[:, cc, tb, 0:32], ash[:], sgh[:], ALU.mult, ["ash", "sgh"], ["zh%d_%d" % (cc, tb)])
                        end_phase()
                    with ExitStack() as esD:
                        diag = T0(esD, "diag", [128, 4, 31, 128], BF16)
                        cv = T0(esD, "cv", [128, 4, 512], F32)
                        cvb = T0(esD, "cvb", [128, 4, 512], BF16)
                        cvsq = T0(esD, "cvsq", [128, 4, 512], BF16)
                        mean = T0(esD, "mean", [128, 512], F32)
                        msq = T0(esD, "msq", [128, 512], F32)
                        var = T0(esD, "var", [128, 512], F32)
                        sd = T0(esD, "sd", [128, 512], F32)
                        rsl = T0(esD, "rsl", [128, 512], F32)
                        y1 = [T0(esD, "y1_%d" % i, [128, 512], F32) for i in range(2)]
                        y2 = [T0(esD, "y2_%d" % i, [128, 512], F32) for i in range(2)]
                        for cc in range(4):
                            for tau in range(31):
                                TS("dve", diag[:, cc, tau, :], ident[:], vcol(V_CONVW + cc * 31 + tau), None, ALU.mult, None, ["ident", "vecs"], ["diag%d" % cc])
                        for tb in range(4):
                            for cc in range(4):
                                for tau in range(31):
                                    MM(ps[:, cc, :], diag[:, cc, tau, :], zT[:, cc, tb, tau + 2:tau + 2 + 512], tau == 0, tau == 30,
                                       ["diag%d" % cc, "z%d_%d" % (cc, tb), "zh%d_%d" % (cc, tb)], [PSB(cc)])
                                ACT(cv[:, cc, :], ps[:, cc, :], AF.Identity, [PSB(cc), "vecs"], ["cv%d" % cc], bias=vcol(V_CONVB + cc))
                                CP("dve", cvb[:, cc, :], cv[:, cc, :], ["cv%d" % cc], ["cvb%d" % cc])
                                ACT(cvsq[:, cc, :], cv[:, cc, :], AF.Square, ["cv%d" % cc], ["cvsq%d" % cc])
                            for cc in range(4):
                                MM(ps[:, 4, :], ones[:], cvb[:, cc, :], cc == 0, cc == 3, ["ones", "cvb%d" % cc], [PSB(4)])
                            for cc in range(4):
                                MM(ps[:, 5, :], ones[:], cvsq[:, cc, :], cc == 0, cc == 3, ["ones", "cvsq%d" % cc], [PSB(5)])
                            TS("dve", mean[:], ps[:, 4, :], 1.0 / 512, None, ALU.mult, None, [PSB(4)], ["mean"])
                            TT("dve", msq[:], mean[:], mean[:], ALU.mult, ["mean"], ["msq"])
                            STT(var[:], ps[:, 5, :], 1.0 / 512, msq[:], ALU.mult, ALU.subtract, [PSB(5), "msq"], ["var"])
                            TS("dve", var[:], var[:], 0.0, None, ALU.max, None, ["var"], ["var"])
                            ACT(sd[:], var[:], AF.Sqrt, ["var", "epsb"], ["sd"], bias=epsb[:], scale=1.0)
                            RECIP(rsl[:], sd[:], ["sd"], ["rsl"])
                            for cc in range(4):
                                i2 = cc % 2
                                TT("dve", y1[i2][:], cv[:, cc, :], mean[:], ALU.subtract, ["cv%d" % cc, "mean"], ["y1_%d" % i2])
                                TT("dve", y2[i2][:], y1[i2][:], rsl[:], ALU.mult, ["y1_%d" % i2, "rsl"], ["y2_%d" % i2])
                                ACT(convact[:, cc, tb * 512:(tb + 1) * 512], y2[i2][:], AF.Silu, ["y2_%d" % i2, "vecs"], ["ca%d_%d" % (cc, tb)],
                                    scale=vcol(V_LNG + cc), bias=vcol(V_LNB + cc))
                        end_phase()
                if stop_after == "D1":
                    for cc in range(4):
                        dump(convact[:, cc, :], T, cc * T)
                    debug_finish()
                    return nc
                with ExitStack() as esD:
                    xb = T0(esD, "xbD2", [128, 8, 512], BF16)
                    wco = T0(esD, "wco", [128, 4, 1024], BF16)
                    wmo = T0(esD, "wmo", [128, 4, 1024], BF16)
                    wout = T0(esD, "wout", [128, 8, 1024], BF16)
                    gw = [T0(esD, "gw%d" % i, [128, 8, 512], BF16) for i in range(2)]
                    sig = T0(esD, "sig", [128, 16, 512], BF16)
                    mg = T0(esD, "mg", [128, 8, 512], BF16)
                    gs = [T0(esD, "gsD%d" % i, [128, 512], F32) for i in range(2)]
                    m1 = [T0(esD, "m1_%d" % i, [128, 512], F32) for i in range(2)]
                    m2 = [T0(esD, "m2_%d" % i, [128, 512], F32) for i in range(2)]
                    DMA("pool", wco[:], kp(w_conv_out), [], ["wco"])
                    DMA("pool", wmo[:], kp(w_mla_out), [], ["wmo"])
                    w_out_v = kp(w_out)
                    DMA("pool", wout[:, :, 0:512], w_out_v[:, :, 0:512], [], ["wout0"])
                    DMA("pool", wout[:, :, 512:1024], w_out_v[:, :, 512:1024], [], ["wout1"])
                    gcount = 0
                    for tb in range(4):
                        tc_ = slice(tb * 512, (tb + 1) * 512)
                        load_own_block(tb, xb, hT)
                        for gi in range(4):
                            gb_ = gcount % 2
                            gcount += 1
                            DMA("pool", gw[gb_][:], w_in_v[:, :, 1696 + gi * 512:1696 + (gi + 1) * 512], [], ["gw%d" % gb_])
                            for j in range(4):
                                oc = gi * 4 + j
                                bank = oc % 4
                                i2 = oc % 2
                                for k in range(8):
                                    MM(ps[:, bank, :], gw[gb_][:, k, j * 128:(j + 1) * 128], xb[:, k, :], k == 0, k == 7, ["gw%d" % gb_, XBK[k]], [PSB(bank)])
                                TT("dve", gs[i2][:], ps[:, bank, :], rstd1[:, tc_], ALU.mult, [PSB(bank), "rstd1"], ["gsD%d" % i2])
                                ACT(sig[:, oc, :], gs[i2][:], AF.Sigmoid, ["gsD%d" % i2], ["sig%d" % oc])
                        for c in range(8):
                            b1 = 4 + (c % 2) * 2
                            b2 = b1 + 1
                            i2 = c % 2
                            for k4 in range(4):
                                MM(ps[:, b1, :], wco[:, k4, c * 128:(c + 1) * 128], convact[:, k4, tc_], k4 == 0, k4 == 3,
                                   ["wco", "ca%d_%d" % (k4, tb)], [PSB(b1)])
                            for hp in range(4):
                                MM(ps[:, b2, :], wmo[:, hp, c * 128:(c + 1) * 128], Onorm[:, hp, tc_], hp == 0, hp == 3,
                                   ["wmo", "On%d_%d" % (2 * hp, tb), "On%d_%d" % (2 * hp + 1, tb)], [PSB(b2)])
                            TT("dve", m1[i2][:], ps[:, b1, :], sig[:, c, :], ALU.mult, [PSB(b1), "sig%d" % c], ["m1_%d" % i2])
                            TT("dve", m2[i2][:], ps[:, b2, :], sig[:, 8 + c, :], ALU.mult, [PSB(b2), "sig%d" % (8 + c)], ["m2_%d" % i2])
                            TT("dve", mg[:, c, :], m1[i2][:], m2[i2][:], ALU.add, ["m1_%d" % i2, "m2_%d" % i2], ["mg%d" % c])
                        for c in range(8):
                            bank = c % 4
                            for k in range(8):
                                MM(ps[:, bank, :], wout[:, k, c * 128:(c + 1) * 128], mg[:, k, :], k == 0, k == 7,
                                   ["wout0", "wout1", "mg%d" % k], [PSB(bank)])
                            TT("dve", hT[:, c, tc_], ps[:, bank, :], hT[:, c, tc_], ALU.add, [PSB(bank), "h%d_%d" % (c, tb)], ["h%d_%d" % (c, tb)])
                    end_phase()
            if stop_after == "D2":
                for c in range(8):
                    dump(hT[:, c, :], T, c * T)
                debug_finish()
                return nc
            with ExitStack() as esE:
                memx = T0(esE, "memx", [128, 8, 256], F32)
                memb = T0(esE, "memb", [128, 8, 256], BF16)
                msqm = T0(esE, "msqm", [128, 8, 256], BF16)
                rmem = T0(esE, "rmem", [128, 256], F32)
                rmemT = T0(esE, "rmemT", [128, 2], F32)
                rtmpE = T0(esE, "rtmpE", [128, 512], F32)
                wxkv = T0(esE, "wxkv", [128, 8, 1024], BF16)
                wxq = T0(esE, "wxq", [128, 8, 512], BF16)
                wxo = T0(esE, "wxo", [128, 4, 1024], BF16)
                Kx = T0(esE, "Kx", [128, 4, 256], BF16)
                Vx = T0(esE, "Vx", [128, 2, 512], BF16)
                kxm = T0(esE, "kxm", [128, 4], F32)
                kmm = T0(esE, "kmm", [128, 4, 128], BF16)
                hb = T0(esE, "hb", [128, 8, 512], BF16)
                hsq = T0(esE, "hsqE", [128, 8, 512], BF16)
                rstd2 = T0(esE, "rstd2", [128, 512], F32)
                Qx = T0(esE, "Qx", [128, 4, 512], BF16)
                aq = T0(esE, "aq", [128, 4, 512], BF16)
                Px = [T0(esE, "Px%d" % i, [128, 512], BF16) for i in range(2)]
                lr = T0(esE, "lr", [128, 512], F32)
                Ox = T0(esE, "Ox", [128, 4, 512], BF16)
                DMA("sp", memx[:], kp(memT), [], ["memx"])
                w_xkv_v = kp(w_xkv)
                DMA("pool", wxkv[:, :, 0:512], w_xkv_v[:, :, 0:512], [], ["wxkv0"])
                DMA("pool", wxkv[:, :, 512:1024], w_xkv_v[:, :, 512:1024], [], ["wxkv1"])
                DMA("pool", wxq[:], kp(w_xq), [], ["wxq"])
                DMA("pool", wxo[:], kp(w_xo), [], ["wxo"])
                for k in range(8):
                    TS("dve", memb[:, k, :], memx[:, k, :], vcol(V_GMEM + k), None, ALU.mult, None, ["memx", "vecs"], ["memb"])
                ACT(msqm[:], memx[:], AF.Square, ["memx"], ["msqm"])
                for k in range(8):
                    MM(ps[:, 0, 0:256], ones[:], msqm[:, k, :], k == 0, k == 7, ["ones", "msqm"], [PSB(0)])
                rstd_from_ps(0, 256, 1.0 / D, rmem[:], "rmem", rtmpE, "rtmpE")
                for kt in range(2):
                    for k in range(8):
                        MM(ps[:, 1, kt:kt + 1], msqm[:, k, kt * 128:(kt + 1) * 128], ones[:, 0:1], k == 0, k == 7, ["ones", "msqm"], [PSB(1)],
                           skip_group_check=True)
                rstd_from_ps(1, 2, 1.0 / D, rmemT[:], "rmemT", rtmpE, "rtmpE")
                for h in range(4):
                    bank = 2 + h % 2
                    for k in range(8):
                        MM(ps[:, bank, 0:256], wxkv[:, k, h * 128:(h + 1) * 128], memb[:, k, :], k == 0, k == 7, ["wxkv0", "memb"], [PSB(bank)])
                    TT("dve", Kx[:, h, :], ps[:, bank, 0:256], rmem[:], ALU.mult, [PSB(bank), "rmem"], ["Kx%d" % h])
                    P.op("dve", lambda e, h=h: e.tensor_reduce(out=kxm[:, h:h + 1], in_=Kx[:, h, :], axis=AX.X, op=ALU.max, apply_absolute_value=True),
                         ["Kx%d" % h], ["kxm%d" % h])
                    TS("dve", kmm[:, h, :], ones[:], kxm[:, h:h + 1], -1.01, ALU.mult, ALU.mult, ["ones", "kxm%d" % h], ["kmm%d" % h])
                for kt in range(2):
                    for k in range(8):
                        MM(ps[:, 4 + kt, :], memb[:, k, kt * 128:(kt + 1) * 128], wxkv[:, k, 512:1024], k == 0, k == 7, ["wxkv1", "memb"], [PSB(4 + kt)])
                    TS("dve", Vx[:, kt, :], ps[:, 4 + kt, :], rmemT[:, kt:kt + 1], None, ALU.mult, None, [PSB(4 + kt), "rmemT"], ["Vx%d" % kt])
                XS_ = 128.0 ** -0.5
                for tb in range(4):
                    tc_ = slice(tb * 512, (tb + 1) * 512)
                    HR = ["h%d_%d" % (k, tb) for k in range(8)]
                    for k in range(8):
                        TS("dve", hb[:, k, :], hT[:, k, tc_], vcol(V_GX + k), None, ALU.mult, None, [HR[k], "vecs"], ["hb%d" % k])
                    blk_stats(hT, tb, hsq, rstd2[:], "rstd2", rtmpE)
                    for h in range(4):
                        for k in range(8):
                            MM(ps[:, 1, :], wxq[:, k, h * 128:(h + 1) * 128], hb[:, k, :], k == 0, k == 7, ["wxq", "hb%d" % k], [PSB(1)])
                        STT(Qx[:, h, :], ps[:, 1, :], XS_, rstd2[:], ALU.mult, ALU.mult, [PSB(1), "rstd2"], ["Qx%d" % h])
                        STT(aq[:, h, :], Qx[:, h, :], -1.0, Qx[:, h, :], ALU.mult, ALU.max, ["Qx%d" % h], ["aq%d" % h])
                        for kt in range(2):
                            bank = 2 + kt
                            MM(ps[:, bank, :], Kx[:, h, kt * 128:(kt + 1) * 128], Qx[:, h, :], True, False, ["Kx%d" % h, "Qx%d" % h], [PSB(bank)])
                            MM(ps[:, bank, :], kmm[:, h, :], aq[:, h, :], False, True, ["kmm%d" % h, "aq%d" % h], [PSB(bank)])
                            ACT(Px[kt][:], ps[:, bank, :], AF.Exp, [PSB(bank)], ["Px%d" % kt])
                        for kt in range(2):
                            MM(ps[:, 4, :], Vx[:, kt, h * 128:(h + 1) * 128], Px[kt][:], kt == 0, kt == 1, ["Vx%d" % kt, "Px%d" % kt], [PSB(4)])
                        for kt in range(2):
                            MM(ps[:, 5, :], ones[:], Px[kt][:], kt == 0, kt == 1, ["ones", "Px%d" % kt], [PSB(5)])
                        RECIP(lr[:], ps[:, 5, :], [PSB(5)], ["lr"])
                        TT("dve", Ox[:, h, :], ps[:, 4, :], lr[:], ALU.mult, [PSB(4), "lr"], ["Ox%d" % h])
                    for c in range(8):
                        bank = 6 + c % 2
                        for h in range(4):
                            MM(ps[:, bank, :], wxo[:, h, c * 128:(c + 1) * 128], Ox[:, h, :], h == 0, h == 3, ["wxo", "Ox%d" % h], [PSB(bank)])
                        TT("dve", hT[:, c, tc_], ps[:, bank, :], hT[:, c, tc_], ALU.add, [PSB(bank), "h%d_%d" % (c, tb)], ["h%d_%d" % (c, tb)])
                end_phase()
            if stop_after == "E":
                for c in range(8):
                    dump(hT[:, c, :], T, c * T)
                debug_finish()
                return nc
            with ExitStack() as esF:
                hb3 = T0(esF, "hb3", [128, 8, T], BF16)
                W1 = [T0(esF, "W1_%d" % i, [128, 8, 256], BF16) for i in range(2)]
                W2 = [T0(esF, "W2_%d" % i, [128, 2, 1024], BF16) for i in range(2)]
                hid = [T0(esF, "hid%d" % i, [128, 2, T], BF16) for i in range(2)]
                rstd3 = T0(esF, "rstd3", [128, T], F32)
                rstd4 = T0(esF, "rstd4", [128, 512], F32)
                hsq = T0(esF, "hsqF", [128, 8, 512], BF16)
                rtmpF = T0(esF, "rtmpF", [128, 512], F32)
                uu = [T0(esF, "uu%d" % i, [128, 512], F32) for i in range(2)]
                vv = [T0(esF, "vv%d" % i, [128, 512], F32) for i in range(2)]
                w1v = kp(w_mlp1)
                w2v = kp(w_mlp2)
                for tb in range(4):
                    tc_ = slice(tb * 512, (tb + 1) * 512)
                    for k in range(8):
                        TS("dve", hb3[:, k, tc_], hT[:, k, tc_], vcol(V_GMLP + k), None, ALU.mult, None, ["h%d_%d" % (k, tb), "vecs"], ["hb3_%d_%d" % (k, tb)])
                    blk_stats(hT, tb, hsq, rstd3[:, tc_], "rstd3_%d" % tb, rtmpF)
                cnt = 0
                for g in range(16):
                    gb_ = g % 2
                    DMA("pool", W1[gb_][:], w1v[:, :, g * 256:(g + 1) * 256], [], ["W1_%d" % gb_])
                    DMA("pool", W2[gb_][:], w2v[:, g * 2:(g + 1) * 2, :], [], ["W2_%d" % gb_])
                    for j in range(2):
                        for tb in range(4):
                            tc_ = slice(tb * 512, (tb + 1) * 512)
                            bank = cnt % 4
                            i2 = cnt % 2
                            cnt += 1
                            for k in range(8):
                                MM(ps[:, bank, :], W1[gb_][:, k, j * 128:(j + 1) * 128], hb3[:, k, tc_], k == 0, k == 7,
                                   ["W1_%d" % gb_, "hb3_%d_%d" % (k, tb)], [PSB(bank)])
                            ACT(uu[i2][:], ps[:, bank, :], AF.Relu, [PSB(bank)], ["uu%d" % i2])
                            TT("dve", vv[i2][:], uu[i2][:], rstd3[:, tc_], ALU.mult, ["uu%d" % i2, "rstd3_%d" % tb], ["vv%d" % i2])
                            TT("pool", hid[gb_][:, j, tc_], vv[i2][:], vv[i2][:], ALU.mult, ["vv%d" % i2], ["hid%d_%d_%d" % (gb_, j, tb)])
                    for c in range(8):
                        for tb in range(4):
                            tc_ = slice(tb * 512, (tb + 1) * 512)
                            bank = 4 + cnt % 4
                            cnt += 1
                            for j in range(2):
                                MM(ps[:, bank, :], W2[gb_][:, j, c * 128:(c + 1) * 128], hid[gb_][:, j, tc_], j == 0, j == 1,
                                   ["W2_%d" % gb_, "hid%d_%d_%d" % (gb_, j, tb)], [PSB(bank)])
                            TT("dve", hT[:, c, tc_], ps[:, bank, :], hT[:, c, tc_], ALU.add, [PSB(bank), "h%d_%d" % (c, tb)], ["h%d_%d" % (c, tb)])
                for tb in range(4):
                    tc_ = slice(tb * 512, (tb + 1) * 512)
                    blk_stats(hT, tb, hsq, rstd4[:], "rstd4", rtmpF)
                    for c in range(8):
                        slot = ring[0] % 4
                        ring[0] += 1
                        XS = "xst%d" % slot
                        STT(xst[:, slot, :], hT[:, c, tc_], vcol(V_GFIN + c), rstd4[:], ALU.mult, ALU.mult, ["h%d_%d" % (c, tb), "rstd4", "vecs"], [XS])
                        DMA("sp", out_d[c * 128:(c + 1) * 128, tc_], xst[:, slot, :], [XS], ["out"])
                end_phase(final=True)
    return nc


def own_chunks(j):
    return [j, 7 - j, 8 + j, 15 - j]


def make_vecs(inp):
    v = np.zeros((128, NV), np.float32)

    def colmajor(g, n):
        return np.ascontiguousarray(np.asarray(g, np.float32).reshape(n, 128).T)

    v[:, V_GMIX:V_GMIX + 8] = colmajor(inp["norm_mix_g"][0], 8)
    v[:, V_GX:V_GX + 8] = colmajor(inp["norm_xattn_g"][0], 8)
    v[:, V_GMLP:V_GMLP + 8] = colmajor(inp["norm_mlp_g"][0], 8)
    v[:, V_GFIN:V_GFIN + 8] = colmajor(inp["final_norm_g"], 8)
    v[:, V_GMEM:V_GMEM + 8] = colmajor(inp["norm_mem_g"][0], 8)
    cw = np.asarray(inp["conv_w"][0], np.float32)
    v[:, V_CONVW:V_CONVW + 124] = cw.T.reshape(4, 128, 31).transpose(1, 0, 2).reshape(128, 124)
    v[:, V_CONVB:V_CONVB + 4] = colmajor(inp["conv_b"][0], 4)
    v[:, V_LNG:V_LNG + 4] = colmajor(inp["conv_ln_g"][0], 4)
    v[:, V_LNB:V_LNB + 4] = colmajor(inp["conv_ln_b"][0], 4)
    v[:, V_GQ:V_GQ + 3] = colmajor(inp["q_norm_g"][0], 3)
    v[:, V_GKV:V_GKV + 2] = colmajor(inp["kv_norm_g"][0], 2)
    half = 16
    invf = (np.float32(10000.0) ** (-np.arange(half, dtype=np.float32) / np.float32(half))).astype(np.float32)
    v[64:80, V_INVF] = invf
    v[80:96, V_INVF] = invf
    return v


def make_core_inputs(inp, core, shared):
    b, j = core // 4, core % 4
    x = np.asarray(inp["x"], np.float32)
    pos = np.asarray(inp["positions"], np.int32)
    chunks = own_chunks(j)
    xT = shared["xT"][b]
    xcat = np.zeros((D, NCAT + 128), np.float32)
    xcat[:, :SEQ] = xT
    poscat = np.zeros((32, NCAT), np.int32)
    poscat[:, :SEQ] = pos[b][None, :]
    qa = np.zeros((16, T), np.float32)
    for s, c in enumerate(chunks):
        xcat[:, SEQ + s * 512:SEQ + (s + 1) * 512] = xT[:, c * 512:(c + 1) * 512]
        poscat[:, SEQ + s * 512:SEQ + (s + 1) * 512] = pos[b][None, c * 512:(c + 1) * 512]
        if c > 0:
            xcat[:, NCAT + s * 32:NCAT + (s + 1) * 32] = xT[:, c * 512 - 32:c * 512]
        for u in range(16):
            if u >= c:
                qa[u, s * 512:(s + 1) * 512] = NEG
    m = dict(shared["common"])
    m.update({"xcat": xcat, "poscat": poscat, "qa": qa, "memT": shared["memT"][b]})
    return m


def make_shared(inp):
    x = np.asarray(inp["x"], np.float32)
    shared = {"xT": [np.ascontiguousarray(x[b].T) for b in range(2)],
              "memT": [np.ascontiguousarray(np.asarray(inp["mem"], np.float32)[b].T) for b in range(2)]}
    ka = np.zeros((17, NCAT), np.float32)
    ka[0, :] = 1.0
    for u in range(16):
        ka[1 + u, u * 512:(u + 1) * 512] = 1.0
    cmat = np.zeros((128, 352), np.float32)
    for i in range(16):
        cmat[64 + 16 + i, 256 + 64 + i] = -1.0
        cmat[64 + i, 256 + 64 + 16 + i] = 1.0
    cmat[:, :128] = np.eye(128, dtype=np.float32)
    kk, qq = np.meshgrid(np.arange(128), np.arange(128), indexing="ij")
    cmat[:, 128:256] = np.where(kk > qq, NEG, 0.0).astype(np.float32)
    common = {"ka": ka, "cmat": cmat, "vecs": make_vecs(inp)}
    for name in ["w_in", "w_conv_out", "w_uq", "w_ukv", "w_mla_out", "w_out", "w_xq", "w_xkv", "w_xo", "w_mlp1", "w_mlp2"]:
        common[name] = np.ascontiguousarray(np.asarray(inp[name], np.float32)[0])
    shared["common"] = common
    return shared


_NC_CACHE = {}


def kernel(**inputs):
    shared = make_shared(inputs)
    in_maps = [make_core_inputs(inputs, c, shared) for c in range(8)]
    if "nc" not in _NC_CACHE:
        _NC_CACHE["nc"] = build_program()
    nc = _NC_CACHE["nc"]
    res = run_bass_kernel_spmd(nc, in_maps, core_ids=list(range(8)))
    out = np.zeros((2, SEQ, D), np.float32)
    for core in range(8):
        b, j = core // 4, core % 4
        o = res.results[core]["out"]
        for s, c in enumerate(own_chunks(j)):
            out[b, c * 512:(c + 1) * 512, :] = o[:, s * 512:(s + 1) * 512].T
    return out
```

```python
import math
import numpy as np
import concourse.bass as bass
import concourse.mybir as mybir
from concourse.alu_op_type import AluOpType as ALU
from concourse.bass_utils import run_bass_kernel_spmd

F32 = mybir.dt.float32
BF16 = mybir.dt.bfloat16
I32 = mybir.dt.int32
AF = mybir.ActivationFunctionType
AX = mybir.AxisListType

D = 1024
SEQ = 8192
T = 2048
NB = 4
NBLK = 20
NCAT = NBLK * 512
EPS = 1e-6
SCALE = 96.0 ** -0.5
NSLOT_UNITS = (3, 7, 11, 15)
NEG = -30000.0
TWO_PI = 2.0 * math.pi

V_GMIX, V_GX, V_GMLP, V_GFIN, V_GMEM = 0, 8, 16, 24, 32
V_CONVW = 40
V_CONVB = 164
V_LNG = 168
V_LNB = 172
V_GQ = 176
V_GKV = 179
V_INVF = 181
NV = 184


class Op:
    __slots__ = ("fn", "waits", "tl", "idx", "is_dma")

    def __init__(self, fn, waits, tl, idx, is_dma):
        self.fn, self.waits, self.tl, self.idx, self.is_dma = fn, waits, tl, idx, is_dma


class Prog:
    ENGS = ("pe", "act", "dve", "pool", "sp")
    COMP = ("pe", "act", "dve", "pool")

    def __init__(self, n_dma=24):
        self.ops = {e: [] for e in self.ENGS}
        self.reg = {}
        self.seen = {e: {} for e in self.ENGS}
        self.cnt = {}
        self.n_dma = n_dma
        self.rr = 0
        self.bar = {}
        self.rank = {}
        self.sigbase = {e: 0 for e in self.COMP}
        self.forced = set()

    def _add(self, eng, fn, reads, writes, tl, is_dma):
        idx = self.cnt.get(tl, 0) + 1
        self.cnt[tl] = idx
        need = dict(self.bar)

        def req(t, i):
            if need.get(t, 0) < i:
                need[t] = i

        for r in reads:
            e = self.reg.get(r)
            if e is not None and e[0] is not None:
                req(*e[0])
        for r in writes:
            e = self.reg.get(r)
            if e is not None:
                if e[0] is not None:
                    req(*e[0])
                for t, i in e[1].items():
                    req(t, i)
        if is_dma and idx > 1:
            req(tl, idx - 1)
        waits = []
        sn = self.seen[eng]
        for t, i in need.items():
            if t == eng and not is_dma:
                continue
            if sn.get(t, 0) >= i:
                continue
            sn[t] = i
            waits.append((t, i))
        self.ops[eng].append(Op(fn, waits, tl, idx, is_dma))
        for r in reads:
            e = self.reg.setdefault(r, [None, {}])
            if e[1].get(tl, 0) < idx:
                e[1][tl] = idx
        for r in writes:
            self.reg[r] = [(tl, idx), {}]
        return idx

    def op(self, eng, fn, reads=(), writes=()):
        self._add(eng, fn, reads, writes, eng, False)

    def dma(self, eng, fn, reads=(), writes=()):
        tl = "q%d" % self.rr
        self.rr = (self.rr + 1) % self.n_dma
        self._add(eng, fn, reads, writes, tl, True)

    def phase_end(self, sigfns=None):
        for eng in self.ENGS:
            for t in self.COMP:
                self.seen[eng][t] = self.cnt.get(t, 0)

    def finish_waits(self, eng):
        waits = []
        for t, i in self.cnt.items():
            if t.startswith("q") and self.seen[eng].get(t, 0) < i:
                self.seen[eng][t] = i
                waits.append((t, i))
        self.ops[eng].append(Op(None, waits, None, 0, False))

    def emit(self, block, sems):
        sig = {e: set() for e in self.COMP}
        for e in self.ENGS:
            for o in self.ops[e]:
                for t, i in o.waits:
                    if t in sig and (t, i) not in self.rank:
                        sig[t].add(i)
        for (t, i) in self.forced:
            sig[t].add(i)
        for t, s in sig.items():
            for r, i in enumerate(sorted(s)):
                self.rank[(t, i)] = self.sigbase[t] + r + 1
            self.sigbase[t] += len(s)
        rank = self.rank

        def val(t, i):
            return 16 * i if t.startswith("q") else rank[(t, i)]

        def run(e, handle):
            for o in self.ops[e]:
                for t, i in o.waits:
                    handle.wait_ge(sems[t], val(t, i))
                if o.fn is None:
                    continue
                ins = o.fn(handle)
                if o.is_dma:
                    ins.then_inc(sems[o.tl], 16)
                elif (o.tl, o.idx) in rank:
                    ins.then_inc(sems[o.tl], 1)

        if self.ops["pe"]:
            block.tensor(lambda h: run("pe", h))
        if self.ops["act"]:
            block.scalar(lambda h: run("act", h))
        if self.ops["dve"]:
            block.vector(lambda h: run("dve", h))
        if self.ops["pool"]:
            block.gpsimd(lambda h: run("pool", h))
        if self.ops["sp"]:
            block.sync(lambda h: run("sp", h))

        self.ops = {e: [] for e in self.ENGS}
        self.forced = set()


def build_program(debug=None):
    from contextlib import ExitStack
    nc = bass.Bass("TRN2", target_bir_lowering=False)
    P = Prog()
    dbg = debug or {}
    stop_after = dbg.get("stop", "Z")

    def din(name, shape, dt=F32):
        return nc.dram_tensor(name, list(shape), dt, kind="ExternalInput").ap()

    xcat = din("xcat", [D, NCAT + 128])
    poscat = din("poscat", [32, NCAT], I32)
    qa = din("qa", [16, T])
    ka = din("ka", [17, NCAT])
    cmat = din("cmat", [128, 352])
    vecs_d = din("vecs", [128, NV])
    memT = din("memT", [D, 256])
    w_in = din("w_in", [D, 3744])
    w_conv_out = din("w_conv_out", [512, D])
    w_uq = din("w_uq", [384, 768])
    w_ukv = din("w_ukv", [256, 1024])
    w_mla_out = din("w_mla_out", [512, D])
    w_out = din("w_out", [D, D])
    w_xq = din("w_xq", [D, 512])
    w_xkv = din("w_xkv", [D, 1024])
    w_xo = din("w_xo", [512, D])
    w_mlp1 = din("w_mlp1", [D, 4096])
    w_mlp2 = din("w_mlp2", [4096, D])
    out_d = nc.dram_tensor("out", [D, T], F32, kind="ExternalOutput").ap()
    dbg_d = nc.dram_tensor("dbg", [128, dbg["n"]], F32, kind="ExternalOutput").ap() if debug else None

    def kp(ap):
        return ap.rearrange("(k p) n -> p k n", p=128)

    xcat_v = kp(xcat)
    w_in_v = kp(w_in)

    def MM(out, lhsT, rhs, start, stop, reads, writes, **kw):
        P.op("pe", lambda e: e.matmul(out, lhsT=lhsT, rhs=rhs, start=start, stop=stop, **kw), reads, writes)

    def ACT(out, in_, func, reads, writes, **kw):
        P.op("act", lambda e: e.activation(out=out, in_=in_, func=func, **kw), reads, writes)

    def TT(eng, out, in0, in1, op, reads, writes):
        P.op(eng, lambda e: e.tensor_tensor(out=out, in0=in0, in1=in1, op=op), reads, writes)

    def TS(eng, out, in0, s1, s2, op0, op1, reads, writes):
        if op1 is None:
            P.op(eng, lambda e: e.tensor_scalar(out=out, in0=in0, scalar1=s1, scalar2=None, op0=op0), reads, writes)
        else:
            P.op(eng, lambda e: e.tensor_scalar(out=out, in0=in0, scalar1=s1, scalar2=s2, op0=op0, op1=op1), reads, writes)

    def STT(out, in0, scalar, in1, op0, op1, reads, writes):
        P.op("dve", lambda e: e.scalar_tensor_tensor(out=out, in0=in0, scalar=scalar, in1=in1, op0=op0, op1=op1), reads, writes)

    def CP(eng, out, in_, reads, writes):
        P.op(eng, lambda e: e.tensor_copy(out=out, in_=in_), reads, writes)

    def MS(eng, out, val, writes):
        P.op(eng, lambda e: e.memset(out, val), (), writes)

    def RECIP(out, in_, reads, writes):
        P.op("dve", lambda e: e.reciprocal(out=out, in_=in_), reads, writes)

    def DMA(eng, out, in_, reads, writes):
        P.dma(eng, lambda e: e.dma_start(out=out, in_=in_), reads, writes)

    def PSB(b):
        return "ps%d" % b

    with ExitStack() as es0:
        def T0(es, name, shape, dt):
            return es.enter_context(nc.sbuf_tensor("sb_" + name, list(shape), dt))

        ps = es0.enter_context(nc.psum_tensor("ps", [128, 8, 512], F32))
        sems = {}
        for t in list(Prog.COMP) + ["q%d" % i for i in range(P.n_dma)]:
            sems[t] = es0.enter_context(nc.semaphore("s_" + t))
        vecs = T0(es0, "vecs", [128, NV], F32)
        ident = T0(es0, "ident", [128, 128], BF16)
        tri = T0(es0, "tri", [128, 128], BF16)
        rotm = T0(es0, "rotm", [128, 96], BF16)
        ones = T0(es0, "ones", [128, 128], BF16)
        onesf = T0(es0, "onesf", [128, 128], F32)
        epsb = T0(es0, "epsb", [128, 1], F32)
        scr = T0(es0, "scr", [128, 16], F32)
        rstd1 = T0(es0, "rstd1", [128, T], F32)
        Onorm = T0(es0, "Onorm", [128, 4, T], BF16)
        xst = T0(es0, "xst", [128, 4, 512], F32)

        def vcol(c, lo=0, hi=128):
            return vecs[lo:hi, c:c + 1]

        sigfns = {
            "pe": lambda e: e.matmul(ps[0:1, 7, 0:1], lhsT=ones[0:1, 0:1], rhs=ones[0:1, 0:1], start=True, stop=True),
            "act": lambda e: e.activation(out=scr[0:1, 0:1], in_=scr[0:1, 1:2], func=AF.Copy),
            "dve": lambda e: e.memset(scr[0:1, 2:3], 0.0),
            "pool": lambda e: e.memset(scr[0:1, 3:4], 0.0),
        }

        def end_phase(final=False):
            if final:
                P.finish_waits("sp")
            else:
                P.phase_end(sigfns)
            with nc.Block() as block:
                P.emit(block, sems)

        dumps = []

        def dump(ap, n, col):
            dumps.append((ap, n, col))

        def debug_finish():
            for (ap, n, col) in dumps:
                for o in range(0, n, 512):
                    w = min(512, n - o)
                    slot = (o // 512) % 4
                    CP("dve", xst[:, slot, 0:w], ap[:, o:o + w], [], ["xst%d" % slot])
                    DMA("sp", dbg_d[:, col + o:col + o + w], xst[:, slot, 0:w], ["xst%d" % slot], ["dbgout"])
            MS("dve", xst[:, 0, :], 0.0, ["xst0"])
            for c in range(8):
                for tb in range(4):
                    DMA("sp", out_d[c * 128:(c + 1) * 128, tb * 512:(tb + 1) * 512], xst[:, 0, :], ["xst0"], ["out"])
            end_phase(final=True)

        DMA("sp", vecs[:], vecs_d, [], ["vecs"])
        DMA("pool", ident[:], cmat[:, 0:128], [], ["ident"])
        DMA("pool", tri[:], cmat[:, 128:256], [], ["tri"])
        DMA("pool", rotm[:], cmat[:, 256:352], [], ["rotm"])
        MS("dve", ones[:], 1.0, ["ones"])
        MS("dve", onesf[:], 1.0, ["onesf"])
        MS("dve", epsb[:], EPS, ["epsb"])
        MS("dve", scr[:], 0.0, ["scr"])

        def rstd_from_ps(bank, n, inv_count, out_ap, out_reg, tmp, tmp_reg):
            ACT(tmp[:, 0:n], ps[:, bank, 0:n], AF.Sqrt, [PSB(bank), "epsb"], [tmp_reg], bias=epsb[:], scale=inv_count)
            RECIP(out_ap, tmp[:, 0:n], [tmp_reg], [out_reg])

        RB = slice(64, 96)
        ring = [0]

        def load_xblock(col0, xb, gcol, sqt=None, width=512, tag="", eng="dve"):
            for k in range(8):
                slot = ring[0] % 4
                ring[0] += 1
                XS = "xst%d" % slot
                DMA("sp", xst[:, slot, 0:width], xcat_v[:, k, col0:col0 + width], [], [XS])
                TS(eng, xb[:, k, 0:width], xst[:, slot, 0:width], vcol(gcol + k), None, ALU.mult, None, [XS, "vecs"], ["xb%s%d" % (tag, k)])
                if sqt is not None:
                    ACT(sqt[:, k, 0:width], xst[:, slot, 0:width], AF.Square, [XS], ["sq%s%d" % (tag, k)])

        XBK = ["xb%d" % k for k in range(8)]
        SQK = ["sq%d" % k for k in range(8)]

        with ExitStack() as esAC:
            ckvn = T0(esAC, "ckvn", [128, 2, NCAT], BF16)
            Kb = T0(esAC, "Kb", [128, NCAT], BF16)
            cqn = T0(esAC, "cqn", [128, 3, T], BF16)
            qcos = T0(esAC, "qcos", [128, T], BF16)
            qsin = T0(esAC, "qsin", [128, T], BF16)
            kmxr = T0(esAC, "kmxr", [128, NBLK], F32)

            blks = dbg.get("blks", [b for b in range(NBLK) if b != 15])
            with ExitStack() as esA:
                xbs = [T0(esA, "xbA%d" % i, [128, 8, 512], BF16) for i in range(3)]
                sqs = [T0(esA, "sqA%d" % i, [128, 8, 512], BF16) for i in range(1)]
                wA = T0(esA, "wA", [128, 8, 832], BF16)
                ckv = T0(esA, "ckv", [128, 2, 512], F32)
                cq = T0(esA, "cq", [128, 3, 512], F32)
                sq2 = T0(esA, "sq2", [128, 3, 512], BF16)
                rtmp = T0(esA, "rtmp", [128, 512], F32)
                rstdA = T0(esA, "rstdA", [128, 512], F32)
                rkv = T0(esA, "rkv", [128, 512], F32)
                rq = T0(esA, "rq", [128, 512], F32)
                posi = T0(esA, "posi", [128, 512], I32)
                ti = T0(esA, "ti", [128, 512], I32)
                ang = T0(esA, "ang", [128, 512], F32)
                tf = T0(esA, "tf", [128, 512], F32)
                rr_ = T0(esA, "rr", [128, 512], F32)
                mm_ = T0(esA, "mm", [128, 512], F32)
                sinb = T0(esA, "sinb", [128, 512], F32)
                cosb = T0(esA, "cosb", [128, 512], F32)
                t1 = T0(esA, "t1A", [128, 512], F32)
                t2 = T0(esA, "t2A", [128, 512], F32)

                DMA("pool", wA[:, :, 0:640], w_in_v[:, :, 1024:1664], [], ["wA"])
                MS("dve", wA[:, :, 640:704], 0.0, ["wAz1"])
                MS("dve", wA[:, :, 736:800], 0.0, ["wAz2"])
                DMA("pool", wA[:, :, 704:736], w_in_v[:, :, 1664:1696], [], ["wAr"])
                DMA("pool", wA[:, :, 800:816], w_in_v[:, :, 1680:1696], [], ["wArot1"])
                DMA("pool", wA[:, :, 816:832], w_in_v[:, :, 1664:1680], [], ["wArot2"])
                TS("dve", wA[:, :, 800:816], wA[:, :, 800:816], -1.0, None, ALU.mult, None, ["wArot1"], ["wArot1"])
                WA_ALL = ["wA", "wAz1", "wAz2", "wAr", "wArot1", "wArot2"]
                for k in range(8):
                    TS("dve", wA[:, k, :], wA[:, k, :], vcol(V_GMIX + k), None, ALU.mult, None, WA_ALL + ["vecs"], WA_ALL)
                DMA("pool", Kb[96:113, :], ka, [], ["Kconst"])
                MS("dve", kmxr[:], 0.0, ["kmxr"])

                krs = T0(esA, "krs", [128, 512], BF16)

                def stage1(bn, bi):
                    c0 = bi * 512
                    xb = xbs[bn % 3]
                    sq = sqs[0]
                    XB = "xb%d" % (bn % 3)
                    SQ = "sq0"
                    B0 = 4 * (bn % 2)
                    DMA("pool", xb[:], xcat_v[:, :, c0:c0 + 512], [], [XB])
                    ACT(sq[:], xb[:], AF.Square, [XB], [SQ])
                    for k in range(8):
                        MM(ps[:, B0, :], ones[:], sq[:, k, :], k == 0, k == 7, ["ones", SQ], [PSB(B0)])
                    for c in range(2):
                        for k in range(8):
                            MM(ps[:, B0 + 1 + c, :], wA[:, k, 384 + c * 128:384 + (c + 1) * 128], xb[:, k, :], k == 0, k == 7,
                               WA_ALL + [XB], [PSB(B0 + 1 + c)])
                    for k in range(8):
                        MM(ps[0:96, B0 + 3, :], wA[:, k, 640:736], xb[:, k, :], k == 0, k == 7, WA_ALL + [XB], [PSB(B0 + 3)])

                def stage2(bn, bi):
                    own = bi >= 16
                    c0 = bi * 512
                    oc0 = (bi - 16) * 512
                    xb = xbs[bn % 3]
                    XB = "xb%d" % (bn % 3)
                    B0 = 4 * (bn % 2)
                    DMA("sp", posi[RB, :], poscat[:, c0:c0 + 512], [], ["posi"])
                    rs_ap = rstd1[:, oc0:oc0 + 512] if own else rstdA[:]
                    rs_reg = "rstd1" if own else "rstdA"
                    rstd_from_ps(B0, 512, 1.0 / D, rs_ap, rs_reg, rtmp, "rtmp")
                    for c in range(2):
                        TT("dve", ckv[:, c, :], ps[:, B0 + 1 + c, :], rs_ap, ALU.mult, [PSB(B0 + 1 + c), rs_reg], ["ckv"])
                    TT("dve", krs[RB, :], ps[RB, B0 + 3, :], rs_ap[RB, :], ALU.mult, [PSB(B0 + 3), rs_reg], ["krs"])
                    ACT(sq2[:, 0:2, :], ckv[:], AF.Square, ["ckv"], ["sq2"])
                    for c in range(2):
                        MM(ps[:, B0, :], ones[:], sq2[:, c, :], c == 0, c == 1, ["ones", "sq2"], [PSB(B0)])
                    MM(ps[0:96, B0 + 3, :], rotm[RB, :], krs[RB, :], True, True, ["rotm", "krs"], [PSB(B0 + 3)])
                    TS("dve", ang[RB, :], posi[RB, :], vcol(V_INVF, 64, 96), None, ALU.mult, None, ["posi", "vecs"], ["ang"])
                    TS("dve", ti[RB, :], ang[RB, :], 1.0 / TWO_PI, None, ALU.mult, None, ["ang"], ["ti"])
                    CP("dve", tf[RB, :], ti[RB, :], ["ti"], ["tf"])
                    STT(rr_[RB, :], tf[RB, :], -TWO_PI, ang[RB, :], ALU.mult, ALU.add, ["tf", "ang"], ["rr"])
                    TS("dve", rr_[RB, :], rr_[RB, :], math.pi, -math.pi, ALU.min, ALU.max, ["rr"], ["rr"])
                    TS("dve", mm_[RB, :], rr_[RB, :], math.pi / 2, -TWO_PI, ALU.is_gt, ALU.mult, ["rr"], ["mm"])
                    STT(mm_[RB, :], rr_[RB, :], math.pi / 2, mm_[RB, :], ALU.add, ALU.add, ["rr", "mm"], ["mm"])
                    TS("dve", mm_[RB, :], mm_[RB, :], math.pi, -math.pi, ALU.min, ALU.max, ["mm"], ["mm"])
                    ACT(sinb[RB, :], rr_[RB, :], AF.Sin, ["rr"], ["sinb"])
                    ACT(cosb[RB, :], mm_[RB, :], AF.Sin, ["mm"], ["cosb"])
                    rstd_from_ps(B0, 512, 1.0 / 256, rkv[:], "rkv", rtmp, "rtmp")
                    for c in range(2):
                        STT(ckvn[:, c, c0:c0 + 512], ckv[:, c, :], vcol(V_GKV + c), rkv[:], ALU.mult, ALU.mult,
                            ["ckv", "rkv", "vecs"], ["ckvn%d" % bi])
                    if own:
                        for c in range(3):
                            for k in range(8):
                                MM(ps[:, B0 + c, :], wA[:, k, c * 128:(c + 1) * 128], xb[:, k, :], k == 0, k == 7,
                                   WA_ALL + [XB], [PSB(B0 + c)])
                    if own:
                        TS("dve", qcos[RB, oc0:oc0 + 512], cosb[RB, :], SCALE, None, ALU.mult, None, ["cosb"], ["qcos"])
                        TS("dve", qsin[RB, oc0:oc0 + 512], sinb[RB, :], SCALE, None, ALU.mult, None, ["sinb"], ["qsin"])
                    TT("dve", t1[RB, :], krs[RB, :], cosb[RB, :], ALU.mult, ["krs", "cosb"], ["t1"])
                    TT("dve", t2[RB, :], ps[RB, B0 + 3, :], sinb[RB, :], ALU.mult, [PSB(B0 + 3), "sinb"], ["t2"])
                    TT("dve", Kb[RB, c0:c0 + 512], t1[RB, :], t2[RB, :], ALU.add, ["t1", "t2"], ["Kr%d" % bi])
                    P.op("dve", lambda e: e.tensor_reduce(out=kmxr[RB, bi:bi + 1], in_=Kb[RB, c0:c0 + 512], axis=AX.X,
                                                          op=ALU.max, apply_absolute_value=True),
                         ["Kr%d" % bi], ["kmxr"])
                    if own:
                        for c in range(3):
                            TT("dve", cq[:, c, :], ps[:, B0 + c, :], rs_ap, ALU.mult, [PSB(B0 + c), rs_reg], ["cq"])
                        ACT(sq2[:], cq[:], AF.Square, ["cq"], ["sq2"])
                        for c in range(3):
                            MM(ps[:, B0 + 3, :], ones[:], sq2[:, c, :], c == 0, c == 2, ["ones", "sq2"], [PSB(B0 + 3)])
                        rstd_from_ps(B0 + 3, 512, 1.0 / 384, rq[:], "rq", rtmp, "rtmp")
                        for c in range(3):
                            STT(cqn[:, c, oc0:oc0 + 512], cq[:, c, :], vcol(V_GQ + c), rq[:], ALU.mult, ALU.mult,
                                ["cq", "rq", "vecs"], ["cqn"])

                for bn, bi in enumerate(blks):
                    stage1(bn, bi)
                    if bn > 0:
                        stage2(bn - 1, blks[bn - 1])
                stage2(len(blks) - 1, blks[-1])
                end_phase()

            if stop_after == "A":
                dump(ckvn[:, 0, :], NCAT, 0)
                dump(ckvn[:, 1, :], NCAT, NCAT)
                dump(Kb[:, :], NCAT, 2 * NCAT)
                for c in range(3):
                    dump(cqn[:, c, :], T, 3 * NCAT + c * T)
                dump(rstd1[:, :], T, 3 * NCAT + 3 * T)
                dump(qcos[:, :], T, 3 * NCAT + 4 * T)
                dump(qsin[:, :], T, 3 * NCAT + 5 * T)
                debug_finish()
                return nc
            with ExitStack() as esC:
                Vb = T0(esC, "Vb", [128, 80, 192], BF16)
                Qb = [T0(esC, "Qb%d" % i, [128, T], BF16) for i in range(2)]
                Pb = [T0(esC, "Pb%d" % i, [128, 2, 512], BF16) for i in range(4)]
                accs = [T0(esC, "accs%d" % i, [128, 512], F32) for i in range(2)]
                wukv = T0(esC, "wukv", [128, 2, 1024], BF16)
                wuq = T0(esC, "wuq", [128, 3, 768], BF16)
                wqrot = T0(esC, "wqrot", [128, 3, 8, 96], BF16)
                absq = T0(esC, "absq", [128, T], BF16)
                kmxmat = T0(esC, "kmxmat", [128, 97], BF16)
                kmxn = T0(esC, "kmxn", [128, NBLK], F32)
                kmxf = T0(esC, "kmxf", [128, 2], F32)
                rl = T0(esC, "rl", [128, 512], F32)
                bc = T0(esC, "bc", [128, 512], F32)
                t1 = T0(esC, "t1C", [128, 512], F32)
                t2 = T0(esC, "t2C", [128, 512], F32)

                DMA("pool", wukv[:], kp(w_ukv), [], ["wukv"])
                DMA("pool", wuq[:], kp(w_uq), [], ["wuq"])
                w_uq4 = w_uq.rearrange("(k p) (h c) -> p k h c", p=128, c=96)
                MS("dve", wqrot[:, :, :, 0:64], 0.0, ["wqrot0"])
                for c in range(3):
                    DMA("pool", wqrot[:, c, :, 64:80], w_uq4[:, c, :, 80:96], [], ["wqrot1_%d" % c])
                    DMA("pool", wqrot[:, c, :, 80:96], w_uq4[:, c, :, 64:80], [], ["wqrot2_%d" % c])
                TS("dve", wqrot[:, :, :, 64:80], wqrot[:, :, :, 64:80], -1.0, None, ALU.mult, None, ["wqrot1_0", "wqrot1_1", "wqrot1_2"], ["wqrot1"])
                WQR = ["wqrot0", "wqrot1", "wqrot2_0", "wqrot2_1", "wqrot2_2"]
                for i in range(2):
                    DMA("pool", Qb[i][97:113, :], qa, [], ["Qm%d" % i])
                MS("pool", Vb[:, :, 64:65], 1.0, ["Vc1"])
                MS("pool", Vb[:, :, 65:128], 0.0, ["Vc0"])
                MS("dve", kmxmat[:], 0.0, ["kmxmat"])
                MS("dve", kmxn[:], 0.0, ["kmxn"])
                P.op("dve", lambda e: e.tensor_reduce(out=kmxf[RB, 1:2], in_=kmxr[RB, :], axis=AX.X, op=ALU.max), ["kmxr"], ["kmxf1"])
                TS("dve", kmxmat[RB, 96:97], kmxf[RB, 1:2], 1.01, None, ALU.mult, None, ["kmxf1", "kmxmat"], ["kmxmat_r"])

                kv_blocks = [b for b in range(NBLK) if b != 15]

                def prepK(h, bi):
                    cols = bi * 512
                    for c in range(2):
                        MM(ps[0:64, 7, :], wukv[:, c, h * 128:h * 128 + 64], ckvn[:, c, cols:cols + 512], c == 0, c == 1,
                           ["wukv", "ckvn%d" % bi], [PSB(7)])
                    CP("dve", Kb[0:64, cols:cols + 512], ps[0:64, 7, :], [PSB(7)], ["Kn%d" % bi])
                    P.op("dve", lambda e: e.tensor_reduce(out=kmxn[0:64, bi:bi + 1], in_=ps[0:64, 7, :], axis=AX.X, op=ALU.max,
                                                          apply_absolute_value=True), [PSB(7)], ["kmxn"])

                def prepV(h, bi):
                    cols = bi * 512
                    vdat = 0 if h % 2 == 0 else 128
                    for t in range(4):
                        for c in range(2):
                            MM(ps[:, 7, t * 64:(t + 1) * 64], ckvn[:, c, cols + t * 128:cols + (t + 1) * 128],
                               wukv[:, c, h * 128 + 64:h * 128 + 128], c == 0, c == 1, ["wukv", "ckvn%d" % bi], [PSB(7)], skip_group_check=True)
                    CP("dve", Vb[:, bi * 4:(bi + 1) * 4, vdat:vdat + 64], ps[:, 7, 0:256].rearrange("p (t d) -> p t d", d=64),
                       [PSB(7)], ["V%d_%d" % (h % 2, bi)])

                def prepKV(h, bi):
                    prepK(h, bi)
                    prepV(h, bi)

                def prepQ(h, tbs=(0, 1, 2, 3)):
                    qb = h % 2
                    for tb in tbs:
                        for hf in range(2):
                            cols = tb * 512 + hf * 256
                            cs = slice(cols, cols + 256)
                            for c in range(3):
                                MM(ps[0:96, 7, 0:256], wuq[:, c, h * 96:(h + 1) * 96], cqn[:, c, cs], c == 0, c == 2,
                                   ["wuq", "cqn"], [PSB(7)], skip_group_check=True)
                            for c in range(3):
                                MM(ps[0:96, 7, 256:512], wqrot[:, c, h, :], cqn[:, c, cs], c == 0, c == 2,
                                   WQR + ["cqn"], [PSB(7)], skip_group_check=True)
                            QN = "Qn%d_%d_%d" % (qb, tb, hf)
                            QR = "Qr%d_%d_%d" % (qb, tb, hf)
                            TS("dve", Qb[qb][0:64, cs], ps[0:64, 7, 0:256], SCALE, None, ALU.mult, None, [PSB(7)], [QN])
                            TT("dve", t1[RB, 0:256], ps[RB, 7, 0:256], qcos[RB, cs], ALU.mult, [PSB(7), "qcos"], ["t1"])
                            TT("dve", t2[RB, 0:256], ps[RB, 7, 256:512], qsin[RB, cs], ALU.mult, [PSB(7), "qsin"], ["t2"])
                            TT("dve", Qb[qb][RB, cs], t1[RB, 0:256], t2[RB, 0:256], ALU.add, ["t1", "t2"], [QR])
                            STT(absq[0:96, cs], Qb[qb][0:96, cs], -1.0, Qb[qb][0:96, cs], ALU.mult, ALU.max, [QN, QR], ["absq%d_%d" % (tb, hf)])

                def prepQstab(h):
                    qb = h % 2
                    P.op("dve", lambda e: e.tensor_reduce(out=kmxf[0:64, 0:1], in_=kmxn[0:64, :], axis=AX.X, op=ALU.max), ["kmxn"], ["kmxf0"])
                    TS("dve", kmxmat[0:64, 96:97], kmxf[0:64, 0:1], 1.01, None, ALU.mult, None, ["kmxf0", "kmxmat"], ["kmxmat_n"])
                    for tb in range(4):
                        cols = tb * 512
                        MM(ps[0:97, 7, :], kmxmat[0:96, 0:97], absq[0:96, cols:cols + 512], True, True,
                           ["kmxmat", "kmxmat_r", "kmxmat_n", "absq%d_0" % tb, "absq%d_1" % tb], [PSB(7)])
                        ACT(Qb[qb][96:97, cols:cols + 512], ps[96:97, 7, :], AF.Copy, [PSB(7)], ["Qs%d_%d" % (qb, tb)], scale=-1.0)

                grp = [0]

                def make_groups(h):
                    out = []
                    for si, s in enumerate((3, 2, 1, 0)):
                        tiles = [(kt, 0) for kt in range(4 * NSLOT_UNITS[s])] + [(64 + 4 * s + a, 128 * a) for a in range(4)]
                        ntile = len(tiles)
                        accb = 6
                        for g0 in range(0, ntile, 2):
                            gi = grp[0]
                            grp[0] += 1
                            out.append(dict(h=h, s=s, pair=tiles[g0:g0 + 2], g0=g0, ntile=ntile, accb=accb,
                                            gb=2 * (gi % 3), pi=gi % 4, last=(g0 + 2 >= ntile), si=si))
                    return out

                def emit_S(G):
                    h, s, gb, pi = G["h"], G["s"], G["gb"], G["pi"]
                    qb = h % 2
                    qc = s * 512
                    pair = G["pair"]
                    PREG = "P%d" % pi
                    qreads = ["Qn%d_%d_0" % (qb, s), "Qr%d_%d_0" % (qb, s), "Qn%d_%d_1" % (qb, s), "Qr%d_%d_1" % (qb, s), "Qs%d_%d" % (qb, s), "Qm%d" % qb]
                    for i, (kt, off) in enumerate(pair):
                        bi = kt // 4
                        diag = bi >= 16
                        MM(ps[:, gb + i, off:512], Kb[0:113, kt * 128:(kt + 1) * 128], Qb[qb][0:113, qc + off:qc + 512], True, not diag,
                           ["Kn%d" % bi, "Kr%d" % bi, "Kconst"] + qreads, [PSB(gb + i)], skip_group_check=True)
                        if diag:
                            MM(ps[:, gb + i, off:off + 128], ident[:], tri[:], False, True, ["ident", "tri"], [PSB(gb + i)], skip_group_check=True)
                    if all(off == 0 for _, off in pair) and len(pair) == 2:
                        ACT(Pb[pi][:, 0:2, :], ps[:, gb:gb + 2, :], AF.Exp, [PSB(gb), PSB(gb + 1)], [PREG])
                    else:
                        for i, (kt, off) in enumerate(pair):
                            ACT(Pb[pi][:, i, off:512], ps[:, gb + i, off:512], AF.Exp, [PSB(gb + i)], [PREG])

                def emit_PV(G):
                    h, s, gb, pi, accb = G["h"], G["s"], G["gb"], G["pi"], G["accb"]
                    voff = 0 if h % 2 == 0 else 64
                    qc = s * 512
                    PREG = "P%d" % pi
                    for i, (kt, off) in enumerate(G["pair"]):
                        bi = kt // 4
                        first = (G["g0"] + i == 0)
                        last = (G["g0"] + i == G["ntile"] - 1)
                        MM(ps[:, accb, off:512], Vb[:, kt, voff:voff + 128], Pb[pi][:, i, off:512], first, last,
                           ["V%d_%d" % (h % 2, bi), "Vc1", "Vc0", PREG], [PSB(accb)], skip_group_check=True)
                    if G["last"]:
                        r0 = 64 if h % 2 == 0 else 0
                        rows = slice(0, 64) if h % 2 == 0 else slice(64, 128)
                        ai = (h * 4 + G["si"]) % 2
                        AC = "accs%d" % ai
                        ACT(accs[ai][:], ps[:, accb, :], AF.Copy, [PSB(accb)], [AC])
                        RECIP(rl[r0:r0 + 1, :], accs[ai][r0:r0 + 1, :], [AC], ["rl"])
                        MM(ps[:, 7, :], onesf[r0:r0 + 1, :], rl[r0:r0 + 1, :], True, True, ["onesf", "rl"], [PSB(7)])
                        TT("dve", Onorm[rows, h // 2, qc:qc + 512], accs[ai][rows, :], ps[rows, 7, :], ALU.mult, [AC, PSB(7)], ["On%d_%d" % (h, s)])

                NH = dbg.get("nheads", 8)
                prepQ(0)
                for bi in kv_blocks:
                    prepKV(0, bi)
                prepQstab(0)
                freed = {3: [11, 12, 13, 14, 19], 2: [7, 8, 9, 10, 18], 1: [3, 4, 5, 6, 17], 0: [0, 1, 2, 16]}
                pend = []
                EVERY = dbg.get("every", 2)
                for h in range(NH):
                    tasks = []
                    if h + 1 < NH:
                        for tb in range(4):
                            tasks.append(lambda h=h, tb=tb: prepQ(h + 1, (tb,)))
                        for bi in kv_blocks:
                            tasks.append(lambda h=h, bi=bi: prepV(h + 1, bi))
                    groups = make_groups(h)
                    for gi_, G in enumerate(groups):
                        emit_S(G)
                        pend.append(G)
                        if len(pend) > 2:
                            emit_PV(pend.pop(0))
                        if G["last"] and h + 1 < NH:
                            newt = [(lambda h=h, bi=bi: prepK(h + 1, bi)) for bi in freed[G["s"]]]
                            tasks = newt + tasks
                        if tasks and gi_ % EVERY == 0:
                            tasks.pop(0)()
                    while tasks:
                        tasks.pop(0)()
                    if h + 1 < NH:
                        prepQstab(h + 1)
                while pend:
                    emit_PV(pend.pop(0))
                end_phase()

            if stop_after == "C":
                for hp in range(4):
                    dump(Onorm[:, hp, :], T, hp * T)
                debug_finish()
                return nc
        def load_own_block(tb, xb, hT=None):
            col0 = SEQ + tb * 512
            for k in range(8):
                slot = ring[0] % 4
                ring[0] += 1
                XS = "xst%d" % slot
                DMA("sp", xst[:, slot, :], xcat_v[:, k, col0:col0 + 512], [], [XS])
                TS("dve", xb[:, k, :], xst[:, slot, :], vcol(V_GMIX + k), None, ALU.mult, None, [XS, "vecs"], ["xb%d" % k])
                if hT is not None:
                    ACT(hT[:, k, tb * 512:(tb + 1) * 512], xst[:, slot, :], AF.Copy, [XS], ["h%d_%d" % (k, tb)])
            return

        def load_own_block_h(tb, xb, hT):
            col0 = SEQ + tb * 512
            for k in range(8):
                HR = "h%d_%d" % (k, tb)
                DMA("sp", hT[:, k, tb * 512:(tb + 1) * 512], xcat_v[:, k, col0:col0 + 512], [], [HR])
                TS("dve", xb[:, k, :], hT[:, k, tb * 512:(tb + 1) * 512], vcol(V_GMIX + k), None, ALU.mult, None, [HR, "vecs"], ["xb%d" % k])

        def blk_stats(hT, tb, hsq, rout, rout_reg, rtmp):
            ACT(hsq[:], hT[:, :, tb * 512:(tb + 1) * 512], AF.Square, ["h%d_%d" % (k, tb) for k in range(8)], ["hsq"])
            for k in range(8):
                MM(ps[:, 0, :], ones[:], hsq[:, k, :], k == 0, k == 7, ["ones", "hsq"], [PSB(0)])
            rstd_from_ps(0, 512, 1.0 / D, rout, rout_reg, rtmp, "rtmpS")

        with ExitStack() as esH:
            hT = T0(esH, "hT", [128, 8, T], F32)
            with ExitStack() as esCA:
                convact = T0(esCA, "convact", [128, 4, T], BF16)
                with ExitStack() as esZ:
                    zT = T0(esZ, "zT", [128, 4, 4, 544], BF16)
                    with ExitStack() as esD:
                        wconv = T0(esD, "wconv", [128, 8, 1024], BF16)
                        xb = T0(esD, "xbD", [128, 8, 512], BF16)
                        xh = T0(esD, "xh", [128, 8, 32], F32)
                        xbh = T0(esD, "xbh", [128, 8, 32], BF16)
                        sqh = T0(esD, "sqh", [128, 8, 32], BF16)
                        rh = T0(esD, "rh", [128, 32], F32)
                        rtmpD = T0(esD, "rtmpD", [128, 32], F32)
                        gs = [T0(esD, "gs%d" % i, [128, 512], F32) for i in range(2)]
                        sg = [T0(esD, "sg%d" % i, [128, 512], F32) for i in range(2)]
                        as_ = [T0(esD, "as%d" % i, [128, 512], F32) for i in range(2)]
                        gsh = T0(esD, "gsh", [128, 32], F32)
                        sgh = T0(esD, "sgh", [128, 32], F32)
                        ash = T0(esD, "ash", [128, 32], F32)
                        DMA("pool", wconv[:, :, 0:512], w_in_v[:, :, 0:512], [], ["wconv0"])
                        DMA("pool", wconv[:, :, 512:1024], w_in_v[:, :, 512:1024], [], ["wconv1"])
                        for tb in range(4):
                            load_own_block(tb, xb)
                            DMA("sp", xh[:], xcat_v[:, :, NCAT + tb * 32:NCAT + (tb + 1) * 32], [], ["xh"])
                            for k in range(8):
                                TS("dve", xbh[:, k, :], xh[:, k, :], vcol(V_GMIX + k), None, ALU.mult, None, ["xh", "vecs"], ["xbh"])
                            ACT(sqh[:], xh[:], AF.Square, ["xh"], ["sqh"])
                            for k in range(8):
                                MM(ps[:, 6, 0:32], ones[:], sqh[:, k, :], k == 0, k == 7, ["ones", "sqh"], [PSB(6)])
                            rstd_from_ps(6, 32, 1.0 / D, rh[:], "rh", rtmpD, "rtmpD")
                            rs = rstd1[:, tb * 512:(tb + 1) * 512]
                            for cc in range(4):
                                ba = 2 * (cc % 2)
                                bg = ba + 1
                                i2 = cc % 2
                                for k in range(8):
                                    MM(ps[:, ba, :], wconv[:, k, cc * 128:(cc + 1) * 128], xb[:, k, :], k == 0, k == 7, ["wconv0", XBK[k]], [PSB(ba)])
                                for k in range(8):
                                    MM(ps[:, bg, :], wconv[:, k, 512 + cc * 128:512 + (cc + 1) * 128], xb[:, k, :], k == 0, k == 7, ["wconv1", XBK[k]], [PSB(bg)])
                                for k in range(8):
                                    MM(ps[:, 4, cc * 32:(cc + 1) * 32], wconv[:, k, cc * 128:(cc + 1) * 128], xbh[:, k, :], k == 0, k == 7,
                                       ["wconv0", "xbh"], [PSB(4)], skip_group_check=True)
                                for k in range(8):
                                    MM(ps[:, 5, cc * 32:(cc + 1) * 32], wconv[:, k, 512 + cc * 128:512 + (cc + 1) * 128], xbh[:, k, :], k == 0, k == 7,
                                       ["wconv1", "xbh"], [PSB(5)], skip_group_check=True)
                                TT("dve", gs[i2][:], ps[:, bg, :], rs, ALU.mult, [PSB(bg), "rstd1"], ["gs%d" % i2])
                                ACT(sg[i2][:], gs[i2][:], AF.Sigmoid, ["gs%d" % i2], ["sg%d" % i2])
                                TT("dve", as_[i2][:], ps[:, ba, :], rs, ALU.mult, [PSB(ba), "rstd1"], ["as%d" % i2])
                                TT("dve", zT[:, cc, tb, 32:544], as_[i2][:], sg[i2][:], ALU.mult, ["as%d" % i2, "sg%d" % i2], ["z%d_%d" % (cc, tb)])
                                TT("dve", gsh[:], ps[:, 5, cc * 32:(cc + 1) * 32], rh[:], ALU.mult, [PSB(5), "rh"], ["gsh"])
                                ACT(sgh[:], gsh[:], AF.Sigmoid, ["gsh"], ["sgh"])
                                TT("dve", ash[:], ps[:, 4, cc * 32:(cc + 1) * 32], rh[:], ALU.mult, [PSB(4), "rh"], ["ash"])
                                TT("dve", zT[:, cc, tb, 0:32], ash[:], sgh[:], ALU.mult, ["ash", "sgh"], ["zh%d_%d" % (cc, tb)])
                        end_phase()
                    with ExitStack() as esD:
                        diag = T0(esD, "diag", [128, 4, 31, 128], BF16)
                        cv = T0(esD, "cv", [128, 4, 512], F32)
                        cvb = T0(esD, "cvb", [128, 4, 512], BF16)
                        cvsq = T0(esD, "cvsq", [128, 4, 512], BF16)
                        mean = T0(esD, "mean", [128, 512], F32)
                        msq = T0(esD, "msq", [128, 512], F32)
                        var = T0(esD, "var", [128, 512], F32)
                        sd = T0(esD, "sd", [128, 512], F32)
                        rsl = T0(esD, "rsl", [128, 512], F32)
                        y1 = [T0(esD, "y1_%d" % i, [128, 512], F32) for i in range(2)]
                        y2 = [T0(esD, "y2_%d" % i, [128, 512], F32) for i in range(2)]
                        for cc in range(4):
                            for tau in range(31):
                                TS("dve", diag[:, cc, tau, :], ident[:], vcol(V_CONVW + cc * 31 + tau), None, ALU.mult, None, ["ident", "vecs"], ["diag%d" % cc])
                        for tb in range(4):
                            for cc in range(4):
                                for tau in range(31):
                                    MM(ps[:, cc, :], diag[:, cc, tau, :], zT[:, cc, tb, tau + 2:tau + 2 + 512], tau == 0, tau == 30,
                                       ["diag%d" % cc, "z%d_%d" % (cc, tb), "zh%d_%d" % (cc, tb)], [PSB(cc)])
                                ACT(cv[:, cc, :], ps[:, cc, :], AF.Identity, [PSB(cc), "vecs"], ["cv%d" % cc], bias=vcol(V_CONVB + cc))
                                CP("dve", cvb[:, cc, :], cv[:, cc, :], ["cv%d" % cc], ["cvb%d" % cc])
                                ACT(cvsq[:, cc, :], cv[:, cc, :], AF.Square, ["cv%d" % cc], ["cvsq%d" % cc])
                            for cc in range(4):
                                MM(ps[:, 4, :], ones[:], cvb[:, cc, :], cc == 0, cc == 3, ["ones", "cvb%d" % cc], [PSB(4)])
                            for cc in range(4):
                                MM(ps[:, 5, :], ones[:], cvsq[:, cc, :], cc == 0, cc == 3, ["ones", "cvsq%d" % cc], [PSB(5)])
                            TS("dve", mean[:], ps[:, 4, :], 1.0 / 512, None, ALU.mult, None, [PSB(4)], ["mean"])
                            TT("dve", msq[:], mean[:], mean[:], ALU.mult, ["mean"], ["msq"])
                            STT(var[:], ps[:, 5, :], 1.0 / 512, msq[:], ALU.mult, ALU.subtract, [PSB(5), "msq"], ["var"])
                            TS("dve", var[:], var[:], 0.0, None, ALU.max, None, ["var"], ["var"])
                            ACT(sd[:], var[:], AF.Sqrt, ["var", "epsb"], ["sd"], bias=epsb[:], scale=1.0)
                            RECIP(rsl[:], sd[:], ["sd"], ["rsl"])
                            for cc in range(4):
                                i2 = cc % 2
                                TT("dve", y1[i2][:], cv[:, cc, :], mean[:], ALU.subtract, ["cv%d" % cc, "mean"], ["y1_%d" % i2])
                                TT("dve", y2[i2][:], y1[i2][:], rsl[:], ALU.mult, ["y1_%d" % i2, "rsl"], ["y2_%d" % i2])
                                ACT(convact[:, cc, tb * 512:(tb + 1) * 512], y2[i2][:], AF.Silu, ["y2_%d" % i2, "vecs"], ["ca%d_%d" % (cc, tb)],
                                    scale=vcol(V_LNG + cc), bias=vcol(V_LNB + cc))
                        end_phase()
                if stop_after == "D1":
                    for cc in range(4):
                        dump(convact[:, cc, :], T, cc * T)
                    debug_finish()
                    return nc
                with ExitStack() as esD:
                    xb = T0(esD, "xbD2", [128, 8, 512], BF16)
                    wco = T0(esD, "wco", [128, 4, 1024], BF16)
                    wmo = T0(esD, "wmo", [128, 4, 1024], BF16)
                    wout = T0(esD, "wout", [128, 8, 1024], BF16)
                    gw = [T0(esD, "gw%d" % i, [128, 8, 512], BF16) for i in range(2)]
                    sig = T0(esD, "sig", [128, 16, 512], BF16)
                    mg = T0(esD, "mg", [128, 8, 512], BF16)
                    gs = [T0(esD, "gsD%d" % i, [128, 512], F32) for i in range(2)]
                    m1 = [T0(esD, "m1_%d" % i, [128, 512], F32) for i in range(2)]
                    m2 = [T0(esD, "m2_%d" % i, [128, 512], F32) for i in range(2)]
                    DMA("pool", wco[:], kp(w_conv_out), [], ["wco"])
                    DMA("pool", wmo[:], kp(w_mla_out), [], ["wmo"])
                    w_out_v = kp(w_out)
                    DMA("pool", wout[:, :, 0:512], w_out_v[:, :, 0:512], [], ["wout0"])
                    DMA("pool", wout[:, :, 512:1024], w_out_v[:, :, 512:1024], [], ["wout1"])
                    gcount = 0
                    for tb in range(4):
                        tc_ = slice(tb * 512, (tb + 1) * 512)
                        load_own_block(tb, xb, hT)
                        for gi in range(4):
                            gb_ = gcount % 2
                            gcount += 1
                            DMA("pool", gw[gb_][:], w_in_v[:, :, 1696 + gi * 512:1696 + (gi + 1) * 512], [], ["gw%d" % gb_])
                            for j in range(4):
                                oc = gi * 4 + j
                                bank = oc % 4
                                i2 = oc % 2
                                for k in range(8):
                                    MM(ps[:, bank, :], gw[gb_][:, k, j * 128:(j + 1) * 128], xb[:, k, :], k == 0, k == 7, ["gw%d" % gb_, XBK[k]], [PSB(bank)])
                                TT("dve", gs[i2][:], ps[:, bank, :], rstd1[:, tc_], ALU.mult, [PSB(bank), "rstd1"], ["gsD%d" % i2])
                                ACT(sig[:, oc, :], gs[i2][:], AF.Sigmoid, ["gsD%d" % i2], ["sig%d" % oc])
                        for c in range(8):
                            b1 = 4 + (c % 2) * 2
                            b2 = b1 + 1
                            i2 = c % 2
                            for k4 in range(4):
                                MM(ps[:, b1, :], wco[:, k4, c * 128:(c + 1) * 128], convact[:, k4, tc_], k4 == 0, k4 == 3,
                                   ["wco", "ca%d_%d" % (k4, tb)], [PSB(b1)])
                            for hp in range(4):
                                MM(ps[:, b2, :], wmo[:, hp, c * 128:(c + 1) * 128], Onorm[:, hp, tc_], hp == 0, hp == 3,
                                   ["wmo", "On%d_%d" % (2 * hp, tb), "On%d_%d" % (2 * hp + 1, tb)], [PSB(b2)])
                            TT("dve", m1[i2][:], ps[:, b1, :], sig[:, c, :], ALU.mult, [PSB(b1), "sig%d" % c], ["m1_%d" % i2])
                            TT("dve", m2[i2][:], ps[:, b2, :], sig[:, 8 + c, :], ALU.mult, [PSB(b2), "sig%d" % (8 + c)], ["m2_%d" % i2])
                            TT("dve", mg[:, c, :], m1[i2][:], m2[i2][:], ALU.add, ["m1_%d" % i2, "m2_%d" % i2], ["mg%d" % c])
                        for c in range(8):
                            bank = c % 4
                            for k in range(8):
                                MM(ps[:, bank, :], wout[:, k, c * 128:(c + 1) * 128], mg[:, k, :], k == 0, k == 7,
                                   ["wout0", "wout1", "mg%d" % k], [PSB(bank)])
                            TT("dve", hT[:, c, tc_], ps[:, bank, :], hT[:, c, tc_], ALU.add, [PSB(bank), "h%d_%d" % (c, tb)], ["h%d_%d" % (c, tb)])
                    end_phase()
            if stop_after == "D2":
                for c in range(8):
                    dump(hT[:, c, :], T, c * T)
                debug_finish()
                return nc
            with ExitStack() as esE:
                memx = T0(esE, "memx", [128, 8, 256], F32)
                memb = T0(esE, "memb", [128, 8, 256], BF16)
                msqm = T0(esE, "msqm", [128, 8, 256], BF16)
                rmem = T0(esE, "rmem", [128, 256], F32)
                rmemT = T0(esE, "rmemT", [128, 2], F32)
                rtmpE = T0(esE, "rtmpE", [128, 512], F32)
                wxkv = T0(esE, "wxkv", [128, 8, 1024], BF16)
                wxq = T0(esE, "wxq", [128, 8, 512], BF16)
                wxo = T0(esE, "wxo", [128, 4, 1024], BF16)
                Kx = T0(esE, "Kx", [128, 4, 256], BF16)
                Vx = T0(esE, "Vx", [128, 2, 512], BF16)
                kxm = T0(esE, "kxm", [128, 4], F32)
                kmm = T0(esE, "kmm", [128, 4, 128], BF16)
                hb = T0(esE, "hb", [128, 8, 512], BF16)
                hsq = T0(esE, "hsqE", [128, 8, 512], BF16)
                rstd2 = T0(esE, "rstd2", [128, 512], F32)
                Qx = T0(esE, "Qx", [128, 4, 512], BF16)
                aq = T0(esE, "aq", [128, 4, 512], BF16)
                Px = [T0(esE, "Px%d" % i, [128, 512], BF16) for i in range(2)]
                lr = T0(esE, "lr", [128, 512], F32)
                Ox = T0(esE, "Ox", [128, 4, 512], BF16)
                DMA("sp", memx[:], kp(memT), [], ["memx"])
                w_xkv_v = kp(w_xkv)
                DMA("pool", wxkv[:, :, 0:512], w_xkv_v[:, :, 0:512], [], ["wxkv0"])
                DMA("pool", wxkv[:, :, 512:1024], w_xkv_v[:, :, 512:1024], [], ["wxkv1"])
                DMA("pool", wxq[:], kp(w_xq), [], ["wxq"])
                DMA("pool", wxo[:], kp(w_xo), [], ["wxo"])
                for k in range(8):
                    TS("dve", memb[:, k, :], memx[:, k, :], vcol(V_GMEM + k), None, ALU.mult, None, ["memx", "vecs"], ["memb"])
                ACT(msqm[:], memx[:], AF.Square, ["memx"], ["msqm"])
                for k in range(8):
                    MM(ps[:, 0, 0:256], ones[:], msqm[:, k, :], k == 0, k == 7, ["ones", "msqm"], [PSB(0)])
                rstd_from_ps(0, 256, 1.0 / D, rmem[:], "rmem", rtmpE, "rtmpE")
                for kt in range(2):
                    for k in range(8):
                        MM(ps[:, 1, kt:kt + 1], msqm[:, k, kt * 128:(kt + 1) * 128], ones[:, 0:1], k == 0, k == 7, ["ones", "msqm"], [PSB(1)],
                           skip_group_check=True)
                rstd_from_ps(1, 2, 1.0 / D, rmemT[:], "rmemT", rtmpE, "rtmpE")
                for h in range(4):
                    bank = 2 + h % 2
                    for k in range(8):
                        MM(ps[:, bank, 0:256], wxkv[:, k, h * 128:(h + 1) * 128], memb[:, k, :], k == 0, k == 7, ["wxkv0", "memb"], [PSB(bank)])
                    TT("dve", Kx[:, h, :], ps[:, bank, 0:256], rmem[:], ALU.mult, [PSB(bank), "rmem"], ["Kx%d" % h])
                    P.op("dve", lambda e, h=h: e.tensor_reduce(out=kxm[:, h:h + 1], in_=Kx[:, h, :], axis=AX.X, op=ALU.max, apply_absolute_value=True),
                         ["Kx%d" % h], ["kxm%d" % h])
                    TS("dve", kmm[:, h, :], ones[:], kxm[:, h:h + 1], -1.01, ALU.mult, ALU.mult, ["ones", "kxm%d" % h], ["kmm%d" % h])
                for kt in range(2):
                    for k in range(8):
                        MM(ps[:, 4 + kt, :], memb[:, k, kt * 128:(kt + 1) * 128], wxkv[:, k, 512:1024], k == 0, k == 7, ["wxkv1", "memb"], [PSB(4 + kt)])
                    TS("dve", Vx[:, kt, :], ps[:, 4 + kt, :], rmemT[:, kt:kt + 1], None, ALU.mult, None, [PSB(4 + kt), "rmemT"], ["Vx%d" % kt])
                XS_ = 128.0 ** -0.5
                for tb in range(4):
                    tc_ = slice(tb * 512, (tb + 1) * 512)
                    HR = ["h%d_%d" % (k, tb) for k in range(8)]
                    for k in range(8):
                        TS("dve", hb[:, k, :], hT[:, k, tc_], vcol(V_GX + k), None, ALU.mult, None, [HR[k], "vecs"], ["hb%d" % k])
                    blk_stats(hT, tb, hsq, rstd2[:], "rstd2", rtmpE)
                    for h in range(4):
                        for k in range(8):
                            MM(ps[:, 1, :], wxq[:, k, h * 128:(h + 1) * 128], hb[:, k, :], k == 0, k == 7, ["wxq", "hb%d" % k], [PSB(1)])
                        STT(Qx[:, h, :], ps[:, 1, :], XS_, rstd2[:], ALU.mult, ALU.mult, [PSB(1), "rstd2"], ["Qx%d" % h])
                        STT(aq[:, h, :], Qx[:, h, :], -1.0, Qx[:, h, :], ALU.mult, ALU.max, ["Qx%d" % h], ["aq%d" % h])
                        for kt in range(2):
                            bank = 2 + kt
                            MM(ps[:, bank, :], Kx[:, h, kt * 128:(kt + 1) * 128], Qx[:, h, :], True, False, ["Kx%d" % h, "Qx%d" % h], [PSB(bank)])
                            MM(ps[:, bank, :], kmm[:, h, :], aq[:, h, :], False, True, ["kmm%d" % h, "aq%d" % h], [PSB(bank)])
                            ACT(Px[kt][:], ps[:, bank, :], AF.Exp, [PSB(bank)], ["Px%d" % kt])
                        for kt in range(2):
                            MM(ps[:, 4, :], Vx[:, kt, h * 128:(h + 1) * 128], Px[kt][:], kt == 0, kt == 1, ["Vx%d" % kt, "Px%d" % kt], [PSB(4)])
                        for kt in range(2):
                            MM(ps[:, 5, :], ones[:], Px[kt][:], kt == 0, kt == 1, ["ones", "Px%d" % kt], [PSB(5)])
                        RECIP(lr[:], ps[:, 5, :], [PSB(5)], ["lr"])
                        TT("dve", Ox[:, h, :], ps[:, 4, :], lr[:], ALU.mult, [PSB(4), "lr"], ["Ox%d" % h])
                    for c in range(8):
                        bank = 6 + c % 2
                        for h in range(4):
                            MM(ps[:, bank, :], wxo[:, h, c * 128:(c + 1) * 128], Ox[:, h, :], h == 0, h == 3, ["wxo", "Ox%d" % h], [PSB(bank)])
                        TT("dve", hT[:, c, tc_], ps[:, bank, :], hT[:, c, tc_], ALU.add, [PSB(bank), "h%d_%d" % (c, tb)], ["h%d_%d" % (c, tb)])
                end_phase()
            if stop_after == "E":
                for c in range(8):
                    dump(hT[:, c, :], T, c * T)
                debug_finish()
                return nc
            with ExitStack() as esF:
                hb3 = T0(esF, "hb3", [128, 8, T], BF16)
                W1 = [T0(esF, "W1_%d" % i, [128, 8, 256], BF16) for i in range(2)]
                W2 = [T0(esF, "W2_%d" % i, [128, 2, 1024], BF16) for i in range(2)]
                hid = [T0(esF, "hid%d" % i, [128, 2, T], BF16) for i in range(2)]
                rstd3 = T0(esF, "rstd3", [128, T], F32)
                rstd4 = T0(esF, "rstd4", [128, 512], F32)
                hsq = T0(esF, "hsqF", [128, 8, 512], BF16)
                rtmpF = T0(esF, "rtmpF", [128, 512], F32)
                uu = [T0(esF, "uu%d" % i, [128, 512], F32) for i in range(2)]
                vv = [T0(esF, "vv%d" % i, [128, 512], F32) for i in range(2)]
                w1v = kp(w_mlp1)
                w2v = kp(w_mlp2)
                for tb in range(4):
                    tc_ = slice(tb * 512, (tb + 1) * 512)
                    for k in range(8):
                        TS("dve", hb3[:, k, tc_], hT[:, k, tc_], vcol(V_GMLP + k), None, ALU.mult, None, ["h%d_%d" % (k, tb), "vecs"], ["hb3_%d_%d" % (k, tb)])
                    blk_stats(hT, tb, hsq, rstd3[:, tc_], "rstd3_%d" % tb, rtmpF)
                cnt = 0
                for g in range(16):
                    gb_ = g % 2
                    DMA("pool", W1[gb_][:], w1v[:, :, g * 256:(g + 1) * 256], [], ["W1_%d" % gb_])
                    DMA("pool", W2[gb_][:], w2v[:, g * 2:(g + 1) * 2, :], [], ["W2_%d" % gb_])
                    for j in range(2):
                        for tb in range(4):
                            tc_ = slice(tb * 512, (tb + 1) * 512)
                            bank = cnt % 4
                            i2 = cnt % 2
                            cnt += 1
                            for k in range(8):
                                MM(ps[:, bank, :], W1[gb_][:, k, j * 128:(j + 1) * 128], hb3[:, k, tc_], k == 0, k == 7,
                                   ["W1_%d" % gb_, "hb3_%d_%d" % (k, tb)], [PSB(bank)])
                            ACT(uu[i2][:], ps[:, bank, :], AF.Relu, [PSB(bank)], ["uu%d" % i2])
                            TT("dve", vv[i2][:], uu[i2][:], rstd3[:, tc_], ALU.mult, ["uu%d" % i2, "rstd3_%d" % tb], ["vv%d" % i2])
                            TT("pool", hid[gb_][:, j, tc_], vv[i2][:], vv[i2][:], ALU.mult, ["vv%d" % i2], ["hid%d_%d_%d" % (gb_, j, tb)])
                    for c in range(8):
                        for tb in range(4):
                            tc_ = slice(tb * 512, (tb + 1) * 512)
                            bank = 4 + cnt % 4
                            cnt += 1
                            for j in range(2):
                                MM(ps[:, bank, :], W2[gb_][:, j, c * 128:(c + 1) * 128], hid[gb_][:, j, tc_], j == 0, j == 1,
                                   ["W2_%d" % gb_, "hid%d_%d_%d" % (gb_, j, tb)], [PSB(bank)])
                            TT("dve", hT[:, c, tc_], ps[:, bank, :], hT[:, c, tc_], ALU.add, [PSB(bank), "h%d_%d" % (c, tb)], ["h%d_%d" % (c, tb)])
                for tb in range(4):
                    tc_ = slice(tb * 512, (tb + 1) * 512)
                    blk_stats(hT, tb, hsq, rstd4[:], "rstd4", rtmpF)
                    for c in range(8):
                        slot = ring[0] % 4
                        ring[0] += 1
                        XS = "xst%d" % slot
                        STT(xst[:, slot, :], hT[:, c, tc_], vcol(V_GFIN + c), rstd4[:], ALU.mult, ALU.mult, ["h%d_%d" % (c, tb), "rstd4", "vecs"], [XS])
                        DMA("sp", out_d[c * 128:(c + 1) * 128, tc_], xst[:, slot, :], [XS], ["out"])
                end_phase(final=True)
    return nc


def own_chunks(j):
    return [j, 7 - j, 8 + j, 15 - j]


def make_vecs(inp):
    v = np.zeros((128, NV), np.float32)

    def colmajor(g, n):
        return np.ascontiguousarray(np.asarray(g, np.float32).reshape(n, 128).T)

    v[:, V_GMIX:V_GMIX + 8] = colmajor(inp["norm_mix_g"][0], 8)
    v[:, V_GX:V_GX + 8] = colmajor(inp["norm_xattn_g"][0], 8)
    v[:, V_GMLP:V_GMLP + 8] = colmajor(inp["norm_mlp_g"][0], 8)
    v[:, V_GFIN:V_GFIN + 8] = colmajor(inp["final_norm_g"], 8)
    v[:, V_GMEM:V_GMEM + 8] = colmajor(inp["norm_mem_g"][0], 8)
    cw = np.asarray(inp["conv_w"][0], np.float32)
    v[:, V_CONVW:V_CONVW + 124] = cw.T.reshape(4, 128, 31).transpose(1, 0, 2).reshape(128, 124)
    v[:, V_CONVB:V_CONVB + 4] = colmajor(inp["conv_b"][0], 4)
    v[:, V_LNG:V_LNG + 4] = colmajor(inp["conv_ln_g"][0], 4)
    v[:, V_LNB:V_LNB + 4] = colmajor(inp["conv_ln_b"][0], 4)
    v[:, V_GQ:V_GQ + 3] = colmajor(inp["q_norm_g"][0], 3)
    v[:, V_GKV:V_GKV + 2] = colmajor(inp["kv_norm_g"][0], 2)
    half = 16
    invf = (np.float32(10000.0) ** (-np.arange(half, dtype=np.float32) / np.float32(half))).astype(np.float32)
    v[64:80, V_INVF] = invf
    v[80:96, V_INVF] = invf
    return v


def make_core_inputs(inp, core, shared):
    b, j = core // 4, core % 4
    x = np.asarray(inp["x"], np.float32)
    pos = np.asarray(inp["positions"], np.int32)
    chunks = own_chunks(j)
    xT = shared["xT"][b]
    xcat = np.zeros((D, NCAT + 128), np.float32)
    xcat[:, :SEQ] = xT
    poscat = np.zeros((32, NCAT), np.int32)
    poscat[:, :SEQ] = pos[b][None, :]
    qa = np.zeros((16, T), np.float32)
    for s, c in enumerate(chunks):
        xcat[:, SEQ + s * 512:SEQ + (s + 1) * 512] = xT[:, c * 512:(c + 1) * 512]
        poscat[:, SEQ + s * 512:SEQ + (s + 1) * 512] = pos[b][None, c * 512:(c + 1) * 512]
        if c > 0:
            xcat[:, NCAT + s * 32:NCAT + (s + 1) * 32] = xT[:, c * 512 - 32:c * 512]
        for u in range(16):
            if u >= c:
                qa[u, s * 512:(s + 1) * 512] = NEG
    m = dict(shared["common"])
    m.update({"xcat": xcat, "poscat": poscat, "qa": qa, "memT": shared["memT"][b]})
    return m


def make_shared(inp):
    x = np.asarray(inp["x"], np.float32)
    shared = {"xT": [np.ascontiguousarray(x[b].T) for b in range(2)],
              "memT": [np.ascontiguousarray(np.asarray(inp["mem"], np.float32)[b].T) for b in range(2)]}
    ka = np.zeros((17, NCAT), np.float32)
    ka[0, :] = 1.0
    for u in range(16):
        ka[1 + u, u * 512:(u + 1) * 512] = 1.0
    cmat = np.zeros((128, 352), np.float32)
    for i in range(16):
        cmat[64 + 16 + i, 256 + 64 + i] = -1.0
        cmat[64 + i, 256 + 64 + 16 + i] = 1.0
    cmat[:, :128] = np.eye(128, dtype=np.float32)
    kk, qq = np.meshgrid(np.arange(128), np.arange(128), indexing="ij")
    cmat[:, 128:256] = np.where(kk > qq, NEG, 0.0).astype(np.float32)
    common = {"ka": ka, "cmat": cmat, "vecs": make_vecs(inp)}
    for name in ["w_in", "w_conv_out", "w_uq", "w_ukv", "w_mla_out", "w_out", "w_xq", "w_xkv", "w_xo", "w_mlp1", "w_mlp2"]:
        common[name] = np.ascontiguousarray(np.asarray(inp[name], np.float32)[0])
    shared["common"] = common
    return shared


_NC_CACHE = {}


def kernel(**inputs):
    shared = make_shared(inputs)
    in_maps = [make_core_inputs(inputs, c, shared) for c in range(8)]
    if "nc" not in _NC_CACHE:
        _NC_CACHE["nc"] = build_program()
    nc = _NC_CACHE["nc"]
    res = run_bass_kernel_spmd(nc, in_maps, core_ids=list(range(8)))
    out = np.zeros((2, SEQ, D), np.float32)
    for core in range(8):
        b, j = core // 4, core % 4
        o = res.results[core]["out"]
        for s, c in enumerate(own_chunks(j)):
            out[b, c * 512:(c + 1) * 512, :] = o[:, s * 512:(s + 1) * 512].T
    return out
```

```python
import math
import numpy as np
import concourse.bass as bass
import concourse.mybir as mybir
from concourse.alu_op_type import AluOpType as ALU
from concourse.bass_utils import run_bass_kernel_spmd

F32 = mybir.dt.float32
BF16 = mybir.dt.bfloat16
I32 = mybir.dt.int32
AF = mybir.ActivationFunctionType
AX = mybir.AxisListType

D = 1024
SEQ = 8192
T = 2048
NB = 4
NBLK = 20
NCAT = NBLK * 512
EPS = 1e-6
SCALE = 96.0 ** -0.5
NSLOT_UNITS = (3, 7, 11, 15)
NEG = -30000.0
TWO_PI = 2.0 * math.pi

V_GMIX, V_GX, V_GMLP, V_GFIN, V_GMEM = 0, 8, 16, 24, 32
V_CONVW = 40
V_CONVB = 164
V_LNG = 168
V_LNB = 172
V_GQ = 176
V_GKV = 179
V_INVF = 181
NV = 184


class Op:
    __slots__ = ("fn", "waits", "tl", "idx", "is_dma")

    def __init__(self, fn, waits, tl, idx, is_dma):
        self.fn, self.waits, self.tl, self.idx, self.is_dma = fn, waits, tl, idx, is_dma


class Prog:
    ENGS = ("pe", "act", "dve", "pool", "sp")
    COMP = ("pe", "act", "dve", "pool")

    def __init__(self, n_dma=24):
        self.ops = {e: [] for e in self.ENGS}
        self.reg = {}
        self.seen = {e: {} for e in self.ENGS}
        self.cnt = {}
        self.n_dma = n_dma
        self.rr = 0
        self.bar = {}
        self.rank = {}
        self.sigbase = {e: 0 for e in self.COMP}
        self.forced = set()

    def _add(self, eng, fn, reads, writes, tl, is_dma):
        idx = self.cnt.get(tl, 0) + 1
        self.cnt[tl] = idx
        need = dict(self.bar)

        def req(t, i):
            if need.get(t, 0) < i:
                need[t] = i

        for r in reads:
            e = self.reg.get(r)
            if e is not None and e[0] is not None:
                req(*e[0])
        for r in writes:
            e = self.reg.get(r)
            if e is not None:
                if e[0] is not None:
                    req(*e[0])
                for t, i in e[1].items():
                    req(t, i)
        if is_dma and idx > 1:
            req(tl, idx - 1)
        waits = []
        sn = self.seen[eng]
        for t, i in need.items():
            if t == eng and not is_dma:
                continue
            if sn.get(t, 0) >= i:
                continue
            sn[t] = i
            waits.append((t, i))
        self.ops[eng].append(Op(fn, waits, tl, idx, is_dma))
        for r in reads:
            e = self.reg.setdefault(r, [None, {}])
            if e[1].get(tl, 0) < idx:
                e[1][tl] = idx
        for r in writes:
            self.reg[r] = [(tl, idx), {}]
        return idx

    def op(self, eng, fn, reads=(), writes=()):
        self._add(eng, fn, reads, writes, eng, False)

    def dma(self, eng, fn, reads=(), writes=()):
        tl = "q%d" % self.rr
        self.rr = (self.rr + 1) % self.n_dma
        self._add(eng, fn, reads, writes, tl, True)

    def phase_end(self, sigfns=None):
        for eng in self.ENGS:
            for t in self.COMP:
                self.seen[eng][t] = self.cnt.get(t, 0)

    def finish_waits(self, eng):
        waits = []
        for t, i in self.cnt.items():
            if t.startswith("q") and self.seen[eng].get(t, 0) < i:
                self.seen[eng][t] = i
                waits.append((t, i))
        self.ops[eng].append(Op(None, waits, None, 0, False))

    def emit(self, block, sems):
        sig = {e: set() for e in self.COMP}
        for e in self.ENGS:
            for o in self.ops[e]:
                for t, i in o.waits:
                    if t in sig and (t, i) not in self.rank:
                        sig[t].add(i)
        for (t, i) in self.forced:
            sig[t].add(i)
        for t, s in sig.items():
            for r, i in enumerate(sorted(s)):
                self.rank[(t, i)] = self.sigbase[t] + r + 1
            self.sigbase[t] += len(s)
        rank = self.rank

        def val(t, i):
            return 16 * i if t.startswith("q") else rank[(t, i)]

        def run(e, handle):
            for o in self.ops[e]:
                for t, i in o.waits:
                    handle.wait_ge(sems[t], val(t, i))
                if o.fn is None:
                    continue
                ins = o.fn(handle)
                if o.is_dma:
                    ins.then_inc(sems[o.tl], 16)
                elif (o.tl, o.idx) in rank:
                    ins.then_inc(sems[o.tl], 1)

        if self.ops["pe"]:
            block.tensor(lambda h: run("pe", h))
        if self.ops["act"]:
            block.scalar(lambda h: run("act", h))
        if self.ops["dve"]:
            block.vector(lambda h: run("dve", h))
        if self.ops["pool"]:
            block.gpsimd(lambda h: run("pool", h))
        if self.ops["sp"]:
            block.sync(lambda h: run("sp", h))

        self.ops = {e: [] for e in self.ENGS}
        self.forced = set()


def build_program(debug=None):
    from contextlib import ExitStack
    nc = bass.Bass("TRN2", target_bir_lowering=False)
    P = Prog()
    dbg = debug or {}
    stop_after = dbg.get("stop", "Z")

    def din(name, shape, dt=F32):
        return nc.dram_tensor(name, list(shape), dt, kind="ExternalInput").ap()

    xcat = din("xcat", [D, NCAT + 128])
    poscat = din("poscat", [32, NCAT], I32)
    qa = din("qa", [16, T])
    ka = din("ka", [17, NCAT])
    cmat = din("cmat", [128, 352])
    vecs_d = din("vecs", [128, NV])
    memT = din("memT", [D, 256])
    w_in = din("w_in", [D, 3744])
    w_conv_out = din("w_conv_out", [512, D])
    w_uq = din("w_uq", [384, 768])
    w_ukv = din("w_ukv", [256, 1024])
    w_mla_out = din("w_mla_out", [512, D])
    w_out = din("w_out", [D, D])
    w_xq = din("w_xq", [D, 512])
    w_xkv = din("w_xkv", [D, 1024])
    w_xo = din("w_xo", [512, D])
    w_mlp1 = din("w_mlp1", [D, 4096])
    w_mlp2 = din("w_mlp2", [4096, D])
    out_d = nc.dram_tensor("out", [D, T], F32, kind="ExternalOutput").ap()
    dbg_d = nc.dram_tensor("dbg", [128, dbg["n"]], F32, kind="ExternalOutput").ap() if debug else None

    def kp(ap):
        return ap.rearrange("(k p) n -> p k n", p=128)

    xcat_v = kp(xcat)
    w_in_v = kp(w_in)

    def MM(out, lhsT, rhs, start, stop, reads, writes, **kw):
        P.op("pe", lambda e: e.matmul(out, lhsT=lhsT, rhs=rhs, start=start, stop=stop, **kw), reads, writes)

    def ACT(out, in_, func, reads, writes, **kw):
        P.op("act", lambda e: e.activation(out=out, in_=in_, func=func, **kw), reads, writes)

    def TT(eng, out, in0, in1, op, reads, writes):
        P.op(eng, lambda e: e.tensor_tensor(out=out, in0=in0, in1=in1, op=op), reads, writes)

    def TS(eng, out, in0, s1, s2, op0, op1, reads, writes):
        if op1 is None:
            P.op(eng, lambda e: e.tensor_scalar(out=out, in0=in0, scalar1=s1, scalar2=None, op0=op0), reads, writes)
        else:
            P.op(eng, lambda e: e.tensor_scalar(out=out, in0=in0, scalar1=s1, scalar2=s2, op0=op0, op1=op1), reads, writes)

    def STT(out, in0, scalar, in1, op0, op1, reads, writes):
        P.op("dve", lambda e: e.scalar_tensor_tensor(out=out, in0=in0, scalar=scalar, in1=in1, op0=op0, op1=op1), reads, writes)

    def CP(eng, out, in_, reads, writes):
        P.op(eng, lambda e: e.tensor_copy(out=out, in_=in_), reads, writes)

    def MS(eng, out, val, writes):
        P.op(eng, lambda e: e.memset(out, val), (), writes)

    def RECIP(out, in_, reads, writes):
        P.op("dve", lambda e: e.reciprocal(out=out, in_=in_), reads, writes)

    def DMA(eng, out, in_, reads, writes):
        P.dma(eng, lambda e: e.dma_start(out=out, in_=in_), reads, writes)

    def PSB(b):
        return "ps%d" % b

    with ExitStack() as es0:
        def T0(es, name, shape, dt):
            return es.enter_context(nc.sbuf_tensor("sb_" + name, list(shape), dt))

        ps = es0.enter_context(nc.psum_tensor("ps", [128, 8, 512], F32))
        sems = {}
        for t in list(Prog.COMP) + ["q%d" % i for i in range(P.n_dma)]:
            sems[t] = es0.enter_context(nc.semaphore("s_" + t))
        vecs = T0(es0, "vecs", [128, NV], F32)
        ident = T0(es0, "ident", [128, 128], BF16)
        tri = T0(es0, "tri", [128, 128], BF16)
        rotm = T0(es0, "rotm", [128, 96], BF16)
        ones = T0(es0, "ones", [128, 128], BF16)
        onesf = T0(es0, "onesf", [128, 128], F32)
        epsb = T0(es0, "epsb", [128, 1], F32)
        scr = T0(es0, "scr", [128, 16], F32)
        rstd1 = T0(es0, "rstd1", [128, T], F32)
        Onorm = T0(es0, "Onorm", [128, 4, T], BF16)
        xst = T0(es0, "xst", [128, 4, 512], F32)

        def vcol(c, lo=0, hi=128):
            return vecs[lo:hi, c:c + 1]

        sigfns = {
            "pe": lambda e: e.matmul(ps[0:1, 7, 0:1], lhsT=ones[0:1, 0:1], rhs=ones[0:1, 0:1], start=True, stop=True),
            "act": lambda e: e.activation(out=scr[0:1, 0:1], in_=scr[0:1, 1:2], func=AF.Copy),
            "dve": lambda e: e.memset(scr[0:1, 2:3], 0.0),
            "pool": lambda e: e.memset(scr[0:1, 3:4], 0.0),
        }

        def end_phase(final=False):
            if final:
                P.finish_waits("sp")
            else:
                P.phase_end(sigfns)
            with nc.Block() as block:
                P.emit(block, sems)

        dumps = []

        def dump(ap, n, col):
            dumps.append((ap, n, col))

        def debug_finish():
            for (ap, n, col) in dumps:
                for o in range(0, n, 512):
                    w = min(512, n - o)
                    slot = (o // 512) % 4
                    CP("dve", xst[:, slot, 0:w], ap[:, o:o + w], [], ["xst%d" % slot])
                    DMA("sp", dbg_d[:, col + o:col + o + w], xst[:, slot, 0:w], ["xst%d" % slot], ["dbgout"])
            MS("dve", xst[:, 0, :], 0.0, ["xst0"])
            for c in range(8):
                for tb in range(4):
                    DMA("sp", out_d[c * 128:(c + 1) * 128, tb * 512:(tb + 1) * 512], xst[:, 0, :], ["xst0"], ["out"])
            end_phase(final=True)

        DMA("sp", vecs[:], vecs_d, [], ["vecs"])
        DMA("pool", ident[:], cmat[:, 0:128], [], ["ident"])
        DMA("pool", tri[:], cmat[:, 128:256], [], ["tri"])
        DMA("pool", rotm[:], cmat[:, 256:352], [], ["rotm"])
        MS("dve", ones[:], 1.0, ["ones"])
        MS("dve", onesf[:], 1.0, ["onesf"])
        MS("dve", epsb[:], EPS, ["epsb"])
        MS("dve", scr[:], 0.0, ["scr"])

        def rstd_from_ps(bank, n, inv_count, out_ap, out_reg, tmp, tmp_reg):
            ACT(tmp[:, 0:n], ps[:, bank, 0:n], AF.Sqrt, [PSB(bank), "epsb"], [tmp_reg], bias=epsb[:], scale=inv_count)
            RECIP(out_ap, tmp[:, 0:n], [tmp_reg], [out_reg])

        RB = slice(64, 96)
        ring = [0]

        def load_xblock(col0, xb, gcol, sqt=None, width=512, tag="", eng="dve"):
            for k in range(8):
                slot = ring[0] % 4
                ring[0] += 1
                XS = "xst%d" % slot
                DMA("sp", xst[:, slot, 0:width], xcat_v[:, k, col0:col0 + width], [], [XS])
                TS(eng, xb[:, k, 0:width], xst[:, slot, 0:width], vcol(gcol + k), None, ALU.mult, None, [XS, "vecs"], ["xb%s%d" % (tag, k)])
                if sqt is not None:
                    ACT(sqt[:, k, 0:width], xst[:, slot, 0:width], AF.Square, [XS], ["sq%s%d" % (tag, k)])

        XBK = ["xb%d" % k for k in range(8)]
        SQK = ["sq%d" % k for k in range(8)]

        with ExitStack() as esAC:
            ckvn = T0(esAC, "ckvn", [128, 2, NCAT], BF16)
            Kb = T0(esAC, "Kb", [128, NCAT], BF16)
            cqn = T0(esAC, "cqn", [128, 3, T], BF16)
            qcos = T0(esAC, "qcos", [128, T], BF16)
            qsin = T0(esAC, "qsin", [128, T], BF16)
            kmxr = T0(esAC, "kmxr", [128, NBLK], F32)

            blks = dbg.get("blks", [b for b in range(NBLK) if b != 15])
            with ExitStack() as esA:
                xbs = [T0(esA, "xbA%d" % i, [128, 8, 512], BF16) for i in range(3)]
                sqs = [T0(esA, "sqA%d" % i, [128, 8, 512], BF16) for i in range(1)]
                wA = T0(esA, "wA", [128, 8, 832], BF16)
                ckv = T0(esA, "ckv", [128, 2, 512], F32)
                cq = T0(esA, "cq", [128, 3, 512], F32)
                sq2 = T0(esA, "sq2", [128, 3, 512], BF16)
                rtmp = T0(esA, "rtmp", [128, 512], F32)
                rstdA = T0(esA, "rstdA", [128, 512], F32)
                rkv = T0(esA, "rkv", [128, 512], F32)
                rq = T0(esA, "rq", [128, 512], F32)
                posi = T0(esA, "posi", [128, 512], I32)
                ti = T0(esA, "ti", [128, 512], I32)
                ang = T0(esA, "ang", [128, 512], F32)
                tf = T0(esA, "tf", [128, 512], F32)
                rr_ = T0(esA, "rr", [128, 512], F32)
                mm_ = T0(esA, "mm", [128, 512], F32)
                sinb = T0(esA, "sinb", [128, 512], F32)
                cosb = T0(esA, "cosb", [128, 512], F32)
                t1 = T0(esA, "t1A", [128, 512], F32)
                t2 = T0(esA, "t2A", [128, 512], F32)

                DMA("pool", wA[:, :, 0:640], w_in_v[:, :, 1024:1664], [], ["wA"])
                MS("dve", wA[:, :, 640:704], 0.0, ["wAz1"])
                MS("dve", wA[:, :, 736:800], 0.0, ["wAz2"])
                DMA("pool", wA[:, :, 704:736], w_in_v[:, :, 1664:1696], [], ["wAr"])
                DMA("pool", wA[:, :, 800:816], w_in_v[:, :, 1680:1696], [], ["wArot1"])
                DMA("pool", wA[:, :, 816:832], w_in_v[:, :, 1664:1680], [], ["wArot2"])
                TS("dve", wA[:, :, 800:816], wA[:, :, 800:816], -1.0, None, ALU.mult, None, ["wArot1"], ["wArot1"])
                WA_ALL = ["wA", "wAz1", "wAz2", "wAr", "wArot1", "wArot2"]
                for k in range(8):
                    TS("dve", wA[:, k, :], wA[:, k, :], vcol(V_GMIX + k), None, ALU.mult, None, WA_ALL + ["vecs"], WA_ALL)
                DMA("pool", Kb[96:113, :], ka, [], ["Kconst"])
                MS("dve", kmxr[:], 0.0, ["kmxr"])

                krs = T0(esA, "krs", [128, 512], BF16)

                def stage1(bn, bi):
                    c0 = bi * 512
                    xb = xbs[bn % 3]
                    sq = sqs[0]
                    XB = "xb%d" % (bn % 3)
                    SQ = "sq0"
                    B0 = 4 * (bn % 2)
                    DMA("pool", xb[:], xcat_v[:, :, c0:c0 + 512], [], [XB])
                    ACT(sq[:], xb[:], AF.Square, [XB], [SQ])
                    for k in range(8):
                        MM(ps[:, B0, :], ones[:], sq[:, k, :], k == 0, k == 7, ["ones", SQ], [PSB(B0)])
                    for c in range(2):
                        for k in range(8):
                            MM(ps[:, B0 + 1 + c, :], wA[:, k, 384 + c * 128:384 + (c + 1) * 128], xb[:, k, :], k == 0, k == 7,
                               WA_ALL + [XB], [PSB(B0 + 1 + c)])
                    for k in range(8):
                        MM(ps[0:96, B0 + 3, :], wA[:, k, 640:736], xb[:, k, :], k == 0, k == 7, WA_ALL + [XB], [PSB(B0 + 3)])

                def stage2(bn, bi):
                    own = bi >= 16
                    c0 = bi * 512
                    oc0 = (bi - 16) * 512
                    xb = xbs[bn % 3]
                    XB = "xb%d" % (bn % 3)
                    B0 = 4 * (bn % 2)
                    DMA("sp", posi[RB, :], poscat[:, c0:c0 + 512], [], ["posi"])
                    rs_ap = rstd1[:, oc0:oc0 + 512] if own else rstdA[:]
                    rs_reg = "rstd1" if own else "rstdA"
                    rstd_from_ps(B0, 512, 1.0 / D, rs_ap, rs_reg, rtmp, "rtmp")
                    for c in range(2):
                        TT("dve", ckv[:, c, :], ps[:, B0 + 1 + c, :], rs_ap, ALU.mult, [PSB(B0 + 1 + c), rs_reg], ["ckv"])
                    TT("dve", krs[RB, :], ps[RB, B0 + 3, :], rs_ap[RB, :], ALU.mult, [PSB(B0 + 3), rs_reg], ["krs"])
                    ACT(sq2[:, 0:2, :], ckv[:], AF.Square, ["ckv"], ["sq2"])
                    for c in range(2):
                        MM(ps[:, B0, :], ones[:], sq2[:, c, :], c == 0, c == 1, ["ones", "sq2"], [PSB(B0)])
                    MM(ps[0:96, B0 + 3, :], rotm[RB, :], krs[RB, :], True, True, ["rotm", "krs"], [PSB(B0 + 3)])
                    TS("dve", ang[RB, :], posi[RB, :], vcol(V_INVF, 64, 96), None, ALU.mult, None, ["posi", "vecs"], ["ang"])
                    TS("dve", ti[RB, :], ang[RB, :], 1.0 / TWO_PI, None, ALU.mult, None, ["ang"], ["ti"])
                    CP("dve", tf[RB, :], ti[RB, :], ["ti"], ["tf"])
                    STT(rr_[RB, :], tf[RB, :], -TWO_PI, ang[RB, :], ALU.mult, ALU.add, ["tf", "ang"], ["rr"])
                    TS("dve", rr_[RB, :], rr_[RB, :], math.pi, -math.pi, ALU.min, ALU.max, ["rr"], ["rr"])
                    TS("dve", mm_[RB, :], rr_[RB, :], math.pi / 2, -TWO_PI, ALU.is_gt, ALU.mult, ["rr"], ["mm"])
                    STT(mm_[RB, :], rr_[RB, :], math.pi / 2, mm_[RB, :], ALU.add, ALU.add, ["rr", "mm"], ["mm"])
                    TS("dve", mm_[RB, :], mm_[RB, :], math.pi, -math.pi, ALU.min, ALU.max, ["mm"], ["mm"])
                    ACT(sinb[RB, :], rr_[RB, :], AF.Sin, ["rr"], ["sinb"])
                    ACT(cosb[RB, :], mm_[RB, :], AF.Sin, ["mm"], ["cosb"])
                    rstd_from_ps(B0, 512, 1.0 / 256, rkv[:], "rkv", rtmp, "rtmp")
                    for c in range(2):
                        STT(ckvn[:, c, c0:c0 + 512], ckv[:, c, :], vcol(V_GKV + c), rkv[:], ALU.mult, ALU.mult,
                            ["ckv", "rkv", "vecs"], ["ckvn%d" % bi])
                    if own:
                        for c in range(3):
                            for k in range(8):
                                MM(ps[:, B0 + c, :], wA[:, k, c * 128:(c + 1) * 128], xb[:, k, :], k == 0, k == 7,
                                   WA_ALL + [XB], [PSB(B0 + c)])
                    if own:
                        TS("dve", qcos[RB, oc0:oc0 + 512], cosb[RB, :], SCALE, None, ALU.mult, None, ["cosb"], ["qcos"])
                        TS("dve", qsin[RB, oc0:oc0 + 512], sinb[RB, :], SCALE, None, ALU.mult, None, ["sinb"], ["qsin"])
                    TT("dve", t1[RB, :], krs[RB, :], cosb[RB, :], ALU.mult, ["krs", "cosb"], ["t1"])
                    TT("dve", t2[RB, :], ps[RB, B0 + 3, :], sinb[RB, :], ALU.mult, [PSB(B0 + 3), "sinb"], ["t2"])
                    TT("dve", Kb[RB, c0:c0 + 512], t1[RB, :], t2[RB, :], ALU.add, ["t1", "t2"], ["Kr%d" % bi])
                    P.op("dve", lambda e: e.tensor_reduce(out=kmxr[RB, bi:bi + 1], in_=Kb[RB, c0:c0 + 512], axis=AX.X,
                                                          op=ALU.max, apply_absolute_value=True),
                         ["Kr%d" % bi], ["kmxr"])
                    if own:
                        for c in range(3):
                            TT("dve", cq[:, c, :], ps[:, B0 + c, :], rs_ap, ALU.mult, [PSB(B0 + c), rs_reg], ["cq"])
                        ACT(sq2[:], cq[:], AF.Square, ["cq"], ["sq2"])
                        for c in range(3):
                            MM(ps[:, B0 + 3, :], ones[:], sq2[:, c, :], c == 0, c == 2, ["ones", "sq2"], [PSB(B0 + 3)])
                        rstd_from_ps(B0 + 3, 512, 1.0 / 384, rq[:], "rq", rtmp, "rtmp")
                        for c in range(3):
                            STT(cqn[:, c, oc0:oc0 + 512], cq[:, c, :], vcol(V_GQ + c), rq[:], ALU.mult, ALU.mult,
                                ["cq", "rq", "vecs"], ["cqn"])

                for bn, bi in enumerate(blks):
                    stage1(bn, bi)
                    if bn > 0:
                        stage2(bn - 1, blks[bn - 1])
                stage2(len(blks) - 1, blks[-1])
                end_phase()

            if stop_after == "A":
                dump(ckvn[:, 0, :], NCAT, 0)
                dump(ckvn[:, 1, :], NCAT, NCAT)
                dump(Kb[:, :], NCAT, 2 * NCAT)
                for c in range(3):
                    dump(cqn[:, c, :], T, 3 * NCAT + c * T)
                dump(rstd1[:, :], T, 3 * NCAT + 3 * T)
                dump(qcos[:, :], T, 3 * NCAT + 4 * T)
                dump(qsin[:, :], T, 3 * NCAT + 5 * T)
                debug_finish()
                return nc
            with ExitStack() as esC:
                Vb = T0(esC, "Vb", [128, 80, 192], BF16)
                Qb = [T0(esC, "Qb%d" % i, [128, T], BF16) for i in range(2)]
                Pb = [T0(esC, "Pb%d" % i, [128, 2, 512], BF16) for i in range(4)]
                accs = [T0(esC, "accs%d" % i, [128, 512], F32) for i in range(2)]
                wukv = T0(esC, "wukv", [128, 2, 1024], BF16)
                wuq = T0(esC, "wuq", [128, 3, 768], BF16)
                wqrot = T0(esC, "wqrot", [128, 3, 8, 96], BF16)
                absq = T0(esC, "absq", [128, T], BF16)
                kmxmat = T0(esC, "kmxmat", [128, 97], BF16)
                kmxn = T0(esC, "kmxn", [128, NBLK], F32)
                kmxf = T0(esC, "kmxf", [128, 2], F32)
                rls = [T0(esC, "rl%d" % i, [128, 512], F32) for i in range(2)]
                deferred = []
                bc = T0(esC, "bc", [128, 512], F32)
                t1 = T0(esC, "t1C", [128, 512], F32)
                t2 = T0(esC, "t2C", [128, 512], F32)

                DMA("pool", wukv[:], kp(w_ukv), [], ["wukv"])
                DMA("pool", wuq[:], kp(w_uq), [], ["wuq"])
                w_uq4 = w_uq.rearrange("(k p) (h c) -> p k h c", p=128, c=96)
                MS("dve", wqrot[:, :, :, 0:64], 0.0, ["wqrot0"])
                for c in range(3):
                    DMA("pool", wqrot[:, c, :, 64:80], w_uq4[:, c, :, 80:96], [], ["wqrot1_%d" % c])
                    DMA("pool", wqrot[:, c, :, 80:96], w_uq4[:, c, :, 64:80], [], ["wqrot2_%d" % c])
                TS("dve", wqrot[:, :, :, 64:80], wqrot[:, :, :, 64:80], -1.0, None, ALU.mult, None, ["wqrot1_0", "wqrot1_1", "wqrot1_2"], ["wqrot1"])
                WQR = ["wqrot0", "wqrot1", "wqrot2_0", "wqrot2_1", "wqrot2_2"]
                for i in range(2):
                    DMA("pool", Qb[i][97:113, :], qa, [], ["Qm%d" % i])
                MS("pool", Vb[:, :, 64:65], 1.0, ["Vc1"])
                MS("pool", Vb[:, :, 65:128], 0.0, ["Vc0"])
                MS("dve", kmxmat[:], 0.0, ["kmxmat"])
                MS("dve", kmxn[:], 0.0, ["kmxn"])
                P.op("dve", lambda e: e.tensor_reduce(out=kmxf[RB, 1:2], in_=kmxr[RB, :], axis=AX.X, op=ALU.max), ["kmxr"], ["kmxf1"])
                TS("dve", kmxmat[RB, 96:97], kmxf[RB, 1:2], 1.01, None, ALU.mult, None, ["kmxf1", "kmxmat"], ["kmxmat_r"])

                kv_blocks = [b for b in range(NBLK) if b != 15]

                def prepK(h, bi):
                    cols = bi * 512
                    for c in range(2):
                        MM(ps[0:64, 7, :], wukv[:, c, h * 128:h * 128 + 64], ckvn[:, c, cols:cols + 512], c == 0, c == 1,
                           ["wukv", "ckvn%d" % bi], [PSB(7)])
                    CP("dve", Kb[0:64, cols:cols + 512], ps[0:64, 7, :], [PSB(7)], ["Kn%d" % bi])
                    P.op("dve", lambda e: e.tensor_reduce(out=kmxn[0:64, bi:bi + 1], in_=ps[0:64, 7, :], axis=AX.X, op=ALU.max,
                                                          apply_absolute_value=True), [PSB(7)], ["kmxn"])

                def prepV(h, bi):
                    cols = bi * 512
                    vdat = 0 if h % 2 == 0 else 128
                    for t in range(4):
                        for c in range(2):
                            MM(ps[:, 7, t * 64:(t + 1) * 64], ckvn[:, c, cols + t * 128:cols + (t + 1) * 128],
                               wukv[:, c, h * 128 + 64:h * 128 + 128], c == 0, c == 1, ["wukv", "ckvn%d" % bi], [PSB(7)], skip_group_check=True)
                    CP("dve", Vb[:, bi * 4:(bi + 1) * 4, vdat:vdat + 64], ps[:, 7, 0:256].rearrange("p (t d) -> p t d", d=64),
                       [PSB(7)], ["V%d_%d" % (h % 2, bi)])

                def prepKV(h, bi):
                    prepK(h, bi)
                    prepV(h, bi)

                def prepQ(h, tbs=(0, 1, 2, 3)):
                    qb = h % 2
                    for tb in tbs:
                        for hf in range(2):
                            cols = tb * 512 + hf * 256
                            cs = slice(cols, cols + 256)
                            for c in range(3):
                                MM(ps[0:96, 7, 0:256], wuq[:, c, h * 96:(h + 1) * 96], cqn[:, c, cs], c == 0, c == 2,
                                   ["wuq", "cqn"], [PSB(7)], skip_group_check=True)
                            for c in range(3):
                                MM(ps[0:96, 7, 256:512], wqrot[:, c, h, :], cqn[:, c, cs], c == 0, c == 2,
                                   WQR + ["cqn"], [PSB(7)], skip_group_check=True)
                            QN = "Qn%d_%d_%d" % (qb, tb, hf)
                            QR = "Qr%d_%d_%d" % (qb, tb, hf)
                            TS("dve", Qb[qb][0:64, cs], ps[0:64, 7, 0:256], SCALE, None, ALU.mult, None, [PSB(7)], [QN])
                            TT("dve", t1[RB, 0:256], ps[RB, 7, 0:256], qcos[RB, cs], ALU.mult, [PSB(7), "qcos"], ["t1"])
                            TT("dve", t2[RB, 0:256], ps[RB, 7, 256:512], qsin[RB, cs], ALU.mult, [PSB(7), "qsin"], ["t2"])
                            TT("dve", Qb[qb][RB, cs], t1[RB, 0:256], t2[RB, 0:256], ALU.add, ["t1", "t2"], [QR])
                            STT(absq[0:96, cs], Qb[qb][0:96, cs], -1.0, Qb[qb][0:96, cs], ALU.mult, ALU.max, [QN, QR], ["absq%d_%d" % (tb, hf)])

                def prepQstab(h):
                    qb = h % 2
                    P.op("dve", lambda e: e.tensor_reduce(out=kmxf[0:64, 0:1], in_=kmxn[0:64, :], axis=AX.X, op=ALU.max), ["kmxn"], ["kmxf0"])
                    TS("dve", kmxmat[0:64, 96:97], kmxf[0:64, 0:1], 1.01, None, ALU.mult, None, ["kmxf0", "kmxmat"], ["kmxmat_n"])
                    for tb in range(4):
                        cols = tb * 512
                        MM(ps[0:97, 7, :], kmxmat[0:96, 0:97], absq[0:96, cols:cols + 512], True, True,
                           ["kmxmat", "kmxmat_r", "kmxmat_n", "absq%d_0" % tb, "absq%d_1" % tb], [PSB(7)])
                        ACT(Qb[qb][96:97, cols:cols + 512], ps[96:97, 7, :], AF.Copy, [PSB(7)], ["Qs%d_%d" % (qb, tb)], scale=-1.0)

                grp = [0]

                def make_groups(h):
                    out = []
                    for si, s in enumerate((3, 2, 1, 0)):
                        tiles = [(kt, 0) for kt in range(4 * NSLOT_UNITS[s])] + [(64 + 4 * s + a, 128 * a) for a in range(4)]
                        ntile = len(tiles)
                        accb = 6
                        for g0 in range(0, ntile, 2):
                            gi = grp[0]
                            grp[0] += 1
                            out.append(dict(h=h, s=s, pair=tiles[g0:g0 + 2], g0=g0, ntile=ntile, accb=accb,
                                            gb=2 * (gi % 3), pi=gi % 4, last=(g0 + 2 >= ntile), si=si))
                    return out

                def emit_S(G):
                    h, s, gb, pi = G["h"], G["s"], G["gb"], G["pi"]
                    qb = h % 2
                    qc = s * 512
                    pair = G["pair"]
                    PREG = "P%d" % pi
                    qreads = ["Qn%d_%d_0" % (qb, s), "Qr%d_%d_0" % (qb, s), "Qn%d_%d_1" % (qb, s), "Qr%d_%d_1" % (qb, s), "Qs%d_%d" % (qb, s), "Qm%d" % qb]
                    for i, (kt, off) in enumerate(pair):
                        bi = kt // 4
                        diag = bi >= 16
                        MM(ps[:, gb + i, off:512], Kb[0:113, kt * 128:(kt + 1) * 128], Qb[qb][0:113, qc + off:qc + 512], True, not diag,
                           ["Kn%d" % bi, "Kr%d" % bi, "Kconst"] + qreads, [PSB(gb + i)], skip_group_check=True)
                        if diag:
                            MM(ps[:, gb + i, off:off + 128], ident[:], tri[:], False, True, ["ident", "tri"], [PSB(gb + i)], skip_group_check=True)
                    if all(off == 0 for _, off in pair) and len(pair) == 2:
                        ACT(Pb[pi][:, 0:2, :], ps[:, gb:gb + 2, :], AF.Exp, [PSB(gb), PSB(gb + 1)], [PREG])
                    else:
                        for i, (kt, off) in enumerate(pair):
                            ACT(Pb[pi][:, i, off:512], ps[:, gb + i, off:512], AF.Exp, [PSB(gb + i)], [PREG])

                def emit_PV(G):
                    h, s, gb, pi, accb = G["h"], G["s"], G["gb"], G["pi"], G["accb"]
                    voff = 0 if h % 2 == 0 else 64
                    qc = s * 512
                    PREG = "P%d" % pi
                    for i, (kt, off) in enumerate(G["pair"]):
                        bi = kt // 4
                        first = (G["g0"] + i == 0)
                        last = (G["g0"] + i == G["ntile"] - 1)
                        MM(ps[:, accb, off:512], Vb[:, kt, voff:voff + 128], Pb[pi][:, i, off:512], first, last,
                           ["V%d_%d" % (h % 2, bi), "Vc1", "Vc0", PREG], [PSB(accb)], skip_group_check=True)
                    if G["last"]:
                        r0 = 64 if h % 2 == 0 else 0
                        rows = slice(0, 64) if h % 2 == 0 else slice(64, 128)
                        ai = (h * 4 + G["si"]) % 2
                        AC = "accs%d" % ai
                        ACT(accs[ai][:], ps[:, accb, :], AF.Copy, [PSB(accb)], [AC])
                        RL = "rl%d" % ai
                        RECIP(rls[ai][r0:r0 + 1, :], accs[ai][r0:r0 + 1, :], [AC], [RL])

                        def fin(ai=ai, r0=r0, rows=rows, h=h, s=s, qc=qc, AC=AC, RL=RL):
                            MM(ps[:, 7, :], onesf[r0:r0 + 1, :], rls[ai][r0:r0 + 1, :], True, True, ["onesf", RL], [PSB(7)])
                            TT("dve", Onorm[rows, h // 2, qc:qc + 512], accs[ai][rows, :], ps[rows, 7, :], ALU.mult, [AC, PSB(7)], ["On%d_%d" % (h, s)])
                        deferred.append([3, fin])

                NH = dbg.get("nheads", 8)
                prepQ(0)
                for bi in kv_blocks:
                    prepKV(0, bi)
                prepQstab(0)
                freed = {3: [11, 12, 13, 14, 19], 2: [7, 8, 9, 10, 18], 1: [3, 4, 5, 6, 17], 0: [0, 1, 2, 16]}
                pend = []
                EVERY = dbg.get("every", 2)
                for h in range(NH):
                    tasks = []
                    if h + 1 < NH:
                        for tb in range(4):
                            tasks.append(lambda h=h, tb=tb: prepQ(h + 1, (tb,)))
                        for bi in kv_blocks:
                            tasks.append(lambda h=h, bi=bi: prepV(h + 1, bi))
                    groups = make_groups(h)
                    for gi_, G in enumerate(groups):
                        emit_S(G)
                        pend.append(G)
                        if len(pend) > 2:
                            emit_PV(pend.pop(0))
                        for d_ in list(deferred):
                            d_[0] -= 1
                            if d_[0] <= 0:
                                deferred.remove(d_)
                                d_[1]()
                        if G["last"] and h + 1 < NH:
                            newt = [(lambda h=h, bi=bi: prepK(h + 1, bi)) for bi in freed[G["s"]]]
                            tasks = newt + tasks
                        if tasks and gi_ % EVERY == 0:
                            tasks.pop(0)()
                    while tasks:
                        tasks.pop(0)()
                    if h + 1 < NH:
                        prepQstab(h + 1)
                while pend:
                    emit_PV(pend.pop(0))
                for d_ in deferred:
                    d_[1]()
                end_phase()

            if stop_after == "C":
                for hp in range(4):
                    dump(Onorm[:, hp, :], T, hp * T)
                debug_finish()
                return nc
        def load_own_block(tb, xb, hT=None):
            col0 = SEQ + tb * 512
            for k in range(8):
                slot = ring[0] % 4
                ring[0] += 1
                XS = "xst%d" % slot
                DMA("sp", xst[:, slot, :], xcat_v[:, k, col0:col0 + 512], [], [XS])
                TS("dve", xb[:, k, :], xst[:, slot, :], vcol(V_GMIX + k), None, ALU.mult, None, [XS, "vecs"], ["xb%d" % k])
                if hT is not None:
                    ACT(hT[:, k, tb * 512:(tb + 1) * 512], xst[:, slot, :], AF.Copy, [XS], ["h%d_%d" % (k, tb)])
            return

        def load_own_block_h(tb, xb, hT):
            col0 = SEQ + tb * 512
            for k in range(8):
                HR = "h%d_%d" % (k, tb)
                DMA("sp", hT[:, k, tb * 512:(tb + 1) * 512], xcat_v[:, k, col0:col0 + 512], [], [HR])
                TS("dve", xb[:, k, :], hT[:, k, tb * 512:(tb + 1) * 512], vcol(V_GMIX + k), None, ALU.mult, None, [HR, "vecs"], ["xb%d" % k])

        def blk_stats(hT, tb, hsq, rout, rout_reg, rtmp):
            ACT(hsq[:], hT[:, :, tb * 512:(tb + 1) * 512], AF.Square, ["h%d_%d" % (k, tb) for k in range(8)], ["hsq"])
            for k in range(8):
                MM(ps[:, 0, :], ones[:], hsq[:, k, :], k == 0, k == 7, ["ones", "hsq"], [PSB(0)])
            rstd_from_ps(0, 512, 1.0 / D, rout, rout_reg, rtmp, "rtmpS")

        with ExitStack() as esH:
            hT = T0(esH, "hT", [128, 8, T], F32)
            with ExitStack() as esCA:
                convact = T0(esCA, "convact", [128, 4, T], BF16)
                with ExitStack() as esZ:
                    zT = T0(esZ, "zT", [128, 4, 4, 544], BF16)
                    with ExitStack() as esD:
                        wconv = T0(esD, "wconv", [128, 8, 1024], BF16)
                        xb = T0(esD, "xbD", [128, 8, 512], BF16)
                        xh = T0(esD, "xh", [128, 8, 32], F32)
                        xbh = T0(esD, "xbh", [128, 8, 32], BF16)
                        sqh = T0(esD, "sqh", [128, 8, 32], BF16)
                        rh = T0(esD, "rh", [128, 32], F32)
                        rtmpD = T0(esD, "rtmpD", [128, 32], F32)
                        gs = [T0(esD, "gs%d" % i, [128, 512], F32) for i in range(2)]
                        sg = [T0(esD, "sg%d" % i, [128, 512], F32) for i in range(2)]
                        as_ = [T0(esD, "as%d" % i, [128, 512], F32) for i in range(2)]
                        gsh = T0(esD, "gsh", [128, 32], F32)
                        sgh = T0(esD, "sgh", [128, 32], F32)
                        ash = T0(esD, "ash", [128, 32], F32)
                        DMA("pool", wconv[:, :, 0:512], w_in_v[:, :, 0:512], [], ["wconv0"])
                        DMA("pool", wconv[:, :, 512:1024], w_in_v[:, :, 512:1024], [], ["wconv1"])
                        for tb in range(4):
                            load_own_block(tb, xb)
                            DMA("sp", xh[:], xcat_v[:, :, NCAT + tb * 32:NCAT + (tb + 1) * 32], [], ["xh"])
                            for k in range(8):
                                TS("dve", xbh[:, k, :], xh[:, k, :], vcol(V_GMIX + k), None, ALU.mult, None, ["xh", "vecs"], ["xbh"])
                            ACT(sqh[:], xh[:], AF.Square, ["xh"], ["sqh"])
                            for k in range(8):
                                MM(ps[:, 6, 0:32], ones[:], sqh[:, k, :], k == 0, k == 7, ["ones", "sqh"], [PSB(6)])
                            rstd_from_ps(6, 32, 1.0 / D, rh[:], "rh", rtmpD, "rtmpD")
                            rs = rstd1[:, tb * 512:(tb + 1) * 512]
                            for cc in range(4):
                                ba = 2 * (cc % 2)
                                bg = ba + 1
                                i2 = cc % 2
                                for k in range(8):
                                    MM(ps[:, ba, :], wconv[:, k, cc * 128:(cc + 1) * 128], xb[:, k, :], k == 0, k == 7, ["wconv0", XBK[k]], [PSB(ba)])
                                for k in range(8):
                                    MM(ps[:, bg, :], wconv[:, k, 512 + cc * 128:512 + (cc + 1) * 128], xb[:, k, :], k == 0, k == 7, ["wconv1", XBK[k]], [PSB(bg)])
                                for k in range(8):
                                    MM(ps[:, 4, cc * 32:(cc + 1) * 32], wconv[:, k, cc * 128:(cc + 1) * 128], xbh[:, k, :], k == 0, k == 7,
                                       ["wconv0", "xbh"], [PSB(4)], skip_group_check=True)
                                for k in range(8):
                                    MM(ps[:, 5, cc * 32:(cc + 1) * 32], wconv[:, k, 512 + cc * 128:512 + (cc + 1) * 128], xbh[:, k, :], k == 0, k == 7,
                                       ["wconv1", "xbh"], [PSB(5)], skip_group_check=True)
                                TT("dve", gs[i2][:], ps[:, bg, :], rs, ALU.mult, [PSB(bg), "rstd1"], ["gs%d" % i2])
                                ACT(sg[i2][:], gs[i2][:], AF.Sigmoid, ["gs%d" % i2], ["sg%d" % i2])
                                TT("dve", as_[i2][:], ps[:, ba, :], rs, ALU.mult, [PSB(ba), "rstd1"], ["as%d" % i2])
                                TT("dve", zT[:, cc, tb, 32:544], as_[i2][:], sg[i2][:], ALU.mult, ["as%d" % i2, "sg%d" % i2], ["z%d_%d" % (cc, tb)])
                                TT("dve", gsh[:], ps[:, 5, cc * 32:(cc + 1) * 32], rh[:], ALU.mult, [PSB(5), "rh"], ["gsh"])
                                ACT(sgh[:], gsh[:], AF.Sigmoid, ["gsh"], ["sgh"])
                                TT("dve", ash[:], ps[:, 4, cc * 32:(cc + 1) * 32], rh[:], ALU.mult, [PSB(4), "rh"], ["ash"])
                                TT("dve", zT[:, cc, tb, 0:32], ash[:], sgh[:], ALU.mult, ["ash", "sgh"], ["zh%d_%d" % (cc, tb)])
                        end_phase()
                    with ExitStack() as esD:
                        diag = T0(esD, "diag", [128, 4, 31, 128], BF16)
                        cv = T0(esD, "cv", [128, 4, 512], F32)
                        cvb = T0(esD, "cvb", [128, 4, 512], BF16)
                        cvsq = T0(esD, "cvsq", [128, 4, 512], BF16)
                        mean = T0(esD, "mean", [128, 512], F32)
                        msq = T0(esD, "msq", [128, 512], F32)
                        var = T0(esD, "var", [128, 512], F32)
                        sd = T0(esD, "sd", [128, 512], F32)
                        rsl = T0(esD, "rsl", [128, 512], F32)
                        y1 = [T0(esD, "y1_%d" % i, [128, 512], F32) for i in range(2)]
                        y2 = [T0(esD, "y2_%d" % i, [128, 512], F32) for i in range(2)]
                        for cc in range(4):
                            for tau in range(31):
                                TS("dve", diag[:, cc, tau, :], ident[:], vcol(V_CONVW + cc * 31 + tau), None, ALU.mult, None, ["ident", "vecs"], ["diag%d" % cc])
                        for tb in range(4):
                            for cc in range(4):
                                for tau in range(31):
                                    MM(ps[:, cc, :], diag[:, cc, tau, :], zT[:, cc, tb, tau + 2:tau + 2 + 512], tau == 0, tau == 30,
                                       ["diag%d" % cc, "z%d_%d" % (cc, tb), "zh%d_%d" % (cc, tb)], [PSB(cc)])
                                ACT(cv[:, cc, :], ps[:, cc, :], AF.Identity, [PSB(cc), "vecs"], ["cv%d" % cc], bias=vcol(V_CONVB + cc))
                                CP("dve", cvb[:, cc, :], cv[:, cc, :], ["cv%d" % cc], ["cvb%d" % cc])
                                ACT(cvsq[:, cc, :], cv[:, cc, :], AF.Square, ["cv%d" % cc], ["cvsq%d" % cc])
                            for cc in range(4):
                                MM(ps[:, 4, :], ones[:], cvb[:, cc, :], cc == 0, cc == 3, ["ones", "cvb%d" % cc], [PSB(4)])
                            for cc in range(4):
                                MM(ps[:, 5, :], ones[:], cvsq[:, cc, :], cc == 0, cc == 3, ["ones", "cvsq%d" % cc], [PSB(5)])
                            TS("dve", mean[:], ps[:, 4, :], 1.0 / 512, None, ALU.mult, None, [PSB(4)], ["mean"])
                            TT("dve", msq[:], mean[:], mean[:], ALU.mult, ["mean"], ["msq"])
                            STT(var[:], ps[:, 5, :], 1.0 / 512, msq[:], ALU.mult, ALU.subtract, [PSB(5), "msq"], ["var"])
                            TS("dve", var[:], var[:], 0.0, None, ALU.max, None, ["var"], ["var"])
                            ACT(sd[:], var[:], AF.Sqrt, ["var", "epsb"], ["sd"], bias=epsb[:], scale=1.0)
                            RECIP(rsl[:], sd[:], ["sd"], ["rsl"])
                            for cc in range(4):
                                i2 = cc % 2
                                TT("dve", y1[i2][:], cv[:, cc, :], mean[:], ALU.subtract, ["cv%d" % cc, "mean"], ["y1_%d" % i2])
                                TT("dve", y2[i2][:], y1[i2][:], rsl[:], ALU.mult, ["y1_%d" % i2, "rsl"], ["y2_%d" % i2])
                                ACT(convact[:, cc, tb * 512:(tb + 1) * 512], y2[i2][:], AF.Silu, ["y2_%d" % i2, "vecs"], ["ca%d_%d" % (cc, tb)],
                                    scale=vcol(V_LNG + cc), bias=vcol(V_LNB + cc))
                        end_phase()
                if stop_after == "D1":
                    for cc in range(4):
                        dump(convact[:, cc, :], T, cc * T)
                    debug_finish()
                    return nc
                with ExitStack() as esD:
                    xb = T0(esD, "xbD2", [128, 8, 512], BF16)
                    wco = T0(esD, "wco", [128, 4, 1024], BF16)
                    wmo = T0(esD, "wmo", [128, 4, 1024], BF16)
                    wout = T0(esD, "wout", [128, 8, 1024], BF16)
                    gw = [T0(esD, "gw%d" % i, [128, 8, 512], BF16) for i in range(2)]
                    sig = T0(esD, "sig", [128, 16, 512], BF16)
                    mg = T0(esD, "mg", [128, 8, 512], BF16)
                    gs = [T0(esD, "gsD%d" % i, [128, 512], F32) for i in range(2)]
                    m1 = [T0(esD, "m1_%d" % i, [128, 512], F32) for i in range(2)]
                    m2 = [T0(esD, "m2_%d" % i, [128, 512], F32) for i in range(2)]
                    DMA("pool", wco[:], kp(w_conv_out), [], ["wco"])
                    DMA("pool", wmo[:], kp(w_mla_out), [], ["wmo"])
                    w_out_v = kp(w_out)
                    DMA("pool", wout[:, :, 0:512], w_out_v[:, :, 0:512], [], ["wout0"])
                    DMA("pool", wout[:, :, 512:1024], w_out_v[:, :, 512:1024], [], ["wout1"])
                    gcount = 0
                    for tb in range(4):
                        tc_ = slice(tb * 512, (tb + 1) * 512)
                        load_own_block(tb, xb, hT)
                        for gi in range(4):
                            gb_ = gcount % 2
                            gcount += 1
                            DMA("pool", gw[gb_][:], w_in_v[:, :, 1696 + gi * 512:1696 + (gi + 1) * 512], [], ["gw%d" % gb_])
                            for j in range(4):
                                oc = gi * 4 + j
                                bank = oc % 4
                                i2 = oc % 2
                                for k in range(8):
                                    MM(ps[:, bank, :], gw[gb_][:, k, j * 128:(j + 1) * 128], xb[:, k, :], k == 0, k == 7, ["gw%d" % gb_, XBK[k]], [PSB(bank)])
                                TT("dve", gs[i2][:], ps[:, bank, :], rstd1[:, tc_], ALU.mult, [PSB(bank), "rstd1"], ["gsD%d" % i2])
                                ACT(sig[:, oc, :], gs[i2][:], AF.Sigmoid, ["gsD%d" % i2], ["sig%d" % oc])
                        for c in range(8):
                            b1 = 4 + (c % 2) * 2
                            b2 = b1 + 1
                            i2 = c % 2
                            for k4 in range(4):
                                MM(ps[:, b1, :], wco[:, k4, c * 128:(c + 1) * 128], convact[:, k4, tc_], k4 == 0, k4 == 3,
                                   ["wco", "ca%d_%d" % (k4, tb)], [PSB(b1)])
                            for hp in range(4):
                                MM(ps[:, b2, :], wmo[:, hp, c * 128:(c + 1) * 128], Onorm[:, hp, tc_], hp == 0, hp == 3,
                                   ["wmo", "On%d_%d" % (2 * hp, tb), "On%d_%d" % (2 * hp + 1, tb)], [PSB(b2)])
                            TT("dve", m1[i2][:], ps[:, b1, :], sig[:, c, :], ALU.mult, [PSB(b1), "sig%d" % c], ["m1_%d" % i2])
                            TT("dve", m2[i2][:], ps[:, b2, :], sig[:, 8 + c, :], ALU.mult, [PSB(b2), "sig%d" % (8 + c)], ["m2_%d" % i2])
                            TT("dve", mg[:, c, :], m1[i2][:], m2[i2][:], ALU.add, ["m1_%d" % i2, "m2_%d" % i2], ["mg%d" % c])
                        for c in range(8):
                            bank = c % 4
                            for k in range(8):
                                MM(ps[:, bank, :], wout[:, k, c * 128:(c + 1) * 128], mg[:, k, :], k == 0, k == 7,
                                   ["wout0", "wout1", "mg%d" % k], [PSB(bank)])
                            TT("dve", hT[:, c, tc_], ps[:, bank, :], hT[:, c, tc_], ALU.add, [PSB(bank), "h%d_%d" % (c, tb)], ["h%d_%d" % (c, tb)])
                    end_phase()
            if stop_after == "D2":
                for c in range(8):
                    dump(hT[:, c, :], T, c * T)
                debug_finish()
                return nc
            with ExitStack() as esE:
                memx = T0(esE, "memx", [128, 8, 256], F32)
                memb = T0(esE, "memb", [128, 8, 256], BF16)
                msqm = T0(esE, "msqm", [128, 8, 256], BF16)
                rmem = T0(esE, "rmem", [128, 256], F32)
                rmemT = T0(esE, "rmemT", [128, 2], F32)
                rtmpE = T0(esE, "rtmpE", [128, 512], F32)
                wxkv = T0(esE, "wxkv", [128, 8, 1024], BF16)
                wxq = T0(esE, "wxq", [128, 8, 512], BF16)
                wxo = T0(esE, "wxo", [128, 4, 1024], BF16)
                Kx = T0(esE, "Kx", [128, 4, 256], BF16)
                Vx = T0(esE, "Vx", [128, 2, 512], BF16)
                kxm = T0(esE, "kxm", [128, 4], F32)
                kmm = T0(esE, "kmm", [128, 4, 128], BF16)
                hb = T0(esE, "hb", [128, 8, 512], BF16)
                hsq = T0(esE, "hsqE", [128, 8, 512], BF16)
                rstd2 = T0(esE, "rstd2", [128, 512], F32)
                Qx = T0(esE, "Qx", [128, 4, 512], BF16)
                aq = T0(esE, "aq", [128, 4, 512], BF16)
                Px = [T0(esE, "Px%d" % i, [128, 512], BF16) for i in range(2)]
                lr = T0(esE, "lr", [128, 512], F32)
                Ox = T0(esE, "Ox", [128, 4, 512], BF16)
                DMA("sp", memx[:], kp(memT), [], ["memx"])
                w_xkv_v = kp(w_xkv)
                DMA("pool", wxkv[:, :, 0:512], w_xkv_v[:, :, 0:512], [], ["wxkv0"])
                DMA("pool", wxkv[:, :, 512:1024], w_xkv_v[:, :, 512:1024], [], ["wxkv1"])
                DMA("pool", wxq[:], kp(w_xq), [], ["wxq"])
                DMA("pool", wxo[:], kp(w_xo), [], ["wxo"])
                for k in range(8):
                    TS("dve", memb[:, k, :], memx[:, k, :], vcol(V_GMEM + k), None, ALU.mult, None, ["memx", "vecs"], ["memb"])
                ACT(msqm[:], memx[:], AF.Square, ["memx"], ["msqm"])
                for k in range(8):
                    MM(ps[:, 0, 0:256], ones[:], msqm[:, k, :], k == 0, k == 7, ["ones", "msqm"], [PSB(0)])
                rstd_from_ps(0, 256, 1.0 / D, rmem[:], "rmem", rtmpE, "rtmpE")
                for kt in range(2):
                    for k in range(8):
                        MM(ps[:, 1, kt:kt + 1], msqm[:, k, kt * 128:(kt + 1) * 128], ones[:, 0:1], k == 0, k == 7, ["ones", "msqm"], [PSB(1)],
                           skip_group_check=True)
                rstd_from_ps(1, 2, 1.0 / D, rmemT[:], "rmemT", rtmpE, "rtmpE")
                for h in range(4):
                    bank = 2 + h % 2
                    for k in range(8):
                        MM(ps[:, bank, 0:256], wxkv[:, k, h * 128:(h + 1) * 128], memb[:, k, :], k == 0, k == 7, ["wxkv0", "memb"], [PSB(bank)])
                    TT("dve", Kx[:, h, :], ps[:, bank, 0:256], rmem[:], ALU.mult, [PSB(bank), "rmem"], ["Kx%d" % h])
                    P.op("dve", lambda e, h=h: e.tensor_reduce(out=kxm[:, h:h + 1], in_=Kx[:, h, :], axis=AX.X, op=ALU.max, apply_absolute_value=True),
                         ["Kx%d" % h], ["kxm%d" % h])
                    TS("dve", kmm[:, h, :], ones[:], kxm[:, h:h + 1], -1.01, ALU.mult, ALU.mult, ["ones", "kxm%d" % h], ["kmm%d" % h])
                for kt in range(2):
                    for k in range(8):
                        MM(ps[:, 4 + kt, :], memb[:, k, kt * 128:(kt + 1) * 128], wxkv[:, k, 512:1024], k == 0, k == 7, ["wxkv1", "memb"], [PSB(4 + kt)])
                    TS("dve", Vx[:, kt, :], ps[:, 4 + kt, :], rmemT[:, kt:kt + 1], None, ALU.mult, None, [PSB(4 + kt), "rmemT"], ["Vx%d" % kt])
                XS_ = 128.0 ** -0.5
                for tb in range(4):
                    tc_ = slice(tb * 512, (tb + 1) * 512)
                    HR = ["h%d_%d" % (k, tb) for k in range(8)]
                    for k in range(8):
                        TS("dve", hb[:, k, :], hT[:, k, tc_], vcol(V_GX + k), None, ALU.mult, None, [HR[k], "vecs"], ["hb%d" % k])
                    blk_stats(hT, tb, hsq, rstd2[:], "rstd2", rtmpE)
                    for h in range(4):
                        for k in range(8):
                            MM(ps[:, 1, :], wxq[:, k, h * 128:(h + 1) * 128], hb[:, k, :], k == 0, k == 7, ["wxq", "hb%d" % k], [PSB(1)])
                        STT(Qx[:, h, :], ps[:, 1, :], XS_, rstd2[:], ALU.mult, ALU.mult, [PSB(1), "rstd2"], ["Qx%d" % h])
                        STT(aq[:, h, :], Qx[:, h, :], -1.0, Qx[:, h, :], ALU.mult, ALU.max, ["Qx%d" % h], ["aq%d" % h])
                        for kt in range(2):
                            bank = 2 + kt
                            MM(ps[:, bank, :], Kx[:, h, kt * 128:(kt + 1) * 128], Qx[:, h, :], True, False, ["Kx%d" % h, "Qx%d" % h], [PSB(bank)])
                            MM(ps[:, bank, :], kmm[:, h, :], aq[:, h, :], False, True, ["kmm%d" % h, "aq%d" % h], [PSB(bank)])
                            ACT(Px[kt][:], ps[:, bank, :], AF.Exp, [PSB(bank)], ["Px%d" % kt])
                        for kt in range(2):
                            MM(ps[:, 4, :], Vx[:, kt, h * 128:(h + 1) * 128], Px[kt][:], kt == 0, kt == 1, ["Vx%d" % kt, "Px%d" % kt], [PSB(4)])
                        for kt in range(2):
                            MM(ps[:, 5, :], ones[:], Px[kt][:], kt == 0, kt == 1, ["ones", "Px%d" % kt], [PSB(5)])
                        RECIP(lr[:], ps[:, 5, :], [PSB(5)], ["lr"])
                        TT("dve", Ox[:, h, :], ps[:, 4, :], lr[:], ALU.mult, [PSB(4), "lr"], ["Ox%d" % h])
                    for c in range(8):
                        bank = 6 + c % 2
                        for h in range(4):
                            MM(ps[:, bank, :], wxo[:, h, c * 128:(c + 1) * 128], Ox[:, h, :], h == 0, h == 3, ["wxo", "Ox%d" % h], [PSB(bank)])
                        TT("dve", hT[:, c, tc_], ps[:, bank, :], hT[:, c, tc_], ALU.add, [PSB(bank), "h%d_%d" % (c, tb)], ["h%d_%d" % (c, tb)])
                end_phase()
            if stop_after == "E":
                for c in range(8):
                    dump(hT[:, c, :], T, c * T)
                debug_finish()
                return nc
            with ExitStack() as esF:
                hb3 = T0(esF, "hb3", [128, 8, T], BF16)
                W1 = [T0(esF, "W1_%d" % i, [128, 8, 256], BF16) for i in range(2)]
                W2 = [T0(esF, "W2_%d" % i, [128, 2, 1024], BF16) for i in range(2)]
                hid = [T0(esF, "hid%d" % i, [128, 2, T], BF16) for i in range(2)]
                rstd3 = T0(esF, "rstd3", [128, T], F32)
                rstd4 = T0(esF, "rstd4", [128, 512], F32)
                hsq = T0(esF, "hsqF", [128, 8, 512], BF16)
                rtmpF = T0(esF, "rtmpF", [128, 512], F32)
                uu = [T0(esF, "uu%d" % i, [128, 512], F32) for i in range(2)]
                vv = [T0(esF, "vv%d" % i, [128, 512], F32) for i in range(2)]
                w1v = kp(w_mlp1)
                w2v = kp(w_mlp2)
                for tb in range(4):
                    tc_ = slice(tb * 512, (tb + 1) * 512)
                    for k in range(8):
                        TS("dve", hb3[:, k, tc_], hT[:, k, tc_], vcol(V_GMLP + k), None, ALU.mult, None, ["h%d_%d" % (k, tb), "vecs"], ["hb3_%d_%d" % (k, tb)])
                    blk_stats(hT, tb, hsq, rstd3[:, tc_], "rstd3_%d" % tb, rtmpF)
                cnt = 0
                for g in range(16):
                    gb_ = g % 2
                    DMA("pool", W1[gb_][:], w1v[:, :, g * 256:(g + 1) * 256], [], ["W1_%d" % gb_])
                    DMA("pool", W2[gb_][:], w2v[:, g * 2:(g + 1) * 2, :], [], ["W2_%d" % gb_])
                    for j in range(2):
                        for tb in range(4):
                            tc_ = slice(tb * 512, (tb + 1) * 512)
                            bank = cnt % 4
                            i2 = cnt % 2
                            cnt += 1
                            for k in range(8):
                                MM(ps[:, bank, :], W1[gb_][:, k, j * 128:(j + 1) * 128], hb3[:, k, tc_], k == 0, k == 7,
                                   ["W1_%d" % gb_, "hb3_%d_%d" % (k, tb)], [PSB(bank)])
                            ACT(uu[i2][:], ps[:, bank, :], AF.Relu, [PSB(bank)], ["uu%d" % i2])
                            TT("dve", vv[i2][:], uu[i2][:], rstd3[:, tc_], ALU.mult, ["uu%d" % i2, "rstd3_%d" % tb], ["vv%d" % i2])
                            TT("pool", hid[gb_][:, j, tc_], vv[i2][:], vv[i2][:], ALU.mult, ["vv%d" % i2], ["hid%d_%d_%d" % (gb_, j, tb)])
                    for c in range(8):
                        for tb in range(4):
                            tc_ = slice(tb * 512, (tb + 1) * 512)
                            bank = 4 + cnt % 4
                            cnt += 1
                            for j in range(2):
                                MM(ps[:, bank, :], W2[gb_][:, j, c * 128:(c + 1) * 128], hid[gb_][:, j, tc_], j == 0, j == 1,
                                   ["W2_%d" % gb_, "hid%d_%d_%d" % (gb_, j, tb)], [PSB(bank)])
                            TT("dve", hT[:, c, tc_], ps[:, bank, :], hT[:, c, tc_], ALU.add, [PSB(bank), "h%d_%d" % (c, tb)], ["h%d_%d" % (c, tb)])
                for tb in range(4):
                    tc_ = slice(tb * 512, (tb + 1) * 512)
                    blk_stats(hT, tb, hsq, rstd4[:], "rstd4", rtmpF)
                    for c in range(8):
                        slot = ring[0] % 4
                        ring[0] += 1
                        XS = "xst%d" % slot
                        STT(xst[:, slot, :], hT[:, c, tc_], vcol(V_GFIN + c), rstd4[:], ALU.mult, ALU.mult, ["h%d_%d" % (c, tb), "rstd4", "vecs"], [XS])
                        DMA("sp", out_d[c * 128:(c + 1) * 128, tc_], xst[:, slot, :], [XS], ["out"])
                end_phase(final=True)
    return nc


def own_chunks(j):
    return [j, 7 - j, 8 + j, 15 - j]


def make_vecs(inp):
    v = np.zeros((128, NV), np.float32)

    def colmajor(g, n):
        return np.ascontiguousarray(np.asarray(g, np.float32).reshape(n, 128).T)

    v[:, V_GMIX:V_GMIX + 8] = colmajor(inp["norm_mix_g"][0], 8)
    v[:, V_GX:V_GX + 8] = colmajor(inp["norm_xattn_g"][0], 8)
    v[:, V_GMLP:V_GMLP + 8] = colmajor(inp["norm_mlp_g"][0], 8)
    v[:, V_GFIN:V_GFIN + 8] = colmajor(inp["final_norm_g"], 8)
    v[:, V_GMEM:V_GMEM + 8] = colmajor(inp["norm_mem_g"][0], 8)
    cw = np.asarray(inp["conv_w"][0], np.float32)
    v[:, V_CONVW:V_CONVW + 124] = cw.T.reshape(4, 128, 31).transpose(1, 0, 2).reshape(128, 124)
    v[:, V_CONVB:V_CONVB + 4] = colmajor(inp["conv_b"][0], 4)
    v[:, V_LNG:V_LNG + 4] = colmajor(inp["conv_ln_g"][0], 4)
    v[:, V_LNB:V_LNB + 4] = colmajor(inp["conv_ln_b"][0], 4)
    v[:, V_GQ:V_GQ + 3] = colmajor(inp["q_norm_g"][0], 3)
    v[:, V_GKV:V_GKV + 2] = colmajor(inp["kv_norm_g"][0], 2)
    half = 16
    invf = (np.float32(10000.0) ** (-np.arange(half, dtype=np.float32) / np.float32(half))).astype(np.float32)
    v[64:80, V_INVF] = invf
    v[80:96, V_INVF] = invf
    return v


def make_core_inputs(inp, core, shared):
    b, j = core // 4, core % 4
    x = np.asarray(inp["x"], np.float32)
    pos = np.asarray(inp["positions"], np.int32)
    chunks = own_chunks(j)
    xT = shared["xT"][b]
    xcat = np.zeros((D, NCAT + 128), np.float32)
    xcat[:, :SEQ] = xT
    poscat = np.zeros((32, NCAT), np.int32)
    poscat[:, :SEQ] = pos[b][None, :]
    qa = np.zeros((16, T), np.float32)
    for s, c in enumerate(chunks):
        xcat[:, SEQ + s * 512:SEQ + (s + 1) * 512] = xT[:, c * 512:(c + 1) * 512]
        poscat[:, SEQ + s * 512:SEQ + (s + 1) * 512] = pos[b][None, c * 512:(c + 1) * 512]
        if c > 0:
            xcat[:, NCAT + s * 32:NCAT + (s + 1) * 32] = xT[:, c * 512 - 32:c * 512]
        for u in range(16):
            if u >= c:
                qa[u, s * 512:(s + 1) * 512] = NEG
    m = dict(shared["common"])
    m.update({"xcat": xcat, "poscat": poscat, "qa": qa, "memT": shared["memT"][b]})
    return m


def make_shared(inp):
    x = np.asarray(inp["x"], np.float32)
    shared = {"xT": [np.ascontiguousarray(x[b].T) for b in range(2)],
              "memT": [np.ascontiguousarray(np.asarray(inp["mem"], np.float32)[b].T) for b in range(2)]}
    ka = np.zeros((17, NCAT), np.float32)
    ka[0, :] = 1.0
    for u in range(16):
        ka[1 + u, u * 512:(u + 1) * 512] = 1.0
    cmat = np.zeros((128, 352), np.float32)
    for i in range(16):
        cmat[64 + 16 + i, 256 + 64 + i] = -1.0
        cmat[64 + i, 256 + 64 + 16 + i] = 1.0
    cmat[:, :128] = np.eye(128, dtype=np.float32)
    kk, qq = np.meshgrid(np.arange(128), np.arange(128), indexing="ij")
    cmat[:, 128:256] = np.where(kk > qq, NEG, 0.0).astype(np.float32)
    common = {"ka": ka, "cmat": cmat, "vecs": make_vecs(inp)}
    for name in ["w_in", "w_conv_out", "w_uq", "w_ukv", "w_mla_out", "w_out", "w_xq", "w_xkv", "w_xo", "w_mlp1", "w_mlp2"]:
        common[name] = np.ascontiguousarray(np.asarray(inp[name], np.float32)[0])
    shared["common"] = common
    return shared


_NC_CACHE = {}


def kernel(**inputs):
    shared = make_shared(inputs)
    in_maps = [make_core_inputs(inputs, c, shared) for c in range(8)]
    if "nc" not in _NC_CACHE:
        _NC_CACHE["nc"] = build_program()
    nc = _NC_CACHE["nc"]
    res = run_bass_kernel_spmd(nc, in_maps, core_ids=list(range(8)))
    out = np.zeros((2, SEQ, D), np.float32)
    for core in range(8):
        b, j = core // 4, core % 4
        o = res.results[core]["out"]
        for s, c in enumerate(own_chunks(j)):
            out[b, c * 512:(c + 1) * 512, :] = o[:, s * 512:(s + 1) * 512].T
    return out
```
